# Optimizing a Trainium2 kernel written in Bass

```python
import math
import jax, jax.numpy as jnp
from jax import lax
import numpy as np

D_MODEL = 1024
BATCH = 8
SEQ = 2048
DEPTH = 4

CHUNK = 64
N_META = 16
N_MIXERS = 2
N_FOX = (DEPTH + 1) // 2
N_RWKV = DEPTH // 2
D_INNER = D_MODEL
FOX_HEAD_DIM = 64
FOX_HEADS = D_INNER // FOX_HEAD_DIM
Q_BLOCK = 128
FOX_IN = 4 * D_INNER + FOX_HEADS
RWKV_HEAD = 64
RWKV_HEADS = D_INNER // RWKV_HEAD
LORA_W = 64
LORA_A = 64
RWKV_IN = 4 * D_INNER + LORA_W + LORA_A
NORM_EPS = 1e-6
GN_EPS = 64e-5
DECAY_SCALE = math.exp(-0.5)

kernel_name = 'fox_rwkv7_meta_hybrid'


def _rmsnorm(x, g):
    xf = x.astype(jnp.float32)
    y = xf * lax.rsqrt(jnp.mean(xf * xf, axis=-1, keepdims=True) + NORM_EPS)
    return (y * g.astype(jnp.float32)).astype(x.dtype)


def _heads(t, n_heads, head_dim):
    B, L, _ = t.shape
    return t.reshape(B, L, n_heads, head_dim).transpose(0, 2, 1, 3)


def _fox_mixer(u, w_in, b_f, w_out):
    B, L, _ = u.shape
    p = u @ w_in
    q, k, v, gate, f_logit = jnp.split(p, [D_INNER, 2 * D_INNER, 3 * D_INNER, 4 * D_INNER], axis=-1)
    q = _heads(q, FOX_HEADS, FOX_HEAD_DIM) * (FOX_HEAD_DIM ** -0.5)
    k = _heads(k, FOX_HEADS, FOX_HEAD_DIM)
    v = _heads(v, FOX_HEADS, FOX_HEAD_DIM)
    log_f = jax.nn.log_sigmoid((f_logit + b_f).astype(jnp.float32))
    cum = jnp.cumsum(log_f, axis=1).transpose(0, 2, 1)
    outs = []
    for start in range(0, L, Q_BLOCK):
        stop = min(start + Q_BLOCK, L)
        logits = jnp.einsum('bhqd,bhkd->bhqk', q[:, :, start:stop], k[:, :, :stop]).astype(jnp.float32)
        logits = logits + cum[:, :, start:stop, None] - cum[:, :, None, :stop]
        causal = jnp.arange(stop)[None, :] <= jnp.arange(start, stop)[:, None]
        logits = jnp.where(causal, logits, -jnp.inf)
        prob = jax.nn.softmax(logits, axis=-1).astype(v.dtype)
        outs.append(jnp.einsum('bhqk,bhkd->bhqd', prob, v[:, :, :stop]))
    o = jnp.concatenate(outs, axis=2).transpose(0, 2, 1, 3).reshape(B, L, D_INNER)
    return (o * jax.nn.silu(gate)) @ w_out


def _wkv7_scan(r, decay, k, v, a_vec, b_vec):
    B, L, H, N = r.shape

    def step(S, inp):
        r_t, w_t, k_t, v_t, a_t, b_t = inp
        sa = jnp.einsum('bhvk,bhk->bhv', S, a_t)
        S = S * w_t[:, :, None, :] + sa[..., None] * b_t[:, :, None, :] + v_t[..., None] * k_t[:, :, None, :]
        return S, jnp.einsum('bhvk,bhk->bhv', S, r_t)

    seq = tuple(jnp.moveaxis(t, 1, 0) for t in (r, decay, k, v, a_vec, b_vec))
    S0 = jnp.zeros((B, H, N, N), jnp.float32)
    _, y = lax.scan(step, S0, seq)
    return jnp.moveaxis(y, 0, 1)


def _rwkv7_mixer(u, w_in, mu, w0, w_up, a0, a_up, k_k, k_a, r_k, ln_w, ln_b, w_out):
    B, L, _ = u.shape
    f32 = jnp.float32
    p = u @ w_in
    p_prev = jnp.pad(p, ((0, 0), (1, 0), (0, 0)))[:, :L]
    p = p + (p_prev - p) * mu
    r, k, v, gate, wd, ad = jnp.split(
        p, [D_INNER, 2 * D_INNER, 3 * D_INNER, 4 * D_INNER, 4 * D_INNER + LORA_W], axis=-1)
    w_log = (w0 + jnp.tanh(wd) @ w_up).astype(f32)
    decay = jnp.exp(-DECAY_SCALE * jax.nn.sigmoid(w_log))
    a = jax.nn.sigmoid((a0 + ad @ a_up).astype(f32))
    r = r.astype(f32)
    k = k.astype(f32)
    v = v.astype(f32)
    hs = lambda t: t.reshape(B, L, RWKV_HEADS, RWKV_HEAD)
    kk = hs(k * k_k.astype(f32))
    kk = kk / jnp.maximum(jnp.linalg.norm(kk, axis=-1, keepdims=True), 1e-12)
    k = k * (1.0 + (a - 1.0) * k_a.astype(f32))
    r_h, k_h, v_h = hs(r), hs(k), hs(v)
    y = _wkv7_scan(r_h, hs(decay), k_h, v_h, -kk, kk * hs(a))
    mean = jnp.mean(y, axis=-1, keepdims=True)
    var = jnp.mean(jnp.square(y - mean), axis=-1, keepdims=True)
    y = (y - mean) * lax.rsqrt(var + GN_EPS)
    y = y * ln_w.astype(f32).reshape(RWKV_HEADS, RWKV_HEAD) + ln_b.astype(f32).reshape(RWKV_HEADS, RWKV_HEAD)
    bonus = jnp.sum(r_h * k_h * r_k.astype(f32), axis=-1, keepdims=True) * v_h
    y = (y + bonus).reshape(B, L, D_INNER).astype(u.dtype)
    return (y * jax.nn.silu(gate)) @ w_out


def setup_inputs(seed: int = 0) -> dict:
    key = jax.random.key(seed)
    ks = jax.random.split(key, 20)
    f32 = jnp.float32
    D = D_MODEL
    nrm = lambda kk, shape, s: jax.random.normal(kk, shape, f32) * s
    return {
        'x': nrm(ks[0], (BATCH, SEQ, D), 1.0),
        'meta_tokens': nrm(ks[1], (N_META, D), 1.0),
        'norm_pre': 1.0 + nrm(ks[2], (DEPTH, D), 0.02),
        'norm_post': 1.0 + nrm(ks[3], (DEPTH, D), 0.02),
        'fox_w_in': nrm(ks[4], (N_FOX, D, FOX_IN), D ** -0.5),
        'fox_b_f': jax.random.uniform(ks[5], (N_FOX, FOX_HEADS), f32, 1.0, 5.0),
        'fox_w_out': nrm(ks[6], (N_FOX, D_INNER, D), D_INNER ** -0.5),
        'rwkv_w_in': nrm(ks[7], (N_RWKV, D, RWKV_IN), D ** -0.5),
        'rwkv_mu': jax.random.uniform(ks[8], (N_RWKV, RWKV_IN), f32, 0.0, 1.0),
        'rwkv_w0': -0.5 + nrm(ks[9], (N_RWKV, D_INNER), 0.5),
        'rwkv_w_up': nrm(ks[10], (N_RWKV, LORA_W, D_INNER), 0.5 * LORA_W ** -0.5),
        'rwkv_a0': nrm(ks[11], (N_RWKV, D_INNER), 0.1),
        'rwkv_a_up': nrm(ks[12], (N_RWKV, LORA_A, D_INNER), 0.5 * LORA_A ** -0.5),
        'rwkv_k_k': 0.85 + nrm(ks[13], (N_RWKV, D_INNER), 0.05),
        'rwkv_k_a': 1.0 + nrm(ks[14], (N_RWKV, D_INNER), 0.05),
        'rwkv_r_k': nrm(ks[15], (N_RWKV, RWKV_HEADS, RWKV_HEAD), 0.1),
        'rwkv_ln_w': 1.0 + nrm(ks[16], (N_RWKV, D_INNER), 0.02),
        'rwkv_ln_b': nrm(ks[17], (N_RWKV, D_INNER), 0.02),
        'rwkv_w_out': nrm(ks[18], (N_RWKV, D_INNER, D), D_INNER ** -0.5),
    }


def reference(x, meta_tokens, norm_pre, norm_post, fox_w_in, fox_b_f, fox_w_out,
              rwkv_w_in, rwkv_mu, rwkv_w0, rwkv_w_up, rwkv_a0, rwkv_a_up, rwkv_k_k,
              rwkv_k_a, rwkv_r_k, rwkv_ln_w, rwkv_ln_b, rwkv_w_out):
    B = x.shape[0]
    meta = jnp.broadcast_to(meta_tokens[None].astype(x.dtype), (B, N_META, D_MODEL))
    h = jnp.concatenate([meta, x], axis=1)
    for i in range(DEPTH):
        j = i // N_MIXERS
        u = _rmsnorm(h, norm_pre[i])
        if i % N_MIXERS == 0:
            m = _fox_mixer(u, fox_w_in[j], fox_b_f[j], fox_w_out[j])
        else:
            m = _rwkv7_mixer(u, rwkv_w_in[j], rwkv_mu[j], rwkv_w0[j], rwkv_w_up[j], rwkv_a0[j],
                             rwkv_a_up[j], rwkv_k_k[j], rwkv_k_a[j], rwkv_r_k[j], rwkv_ln_w[j],
                             rwkv_ln_b[j], rwkv_w_out[j])
        h = h + _rmsnorm(m, norm_post[i])
    return h[:, N_META:]
```

```python
import math
from contextlib import ExitStack

import numpy as np
import concourse.bass as bass
import concourse.mybir as mybir
from concourse.bass_utils import run_bass_kernel_spmd

F32 = mybir.dt.float32
BF16 = mybir.dt.bfloat16
AF = mybir.ActivationFunctionType
ALU = mybir.AluOpType
AX = mybir.AxisListType

D = 1024
SEQ = 2048
NMETA = 16
NT = 17
T = NT * 128
DEPTH = 4
NH = 16
HD = 64
FOX_IN = 4 * D + NH
RWKV_IN = 4 * D + 128
NORM_EPS = 1e-6
GN_EPS = 64e-5
DECAY_SCALE = math.exp(-0.5)
CH = 64
DBG = {"stage": 9, "pairs": 8, "tiles": NT}


def tok_chunks(n=512):
    out = []
    t0 = 0
    while t0 < T:
        m = min(n, T - t0)
        out.append((t0, m))
        t0 += m
    return out


class Buf:
    __slots__ = ("name", "lw", "rd", "excl")

    def __init__(self, name="", excl=False):
        self.name = name
        self.lw = None
        self.rd = {}
        self.excl = excl


class Q:
    def __init__(self, name, eng, sem, self_sync=True):
        self.name = name
        self.eng = eng
        self.sem = sem
        self.cnt = 0
        self.seen = {}
        self.self_sync = self_sync
        self.key = name
        self.ring = []
        self.ring_i = 0


class PEProxy:
    def __init__(self, K, eng):
        self.K = K
        self.eng = eng
        self.partial = False

    def _pre(self, st_ap, out):
        K = self.K
        rg = (st_ap.base_partition(), st_ap.partition_size())
        bank = out.name
        last = K.pe_last
        if last is not None and last[0] != rg and (last[1] == bank or (last[0][1] < 128 and rg[1] < 128)):
            assert K.pe_last_tk is not None, "previous matmul needs a semaphore increment"
            K._wait(K.pe, K.pe_last_tk)
        K.pe_last = (rg, bank)
        self.partial = rg[1] < 128

    def matmul(self, out, lhsT, rhs, **kw):
        self._pre(lhsT, out)
        return self.eng.matmul(out, lhsT=lhsT, rhs=rhs, **kw)

    def transpose(self, out, in_, identity):
        self._pre(in_, out)
        return self.eng.transpose(out=out, in_=in_, identity=identity)


class KB:
    def __init__(self, nc, st):
        self.nc = nc
        self.st = st
        mk = lambda n: st.enter_context(nc.semaphore(n))
        self.pe = Q("pe", nc.tensor, mk("s_pe"), self_sync=False)
        self.act = Q("act", nc.scalar, mk("s_act"))
        self.dve = Q("dve", nc.vector, mk("s_dve"))
        self.pool = Q("pool", nc.gpsimd, mk("s_pool"))
        self.sp = Q("sp", nc.sync, mk("s_sp"))
        self.queues = [self.pe, self.act, self.dve, self.pool, self.sp]
        for q, n in ((self.sp, 24), (self.pool, 24)):
            for i in range(n):
                q.ring.append([mk(f"d_{q.name}{i}"), 0, f"d_{q.name}{i}"])
        self.dma_tickets = []
        self.nbuf = 0
        self.counting = False
        self.nops = 0
        self.pe_last = None
        self.pe_last_tk = None
        self.prox = PEProxy(self, nc.tensor)

    def sb(self, st, name, shape, dt):
        self.nbuf += 1
        return st.enter_context(self.nc.sbuf_tensor(f"{name}_{self.nbuf}", list(shape), dt))

    def psum(self, st, name, shape, dt):
        return st.enter_context(self.nc.psum_tensor(name, list(shape), dt))

    def _wait(self, q, tk):
        sem, val, key = tk
        if q.seen.get(key, 0) >= val:
            return
        q.eng.wait_ge(sem, val)
        q.seen[key] = val

    def _deps(self, q, r, w):
        deps = []
        for b in r:
            if b.lw is not None:
                deps.append(b.lw)
            if b.excl:
                for key, tk in b.rd.items():
                    if key != q.key:
                        deps.append(tk)
        for b in w:
            if b.lw is not None:
                deps.append(b.lw)
            for key, tk in b.rd.items():
                deps.append(tk)
        for tk in deps:
            if tk[2] == q.key and not q.self_sync:
                continue
            self._wait(q, tk)

    def _record(self, tk, r, w):
        for b in r:
            old = b.rd.get(tk[2])
            if old is None or old[1] < tk[1]:
                b.rd[tk[2]] = tk
        for b in w:
            b.lw = tk
            b.rd = {}

    def op(self, q, fn, r=(), w=(), inc=True):
        if self.counting:
            self.nops += 1
            if self.nops > DBG.get("maxops", 10 ** 9):
                return None
        self._deps(q, r, w)
        if q is self.pe:
            self.prox.partial = False
            ins = fn(self.prox)
            if self.prox.partial:
                inc = True
        else:
            ins = fn(q.eng)
        if inc:
            q.cnt += 1
            ins.then_inc(q.sem, 1)
            tk = (q.sem, q.cnt, q.key)
        else:
            tk = (q.sem, q.cnt + 1, q.key)
        if q is self.pe:
            self.pe_last_tk = tk if inc else None
        self._record(tk, r, w)
        return tk

    def dma(self, q, out, in_, r=(), w=()):
        self._deps(q, r, w)
        slot = q.ring[q.ring_i % len(q.ring)]
        q.ring_i += 1
        sem, n, key = slot
        if n > 0:
            self._wait(q, (sem, 16 * n, key))
        q.eng.dma_start(out=out, in_=in_).then_inc(sem, 16)
        slot[1] = n + 1
        tk = (sem, 16 * (n + 1), key)
        self._record(tk, r, w)
        self.dma_tickets.append(tk)
        return tk

    def barrier(self):
        tks = [(q.sem, q.cnt, q.key) for q in self.queues if q.cnt > 0]
        for q in self.queues:
            for slot in q.ring:
                if slot[1] > 0:
                    tks.append((slot[0], 16 * slot[1], slot[2]))
        for q in self.queues:
            for tk in tks:
                if tk[2] == q.key:
                    continue
                self._wait(q, tk)


def build_program(nlayers=DEPTH, dbg=False):
    nc = bass.Bass("TRN2", target_bir_lowering=False)
    dt_in = lambda name, shape: nc.dram_tensor(name, list(shape), F32, kind="ExternalInput").ap()
    h0 = dt_in("h0", [T, D])
    norm_pre = dt_in("norm_pre", [DEPTH, D])
    norm_post = dt_in("norm_post", [DEPTH, D])
    fox_w_in = dt_in("fox_w_in", [2, D, FOX_IN])
    fox_b_f = dt_in("fox_b_f", [2, NH])
    fox_w_out = dt_in("fox_w_out", [2, D, D])
    rwkv_w_in = dt_in("rwkv_w_in", [2, D, RWKV_IN])
    rwkv_mu = dt_in("rwkv_mu", [2, RWKV_IN])
    rwkv_w0 = dt_in("rwkv_w0", [2, D])
    rwkv_w_up = dt_in("rwkv_w_up", [2, 64, D])
    rwkv_a0 = dt_in("rwkv_a0", [2, D])
    rwkv_a_up = dt_in("rwkv_a_up", [2, 64, D])
    rwkv_k_k = dt_in("rwkv_k_k", [2, D])
    rwkv_k_a = dt_in("rwkv_k_a", [2, D])
    rwkv_r_k = dt_in("rwkv_r_k", [2, D])
    rwkv_ln_w = dt_in("rwkv_ln_w", [2, D])
    rwkv_ln_b = dt_in("rwkv_ln_b", [2, D])
    rwkv_w_out = dt_in("rwkv_w_out", [2, D, D])
    c_ident = dt_in("c_ident", [128, 128])
    c_tri = dt_in("c_tri", [128, 128])
    c_ones = dt_in("c_ones", [128, 128])
    c_m64 = dt_in("c_m64", [128, 128])
    c_scan = dt_in("c_scan", [128, 512])
    c_hind = dt_in("c_hind", [128, 2])
    c_m64x2 = dt_in("c_m64x2", [128, 256])
    c_mlow2 = dt_in("c_mlow2", [128, 128])
    c_idx2 = dt_in("c_idx2", [128, 128])
    c_bones = dt_in("c_bones", [128, 128])
    y = nc.dram_tensor("y", [T, D], F32, kind="ExternalOutput").ap()

    with ExitStack() as st:
        K = KB(nc, st)
        pe, act, dve, pool, sp = K.pe, K.act, K.dve, K.pool, K.sp

        uT = K.sb(st, "uT", [128, 8, T + 2], BF16)
        uTb = Buf("uT")
        ogT = K.sb(st, "ogT", [128, 8, T], BF16)
        ogTb = Buf("ogT")
        ident = K.sb(st, "ident", [128, 128], BF16)
        identb = Buf()
        tri_f = K.sb(st, "tri_f", [128, 128], F32)
        ones_f = K.sb(st, "ones_f", [128, 128], F32)
        tri_b = K.sb(st, "tri_b", [128, 128], BF16)
        constb = Buf()
        gpre = K.sb(st, "gpre", [128, D], F32)
        gpost = K.sb(st, "gpost", [128, D], F32)
        gb = Buf()
        hbufs = [(K.sb(st, f"hb{i}", [128, D], F32), Buf()) for i in range(2)]
        hbufs2 = hbufs
        un = K.sb(st, "un", [128, D], BF16)
        unb = Buf()
        mt = K.sb(st, "mt", [128, D], F32)
        mtb = Buf()
        ss = K.sb(st, "ss", [128, 4], F32)
        ssb = Buf()
        epsb = K.sb(st, "epsb", [128, 4], F32)
        K.op(dve, lambda e: e.memset(epsb[:, 0:1], NORM_EPS), w=[ssb])
        K.op(dve, lambda e: e.memset(epsb[:, 1:2], 1.0), w=[ssb])
        K.op(dve, lambda e: e.memset(epsb[:, 2:3], GN_EPS), w=[ssb])
        wout = K.sb(st, "wout", [128, 8, D], BF16)
        woutb = Buf()
        psA = [(K.psum(st, f"psA{i}", [128, 512], F32), Buf(excl=True)) for i in range(6)]
        psT = (K.psum(st, "psT", [128, 8, 128], BF16), Buf(excl=True))
        psM = (K.psum(st, "psM", [128, 512], F32), Buf(excl=True))
        hB = [Buf(f"h{i}") for i in range(NT)]

        K.dma(pool, ident[:], c_ident[:, :], w=[identb])
        K.dma(pool, tri_b[:], c_tri[:, :], w=[constb])
        K.dma(sp, tri_f[:], c_tri[:, :], w=[constb])
        K.dma(sp, ones_f[:], c_ones[:, :], w=[constb])
        K.op(dve, lambda e: e.memset(uT[:, :, 0:1], 0.0), w=[uTb])

        def bcast_row(ap_row, n):
            return ap_row.partition_broadcast(128)

        def phase_prenorm(layer):
            K.dma(sp, gpre[:], bcast_row(norm_pre[layer, :], D), w=[gb])
            K.dma(sp, gpost[:], bcast_row(norm_post[layer, :], D), w=[gb])
            for i in range(NT):
                ht, htb = hbufs[i % 2]
                src = h0 if layer == 0 else y
                K.dma(sp, ht[:], src[128 * i:128 * i + 128, :], r=[hB[i]], w=[htb])
                K.op(act, lambda e: e.activation(out=mt[:], in_=ht[:], func=AF.Square, accum_out=ss[:, 0:1]),
                     r=[htb], w=[mtb, ssb])
                K.op(act, lambda e: e.activation(out=ss[:, 1:2], in_=ss[:, 0:1], func=AF.Ln, bias=epsb[:, 0:1], scale=1.0 / D),
                     r=[ssb], w=[ssb])
                K.op(act, lambda e: e.activation(out=ss[:, 2:3], in_=ss[:, 1:2], func=AF.Exp, scale=-0.5),
                     r=[ssb], w=[ssb])
                K.op(dve, lambda e: e.scalar_tensor_tensor(out=un[:], in0=ht[:], scalar=ss[:, 2:3], in1=gpre[:],
                                                           op0=ALU.mult, op1=ALU.mult), r=[htb, ssb, gb], w=[unb])
                for c in range(8):
                    K.op(pe, lambda e: e.transpose(out=psT[0][:, c, :], in_=un[:, 128 * c:128 * c + 128],
                                                   identity=ident[:]),
                         r=[unb, identb], w=[psT[1]], inc=(c == 7))
                K.op(act, lambda e: e.copy(out=uT[:, :, 1 + 128 * i:1 + 128 * i + 128], in_=psT[0][:, :, :]),
                     r=[psT[1]], w=[uTb])

        def phase_post(layer, w_out_ap, last):
            K.dma(pool, wout[:], w_out_ap.rearrange("(c p) n -> p c n", p=128), w=[woutb])
            for i in range(NT):
                pms = [psA[0], psA[1]]
                for hf in range(2):
                    for c in range(8):
                        K.op(pe, lambda e: e.matmul(pms[hf][0][:, :], lhsT=ogT[:, c, 128 * i:128 * i + 128],
                                                    rhs=wout[:, c, 512 * hf:512 * hf + 512],
                                                    start=(c == 0), stop=(c == 7)),
                             r=[ogTb, woutb], w=[pms[hf][1]], inc=(c == 7))
                K.op(act, lambda e: e.copy(out=mt[:, 0:512], in_=pms[0][0][:, :]), r=[pms[0][1]], w=[mtb])
                K.op(act, lambda e: e.copy(out=mt[:, 512:1024], in_=pms[1][0][:, :]), r=[pms[1][1]], w=[mtb])
                K.op(act, lambda e: e.activation(out=un[:], in_=mt[:], func=AF.Square, accum_out=ss[:, 0:1]),
                     r=[mtb], w=[unb, ssb])
                K.op(act, lambda e: e.activation(out=ss[:, 1:2], in_=ss[:, 0:1], func=AF.Ln, bias=epsb[:, 0:1], scale=1.0 / D),
                     r=[ssb], w=[ssb])
                K.op(act, lambda e: e.activation(out=ss[:, 2:3], in_=ss[:, 1:2], func=AF.Exp, scale=-0.5),
                     r=[ssb], w=[ssb])
                ht, htb = hbufs2[i % 2]
                src = h0 if layer == 0 else y
                K.dma(sp, ht[:], src[128 * i:128 * i + 128, :], r=[hB[i]], w=[htb])
                K.op(dve, lambda e: e.scalar_tensor_tensor(out=mt[:], in0=mt[:], scalar=ss[:, 2:3], in1=gpost[:],
                                                           op0=ALU.mult, op1=ALU.mult), r=[mtb, ssb, gb], w=[mtb])
                K.op(dve, lambda e: e.tensor_tensor(out=ht[:], in0=ht[:], in1=mt[:], op=ALU.add),
                     r=[htb, mtb], w=[htb])
                K.dma(sp, y[128 * i:128 * i + 128, :], ht[:], r=[htb], w=[hB[i]])

        def fox_layer(layer):
            j = layer // 2
            win = fox_w_in[j].rearrange("(c p) n -> p c n", p=128)
            with ExitStack() as ls:
                wb = [[(K.sb(ls, f"fw{s}{t}", [128, 8, 256], BF16), Buf()) for t in range(4)] for s in range(2)]
                wf = K.sb(ls, "fwf", [128, 8, 16], BF16)
                wfb = Buf()
                qT = K.sb(ls, "qT", [128, 2, T], BF16)
                kT = K.sb(ls, "kT", [128, 2, T], BF16)
                sgT = K.sb(ls, "sgT", [128, 2, T], BF16)
                qTb, kTb, sgTb = Buf(), Buf(), Buf()
                vaug = K.sb(ls, "vaug", [128, NT, 4, 128], BF16)
                vaugb = Buf()
                bft = K.sb(ls, "bft", [128, NH], F32)
                lf = K.sb(ls, "lf", [128, NT, NH], F32)
                lfb = Buf()
                cum = K.sb(ls, "cum", [128, NT, NH], F32)
                carry = K.sb(ls, "carry", [128, NT, NH], F32)
                cumb = Buf()
                bias = [(K.sb(ls, f"bias{i}", [128, NT], F32), Buf()) for i in range(4)]
                PT = [(K.sb(ls, f"PT{i}", [128, 512], BF16), Buf()) for i in range(3)]
                rs = [(K.sb(ls, f"rs{i}", [128, 512], F32), Buf()) for i in range(2)]
                tmpo = [(K.sb(ls, f"tmpo{i}", [128, 512], F32), Buf()) for i in range(2)]

                def load_group(g, s):
                    for t in range(4):
                        K.dma(pool, wb[s][t][0][:], win[:, :, 1024 * t + 256 * g:1024 * t + 256 * g + 256],
                              w=[wb[s][t][1]])
                K.dma(pool, wf[:], win[:, :, 4096:4112], w=[wfb])
                load_group(0, 0)
                K.dma(sp, bft[:], fox_b_f[j, :].partition_broadcast(128), w=[lfb])
                K.op(dve, lambda e: e.memset(vaug[:], 1.0), w=[vaugb])

                pf = psM
                for i in range(NT):
                    for c in range(8):
                        K.op(pe, lambda e: e.matmul(pf[0][:, 16 * i:16 * i + 16], lhsT=uT[:, c, 1 + 128 * i:1 + 128 * i + 128],
                                                    rhs=wf[:, c, :], start=(c == 0), stop=(c == 7)),
                             r=[uTb, wfb], w=[pf[1]], inc=(c == 7))
                for i in range(NT):
                    K.op(dve, lambda e: e.tensor_tensor(out=lf[:, i, :], in0=pf[0][:, 16 * i:16 * i + 16], in1=bft[:],
                                                        op=ALU.add), r=[pf[1], lfb], w=[lfb])
                lf2 = lf[:].rearrange("p a b -> p (a b)")
                K.op(act, lambda e: e.activation(out=lf2, in_=lf2, func=AF.Exp, scale=-1.0), r=[lfb], w=[lfb])
                K.op(act, lambda e: e.activation(out=lf2, in_=lf2, func=AF.Ln, bias=epsb[:, 1:2], scale=1.0), r=[lfb, ssb], w=[lfb])
                K.op(dve, lambda e: e.tensor_scalar(out=lf2, in0=lf2, scalar1=-1.0, scalar2=None, op0=ALU.mult),
                     r=[lfb], w=[lfb])
                pc, pl = psA[2], psA[3]
                K.op(dve, lambda e: e.memset(carry[:, 0, :], 0.0), w=[cumb])
                for i in range(NT):
                    for jj in range(i):
                        K.op(pe, lambda e: e.matmul(pc[0][:, 16 * i:16 * i + 16], lhsT=ones_f[:], rhs=lf[:, jj, :],
                                                    start=(jj == 0), stop=(jj == i - 1)),
                             r=[lfb, constb], w=[pc[1]], inc=(jj == i - 1))
                    K.op(pe, lambda e: e.matmul(pl[0][:, 16 * i:16 * i + 16], lhsT=tri_f[:], rhs=lf[:, i, :],
                                                start=True, stop=True), r=[lfb, constb], w=[pl[1]])
                K.op(dve, lambda e: e.tensor_copy(out=carry[:, 1:NT, :].rearrange("p a b -> p (a b)"),
                                                  in_=pc[0][:, 16:16 * NT]), r=[pc[1]], w=[cumb])
                K.op(dve, lambda e: e.tensor_tensor(out=cum[:].rearrange("p a b -> p (a b)"),
                                                    in0=pl[0][:, 0:16 * NT],
                                                    in1=carry[:].rearrange("p a b -> p (a b)"), op=ALU.add),
                     r=[pl[1], cumb], w=[cumb])

                chunks = tok_chunks(512)
                pcount = [0]

                def nextps():
                    p = psA[pcount[0] % 2]
                    pcount[0] += 1
                    return p

                stc = [0]
                otc = [0]
                ptc = [0]
                bc = [0]
                for g in range(4):
                    s = g % 2
                    if g + 1 < 4:
                        load_group(g + 1, (g + 1) % 2)
                    wq, wk, wv, wg = [wb[s][t] for t in range(4)]
                    for pp in range(2):
                        for (t0, n) in chunks:
                            for (wt, dst, dstb, kind) in ((wq, qT, qTb, 0), (wk, kT, kTb, 1), (wg, sgT, sgTb, 2)):
                                p = nextps()
                                for c in range(8):
                                    K.op(pe, lambda e: e.matmul(p[0][:, 0:n], lhsT=wt[0][:, c, 128 * pp:128 * pp + 128],
                                                                rhs=uT[:, c, 1 + t0:1 + t0 + n],
                                                                start=(c == 0), stop=(c == 7)),
                                         r=[uTb, wt[1]], w=[p[1]], inc=(c == 7))
                                if kind == 0:
                                    K.op(act, lambda e: e.activation(out=dst[:, pp, t0:t0 + n], in_=p[0][:, 0:n],
                                                                     func=AF.Copy, scale=HD ** -0.5),
                                         r=[p[1]], w=[dstb])
                                elif kind == 1:
                                    K.op(dve, lambda e: e.tensor_copy(out=dst[:, pp, t0:t0 + n], in_=p[0][:, 0:n]),
                                         r=[p[1]], w=[dstb])
                                else:
                                    K.op(act, lambda e: e.activation(out=dst[:, pp, t0:t0 + n], in_=p[0][:, 0:n],
                                                                     func=AF.Silu), r=[p[1]], w=[dstb])
                    for i in range(NT):
                        p = nextps()
                        for c in range(8):
                            K.op(pe, lambda e: e.matmul(p[0][:, 0:256], lhsT=uT[:, c, 1 + 128 * i:1 + 128 * i + 128],
                                                        rhs=wv[0][:, c, :], start=(c == 0), stop=(c == 7)),
                                 r=[uTb, wv[1]], w=[p[1]], inc=(c == 7))
                        pv = p[0][:, 0:256].rearrange("p (a b c) -> p a b c", a=2, b=2)
                        K.op(dve, lambda e: e.tensor_copy(out=vaug[:, i, 0:4:2, 0:64], in_=pv[:, :, 0, :]),
                             r=[p[1]], w=[vaugb])
                        K.op(dve, lambda e: e.tensor_copy(out=vaug[:, i, 1:4:2, 64:128], in_=pv[:, :, 1, :]),
                             r=[p[1]], w=[vaugb])
                    for hh in range(4):
                        h = 4 * g + hh
                        pp, half = hh // 2, hh % 2
                        lo = 64 * half
                        olo, slo = (0, 64) if half == 0 else (64, 0)
                        for (q0, qn) in chunks:
                            i0 = q0 // 128
                            ni = qn // 128
                            btabs = []
                            for ii in range(ni):
                                bt = bias[bc[0] % 4]
                                bc[0] += 1
                                i = i0 + ii
                                K.op(dve, lambda e: e.tensor_scalar(out=bt[0][:, :], in0=cum[:, :, h], scalar1=-1.0,
                                                                    scalar2=carry[:, i, h:h + 1], op0=ALU.mult,
                                                                    op1=ALU.add), r=[cumb], w=[bt[1]])
                                btabs.append(bt)
                            ot = psA[4 + otc[0] % 2]
                            otc[0] += 1
                            jmax = i0 + ni - 1
                            for jk in range(jmax + 1):
                                qs = max(q0, 128 * jk)
                                n = q0 + qn - qs
                                stp = psA[2 + stc[0] % 2]
                                stc[0] += 1
                                K.op(pe, lambda e: e.matmul(stp[0][:, 0:n], lhsT=kT[lo:lo + 64, pp, 128 * jk:128 * jk + 128],
                                                            rhs=qT[lo:lo + 64, pp, qs:qs + n], start=True, stop=True),
                                     r=[kTb, qTb], w=[stp[1]])
                                pt = PT[ptc[0] % 3]
                                ptc[0] += 1
                                for ii in range((qs - q0) // 128, ni):
                                    i = i0 + ii
                                    co = 128 * i - qs
                                    K.op(act, lambda e: e.activation(out=pt[0][:, co:co + 128], in_=stp[0][:, co:co + 128],
                                                                     func=AF.Exp, bias=btabs[ii][0][:, jk:jk + 1],
                                                                     scale=1.0),
                                         r=[stp[1], btabs[ii][1]], w=[pt[1]])
                                if jk >= i0:
                                    K.op(dve, lambda e: e.tensor_tensor(out=pt[0][:, 0:128], in0=pt[0][:, 0:128],
                                                                        in1=tri_b[:], op=ALU.mult),
                                         r=[pt[1], constb], w=[pt[1]])
                                K.op(pe, lambda e: e.matmul(ot[0][:, qs - q0:qs - q0 + n], lhsT=vaug[:, jk, hh, :],
                                                            rhs=pt[0][:, 0:n], start=(jk == 0), stop=(jk == jmax),
                                                            skip_group_check=True),
                                     r=[vaugb, pt[1]], w=[ot[1]])
                            r_ = rs[otc[0] % 2]
                            tm = tmpo[otc[0] % 2]
                            K.op(dve, lambda e: e.reciprocal(out=r_[0][olo:olo + 64, 0:qn], in_=ot[0][slo:slo + 64, 0:qn]),
                                 r=[ot[1]], w=[r_[1]])
                            K.op(dve, lambda e: e.tensor_tensor(out=tm[0][olo:olo + 64, 0:qn], in0=ot[0][olo:olo + 64, 0:qn],
                                                                in1=r_[0][olo:olo + 64, 0:qn], op=ALU.mult),
                                 r=[ot[1], r_[1]], w=[tm[1]])
                            K.op(dve, lambda e: e.tensor_tensor(out=ogT[olo:olo + 64, 2 * g + pp, q0:q0 + qn],
                                                                in0=tm[0][olo:olo + 64, 0:qn],
                                                                in1=sgT[olo:olo + 64, pp, q0:q0 + qn], op=ALU.mult),
                                 r=[tm[1], sgTb], w=[ogTb])
                K.barrier()


        def rwkv_layer(layer):
            j = layer // 2
            win = rwkv_w_in[j].rearrange("(c p) n -> p c n", p=128)
            with ExitStack() as ls:
                SB = lambda n, shp, dt: K.sb(ls, n, shp, dt)
                wadT = SB("wadT", [128, T], BF16); wadTb = Buf()
                wup = SB("wup", [128, D], BF16); wupb = Buf()
                vecs = SB("vecs", [128, 8, 8], F32); vecb = Buf()
                mub = (SB("mub", [128, 2, 128], F32), Buf())
                wraw = (SB("wraw", [128, 8, 128], F32), Buf())
                bones = SB("bones", [128, 128], F32)
                m64x2 = SB("m64x2", [128, 2, 128], F32)
                mlow2 = SB("mlow2", [128, 2, 64], F32)
                idx2f = SB("idx2f", [128, 2, 64], F32)
                idx2b = SB("idx2b", [128, 2, 64], BF16)
                scanm = SB("scanm", [128, 128], F32)
                hind = SB("hind", [128, 2], BF16)
                cb2 = Buf()

                class Ctx:
                    pass

                ctxs = []
                for ci in range(2):
                    C = Ctx()
                    C.ci = ci
                    F_ = lambda n: (SB(f"{n}{ci}", [128, 128], F32), Buf())
                    B_ = lambda n, shp: (SB(f"{n}{ci}", shp, BF16), Buf())
                    for n in ("rf", "kf", "sgw", "av", "lw", "cm", "cmx", "E1", "E2", "E3", "kkr", "sq", "hsn", "kk",
                              "t1", "kp", "bb", "ke3", "be3", "gs", "ytile", "yn"):
                        setattr(C, n, F_(n))
                    C.sets = []
                    for si in range(2):
                        S = Ctx()
                        S.sg = B_(f"sg{si}", [128, 128]); S.vtok = B_(f"vtok{si}", [128, 128])
                        S.AR = B_(f"AR{si}", [128, 2, 2, 64]); S.BK = B_(f"BK{si}", [128, 2, 2, 64])
                        S.kbhat = B_(f"kbhat{si}", [128, 2, 128])
                        S.MB = B_(f"MB{si}", [128, 2, 128]); S.MK = B_(f"MK{si}", [128, 2, 128])
                        S.TT = B_(f"TTf{si}", [128, 2, 64])
                        S.GC = (SB(f"GC{ci}{si}", [128, 2], F32), Buf())
                        S.bon = (SB(f"bon{ci}{si}", [128, 2], F32), Buf())
                        C.sets.append(S)
                    C.khT = B_("khT", [128, 128]); C.bhT = B_("bhT", [128, 128]); C.rkb = B_("rkb", [128, 128])
                    C.PQ = [B_(f"PQ{i}", [128, 2, 128]) for i in range(2)]
                    C.TTl = [B_(f"TT{i}", [128, 2, 64]) for i in range(2)]
                    C.Xs = B_("Xs", [128, 128]); C.Us = B_("Us", [128, 128]); C.ybf = B_("ybf", [128, 128])
                    C.st6 = (SB(f"st6{ci}", [128, 2, 6], F32), Buf())
                    C.mv = (SB(f"mv{ci}", [128, 2, 2], F32), Buf())
                    C.rstd = (SB(f"rstd{ci}", [128, 2], F32), Buf())
                    C.Hs = SB(f"Hs{ci}", [128, 64], F32); C.Hb = SB(f"Hb{ci}", [128, 64], BF16)
                    C.Hsb = Buf(); C.Hbb = Buf()
                    C.wcp = [(SB(f"wcp{ci}{t_}", [128, 16, 128], BF16), Buf()) for t_ in range(4)]
                    C.lnw = SB(f"lnw{ci}", [128, 128], F32); C.lnb = SB(f"lnb{ci}", [128, 128], F32); C.lnbuf = Buf()
                    C.bX, C.bY, C.bZ = psA[3 * ci], psA[3 * ci + 1], psA[3 * ci + 2]
                    C.front_done = 0
                    C.back_done = 0
                    ctxs.append(C)

                K.dma(sp, bones[:], c_bones[:, :], w=[cb2])
                K.dma(sp, m64x2[:].rearrange("p a b -> p (a b)"), c_m64x2[:, :], w=[cb2])
                K.dma(sp, mlow2[:].rearrange("p a b -> p (a b)"), c_mlow2[:, :], w=[cb2])
                K.dma(sp, idx2f[:].rearrange("p a b -> p (a b)"), c_idx2[:, :], w=[cb2])
                K.dma(pool, idx2b[:].rearrange("p a b -> p (a b)"), c_idx2[:, :], w=[cb2])
                K.dma(sp, scanm[:], c_scan[:, 0:128], w=[cb2])
                K.dma(pool, hind[:], c_hind[:, :], w=[cb2])
                K.dma(pool, wup[0:64, :], rwkv_w_up[j], w=[wupb])
                K.dma(pool, wup[64:128, :], rwkv_a_up[j], w=[wupb])
                with nc.allow_non_contiguous_dma(reason="tiny per-feature vectors"):
                    for vi, src in enumerate((rwkv_w0, rwkv_a0, rwkv_k_k, rwkv_k_a, rwkv_r_k)):
                        K.dma(sp, vecs[:, vi, :], src[j, :].rearrange("(c p) -> p c", p=128), w=[vecb])
                K.op(dve, lambda e: e.tensor_scalar(out=vecs[:, 5, :], in0=vecs[:, 3, :], scalar1=-1.0, scalar2=1.0,
                                                    op0=ALU.mult, op1=ALU.add), r=[vecb], w=[vecb])
                K.op(dve, lambda e: e.tensor_scalar(out=vecs[:, 6, :], in0=vecs[:, 0, :], scalar1=-1.0, scalar2=None,
                                                    op0=ALU.mult), r=[vecb], w=[vecb])
                K.op(dve, lambda e: e.tensor_scalar(out=vecs[:, 7, :], in0=vecs[:, 1, :], scalar1=-1.0, scalar2=None,
                                                    op0=ALU.mult), r=[vecb], w=[vecb])

                def load_w(col0, dst):
                    mb, rw = mub, wraw
                    K.dma(sp, mb[0][:, 0, :], rwkv_mu[j, col0:col0 + 128].partition_broadcast(128), w=[mb[1]])
                    K.dma(sp, rw[0][:], win[:, :, col0:col0 + 128], w=[rw[1]])
                    K.op(dve, lambda e: e.tensor_scalar(out=mb[0][:, 1, :], in0=mb[0][:, 0, :], scalar1=-1.0, scalar2=1.0,
                                                        op0=ALU.mult, op1=ALU.add), r=[mb[1]], w=[mb[1]])
                    for c in range(8):
                        K.op(dve, lambda e: e.tensor_tensor(out=dst[0][:, c, :], in0=rw[0][:, c, :], in1=mb[0][:, 1, :],
                                                            op=ALU.mult), r=[rw[1], mb[1]], w=[dst[1]])
                        K.op(dve, lambda e: e.tensor_tensor(out=dst[0][:, 8 + c, :], in0=rw[0][:, c, :], in1=mb[0][:, 0, :],
                                                            op=ALU.mult), r=[rw[1], mb[1]], w=[dst[1]])

                def proj_fm(out_ps, outb, wt, t0, n):
                    for c in range(16):
                        rhs = uT[:, c, 1 + t0:1 + t0 + n] if c < 8 else uT[:, c - 8, t0:t0 + n]
                        K.op(pe, lambda e: e.matmul(out_ps, lhsT=wt[0][:, c, :], rhs=rhs, start=(c == 0), stop=(c == 15)),
                             r=[uTb, wt[1]], w=[outb], inc=(c == 15))

                wwa = ctxs[0].wcp[0]
                load_w(4096, wwa)
                for ci_, (t0, n) in enumerate(tok_chunks(512)):
                    p_ = psA[ci_ % 2]
                    proj_fm(p_[0][:, 0:n], p_[1], wwa, t0, n)
                    K.op(act, lambda e: e.activation(out=wadT[0:64, t0:t0 + n], in_=p_[0][0:64, 0:n], func=AF.Tanh),
                         r=[p_[1]], w=[wadTb])
                    K.op(act, lambda e: e.copy(out=wadT[64:128, t0:t0 + n], in_=p_[0][64:128, 0:n]), r=[p_[1]], w=[wadTb])

                v3 = lambda t_: t_[0][:].rearrange("p (c s) -> p c s", c=2)
                flat = lambda ap: ap.rearrange("p a b -> p (a b)")
                one_b = epsb[:, 1:2]

                def sigmoid_chain(C, src_ps, srcb, bias_ap, dst, extra_r=()):
                    if bias_ap is None:
                        K.op(act, lambda e: e.activation(out=dst[0][:], in_=src_ps, func=AF.Exp, scale=-1.0),
                             r=[srcb] + list(extra_r), w=[dst[1]])
                    else:
                        K.op(act, lambda e: e.activation(out=dst[0][:], in_=src_ps, func=AF.Exp, bias=bias_ap, scale=-1.0),
                             r=[srcb] + list(extra_r), w=[dst[1]])
                    K.op(act, lambda e: e.activation(out=dst[0][:], in_=dst[0][:], func=AF.Ln, bias=one_b, scale=1.0),
                         r=[dst[1], ssb], w=[dst[1]])
                    K.op(act, lambda e: e.activation(out=dst[0][:], in_=dst[0][:], func=AF.Exp, scale=-1.0),
                         r=[dst[1]], w=[dst[1]])

                def front(C, p):
                    vcol = lambda vi: vecs[:, vi, p:p + 1]
                    wr, wk, wv, wg = C.wcp
                    bX, bY = C.bX, C.bY
                    for i in range(NT):
                        while C.back_done < i - 1:
                            yield False
                        S = C.sets[i % 2]
                        t0 = 128 * i
                        rf, kf, sgw, av, lw, cm, cmx, E1, E2, E3 = C.rf, C.kf, C.sgw, C.av, C.lw, C.cm, C.cmx, C.E1, C.E2, C.E3
                        kkr, sq, hsn, kk, t1, kp, bb, ke3, be3, gs = C.kkr, C.sq, C.hsn, C.kk, C.t1, C.kp, C.bb, C.ke3, C.be3, C.gs
                        AR, BK, MB, MK = S.AR, S.BK, S.MB, S.MK
                        proj_fm(bX[0][:, 0:128], bX[1], wr, t0, 128)
                        proj_fm(bX[0][:, 128:256], bX[1], wk, t0, 128)
                        yield True
                        proj_fm(bX[0][:, 256:384], bX[1], wg, t0, 128)
                        for c in range(16):
                            lhsT = uT[:, c, 1 + t0:1 + t0 + 128] if c < 8 else uT[:, c - 8, t0:t0 + 128]
                            K.op(pe, lambda e: e.matmul(bX[0][:, 384:512], lhsT=lhsT, rhs=wv[0][:, c, :],
                                                        start=(c == 0), stop=(c == 15)),
                                 r=[uTb, wv[1]], w=[bX[1]], inc=(c == 15))
                        K.op(pe, lambda e: e.matmul(bY[0][:, 0:128], lhsT=wup[0:64, 128 * p:128 * p + 128],
                                                    rhs=wadT[0:64, t0:t0 + 128], start=True, stop=True),
                             r=[wupb, wadTb], w=[bY[1]])
                        K.op(pe, lambda e: e.matmul(bY[0][:, 128:256], lhsT=wup[64:128, 128 * p:128 * p + 128],
                                                    rhs=wadT[64:128, t0:t0 + 128], start=True, stop=True),
                             r=[wupb, wadTb], w=[bY[1]])
                        yield True
                        K.op(act, lambda e: e.copy(out=rf[0][:], in_=bX[0][:, 0:128]), r=[bX[1]], w=[rf[1]])
                        K.op(act, lambda e: e.copy(out=kf[0][:], in_=bX[0][:, 128:256]), r=[bX[1]], w=[kf[1]])
                        sigmoid_chain(C, bX[0][:, 256:384], bX[1], None, gs)
                        K.op(dve, lambda e: e.tensor_tensor(out=S.sg[0][:], in0=bX[0][:, 256:384], in1=gs[0][:], op=ALU.mult),
                             r=[bX[1], gs[1]], w=[S.sg[1]])
                        K.op(dve, lambda e: e.tensor_copy(out=S.vtok[0][:], in_=bX[0][:, 384:512]), r=[bX[1]], w=[S.vtok[1]])
                        yield True
                        sigmoid_chain(C, bY[0][:, 0:128], bY[1], vcol(6), sgw, extra_r=[vecb])
                        sigmoid_chain(C, bY[0][:, 128:256], bY[1], vcol(7), av, extra_r=[vecb])
                        K.op(dve, lambda e: e.tensor_scalar(out=lw[0][:], in0=sgw[0][:], scalar1=-DECAY_SCALE, scalar2=None,
                                                            op0=ALU.mult), r=[sgw[1]], w=[lw[1]])
                        K.op(dve, lambda e: e.tensor_tensor_scan(out=cm[0][:], data0=scanm[:], data1=lw[0][:], initial=0.0,
                                                                 op0=ALU.mult, op1=ALU.add), r=[lw[1], cb2], w=[cm[1]])
                        K.op(dve, lambda e: e.tensor_tensor(out=cmx[0][:], in0=cm[0][:], in1=lw[0][:], op=ALU.subtract),
                             r=[cm[1], lw[1]], w=[cmx[1]])
                        yield True
                        K.op(act, lambda e: e.activation(out=E1[0][:], in_=cm[0][:], func=AF.Exp), r=[cm[1]], w=[E1[1]])
                        K.op(act, lambda e: e.activation(out=E2[0][:], in_=cmx[0][:], func=AF.Exp), r=[cmx[1]], w=[E2[1]])
                        K.op(act, lambda e: e.activation(out=E3[0][:], in_=cm[0][:], func=AF.Exp, scale=-1.0),
                             r=[cm[1]], w=[E3[1]])
                        K.op(act, lambda e: e.activation(out=S.GC[0][:], in_=cm[0][:, 63:128:64], func=AF.Exp),
                             r=[cm[1]], w=[S.GC[1]])
                        K.op(dve, lambda e: e.tensor_scalar(out=kkr[0][:], in0=kf[0][:], scalar1=vcol(2), scalar2=None,
                                                            op0=ALU.mult), r=[kf[1], vecb], w=[kkr[1]])
                        K.op(dve, lambda e: e.tensor_tensor(out=sq[0][:], in0=kkr[0][:], in1=kkr[0][:], op=ALU.mult),
                             r=[kkr[1]], w=[sq[1]])
                        K.op(pe, lambda e: e.matmul(bY[0][:, 256:384], lhsT=bones[:], rhs=sq[0][:], start=True, stop=True),
                             r=[cb2, sq[1]], w=[bY[1]])
                        yield True
                        K.op(dve, lambda e: e.tensor_scalar(out=hsn[0][:], in0=bY[0][:, 256:384], scalar1=1e-24, scalar2=None,
                                                            op0=ALU.max), r=[bY[1]], w=[hsn[1]])
                        K.op(act, lambda e: e.activation(out=hsn[0][:], in_=hsn[0][:], func=AF.Ln), r=[hsn[1]], w=[hsn[1]])
                        K.op(act, lambda e: e.activation(out=hsn[0][:], in_=hsn[0][:], func=AF.Exp, scale=-0.5),
                             r=[hsn[1]], w=[hsn[1]])
                        K.op(dve, lambda e: e.tensor_scalar(out=t1[0][:], in0=av[0][:], scalar1=vcol(3), scalar2=vcol(5),
                                                            op0=ALU.mult, op1=ALU.add), r=[av[1], vecb], w=[t1[1]])
                        K.op(dve, lambda e: e.tensor_tensor(out=kp[0][:], in0=kf[0][:], in1=t1[0][:], op=ALU.mult),
                             r=[kf[1], t1[1]], w=[kp[1]])
                        K.op(dve, lambda e: e.tensor_tensor(out=AR[0][:, :, 1, :], in0=v3(rf), in1=v3(E1), op=ALU.mult),
                             r=[rf[1], E1[1]], w=[AR[1]])
                        K.op(dve, lambda e: e.tensor_tensor(out=ke3[0][:], in0=kp[0][:], in1=E3[0][:], op=ALU.mult),
                             r=[kp[1], E3[1]], w=[ke3[1]])
                        yield True
                        K.op(dve, lambda e: e.tensor_tensor(out=kk[0][:], in0=kkr[0][:], in1=hsn[0][:], op=ALU.mult),
                             r=[kkr[1], hsn[1]], w=[kk[1]])
                        K.op(dve, lambda e: e.tensor_tensor(out=bb[0][:], in0=kk[0][:], in1=av[0][:], op=ALU.mult),
                             r=[kk[1], av[1]], w=[bb[1]])
                        K.op(dve, lambda e: e.scalar_tensor_tensor(out=AR[0][:, :, 0, :], in0=v3(kk), scalar=-1.0, in1=v3(E2),
                                                                   op0=ALU.mult, op1=ALU.mult), r=[kk[1], E2[1]], w=[AR[1]])
                        K.op(dve, lambda e: e.tensor_tensor(out=be3[0][:], in0=bb[0][:], in1=E3[0][:], op=ALU.mult),
                             r=[bb[1], E3[1]], w=[be3[1]])
                        K.op(act, lambda e: e.copy(out=BK[0][:, :, 1, :], in_=v3(ke3)), r=[ke3[1]], w=[BK[1]])
                        K.op(act, lambda e: e.copy(out=BK[0][:, :, 0, :], in_=v3(be3)), r=[be3[1]], w=[BK[1]])
                        yield True
                        for c in range(2):
                            K.op(dve, lambda e: e.tensor_scalar(out=C.khT[0][:, 64 * c:64 * c + 64], in0=ke3[0][:, 64 * c:64 * c + 64],
                                                                scalar1=S.GC[0][:, c:c + 1], scalar2=None, op0=ALU.mult),
                                 r=[ke3[1], S.GC[1]], w=[C.khT[1]])
                            K.op(dve, lambda e: e.tensor_scalar(out=C.bhT[0][:, 64 * c:64 * c + 64], in0=be3[0][:, 64 * c:64 * c + 64],
                                                                scalar1=S.GC[0][:, c:c + 1], scalar2=None, op0=ALU.mult),
                                 r=[be3[1], S.GC[1]], w=[C.bhT[1]])
                        K.op(dve, lambda e: e.scalar_tensor_tensor(out=C.rkb[0][:], in0=rf[0][:], scalar=vcol(4), in1=kp[0][:],
                                                                   op0=ALU.mult, op1=ALU.mult), r=[rf[1], kp[1], vecb], w=[C.rkb[1]])
                        for c in range(2):
                            for hf in range(2):
                                lo = 64 * hf
                                blk = 2 * c + hf
                                K.op(pe, lambda e: e.matmul(bY[0][:, 128 * blk:128 * blk + 128], lhsT=flat(BK[0][lo:lo + 64, c, :, :]),
                                                            rhs=flat(AR[0][lo:lo + 64, c, :, :]), start=True, stop=True),
                                     r=[BK[1], AR[1]], w=[bY[1]])
                        for c in range(2):
                            for hf in range(2):
                                lo = 64 * hf
                                blk = 2 * c + hf
                                K.op(pe, lambda e: e.matmul(bX[0][0:64, 64 * blk:64 * blk + 64], lhsT=AR[0][lo:lo + 64, c, 0, :],
                                                            rhs=BK[0][lo:lo + 64, c, 0, :], start=True, stop=True),
                                     r=[BK[1], AR[1]], w=[bX[1]])
                        yield True
                        for c in range(2):
                            pv = 64 * c
                            scv = bY[0][:, 256 * c:256 * c + 256].rearrange("p (a b) -> p a b", a=2)
                            K.op(dve, lambda e: e.tensor_tensor(out=MB[0][pv:pv + 64, :, :], in0=scv[0:64, :, :], in1=m64x2[0:64, :, :],
                                                                op=ALU.mult), r=[bY[1], cb2], w=[MB[1]])
                            K.op(dve, lambda e: e.tensor_tensor(out=MK[0][pv:pv + 64, :, :], in0=scv[64:128, :, :], in1=m64x2[64:128, :, :],
                                                                op=ALU.mult), r=[bY[1], cb2], w=[MK[1]])
                            apv = bX[0][0:64, 128 * c:128 * c + 128].rearrange("p (a b) -> p a b", a=2)
                            K.op(dve, lambda e: e.tensor_tensor(out=C.PQ[0][0][pv:pv + 64, :, 0:64], in0=apv, in1=mlow2[0:64, :, :],
                                                                op=ALU.mult), r=[bX[1], cb2], w=[C.PQ[0][1]])
                        K.op(dve, lambda e: e.tensor_copy(out=C.PQ[0][0][:, :, 64:128], in_=MB[0][:, :, 0:64]), r=[MB[1]], w=[C.PQ[0][1]])
                        K.op(dve, lambda e: e.tensor_tensor(out=C.TTl[0][0][:], in0=MB[0][:, :, 0:64], in1=idx2f[:], op=ALU.add),
                             r=[MB[1], cb2], w=[C.TTl[0][1]])
                        K.op(pe, lambda e: e.transpose(out=psT[0][:, 0, :], in_=C.khT[0][:], identity=ident[:]),
                             r=[C.khT[1], identb], w=[psT[1]], inc=False)
                        K.op(pe, lambda e: e.transpose(out=psT[0][:, 1, :], in_=C.bhT[0][:], identity=ident[:]),
                             r=[C.bhT[1], identb], w=[psT[1]])
                        K.op(act, lambda e: e.copy(out=S.kbhat[0][:], in_=psT[0][:, 0:2, :]), r=[psT[1]], w=[S.kbhat[1]])
                        yield True
                        for lv in range(1, 6):
                            prev, cur = C.PQ[(lv - 1) % 2], C.PQ[lv % 2]
                            Tp = C.TTl[(lv - 1) % 2]
                            Tc = C.TTl[lv % 2] if lv < 5 else S.TT
                            for c in range(2):
                                pv = 64 * c
                                for hf in range(2):
                                    blk = 2 * c + hf
                                    Pm = prev[0][pv:pv + 64, hf, 0:64]
                                    Qm = prev[0][pv:pv + 64, hf, 64:128]
                                    K.op(pe, lambda e: e.matmul(bX[0][0:64, 128 * blk:128 * blk + 64], lhsT=Qm, rhs=Pm, start=True, stop=True),
                                         r=[prev[1]], w=[bX[1]])
                                    if lv < 5:
                                        K.op(pe, lambda e: e.matmul(bX[0][0:64, 128 * blk + 64:128 * blk + 128], lhsT=Pm, rhs=Qm, start=True, stop=True),
                                             r=[prev[1]], w=[bX[1]])
                            yield True
                            for c in range(2):
                                pv = 64 * c
                                src = bX[0][0:64, 256 * c:256 * c + 256].rearrange("p (a b) -> p a b", a=2)
                                if c == 0:
                                    K.op(act, lambda e: e.copy(out=cur[0][pv:pv + 64, :, :], in_=src), r=[bX[1]], w=[cur[1]])
                                else:
                                    K.op(dve, lambda e: e.tensor_copy(out=cur[0][pv:pv + 64, :, :], in_=src), r=[bX[1]], w=[cur[1]])
                            yield True
                            for c in range(2):
                                pv = 64 * c
                                for hf in range(2):
                                    blk = 2 * c + hf
                                    K.op(pe, lambda e: e.matmul(bY[0][0:64, 64 * blk:64 * blk + 64], lhsT=cur[0][pv:pv + 64, hf, 0:64],
                                                                rhs=Tp[0][pv:pv + 64, hf, :], start=True, stop=False),
                                         r=[cur[1], Tp[1]], w=[bY[1]])
                                    K.op(pe, lambda e: e.matmul(bY[0][0:64, 64 * blk:64 * blk + 64], lhsT=idx2b[pv:pv + 64, 0, :],
                                                                rhs=Tp[0][pv:pv + 64, hf, :], start=False, stop=True),
                                         r=[cb2, Tp[1]], w=[bY[1]])
                            yield True
                            for c in range(2):
                                pv = 64 * c
                                src = bY[0][0:64, 128 * c:128 * c + 128].rearrange("p (a b) -> p a b", a=2)
                                if c == 0:
                                    K.op(act, lambda e: e.copy(out=Tc[0][pv:pv + 64, :, :], in_=src), r=[bY[1]], w=[Tc[1]])
                                else:
                                    K.op(dve, lambda e: e.tensor_copy(out=Tc[0][pv:pv + 64, :, :], in_=src), r=[bY[1]], w=[Tc[1]])
                            yield True
                        K.op(pe, lambda e: e.matmul(bY[0][:, 384:386], lhsT=C.rkb[0][:], rhs=hind[:], start=True, stop=True),
                             r=[C.rkb[1], cb2], w=[bY[1]])
                        K.op(dve, lambda e: e.tensor_copy(out=S.bon[0][:], in_=bY[0][:, 384:386]), r=[bY[1]], w=[S.bon[1]])
                        C.front_done = i + 1
                        yield True

                def back(C, p):
                    bZ = C.bZ
                    Hs, Hb, Hsb, Hbb = C.Hs, C.Hb, C.Hsb, C.Hbb
                    Xs, Us, ytile, yn, ybf = C.Xs, C.Us, C.ytile, C.yn, C.ybf
                    for i in range(NT):
                        while C.front_done < i + 1:
                            yield False
                        S = C.sets[i % 2]
                        AR, MB, MK, vtok, kbhat, GC, Tf = S.AR, S.MB, S.MK, S.vtok, S.kbhat, S.GC, S.TT
                        t0 = 128 * i
                        for c in range(2):
                            pv = 64 * c
                            for hf in range(2):
                                lo = 64 * hf
                                K.op(pe, lambda e: e.matmul(bZ[0][0:64, 64 * hf:64 * hf + 64], lhsT=AR[0][lo:lo + 64, c, 0, :],
                                                            rhs=Hb[lo:lo + 64, :], start=True, stop=False),
                                     r=[AR[1], Hbb], w=[bZ[1]])
                                K.op(pe, lambda e: e.matmul(bZ[0][0:64, 64 * hf:64 * hf + 64], lhsT=MK[0][pv:pv + 64, hf, 0:64],
                                                            rhs=vtok[0][pv:pv + 64, 64 * hf:64 * hf + 64], start=False, stop=True),
                                     r=[MK[1], vtok[1]], w=[bZ[1]])
                            yield True
                            K.op(act, lambda e: e.copy(out=Xs[0][pv:pv + 64, :], in_=bZ[0][0:64, 0:128]), r=[bZ[1]], w=[Xs[1]])
                            yield True
                            for hf in range(2):
                                K.op(pe, lambda e: e.matmul(bZ[0][0:64, 128 + 64 * hf:128 + 64 * hf + 64], lhsT=Tf[0][pv:pv + 64, hf, :],
                                                            rhs=Xs[0][pv:pv + 64, 64 * hf:64 * hf + 64], start=True, stop=True),
                                     r=[Tf[1], Xs[1]], w=[bZ[1]])
                            yield True
                            K.op(dve, lambda e: e.tensor_copy(out=Us[0][pv:pv + 64, :], in_=bZ[0][0:64, 128:256]), r=[bZ[1]], w=[Us[1]])
                            yield True
                            for hf in range(2):
                                lo = 64 * hf
                                o_ = bZ[0][0:64, 256 + 64 * hf:256 + 64 * hf + 64]
                                K.op(pe, lambda e: e.matmul(o_, lhsT=AR[0][lo:lo + 64, c, 1, :], rhs=Hb[lo:lo + 64, :], start=True, stop=False),
                                     r=[AR[1], Hbb], w=[bZ[1]])
                                K.op(pe, lambda e: e.matmul(o_, lhsT=MB[0][pv:pv + 64, hf, 64:128], rhs=Us[0][pv:pv + 64, 64 * hf:64 * hf + 64],
                                                            start=False, stop=False), r=[MB[1], Us[1]], w=[bZ[1]])
                                K.op(pe, lambda e: e.matmul(o_, lhsT=MK[0][pv:pv + 64, hf, 64:128], rhs=vtok[0][pv:pv + 64, 64 * hf:64 * hf + 64],
                                                            start=False, stop=True), r=[MK[1], vtok[1]], w=[bZ[1]])
                                o2 = bZ[0][:, 384 + 64 * hf:384 + 64 * hf + 64]
                                K.op(pe, lambda e: e.matmul(o2, lhsT=kbhat[0][pv:pv + 64, 1, :], rhs=Us[0][pv:pv + 64, 64 * hf:64 * hf + 64],
                                                            start=True, stop=False), r=[kbhat[1], Us[1]], w=[bZ[1]])
                                K.op(pe, lambda e: e.matmul(o2, lhsT=kbhat[0][pv:pv + 64, 0, :], rhs=vtok[0][pv:pv + 64, 64 * hf:64 * hf + 64],
                                                            start=False, stop=True), r=[kbhat[1], vtok[1]], w=[bZ[1]])
                            yield True
                            for hf in range(2):
                                lo = 64 * hf
                                K.op(dve, lambda e: e.scalar_tensor_tensor(out=Hs[lo:lo + 64, :], in0=Hs[lo:lo + 64, :], scalar=GC[0][lo:lo + 64, c:c + 1],
                                                                           in1=bZ[0][lo:lo + 64, 384 + 64 * hf:384 + 64 * hf + 64],
                                                                           op0=ALU.mult, op1=ALU.add), r=[Hsb, GC[1], bZ[1]], w=[Hsb])
                            K.op(dve, lambda e: e.tensor_copy(out=Hb[:], in_=Hs[:]), r=[Hsb], w=[Hbb])
                            K.op(dve, lambda e: e.tensor_copy(out=ytile[0][pv:pv + 64, :], in_=bZ[0][0:64, 256:384]), r=[bZ[1]], w=[ytile[1]])
                            yield True
                        for hf in range(2):
                            K.op(dve, lambda e: e.bn_stats(out=C.st6[0][:, hf, :], in_=ytile[0][:, 64 * hf:64 * hf + 64]), r=[ytile[1]], w=[C.st6[1]])
                            K.op(dve, lambda e: e.bn_aggr(out=C.mv[0][:, hf, :], in_=C.st6[0][:, hf, :]), r=[C.st6[1]], w=[C.mv[1]])
                        yield True
                        K.op(act, lambda e: e.activation(out=C.rstd[0][:], in_=C.mv[0][:, :, 1], func=AF.Ln, bias=epsb[:, 2:3], scale=1.0),
                             r=[C.mv[1], ssb], w=[C.rstd[1]])
                        K.op(act, lambda e: e.activation(out=C.rstd[0][:], in_=C.rstd[0][:], func=AF.Exp, scale=-0.5), r=[C.rstd[1]], w=[C.rstd[1]])
                        yield True
                        for hf in range(2):
                            cs = slice(64 * hf, 64 * hf + 64)
                            K.op(dve, lambda e: e.tensor_scalar(out=yn[0][:, cs], in0=ytile[0][:, cs], scalar1=C.mv[0][:, hf, 0:1],
                                                                scalar2=C.rstd[0][:, hf:hf + 1], op0=ALU.subtract, op1=ALU.mult),
                                 r=[ytile[1], C.mv[1], C.rstd[1]], w=[yn[1]])
                        K.op(dve, lambda e: e.tensor_tensor(out=yn[0][:], in0=yn[0][:], in1=C.lnw[:], op=ALU.mult),
                             r=[yn[1], C.lnbuf], w=[yn[1]])
                        K.op(dve, lambda e: e.tensor_tensor(out=yn[0][:], in0=yn[0][:], in1=C.lnb[:], op=ALU.add),
                             r=[yn[1], C.lnbuf], w=[yn[1]])
                        for hf in range(2):
                            cs = slice(64 * hf, 64 * hf + 64)
                            K.op(dve, lambda e: e.scalar_tensor_tensor(out=ybf[0][:, cs], in0=vtok[0][:, cs], scalar=S.bon[0][:, hf:hf + 1],
                                                                       in1=yn[0][:, cs], op0=ALU.mult, op1=ALU.add),
                                 r=[vtok[1], S.bon[1], yn[1]], w=[ybf[1]])
                        yield True
                        K.op(pe, lambda e: e.transpose(out=psT[0][:, 2 + C.ci, :], in_=ybf[0][:], identity=ident[:]), r=[ybf[1], identb], w=[psT[1]])
                        yield True
                        K.op(dve, lambda e: e.tensor_tensor(out=ogT[:, p, t0:t0 + 128], in0=psT[0][:, 2 + C.ci, :], in1=S.sg[0][:], op=ALU.mult),
                             r=[psT[1], S.sg[1]], w=[ogTb])
                        C.back_done = i + 1
                        yield True

                def pair_stream(C):
                    for p in range(C.ci, 8, 2):
                        for t_ in range(4):
                            load_w(1024 * t_ + 128 * p, C.wcp[t_])
                            yield True
                        K.dma(sp, C.lnw[:], rwkv_ln_w[j, 128 * p:128 * p + 128].partition_broadcast(128), w=[C.lnbuf])
                        K.dma(sp, C.lnb[:], rwkv_ln_b[j, 128 * p:128 * p + 128].partition_broadcast(128), w=[C.lnbuf])
                        K.op(dve, lambda e: e.memset(C.Hs[:], 0.0), w=[C.Hsb])
                        K.op(dve, lambda e: e.memset(C.Hb[:], 0.0), w=[C.Hbb])
                        C.front_done = 0
                        C.back_done = 0
                        gens = [front(C, p), back(C, p)]
                        while gens:
                            progressed = False
                            for g_ in list(gens):
                                try:
                                    if next(g_):
                                        progressed = True
                                except StopIteration:
                                    gens.remove(g_)
                                    progressed = True
                            yield progressed

                streams = [pair_stream(C) for C in ctxs]
                while streams:
                    for s_ in list(streams):
                        try:
                            next(s_)
                        except StopIteration:
                            streams.remove(s_)
                K.barrier()


        for layer in range(nlayers):
            phase_prenorm(layer)
            if layer % 2 == 0:
                fox_layer(layer)
                phase_post(layer, fox_w_out[layer // 2], layer == nlayers - 1)
            else:
                rwkv_layer(layer)
                phase_post(layer, rwkv_w_out[layer // 2], layer == nlayers - 1)
            K.barrier()
        K.barrier()
    return nc


_CACHE = {}


def _consts():
    idx = np.arange(128)
    tri = (idx[:, None] <= idx[None, :]).astype(np.float32)
    ident = np.eye(128, dtype=np.float32)
    ones = np.ones((128, 128), np.float32)
    m64 = np.zeros((128, 128), np.float32)
    s = idx[:, None] % 64
    t = idx[None, :] % 64
    m64[:, 0:64] = (s < t)[:, 0:64]
    m64[:, 64:128] = (s <= t)[:, 64:128]
    scan = np.ones((128, 512), np.float32)
    scan[:, ::64] = 0.0
    hind = np.zeros((128, 2), np.float32)
    hind[0:64, 0] = 1.0
    hind[64:128, 1] = 1.0
    m64x2 = np.concatenate([m64, m64], axis=1)
    r64 = idx[:, None] % 64
    c64 = np.arange(64)[None, :]
    mlow = (c64 < r64).astype(np.float32)
    mlow2 = np.concatenate([mlow, mlow], axis=1)
    i64 = (c64 == r64).astype(np.float32)
    idx2 = np.concatenate([i64, i64], axis=1)
    bones = ((idx[:, None] // 64) == (idx[None, :] // 64)).astype(np.float32)
    return dict(c_ident=ident, c_tri=tri, c_ones=ones, c_m64=m64, c_scan=scan, c_hind=hind,
                c_m64x2=m64x2, c_mlow2=mlow2, c_idx2=idx2, c_bones=bones)


def kernel(x, meta_tokens, norm_pre, norm_post, fox_w_in, fox_b_f, fox_w_out,
           rwkv_w_in, rwkv_mu, rwkv_w0, rwkv_w_up, rwkv_a0, rwkv_a_up, rwkv_k_k,
           rwkv_k_a, rwkv_r_k, rwkv_ln_w, rwkv_ln_b, rwkv_w_out, _nlayers=DEPTH):
    f = lambda a: np.ascontiguousarray(np.asarray(a, dtype=np.float32))
    x = f(x)
    B = x.shape[0]
    meta = f(meta_tokens)
    h0 = np.zeros((B, T, D), np.float32)
    h0[:, :NMETA] = meta[None]
    h0[:, NMETA:NMETA + SEQ] = x
    shared = dict(
        norm_pre=f(norm_pre), norm_post=f(norm_post), fox_w_in=f(fox_w_in), fox_b_f=f(fox_b_f),
        fox_w_out=f(fox_w_out), rwkv_w_in=f(rwkv_w_in), rwkv_mu=f(rwkv_mu), rwkv_w0=f(rwkv_w0),
        rwkv_w_up=f(rwkv_w_up), rwkv_a0=f(rwkv_a0), rwkv_a_up=f(rwkv_a_up), rwkv_k_k=f(rwkv_k_k),
        rwkv_k_a=f(rwkv_k_a), rwkv_r_k=f(rwkv_r_k).reshape(2, D), rwkv_ln_w=f(rwkv_ln_w),
        rwkv_ln_b=f(rwkv_ln_b), rwkv_w_out=f(rwkv_w_out))
    shared.update(_consts())
    key = _nlayers
    if key not in _CACHE:
        _CACHE[key] = build_program(_nlayers)
    nc = _CACHE[key]
    in_maps = []
    for b in range(B):
        m = dict(shared)
        m["h0"] = h0[b]
        in_maps.append(m)
    res = run_bass_kernel_spmd(nc, in_maps, core_ids=list(range(B)))
    out = np.stack([np.asarray(r["y"])[NMETA:NMETA + SEQ] for r in res.results], axis=0)
    return out.astype(np.float32)
```

```python
import math
from contextlib import ExitStack

import numpy as np
import concourse.bass as bass
import concourse.mybir as mybir
from concourse.bass_utils import run_bass_kernel_spmd

F32 = mybir.dt.float32
BF16 = mybir.dt.bfloat16
AF = mybir.ActivationFunctionType
ALU = mybir.AluOpType
AX = mybir.AxisListType

D = 1024
SEQ = 2048
NMETA = 16
NT = 17
T = NT * 128
DEPTH = 4
NH = 16
HD = 64
FOX_IN = 4 * D + NH
RWKV_IN = 4 * D + 128
NORM_EPS = 1e-6
GN_EPS = 64e-5
DECAY_SCALE = math.exp(-0.5)
CH = 64
DBG = {"stage": 9, "pairs": 8, "tiles": NT}


def tok_chunks(n=512):
    out = []
    t0 = 0
    while t0 < T:
        m = min(n, T - t0)
        out.append((t0, m))
        t0 += m
    return out


class Buf:
    __slots__ = ("name", "lw", "rd", "excl")

    def __init__(self, name="", excl=False):
        self.name = name
        self.lw = None
        self.rd = {}
        self.excl = excl


class Q:
    def __init__(self, name, eng, sem, self_sync=True):
        self.name = name
        self.eng = eng
        self.sem = sem
        self.cnt = 0
        self.seen = {}
        self.self_sync = self_sync
        self.key = name
        self.ring = []
        self.ring_i = 0


class PEProxy:
    def __init__(self, K, eng):
        self.K = K
        self.eng = eng
        self.partial = False

    def _pre(self, st_ap, out):
        K = self.K
        rg = (st_ap.base_partition(), st_ap.partition_size())
        bank = out.name
        last = K.pe_last
        if last is not None and last[0] != rg and (last[1] == bank or (last[0][1] < 128 and rg[1] < 128)):
            assert K.pe_last_tk is not None, "previous matmul needs a semaphore increment"
            K._wait(K.pe, K.pe_last_tk)
        K.pe_last = (rg, bank)
        self.partial = rg[1] < 128

    def matmul(self, out, lhsT, rhs, **kw):
        self._pre(lhsT, out)
        return self.eng.matmul(out, lhsT=lhsT, rhs=rhs, **kw)

    def transpose(self, out, in_, identity):
        self._pre(in_, out)
        return self.eng.transpose(out=out, in_=in_, identity=identity)


class KB:
    def __init__(self, nc, st):
        self.nc = nc
        self.st = st
        mk = lambda n: st.enter_context(nc.semaphore(n))
        self.pe = Q("pe", nc.tensor, mk("s_pe"), self_sync=False)
        self.act = Q("act", nc.scalar, mk("s_act"))
        self.dve = Q("dve", nc.vector, mk("s_dve"))
        self.pool = Q("pool", nc.gpsimd, mk("s_pool"))
        self.sp = Q("sp", nc.sync, mk("s_sp"))
        self.queues = [self.pe, self.act, self.dve, self.pool, self.sp]
        for q, n in ((self.sp, 24), (self.pool, 24)):
            for i in range(n):
                q.ring.append([mk(f"d_{q.name}{i}"), 0, f"d_{q.name}{i}"])
        self.dma_tickets = []
        self.nbuf = 0
        self.counting = False
        self.nops = 0
        self.pe_last = None
        self.pe_last_tk = None
        self.prox = PEProxy(self, nc.tensor)

    def sb(self, st, name, shape, dt):
        self.nbuf += 1
        return st.enter_context(self.nc.sbuf_tensor(f"{name}_{self.nbuf}", list(shape), dt))

    def psum(self, st, name, shape, dt):
        return st.enter_context(self.nc.psum_tensor(name, list(shape), dt))

    def _wait(self, q, tk):
        sem, val, key = tk
        if q.seen.get(key, 0) >= val:
            return
        q.eng.wait_ge(sem, val)
        q.seen[key] = val

    def _deps(self, q, r, w):
        deps = []
        for b in r:
            if b.lw is not None:
                deps.append(b.lw)
            if b.excl:
                for key, tk in b.rd.items():
                    if key != q.key:
                        deps.append(tk)
        for b in w:
            if b.lw is not None:
                deps.append(b.lw)
            for key, tk in b.rd.items():
                deps.append(tk)
        for tk in deps:
            if tk[2] == q.key and not q.self_sync:
                continue
            self._wait(q, tk)

    def _record(self, tk, r, w):
        for b in r:
            old = b.rd.get(tk[2])
            if old is None or old[1] < tk[1]:
                b.rd[tk[2]] = tk
        for b in w:
            b.lw = tk
            b.rd = {}

    def op(self, q, fn, r=(), w=(), inc=True):
        if self.counting:
            self.nops += 1
            if self.nops > DBG.get("maxops", 10 ** 9):
                return None
        self._deps(q, r, w)
        if q is self.pe:
            self.prox.partial = False
            ins = fn(self.prox)
            if self.prox.partial:
                inc = True
        else:
            ins = fn(q.eng)
        if inc:
            q.cnt += 1
            ins.then_inc(q.sem, 1)
            tk = (q.sem, q.cnt, q.key)
        else:
            tk = (q.sem, q.cnt + 1, q.key)
        if q is self.pe:
            self.pe_last_tk = tk if inc else None
        self._record(tk, r, w)
        return tk

    def dma(self, q, out, in_, r=(), w=()):
        self._deps(q, r, w)
        slot = q.ring[q.ring_i % len(q.ring)]
        q.ring_i += 1
        sem, n, key = slot
        if n > 0:
            self._wait(q, (sem, 16 * n, key))
        q.eng.dma_start(out=out, in_=in_).then_inc(sem, 16)
        slot[1] = n + 1
        tk = (sem, 16 * (n + 1), key)
        self._record(tk, r, w)
        self.dma_tickets.append(tk)
        return tk

    def barrier(self):
        tks = [(q.sem, q.cnt, q.key) for q in self.queues if q.cnt > 0]
        for q in self.queues:
            for slot in q.ring:
                if slot[1] > 0:
                    tks.append((slot[0], 16 * slot[1], slot[2]))
        for q in self.queues:
            for tk in tks:
                if tk[2] == q.key:
                    continue
                self._wait(q, tk)


def build_program(nlayers=DEPTH, dbg=False):
    nc = bass.Bass("TRN2", target_bir_lowering=False)
    dt_in = lambda name, shape: nc.dram_tensor(name, list(shape), F32, kind="ExternalInput").ap()
    h0 = dt_in("h0", [T, D])
    norm_pre = dt_in("norm_pre", [DEPTH, D])
    norm_post = dt_in("norm_post", [DEPTH, D])
    fox_w_in = dt_in("fox_w_in", [2, D, FOX_IN])
    fox_b_f = dt_in("fox_b_f", [2, NH])
    fox_w_out = dt_in("fox_w_out", [2, D, D])
    rwkv_w_in = dt_in("rwkv_w_in", [2, D, RWKV_IN])
    rwkv_mu = dt_in("rwkv_mu", [2, RWKV_IN])
    rwkv_w0 = dt_in("rwkv_w0", [2, D])
    rwkv_w_up = dt_in("rwkv_w_up", [2, 64, D])
    rwkv_a0 = dt_in("rwkv_a0", [2, D])
    rwkv_a_up = dt_in("rwkv_a_up", [2, 64, D])
    rwkv_k_k = dt_in("rwkv_k_k", [2, D])
    rwkv_k_a = dt_in("rwkv_k_a", [2, D])
    rwkv_r_k = dt_in("rwkv_r_k", [2, D])
    rwkv_ln_w = dt_in("rwkv_ln_w", [2, D])
    rwkv_ln_b = dt_in("rwkv_ln_b", [2, D])
    rwkv_w_out = dt_in("rwkv_w_out", [2, D, D])
    c_ident = dt_in("c_ident", [128, 128])
    c_tri = dt_in("c_tri", [128, 128])
    c_ones = dt_in("c_ones", [128, 128])
    c_m64 = dt_in("c_m64", [128, 128])
    c_scan = dt_in("c_scan", [128, 512])
    c_hind = dt_in("c_hind", [128, 2])
    c_m64x2 = dt_in("c_m64x2", [128, 256])
    c_mlow2 = dt_in("c_mlow2", [128, 128])
    c_idx2 = dt_in("c_idx2", [128, 128])
    c_bones = dt_in("c_bones", [128, 128])
    y = nc.dram_tensor("y", [T, D], F32, kind="ExternalOutput").ap()

    with ExitStack() as st:
        K = KB(nc, st)
        pe, act, dve, pool, sp = K.pe, K.act, K.dve, K.pool, K.sp

        uT = K.sb(st, "uT", [128, 8, T + 2], BF16)
        uTb = Buf("uT")
        ogT = K.sb(st, "ogT", [128, 8, T], BF16)
        ogTb = Buf("ogT")
        ident = K.sb(st, "ident", [128, 128], BF16)
        identb = Buf()
        tri_f = K.sb(st, "tri_f", [128, 128], F32)
        ones_f = K.sb(st, "ones_f", [128, 128], F32)
        tri_b = K.sb(st, "tri_b", [128, 128], BF16)
        constb = Buf()
        gpre = K.sb(st, "gpre", [128, D], F32)
        gpost = K.sb(st, "gpost", [128, D], F32)
        gb = Buf()
        hbufs = [(K.sb(st, f"hb{i}", [128, D], F32), Buf()) for i in range(2)]
        hbufs2 = hbufs
        un = K.sb(st, "un", [128, D], BF16)
        unb = Buf()
        un2 = K.sb(st, "un2", [128, D], BF16)
        uns = [(un, unb), (un2, Buf())]
        sss = [(K.sb(st, f"ss{i}", [128, 4], F32), Buf()) for i in range(2)]
        mt = K.sb(st, "mt", [128, D], F32)
        mtb = Buf()
        ss = K.sb(st, "ss", [128, 4], F32)
        ssb = Buf()
        epsb = K.sb(st, "epsb", [128, 4], F32)
        K.op(dve, lambda e: e.memset(epsb[:, 0:1], NORM_EPS), w=[ssb])
        K.op(dve, lambda e: e.memset(epsb[:, 1:2], 1.0), w=[ssb])
        K.op(dve, lambda e: e.memset(epsb[:, 2:3], GN_EPS), w=[ssb])
        wout = K.sb(st, "wout", [128, 8, D], BF16)
        woutb = Buf()
        psA = [(K.psum(st, f"psA{i}", [128, 512], F32), Buf(excl=True)) for i in range(6)]
        psT = (K.psum(st, "psT", [128, 8, 128], BF16), Buf(excl=True))
        psT2 = (K.psum(st, "psT2", [128, 8, 128], BF16), Buf(excl=True))
        psTs = [psT, psT2]
        hB = [Buf(f"h{i}") for i in range(NT)]

        K.dma(pool, ident[:], c_ident[:, :], w=[identb])
        K.dma(pool, tri_b[:], c_tri[:, :], w=[constb])
        K.dma(sp, tri_f[:], c_tri[:, :], w=[constb])
        K.dma(sp, ones_f[:], c_ones[:, :], w=[constb])
        K.op(dve, lambda e: e.memset(uT[:, :, 0:1], 0.0), w=[uTb])

        def bcast_row(ap_row, n):
            return ap_row.partition_broadcast(128)

        def phase_prenorm(layer):
            K.dma(sp, gpre[:], bcast_row(norm_pre[layer, :], D), w=[gb])
            K.dma(sp, gpost[:], bcast_row(norm_post[layer, :], D), w=[gb])
            for i in range(NT):
                ht, htb = hbufs[i % 2]
                ss_, ssb_ = sss[i % 2]
                un_, unb_ = uns[i % 2]
                pT = psTs[i % 2]
                src = h0 if layer == 0 else y
                K.dma(sp, ht[:], src[128 * i:128 * i + 128, :], r=[hB[i]], w=[htb])
                K.op(act, lambda e: e.activation(out=mt[:], in_=ht[:], func=AF.Square, accum_out=ss_[:, 0:1]),
                     r=[htb], w=[mtb, ssb_])
                K.op(act, lambda e: e.activation(out=ss_[:, 1:2], in_=ss_[:, 0:1], func=AF.Ln, bias=epsb[:, 0:1], scale=1.0 / D),
                     r=[ssb_, ssb], w=[ssb_])
                K.op(act, lambda e: e.activation(out=ss_[:, 2:3], in_=ss_[:, 1:2], func=AF.Exp, scale=-0.5),
                     r=[ssb_], w=[ssb_])
                K.op(dve, lambda e: e.scalar_tensor_tensor(out=un_[:], in0=ht[:], scalar=ss_[:, 2:3], in1=gpre[:],
                                                           op0=ALU.mult, op1=ALU.mult), r=[htb, ssb_, gb], w=[unb_])
                for c in range(8):
                    K.op(pe, lambda e: e.transpose(out=pT[0][:, c, :], in_=un_[:, 128 * c:128 * c + 128],
                                                   identity=ident[:]),
                         r=[unb_, identb], w=[pT[1]], inc=(c == 7))
                if i % 2 == 0:
                    K.op(act, lambda e: e.copy(out=uT[:, :, 1 + 128 * i:1 + 128 * i + 128], in_=pT[0][:, :, :]),
                         r=[pT[1]], w=[uTb])
                else:
                    K.op(dve, lambda e: e.tensor_copy(out=uT[:, :, 1 + 128 * i:1 + 128 * i + 128], in_=pT[0][:, :, :]),
                         r=[pT[1]], w=[uTb])

        def phase_post(layer, w_out_ap, last):
            K.dma(pool, wout[:], w_out_ap.rearrange("(c p) n -> p c n", p=128), w=[woutb])
            for i in range(NT):
                pms = [psA[0], psA[1]] if i % 2 == 0 else [psA[2], psA[3]]
                ss_, ssb_ = sss[i % 2]
                for hf in range(2):
                    for c in range(8):
                        K.op(pe, lambda e: e.matmul(pms[hf][0][:, :], lhsT=ogT[:, c, 128 * i:128 * i + 128],
                                                    rhs=wout[:, c, 512 * hf:512 * hf + 512],
                                                    start=(c == 0), stop=(c == 7)),
                             r=[ogTb, woutb], w=[pms[hf][1]], inc=(c == 7))
                for hf in range(2):
                    K.op(act, lambda e: e.activation(out=un[:, 512 * hf:512 * hf + 512], in_=pms[hf][0][:, :], func=AF.Square,
                                                     accum_out=ss_[:, hf:hf + 1]), r=[pms[hf][1]], w=[unb, ssb_])
                K.op(dve, lambda e: e.tensor_tensor(out=ss_[:, 2:3], in0=ss_[:, 0:1], in1=ss_[:, 1:2], op=ALU.add),
                     r=[ssb_], w=[ssb_])
                K.op(act, lambda e: e.activation(out=ss_[:, 3:4], in_=ss_[:, 2:3], func=AF.Ln, bias=epsb[:, 0:1], scale=1.0 / D),
                     r=[ssb_, ssb], w=[ssb_])
                K.op(act, lambda e: e.activation(out=ss_[:, 3:4], in_=ss_[:, 3:4], func=AF.Exp, scale=-0.5),
                     r=[ssb_], w=[ssb_])
                ht, htb = hbufs2[i % 2]
                src = h0 if layer == 0 else y
                K.dma(sp, ht[:], src[128 * i:128 * i + 128, :], r=[hB[i]], w=[htb])
                for hf in range(2):
                    K.op(dve, lambda e: e.scalar_tensor_tensor(out=mt[:, 512 * hf:512 * hf + 512], in0=pms[hf][0][:, :], scalar=ss_[:, 3:4],
                                                               in1=gpost[:, 512 * hf:512 * hf + 512], op0=ALU.mult, op1=ALU.mult),
                         r=[pms[hf][1], ssb_, gb], w=[mtb])
                K.op(dve, lambda e: e.tensor_tensor(out=ht[:], in0=ht[:], in1=mt[:], op=ALU.add),
                     r=[htb, mtb], w=[htb])
                K.dma(sp, y[128 * i:128 * i + 128, :], ht[:], r=[htb], w=[hB[i]])

        def fox_layer(layer):
            j = layer // 2
            win = fox_w_in[j].rearrange("(c p) n -> p c n", p=128)
            with ExitStack() as ls:
                wb = [[(K.sb(ls, f"fw{s}{t}", [128, 8, 256], BF16), Buf()) for t in range(4)] for s in range(2)]
                wf = K.sb(ls, "fwf", [128, 8, 16], BF16)
                wfb = Buf()
                qT = K.sb(ls, "qT", [128, 2, T], BF16)
                kT = K.sb(ls, "kT", [128, 2, T], BF16)
                sgT = K.sb(ls, "sgT", [128, 2, T], BF16)
                qTb, kTb, sgTb = Buf(), Buf(), Buf()
                vaug = K.sb(ls, "vaug", [128, NT, 4, 128], BF16)
                vaugb = Buf()
                bft = K.sb(ls, "bft", [128, NH], F32)
                lf = K.sb(ls, "lf", [128, NT, NH], F32)
                lfb = Buf()
                cum = K.sb(ls, "cum", [128, NT, NH], F32)
                carry = K.sb(ls, "carry", [128, NT, NH], F32)
                cumb = Buf()
                bias = [(K.sb(ls, f"bias{i}", [128, NT], F32), Buf()) for i in range(8)]
                PT = [(K.sb(ls, f"PT{i}", [128, 512], BF16), Buf()) for i in range(3)]
                rs = [(K.sb(ls, f"rs{i}", [128, 512], F32), Buf()) for i in range(2)]
                tmpo = [(K.sb(ls, f"tmpo{i}", [128, 512], F32), Buf()) for i in range(2)]

                def load_group(g, s):
                    for t in range(4):
                        K.dma(pool, wb[s][t][0][:], win[:, :, 1024 * t + 256 * g:1024 * t + 256 * g + 256],
                              w=[wb[s][t][1]])
                K.dma(pool, wf[:], win[:, :, 4096:4112], w=[wfb])
                load_group(0, 0)
                K.dma(sp, bft[:], fox_b_f[j, :].partition_broadcast(128), w=[lfb])
                K.op(dve, lambda e: e.memset(vaug[:], 1.0), w=[vaugb])

                pf = psA[4]
                for i in range(NT):
                    for c in range(8):
                        K.op(pe, lambda e: e.matmul(pf[0][:, 16 * i:16 * i + 16], lhsT=uT[:, c, 1 + 128 * i:1 + 128 * i + 128],
                                                    rhs=wf[:, c, :], start=(c == 0), stop=(c == 7)),
                             r=[uTb, wfb], w=[pf[1]], inc=(c == 7))
                for i in range(NT):
                    K.op(dve, lambda e: e.tensor_tensor(out=lf[:, i, :], in0=pf[0][:, 16 * i:16 * i + 16], in1=bft[:],
                                                        op=ALU.add), r=[pf[1], lfb], w=[lfb])
                lf2 = lf[:].rearrange("p a b -> p (a b)")
                K.op(act, lambda e: e.activation(out=lf2, in_=lf2, func=AF.Exp, scale=-1.0), r=[lfb], w=[lfb])
                K.op(act, lambda e: e.activation(out=lf2, in_=lf2, func=AF.Ln, bias=epsb[:, 1:2], scale=1.0), r=[lfb, ssb], w=[lfb])
                K.op(dve, lambda e: e.tensor_scalar(out=lf2, in0=lf2, scalar1=-1.0, scalar2=None, op0=ALU.mult),
                     r=[lfb], w=[lfb])
                pc, pl = psA[2], psA[3]
                K.op(dve, lambda e: e.memset(carry[:, 0, :], 0.0), w=[cumb])
                for i in range(NT):
                    for jj in range(i):
                        K.op(pe, lambda e: e.matmul(pc[0][:, 16 * i:16 * i + 16], lhsT=ones_f[:], rhs=lf[:, jj, :],
                                                    start=(jj == 0), stop=(jj == i - 1)),
                             r=[lfb, constb], w=[pc[1]], inc=(jj == i - 1))
                    K.op(pe, lambda e: e.matmul(pl[0][:, 16 * i:16 * i + 16], lhsT=tri_f[:], rhs=lf[:, i, :],
                                                start=True, stop=True), r=[lfb, constb], w=[pl[1]])
                K.op(dve, lambda e: e.tensor_copy(out=carry[:, 1:NT, :].rearrange("p a b -> p (a b)"),
                                                  in_=pc[0][:, 16:16 * NT]), r=[pc[1]], w=[cumb])
                K.op(dve, lambda e: e.tensor_tensor(out=cum[:].rearrange("p a b -> p (a b)"),
                                                    in0=pl[0][:, 0:16 * NT],
                                                    in1=carry[:].rearrange("p a b -> p (a b)"), op=ALU.add),
                     r=[pl[1], cumb], w=[cumb])

                chunks = tok_chunks(512)
                pcount = [0]

                def nextps():
                    p = psA[pcount[0] % 2]
                    pcount[0] += 1
                    return p

                stc = [0]
                otc = [0]
                ptc = [0]
                bc = [0]
                for g in range(4):
                    s = g % 2
                    if g + 1 < 4:
                        load_group(g + 1, (g + 1) % 2)
                    wq, wk, wv, wg = [wb[s][t] for t in range(4)]
                    for pp in range(2):
                        for (t0, n) in chunks:
                            for (wt, dst, dstb, kind) in ((wq, qT, qTb, 0), (wk, kT, kTb, 1), (wg, sgT, sgTb, 2)):
                                p = nextps()
                                for c in range(8):
                                    K.op(pe, lambda e: e.matmul(p[0][:, 0:n], lhsT=wt[0][:, c, 128 * pp:128 * pp + 128],
                                                                rhs=uT[:, c, 1 + t0:1 + t0 + n],
                                                                start=(c == 0), stop=(c == 7)),
                                         r=[uTb, wt[1]], w=[p[1]], inc=(c == 7))
                                if kind == 0:
                                    K.op(act, lambda e: e.activation(out=dst[:, pp, t0:t0 + n], in_=p[0][:, 0:n],
                                                                     func=AF.Copy, scale=HD ** -0.5),
                                         r=[p[1]], w=[dstb])
                                elif kind == 1:
                                    K.op(dve, lambda e: e.tensor_copy(out=dst[:, pp, t0:t0 + n], in_=p[0][:, 0:n]),
                                         r=[p[1]], w=[dstb])
                                else:
                                    K.op(act, lambda e: e.activation(out=dst[:, pp, t0:t0 + n], in_=p[0][:, 0:n],
                                                                     func=AF.Silu), r=[p[1]], w=[dstb])
                    for i in range(NT):
                        p = nextps()
                        for c in range(8):
                            K.op(pe, lambda e: e.matmul(p[0][:, 0:256], lhsT=uT[:, c, 1 + 128 * i:1 + 128 * i + 128],
                                                        rhs=wv[0][:, c, :], start=(c == 0), stop=(c == 7)),
                                 r=[uTb, wv[1]], w=[p[1]], inc=(c == 7))
                        pv = p[0][:, 0:256].rearrange("p (a b c) -> p a b c", a=2, b=2)
                        K.op(dve, lambda e: e.tensor_copy(out=vaug[:, i, 0:4:2, 0:64], in_=pv[:, :, 0, :]),
                             r=[p[1]], w=[vaugb])
                        K.op(dve, lambda e: e.tensor_copy(out=vaug[:, i, 1:4:2, 64:128], in_=pv[:, :, 1, :]),
                             r=[p[1]], w=[vaugb])
                    items = []
                    for hh in range(4):
                        for cidx, (q0, qn) in enumerate(chunks):
                            cx = dict(hh=hh, q0=q0, qn=qn, i0=q0 // 128, ni=qn // 128, started=False)
                            cx["jmax"] = cx["i0"] + cx["ni"] - 1
                            for jk in range(cx["jmax"] + 1):
                                items.append((cx, jk))

                    def emit_st(it):
                        cx, jk = it
                        hh = cx["hh"]
                        h = 4 * g + hh
                        pp, half = hh // 2, hh % 2
                        lo = 64 * half
                        q0, qn, i0, ni = cx["q0"], cx["qn"], cx["i0"], cx["ni"]
                        if not cx["started"]:
                            cx["started"] = True
                            base = 4 * (bc[0] % 2)
                            bc[0] += 1
                            btabs = []
                            for ii in range(ni):
                                bt = bias[base + ii]
                                i = i0 + ii
                                K.op(dve, lambda e: e.tensor_scalar(out=bt[0][:, :], in0=cum[:, :, h], scalar1=-1.0,
                                                                    scalar2=carry[:, i, h:h + 1], op0=ALU.mult,
                                                                    op1=ALU.add), r=[cumb], w=[bt[1]])
                                btabs.append(bt)
                            cx["btabs"] = btabs
                            cx["ot"] = psA[4 + otc[0] % 2]
                            otc[0] += 1
                        qs = max(q0, 128 * jk)
                        n = q0 + qn - qs
                        stp = psA[2 + stc[0] % 2]
                        stc[0] += 1
                        K.op(pe, lambda e: e.matmul(stp[0][:, 0:n], lhsT=kT[lo:lo + 64, pp, 128 * jk:128 * jk + 128],
                                                    rhs=qT[lo:lo + 64, pp, qs:qs + n], start=True, stop=True),
                             r=[kTb, qTb], w=[stp[1]])
                        return stp

                    def emit_rest(it, stp):
                        cx, jk = it
                        hh = cx["hh"]
                        pp, half = hh // 2, hh % 2
                        olo, slo = (0, 64) if half == 0 else (64, 0)
                        q0, qn, i0, ni, jmax = cx["q0"], cx["qn"], cx["i0"], cx["ni"], cx["jmax"]
                        btabs, ot = cx["btabs"], cx["ot"]
                        qs = max(q0, 128 * jk)
                        n = q0 + qn - qs
                        pt = PT[ptc[0] % 3]
                        ptc[0] += 1
                        for ii in range((qs - q0) // 128, ni):
                            i = i0 + ii
                            co = 128 * i - qs
                            K.op(act, lambda e: e.activation(out=pt[0][:, co:co + 128], in_=stp[0][:, co:co + 128],
                                                             func=AF.Exp, bias=btabs[ii][0][:, jk:jk + 1], scale=1.0),
                                 r=[stp[1], btabs[ii][1]], w=[pt[1]])
                        if jk >= i0:
                            K.op(dve, lambda e: e.tensor_tensor(out=pt[0][:, 0:128], in0=pt[0][:, 0:128],
                                                                in1=tri_b[:], op=ALU.mult),
                                 r=[pt[1], constb], w=[pt[1]])
                        K.op(pe, lambda e: e.matmul(ot[0][:, qs - q0:qs - q0 + n], lhsT=vaug[:, jk, hh, :],
                                                    rhs=pt[0][:, 0:n], start=(jk == 0), stop=(jk == jmax),
                                                    skip_group_check=True),
                             r=[vaugb, pt[1]], w=[ot[1]])
                        if jk == jmax:
                            r_ = rs[otc[0] % 2]
                            tm = tmpo[otc[0] % 2]
                            K.op(dve, lambda e: e.reciprocal(out=r_[0][olo:olo + 64, 0:qn], in_=ot[0][slo:slo + 64, 0:qn]),
                                 r=[ot[1]], w=[r_[1]])
                            K.op(dve, lambda e: e.tensor_tensor(out=tm[0][olo:olo + 64, 0:qn], in0=ot[0][olo:olo + 64, 0:qn],
                                                                in1=r_[0][olo:olo + 64, 0:qn], op=ALU.mult),
                                 r=[ot[1], r_[1]], w=[tm[1]])
                            K.op(dve, lambda e: e.tensor_tensor(out=ogT[olo:olo + 64, 2 * g + pp, q0:q0 + qn],
                                                                in0=tm[0][olo:olo + 64, 0:qn],
                                                                in1=sgT[olo:olo + 64, pp, q0:q0 + qn], op=ALU.mult),
                                 r=[tm[1], sgTb], w=[ogTb])

                    nxt = emit_st(items[0])
                    for n_ in range(len(items)):
                        cur_st = nxt
                        if n_ + 1 < len(items):
                            nxt = emit_st(items[n_ + 1])
                        emit_rest(items[n_], cur_st)
                K.barrier()


        def rwkv_layer(layer):
            j = layer // 2
            win = rwkv_w_in[j].rearrange("(c p) n -> p c n", p=128)
            with ExitStack() as ls:
                SB = lambda n, shp, dt: K.sb(ls, n, shp, dt)
                wadT = SB("wadT", [128, T], BF16); wadTb = Buf()
                wup = SB("wup", [128, D], BF16); wupb = Buf()
                vecs = SB("vecs", [128, 8, 8], F32); vecb = Buf()
                mub = (SB("mub", [128, 2, 128], F32), Buf())
                wraw = (SB("wraw", [128, 8, 128], F32), Buf())
                bones = SB("bones", [128, 128], F32)
                m64x2 = SB("m64x2", [128, 2, 128], F32)
                mlow2 = SB("mlow2", [128, 2, 64], F32)
                idx2f = SB("idx2f", [128, 2, 64], F32)
                idx2b = SB("idx2b", [128, 2, 64], BF16)
                scanm = SB("scanm", [128, 128], F32)
                hind = SB("hind", [128, 2], BF16)
                cb2 = Buf()

                class Ctx:
                    pass

                ctxs = []
                for ci in range(2):
                    C = Ctx()
                    C.ci = ci
                    F_ = lambda n: (SB(f"{n}{ci}", [128, 128], F32), Buf())
                    B_ = lambda n, shp: (SB(f"{n}{ci}", shp, BF16), Buf())
                    for n in ("rf", "kf", "sgw", "av", "lw", "cm", "cmx", "E1", "E2", "E3", "kkr", "sq", "hsn", "kk",
                              "t1", "kp", "bb", "ke3", "be3", "gs", "ytile", "yn"):
                        setattr(C, n, F_(n))
                    C.sets = []
                    for si in range(2):
                        S = Ctx()
                        S.sg = B_(f"sg{si}", [128, 128]); S.vtok = B_(f"vtok{si}", [128, 128])
                        S.AR = B_(f"AR{si}", [128, 2, 2, 64]); S.BK = B_(f"BK{si}", [128, 2, 2, 64])
                        S.kbhat = B_(f"kbhat{si}", [128, 2, 128])
                        S.MB = B_(f"MB{si}", [128, 2, 128]); S.MK = B_(f"MK{si}", [128, 2, 128])
                        S.TT = B_(f"TTf{si}", [128, 2, 64])
                        S.GC = (SB(f"GC{ci}{si}", [128, 2], F32), Buf())
                        S.bon = (SB(f"bon{ci}{si}", [128, 2], F32), Buf())
                        C.sets.append(S)
                    C.khT = B_("khT", [128, 128]); C.bhT = B_("bhT", [128, 128]); C.rkb = B_("rkb", [128, 128])
                    C.PQ = [B_(f"PQ{i}", [128, 2, 128]) for i in range(2)]
                    C.TTl = [B_(f"TT{i}", [128, 2, 64]) for i in range(2)]
                    C.Xs = B_("Xs", [128, 128]); C.Us = B_("Us", [128, 128]); C.ybf = B_("ybf", [128, 128])
                    C.st6 = (SB(f"st6{ci}", [128, 2, 6], F32), Buf())
                    C.mv = (SB(f"mv{ci}", [128, 2, 2], F32), Buf())
                    C.rstd = (SB(f"rstd{ci}", [128, 2], F32), Buf())
                    C.Hs = SB(f"Hs{ci}", [128, 64], F32); C.Hb = SB(f"Hb{ci}", [128, 64], BF16)
                    C.Hsb = Buf(); C.Hbb = Buf()
                    C.wcp = [(SB(f"wcp{ci}{t_}", [128, 16, 128], BF16), Buf()) for t_ in range(4)]
                    C.lnw = SB(f"lnw{ci}", [128, 128], F32); C.lnb = SB(f"lnb{ci}", [128, 128], F32); C.lnbuf = Buf()
                    C.bX, C.bY, C.bZ = psA[3 * ci], psA[3 * ci + 1], psA[3 * ci + 2]
                    C.front_done = 0
                    C.back_done = 0
                    ctxs.append(C)

                K.dma(sp, bones[:], c_bones[:, :], w=[cb2])
                K.dma(sp, m64x2[:].rearrange("p a b -> p (a b)"), c_m64x2[:, :], w=[cb2])
                K.dma(sp, mlow2[:].rearrange("p a b -> p (a b)"), c_mlow2[:, :], w=[cb2])
                K.dma(sp, idx2f[:].rearrange("p a b -> p (a b)"), c_idx2[:, :], w=[cb2])
                K.dma(pool, idx2b[:].rearrange("p a b -> p (a b)"), c_idx2[:, :], w=[cb2])
                K.dma(sp, scanm[:], c_scan[:, 0:128], w=[cb2])
                K.dma(pool, hind[:], c_hind[:, :], w=[cb2])
                K.dma(pool, wup[0:64, :], rwkv_w_up[j], w=[wupb])
                K.dma(pool, wup[64:128, :], rwkv_a_up[j], w=[wupb])
                with nc.allow_non_contiguous_dma(reason="tiny per-feature vectors"):
                    for vi, src in enumerate((rwkv_w0, rwkv_a0, rwkv_k_k, rwkv_k_a, rwkv_r_k)):
                        K.dma(sp, vecs[:, vi, :], src[j, :].rearrange("(c p) -> p c", p=128), w=[vecb])
                K.op(dve, lambda e: e.tensor_scalar(out=vecs[:, 5, :], in0=vecs[:, 3, :], scalar1=-1.0, scalar2=1.0,
                                                    op0=ALU.mult, op1=ALU.add), r=[vecb], w=[vecb])
                K.op(dve, lambda e: e.tensor_scalar(out=vecs[:, 6, :], in0=vecs[:, 0, :], scalar1=-1.0, scalar2=None,
                                                    op0=ALU.mult), r=[vecb], w=[vecb])
                K.op(dve, lambda e: e.tensor_scalar(out=vecs[:, 7, :], in0=vecs[:, 1, :], scalar1=-1.0, scalar2=None,
                                                    op0=ALU.mult), r=[vecb], w=[vecb])

                def load_w(col0, dst):
                    mb, rw = mub, wraw
                    K.dma(sp, mb[0][:, 0, :], rwkv_mu[j, col0:col0 + 128].partition_broadcast(128), w=[mb[1]])
                    K.dma(sp, rw[0][:], win[:, :, col0:col0 + 128], w=[rw[1]])
                    K.op(dve, lambda e: e.tensor_scalar(out=mb[0][:, 1, :], in0=mb[0][:, 0, :], scalar1=-1.0, scalar2=1.0,
                                                        op0=ALU.mult, op1=ALU.add), r=[mb[1]], w=[mb[1]])
                    for c in range(8):
                        K.op(dve, lambda e: e.tensor_tensor(out=dst[0][:, c, :], in0=rw[0][:, c, :], in1=mb[0][:, 1, :],
                                                            op=ALU.mult), r=[rw[1], mb[1]], w=[dst[1]])
                        K.op(dve, lambda e: e.tensor_tensor(out=dst[0][:, 8 + c, :], in0=rw[0][:, c, :], in1=mb[0][:, 0, :],
                                                            op=ALU.mult), r=[rw[1], mb[1]], w=[dst[1]])

                def proj_fm(out_ps, outb, wt, t0, n):
                    for c in range(16):
                        rhs = uT[:, c, 1 + t0:1 + t0 + n] if c < 8 else uT[:, c - 8, t0:t0 + n]
                        K.op(pe, lambda e: e.matmul(out_ps, lhsT=wt[0][:, c, :], rhs=rhs, start=(c == 0), stop=(c == 15)),
                             r=[uTb, wt[1]], w=[outb], inc=(c == 15))

                wwa = ctxs[0].wcp[0]
                load_w(4096, wwa)
                for ci_, (t0, n) in enumerate(tok_chunks(512)):
                    p_ = psA[ci_ % 2]
                    proj_fm(p_[0][:, 0:n], p_[1], wwa, t0, n)
                    K.op(act, lambda e: e.activation(out=wadT[0:64, t0:t0 + n], in_=p_[0][0:64, 0:n], func=AF.Tanh),
                         r=[p_[1]], w=[wadTb])
                    K.op(act, lambda e: e.copy(out=wadT[64:128, t0:t0 + n], in_=p_[0][64:128, 0:n]), r=[p_[1]], w=[wadTb])

                v3 = lambda t_: t_[0][:].rearrange("p (c s) -> p c s", c=2)
                flat = lambda ap: ap.rearrange("p a b -> p (a b)")
                one_b = epsb[:, 1:2]

                def sigmoid_chain(C, src_ps, srcb, bias_ap, dst, extra_r=()):
                    if bias_ap is None:
                        K.op(act, lambda e: e.activation(out=dst[0][:], in_=src_ps, func=AF.Exp, scale=-1.0),
                             r=[srcb] + list(extra_r), w=[dst[1]])
                    else:
                        K.op(act, lambda e: e.activation(out=dst[0][:], in_=src_ps, func=AF.Exp, bias=bias_ap, scale=-1.0),
                             r=[srcb] + list(extra_r), w=[dst[1]])
                    K.op(act, lambda e: e.activation(out=dst[0][:], in_=dst[0][:], func=AF.Ln, bias=one_b, scale=1.0),
                         r=[dst[1], ssb], w=[dst[1]])
                    K.op(act, lambda e: e.activation(out=dst[0][:], in_=dst[0][:], func=AF.Exp, scale=-1.0),
                         r=[dst[1]], w=[dst[1]])

                def front(C, p):
                    vcol = lambda vi: vecs[:, vi, p:p + 1]
                    wr, wk, wv, wg = C.wcp
                    bX, bY = C.bX, C.bY
                    for i in range(NT):
                        while C.back_done < i - 1:
                            yield False
                        S = C.sets[i % 2]
                        t0 = 128 * i
                        rf, kf, sgw, av, lw, cm, cmx, E1, E2, E3 = C.rf, C.kf, C.sgw, C.av, C.lw, C.cm, C.cmx, C.E1, C.E2, C.E3
                        kkr, sq, hsn, kk, t1, kp, bb, ke3, be3, gs = C.kkr, C.sq, C.hsn, C.kk, C.t1, C.kp, C.bb, C.ke3, C.be3, C.gs
                        AR, BK, MB, MK = S.AR, S.BK, S.MB, S.MK
                        proj_fm(bX[0][:, 0:128], bX[1], wr, t0, 128)
                        proj_fm(bX[0][:, 128:256], bX[1], wk, t0, 128)
                        yield True
                        proj_fm(bX[0][:, 256:384], bX[1], wg, t0, 128)
                        for c in range(16):
                            lhsT = uT[:, c, 1 + t0:1 + t0 + 128] if c < 8 else uT[:, c - 8, t0:t0 + 128]
                            K.op(pe, lambda e: e.matmul(bX[0][:, 384:512], lhsT=lhsT, rhs=wv[0][:, c, :],
                                                        start=(c == 0), stop=(c == 15)),
                                 r=[uTb, wv[1]], w=[bX[1]], inc=(c == 15))
                        K.op(pe, lambda e: e.matmul(bY[0][:, 0:128], lhsT=wup[0:64, 128 * p:128 * p + 128],
                                                    rhs=wadT[0:64, t0:t0 + 128], start=True, stop=True),
                             r=[wupb, wadTb], w=[bY[1]])
                        K.op(pe, lambda e: e.matmul(bY[0][:, 128:256], lhsT=wup[64:128, 128 * p:128 * p + 128],
                                                    rhs=wadT[64:128, t0:t0 + 128], start=True, stop=True),
                             r=[wupb, wadTb], w=[bY[1]])
                        yield True
                        K.op(act, lambda e: e.copy(out=rf[0][:], in_=bX[0][:, 0:128]), r=[bX[1]], w=[rf[1]])
                        K.op(act, lambda e: e.copy(out=kf[0][:], in_=bX[0][:, 128:256]), r=[bX[1]], w=[kf[1]])
                        sigmoid_chain(C, bX[0][:, 256:384], bX[1], None, gs)
                        K.op(dve, lambda e: e.tensor_tensor(out=S.sg[0][:], in0=bX[0][:, 256:384], in1=gs[0][:], op=ALU.mult),
                             r=[bX[1], gs[1]], w=[S.sg[1]])
                        K.op(dve, lambda e: e.tensor_copy(out=S.vtok[0][:], in_=bX[0][:, 384:512]), r=[bX[1]], w=[S.vtok[1]])
                        yield True
                        sigmoid_chain(C, bY[0][:, 0:128], bY[1], vcol(6), sgw, extra_r=[vecb])
                        sigmoid_chain(C, bY[0][:, 128:256], bY[1], vcol(7), av, extra_r=[vecb])
                        K.op(dve, lambda e: e.tensor_scalar(out=lw[0][:], in0=sgw[0][:], scalar1=-DECAY_SCALE, scalar2=None,
                                                            op0=ALU.mult), r=[sgw[1]], w=[lw[1]])
                        K.op(dve, lambda e: e.tensor_tensor_scan(out=cm[0][:], data0=scanm[:], data1=lw[0][:], initial=0.0,
                                                                 op0=ALU.mult, op1=ALU.add), r=[lw[1], cb2], w=[cm[1]])
                        K.op(dve, lambda e: e.tensor_tensor(out=cmx[0][:], in0=cm[0][:], in1=lw[0][:], op=ALU.subtract),
                             r=[cm[1], lw[1]], w=[cmx[1]])
                        yield True
                        K.op(act, lambda e: e.activation(out=E1[0][:], in_=cm[0][:], func=AF.Exp), r=[cm[1]], w=[E1[1]])
                        K.op(act, lambda e: e.activation(out=E2[0][:], in_=cmx[0][:], func=AF.Exp), r=[cmx[1]], w=[E2[1]])
                        K.op(act, lambda e: e.activation(out=E3[0][:], in_=cm[0][:], func=AF.Exp, scale=-1.0),
                             r=[cm[1]], w=[E3[1]])
                        K.op(act, lambda e: e.activation(out=S.GC[0][:], in_=cm[0][:, 63:128:64], func=AF.Exp),
                             r=[cm[1]], w=[S.GC[1]])
                        K.op(dve, lambda e: e.tensor_scalar(out=kkr[0][:], in0=kf[0][:], scalar1=vcol(2), scalar2=None,
                                                            op0=ALU.mult), r=[kf[1], vecb], w=[kkr[1]])
                        K.op(dve, lambda e: e.tensor_tensor(out=sq[0][:], in0=kkr[0][:], in1=kkr[0][:], op=ALU.mult),
                             r=[kkr[1]], w=[sq[1]])
                        K.op(pe, lambda e: e.matmul(bY[0][:, 256:384], lhsT=bones[:], rhs=sq[0][:], start=True, stop=True),
                             r=[cb2, sq[1]], w=[bY[1]])
                        yield True
                        K.op(dve, lambda e: e.tensor_scalar(out=hsn[0][:], in0=bY[0][:, 256:384], scalar1=1e-24, scalar2=None,
                                                            op0=ALU.max), r=[bY[1]], w=[hsn[1]])
                        K.op(act, lambda e: e.activation(out=hsn[0][:], in_=hsn[0][:], func=AF.Ln), r=[hsn[1]], w=[hsn[1]])
                        K.op(act, lambda e: e.activation(out=hsn[0][:], in_=hsn[0][:], func=AF.Exp, scale=-0.5),
                             r=[hsn[1]], w=[hsn[1]])
                        K.op(dve, lambda e: e.tensor_scalar(out=t1[0][:], in0=av[0][:], scalar1=vcol(3), scalar2=vcol(5),
                                                            op0=ALU.mult, op1=ALU.add), r=[av[1], vecb], w=[t1[1]])
                        K.op(dve, lambda e: e.tensor_tensor(out=kp[0][:], in0=kf[0][:], in1=t1[0][:], op=ALU.mult),
                             r=[kf[1], t1[1]], w=[kp[1]])
                        K.op(dve, lambda e: e.tensor_tensor(out=AR[0][:, :, 1, :], in0=v3(rf), in1=v3(E1), op=ALU.mult),
                             r=[rf[1], E1[1]], w=[AR[1]])
                        K.op(dve, lambda e: e.tensor_tensor(out=ke3[0][:], in0=kp[0][:], in1=E3[0][:], op=ALU.mult),
                             r=[kp[1], E3[1]], w=[ke3[1]])
                        yield True
                        K.op(dve, lambda e: e.tensor_tensor(out=kk[0][:], in0=kkr[0][:], in1=hsn[0][:], op=ALU.mult),
                             r=[kkr[1], hsn[1]], w=[kk[1]])
                        K.op(dve, lambda e: e.tensor_tensor(out=bb[0][:], in0=kk[0][:], in1=av[0][:], op=ALU.mult),
                             r=[kk[1], av[1]], w=[bb[1]])
                        K.op(dve, lambda e: e.scalar_tensor_tensor(out=AR[0][:, :, 0, :], in0=v3(kk), scalar=-1.0, in1=v3(E2),
                                                                   op0=ALU.mult, op1=ALU.mult), r=[kk[1], E2[1]], w=[AR[1]])
                        K.op(dve, lambda e: e.tensor_tensor(out=be3[0][:], in0=bb[0][:], in1=E3[0][:], op=ALU.mult),
                             r=[bb[1], E3[1]], w=[be3[1]])
                        K.op(act, lambda e: e.copy(out=BK[0][:, :, 1, :], in_=v3(ke3)), r=[ke3[1]], w=[BK[1]])
                        K.op(act, lambda e: e.copy(out=BK[0][:, :, 0, :], in_=v3(be3)), r=[be3[1]], w=[BK[1]])
                        yield True
                        for c in range(2):
                            K.op(dve, lambda e: e.tensor_scalar(out=C.khT[0][:, 64 * c:64 * c + 64], in0=ke3[0][:, 64 * c:64 * c + 64],
                                                                scalar1=S.GC[0][:, c:c + 1], scalar2=None, op0=ALU.mult),
                                 r=[ke3[1], S.GC[1]], w=[C.khT[1]])
                            K.op(dve, lambda e: e.tensor_scalar(out=C.bhT[0][:, 64 * c:64 * c + 64], in0=be3[0][:, 64 * c:64 * c + 64],
                                                                scalar1=S.GC[0][:, c:c + 1], scalar2=None, op0=ALU.mult),
                                 r=[be3[1], S.GC[1]], w=[C.bhT[1]])
                        K.op(dve, lambda e: e.scalar_tensor_tensor(out=C.rkb[0][:], in0=rf[0][:], scalar=vcol(4), in1=kp[0][:],
                                                                   op0=ALU.mult, op1=ALU.mult), r=[rf[1], kp[1], vecb], w=[C.rkb[1]])
                        for c in range(2):
                            for hf in range(2):
                                lo = 64 * hf
                                blk = 2 * c + hf
                                K.op(pe, lambda e: e.matmul(bY[0][:, 128 * blk:128 * blk + 128], lhsT=flat(BK[0][lo:lo + 64, c, :, :]),
                                                            rhs=flat(AR[0][lo:lo + 64, c, :, :]), start=True, stop=True),
                                     r=[BK[1], AR[1]], w=[bY[1]])
                        for c in range(2):
                            for hf in range(2):
                                lo = 64 * hf
                                blk = 2 * c + hf
                                K.op(pe, lambda e: e.matmul(bX[0][0:64, 64 * blk:64 * blk + 64], lhsT=AR[0][lo:lo + 64, c, 0, :],
                                                            rhs=BK[0][lo:lo + 64, c, 0, :], start=True, stop=True),
                                     r=[BK[1], AR[1]], w=[bX[1]])
                        yield True
                        for c in range(2):
                            pv = 64 * c
                            scv = bY[0][:, 256 * c:256 * c + 256].rearrange("p (a b) -> p a b", a=2)
                            K.op(dve, lambda e: e.tensor_tensor(out=MB[0][pv:pv + 64, :, :], in0=scv[0:64, :, :], in1=m64x2[0:64, :, :],
                                                                op=ALU.mult), r=[bY[1], cb2], w=[MB[1]])
                            K.op(dve, lambda e: e.tensor_tensor(out=MK[0][pv:pv + 64, :, :], in0=scv[64:128, :, :], in1=m64x2[64:128, :, :],
                                                                op=ALU.mult), r=[bY[1], cb2], w=[MK[1]])
                            apv = bX[0][0:64, 128 * c:128 * c + 128].rearrange("p (a b) -> p a b", a=2)
                            K.op(dve, lambda e: e.tensor_tensor(out=C.PQ[0][0][pv:pv + 64, :, 0:64], in0=apv, in1=mlow2[0:64, :, :],
                                                                op=ALU.mult), r=[bX[1], cb2], w=[C.PQ[0][1]])
                        K.op(dve, lambda e: e.tensor_copy(out=C.PQ[0][0][:, :, 64:128], in_=MB[0][:, :, 0:64]), r=[MB[1]], w=[C.PQ[0][1]])
                        K.op(dve, lambda e: e.tensor_tensor(out=C.TTl[0][0][:], in0=MB[0][:, :, 0:64], in1=idx2f[:], op=ALU.add),
                             r=[MB[1], cb2], w=[C.TTl[0][1]])
                        K.op(pe, lambda e: e.transpose(out=psT[0][:, 0, :], in_=C.khT[0][:], identity=ident[:]),
                             r=[C.khT[1], identb], w=[psT[1]], inc=False)
                        K.op(pe, lambda e: e.transpose(out=psT[0][:, 1, :], in_=C.bhT[0][:], identity=ident[:]),
                             r=[C.bhT[1], identb], w=[psT[1]])
                        K.op(act, lambda e: e.copy(out=S.kbhat[0][:], in_=psT[0][:, 0:2, :]), r=[psT[1]], w=[S.kbhat[1]])
                        yield True
                        for lv in range(1, 6):
                            prev, cur = C.PQ[(lv - 1) % 2], C.PQ[lv % 2]
                            Tp = C.TTl[(lv - 1) % 2]
                            Tc = C.TTl[lv % 2] if lv < 5 else S.TT
                            for c in range(2):
                                pv = 64 * c
                                for hf in range(2):
                                    blk = 2 * c + hf
                                    Pm = prev[0][pv:pv + 64, hf, 0:64]
                                    Qm = prev[0][pv:pv + 64, hf, 64:128]
                                    K.op(pe, lambda e: e.matmul(bX[0][0:64, 128 * blk:128 * blk + 64], lhsT=Qm, rhs=Pm, start=True, stop=True),
                                         r=[prev[1]], w=[bX[1]])
                                    if lv < 5:
                                        K.op(pe, lambda e: e.matmul(bX[0][0:64, 128 * blk + 64:128 * blk + 128], lhsT=Pm, rhs=Qm, start=True, stop=True),
                                             r=[prev[1]], w=[bX[1]])
                            yield True
                            for c in range(2):
                                pv = 64 * c
                                src = bX[0][0:64, 256 * c:256 * c + 256].rearrange("p (a b) -> p a b", a=2)
                                if c == 0:
                                    K.op(act, lambda e: e.copy(out=cur[0][pv:pv + 64, :, :], in_=src), r=[bX[1]], w=[cur[1]])
                                else:
                                    K.op(dve, lambda e: e.tensor_copy(out=cur[0][pv:pv + 64, :, :], in_=src), r=[bX[1]], w=[cur[1]])
                            yield True
                            for c in range(2):
                                pv = 64 * c
                                for hf in range(2):
                                    blk = 2 * c + hf
                                    K.op(pe, lambda e: e.matmul(bY[0][0:64, 64 * blk:64 * blk + 64], lhsT=cur[0][pv:pv + 64, hf, 0:64],
                                                                rhs=Tp[0][pv:pv + 64, hf, :], start=True, stop=False),
                                         r=[cur[1], Tp[1]], w=[bY[1]])
                                    K.op(pe, lambda e: e.matmul(bY[0][0:64, 64 * blk:64 * blk + 64], lhsT=idx2b[pv:pv + 64, 0, :],
                                                                rhs=Tp[0][pv:pv + 64, hf, :], start=False, stop=True),
                                         r=[cb2, Tp[1]], w=[bY[1]])
                            yield True
                            for c in range(2):
                                pv = 64 * c
                                src = bY[0][0:64, 128 * c:128 * c + 128].rearrange("p (a b) -> p a b", a=2)
                                if c == 0:
                                    K.op(act, lambda e: e.copy(out=Tc[0][pv:pv + 64, :, :], in_=src), r=[bY[1]], w=[Tc[1]])
                                else:
                                    K.op(dve, lambda e: e.tensor_copy(out=Tc[0][pv:pv + 64, :, :], in_=src), r=[bY[1]], w=[Tc[1]])
                            yield True
                        K.op(pe, lambda e: e.matmul(bY[0][:, 384:386], lhsT=C.rkb[0][:], rhs=hind[:], start=True, stop=True),
                             r=[C.rkb[1], cb2], w=[bY[1]])
                        K.op(dve, lambda e: e.tensor_copy(out=S.bon[0][:], in_=bY[0][:, 384:386]), r=[bY[1]], w=[S.bon[1]])
                        C.front_done = i + 1
                        yield True

                def back(C, p):
                    bZ = C.bZ
                    Hs, Hb, Hsb, Hbb = C.Hs, C.Hb, C.Hsb, C.Hbb
                    Xs, Us, ytile, yn, ybf = C.Xs, C.Us, C.ytile, C.yn, C.ybf
                    for i in range(NT):
                        while C.front_done < i + 1:
                            yield False
                        S = C.sets[i % 2]
                        AR, MB, MK, vtok, kbhat, GC, Tf = S.AR, S.MB, S.MK, S.vtok, S.kbhat, S.GC, S.TT
                        t0 = 128 * i
                        for c in range(2):
                            pv = 64 * c
                            for hf in range(2):
                                lo = 64 * hf
                                K.op(pe, lambda e: e.matmul(bZ[0][0:64, 64 * hf:64 * hf + 64], lhsT=AR[0][lo:lo + 64, c, 0, :],
                                                            rhs=Hb[lo:lo + 64, :], start=True, stop=False),
                                     r=[AR[1], Hbb], w=[bZ[1]])
                                K.op(pe, lambda e: e.matmul(bZ[0][0:64, 64 * hf:64 * hf + 64], lhsT=MK[0][pv:pv + 64, hf, 0:64],
                                                            rhs=vtok[0][pv:pv + 64, 64 * hf:64 * hf + 64], start=False, stop=True),
                                     r=[MK[1], vtok[1]], w=[bZ[1]])
                            yield True
                            K.op(act, lambda e: e.copy(out=Xs[0][pv:pv + 64, :], in_=bZ[0][0:64, 0:128]), r=[bZ[1]], w=[Xs[1]])
                            yield True
                            for hf in range(2):
                                K.op(pe, lambda e: e.matmul(bZ[0][0:64, 128 + 64 * hf:128 + 64 * hf + 64], lhsT=Tf[0][pv:pv + 64, hf, :],
                                                            rhs=Xs[0][pv:pv + 64, 64 * hf:64 * hf + 64], start=True, stop=True),
                                     r=[Tf[1], Xs[1]], w=[bZ[1]])
                            yield True
                            K.op(dve, lambda e: e.tensor_copy(out=Us[0][pv:pv + 64, :], in_=bZ[0][0:64, 128:256]), r=[bZ[1]], w=[Us[1]])
                            yield True
                            for hf in range(2):
                                lo = 64 * hf
                                o_ = bZ[0][0:64, 256 + 64 * hf:256 + 64 * hf + 64]
                                K.op(pe, lambda e: e.matmul(o_, lhsT=AR[0][lo:lo + 64, c, 1, :], rhs=Hb[lo:lo + 64, :], start=True, stop=False),
                                     r=[AR[1], Hbb], w=[bZ[1]])
                                K.op(pe, lambda e: e.matmul(o_, lhsT=MB[0][pv:pv + 64, hf, 64:128], rhs=Us[0][pv:pv + 64, 64 * hf:64 * hf + 64],
                                                            start=False, stop=False), r=[MB[1], Us[1]], w=[bZ[1]])
                                K.op(pe, lambda e: e.matmul(o_, lhsT=MK[0][pv:pv + 64, hf, 64:128], rhs=vtok[0][pv:pv + 64, 64 * hf:64 * hf + 64],
                                                            start=False, stop=True), r=[MK[1], vtok[1]], w=[bZ[1]])
                                o2 = bZ[0][:, 384 + 64 * hf:384 + 64 * hf + 64]
                                K.op(pe, lambda e: e.matmul(o2, lhsT=kbhat[0][pv:pv + 64, 1, :], rhs=Us[0][pv:pv + 64, 64 * hf:64 * hf + 64],
                                                            start=True, stop=False), r=[kbhat[1], Us[1]], w=[bZ[1]])
                                K.op(pe, lambda e: e.matmul(o2, lhsT=kbhat[0][pv:pv + 64, 0, :], rhs=vtok[0][pv:pv + 64, 64 * hf:64 * hf + 64],
                                                            start=False, stop=True), r=[kbhat[1], vtok[1]], w=[bZ[1]])
                            yield True
                            for hf in range(2):
                                lo = 64 * hf
                                K.op(dve, lambda e: e.scalar_tensor_tensor(out=Hs[lo:lo + 64, :], in0=Hs[lo:lo + 64, :], scalar=GC[0][lo:lo + 64, c:c + 1],
                                                                           in1=bZ[0][lo:lo + 64, 384 + 64 * hf:384 + 64 * hf + 64],
                                                                           op0=ALU.mult, op1=ALU.add), r=[Hsb, GC[1], bZ[1]], w=[Hsb])
                            K.op(dve, lambda e: e.tensor_copy(out=Hb[:], in_=Hs[:]), r=[Hsb], w=[Hbb])
                            K.op(dve, lambda e: e.tensor_copy(out=ytile[0][pv:pv + 64, :], in_=bZ[0][0:64, 256:384]), r=[bZ[1]], w=[ytile[1]])
                            yield True
                        for hf in range(2):
                            K.op(dve, lambda e: e.bn_stats(out=C.st6[0][:, hf, :], in_=ytile[0][:, 64 * hf:64 * hf + 64]), r=[ytile[1]], w=[C.st6[1]])
                            K.op(dve, lambda e: e.bn_aggr(out=C.mv[0][:, hf, :], in_=C.st6[0][:, hf, :]), r=[C.st6[1]], w=[C.mv[1]])
                        yield True
                        K.op(act, lambda e: e.activation(out=C.rstd[0][:], in_=C.mv[0][:, :, 1], func=AF.Ln, bias=epsb[:, 2:3], scale=1.0),
                             r=[C.mv[1], ssb], w=[C.rstd[1]])
                        K.op(act, lambda e: e.activation(out=C.rstd[0][:], in_=C.rstd[0][:], func=AF.Exp, scale=-0.5), r=[C.rstd[1]], w=[C.rstd[1]])
                        yield True
                        for hf in range(2):
                            cs = slice(64 * hf, 64 * hf + 64)
                            K.op(dve, lambda e: e.tensor_scalar(out=yn[0][:, cs], in0=ytile[0][:, cs], scalar1=C.mv[0][:, hf, 0:1],
                                                                scalar2=C.rstd[0][:, hf:hf + 1], op0=ALU.subtract, op1=ALU.mult),
                                 r=[ytile[1], C.mv[1], C.rstd[1]], w=[yn[1]])
                        K.op(dve, lambda e: e.tensor_tensor(out=yn[0][:], in0=yn[0][:], in1=C.lnw[:], op=ALU.mult),
                             r=[yn[1], C.lnbuf], w=[yn[1]])
                        K.op(dve, lambda e: e.tensor_tensor(out=yn[0][:], in0=yn[0][:], in1=C.lnb[:], op=ALU.add),
                             r=[yn[1], C.lnbuf], w=[yn[1]])
                        for hf in range(2):
                            cs = slice(64 * hf, 64 * hf + 64)
                            K.op(dve, lambda e: e.scalar_tensor_tensor(out=ybf[0][:, cs], in0=vtok[0][:, cs], scalar=S.bon[0][:, hf:hf + 1],
                                                                       in1=yn[0][:, cs], op0=ALU.mult, op1=ALU.add),
                                 r=[vtok[1], S.bon[1], yn[1]], w=[ybf[1]])
                        yield True
                        K.op(pe, lambda e: e.transpose(out=psT[0][:, 2 + C.ci, :], in_=ybf[0][:], identity=ident[:]), r=[ybf[1], identb], w=[psT[1]])
                        yield True
                        K.op(dve, lambda e: e.tensor_tensor(out=ogT[:, p, t0:t0 + 128], in0=psT[0][:, 2 + C.ci, :], in1=S.sg[0][:], op=ALU.mult),
                             r=[psT[1], S.sg[1]], w=[ogTb])
                        C.back_done = i + 1
                        yield True

                def pair_stream(C):
                    for p in range(C.ci, 8, 2):
                        for t_ in range(4):
                            load_w(1024 * t_ + 128 * p, C.wcp[t_])
                            yield True
                        K.dma(sp, C.lnw[:], rwkv_ln_w[j, 128 * p:128 * p + 128].partition_broadcast(128), w=[C.lnbuf])
                        K.dma(sp, C.lnb[:], rwkv_ln_b[j, 128 * p:128 * p + 128].partition_broadcast(128), w=[C.lnbuf])
                        K.op(dve, lambda e: e.memset(C.Hs[:], 0.0), w=[C.Hsb])
                        K.op(dve, lambda e: e.memset(C.Hb[:], 0.0), w=[C.Hbb])
                        C.front_done = 0
                        C.back_done = 0
                        gens = [front(C, p), back(C, p)]
                        while gens:
                            progressed = False
                            for g_ in list(gens):
                                try:
                                    if next(g_):
                                        progressed = True
                                except StopIteration:
                                    gens.remove(g_)
                                    progressed = True
                            yield progressed

                streams = [pair_stream(C) for C in ctxs]
                while streams:
                    for s_ in list(streams):
                        try:
                            next(s_)
                        except StopIteration:
                            streams.remove(s_)
                K.barrier()


        for layer in range(nlayers):
            phase_prenorm(layer)
            if layer % 2 == 0:
                fox_layer(layer)
                phase_post(layer, fox_w_out[layer // 2], layer == nlayers - 1)
            else:
                rwkv_layer(layer)
                phase_post(layer, rwkv_w_out[layer // 2], layer == nlayers - 1)
            K.barrier()
        K.barrier()
    return nc


_CACHE = {}


def _consts():
    idx = np.arange(128)
    tri = (idx[:, None] <= idx[None, :]).astype(np.float32)
    ident = np.eye(128, dtype=np.float32)
    ones = np.ones((128, 128), np.float32)
    m64 = np.zeros((128, 128), np.float32)
    s = idx[:, None] % 64
    t = idx[None, :] % 64
    m64[:, 0:64] = (s < t)[:, 0:64]
    m64[:, 64:128] = (s <= t)[:, 64:128]
    scan = np.ones((128, 512), np.float32)
    scan[:, ::64] = 0.0
    hind = np.zeros((128, 2), np.float32)
    hind[0:64, 0] = 1.0
    hind[64:128, 1] = 1.0
    m64x2 = np.concatenate([m64, m64], axis=1)
    r64 = idx[:, None] % 64
    c64 = np.arange(64)[None, :]
    mlow = (c64 < r64).astype(np.float32)
    mlow2 = np.concatenate([mlow, mlow], axis=1)
    i64 = (c64 == r64).astype(np.float32)
    idx2 = np.concatenate([i64, i64], axis=1)
    bones = ((idx[:, None] // 64) == (idx[None, :] // 64)).astype(np.float32)
    return dict(c_ident=ident, c_tri=tri, c_ones=ones, c_m64=m64, c_scan=scan, c_hind=hind,
                c_m64x2=m64x2, c_mlow2=mlow2, c_idx2=idx2, c_bones=bones)


def kernel(x, meta_tokens, norm_pre, norm_post, fox_w_in, fox_b_f, fox_w_out,
           rwkv_w_in, rwkv_mu, rwkv_w0, rwkv_w_up, rwkv_a0, rwkv_a_up, rwkv_k_k,
           rwkv_k_a, rwkv_r_k, rwkv_ln_w, rwkv_ln_b, rwkv_w_out, _nlayers=DEPTH):
    f = lambda a: np.ascontiguousarray(np.asarray(a, dtype=np.float32))
    x = f(x)
    B = x.shape[0]
    meta = f(meta_tokens)
    h0 = np.zeros((B, T, D), np.float32)
    h0[:, :NMETA] = meta[None]
    h0[:, NMETA:NMETA + SEQ] = x
    shared = dict(
        norm_pre=f(norm_pre), norm_post=f(norm_post), fox_w_in=f(fox_w_in), fox_b_f=f(fox_b_f),
        fox_w_out=f(fox_w_out), rwkv_w_in=f(rwkv_w_in), rwkv_mu=f(rwkv_mu), rwkv_w0=f(rwkv_w0),
        rwkv_w_up=f(rwkv_w_up), rwkv_a0=f(rwkv_a0), rwkv_a_up=f(rwkv_a_up), rwkv_k_k=f(rwkv_k_k),
        rwkv_k_a=f(rwkv_k_a), rwkv_r_k=f(rwkv_r_k).reshape(2, D), rwkv_ln_w=f(rwkv_ln_w),
        rwkv_ln_b=f(rwkv_ln_b), rwkv_w_out=f(rwkv_w_out))
    shared.update(_consts())
    key = _nlayers
    if key not in _CACHE:
        _CACHE[key] = build_program(_nlayers)
    nc = _CACHE[key]
    in_maps = []
    for b in range(B):
        m = dict(shared)
        m["h0"] = h0[b]
        in_maps.append(m)
    res = run_bass_kernel_spmd(nc, in_maps, core_ids=list(range(B)))
    out = np.stack([np.asarray(r["y"])[NMETA:NMETA + SEQ] for r in res.results], axis=0)
    return out.astype(np.float32)
```

```python
import math
from contextlib import ExitStack

import numpy as np
import concourse.bass as bass
import concourse.mybir as mybir
from concourse.bass_utils import run_bass_kernel_spmd

F32 = mybir.dt.float32
BF16 = mybir.dt.bfloat16
AF = mybir.ActivationFunctionType
ALU = mybir.AluOpType
AX = mybir.AxisListType

D = 1024
SEQ = 2048
NMETA = 16
NT = 17
T = NT * 128
DEPTH = 4
NH = 16
HD = 64
FOX_IN = 4 * D + NH
RWKV_IN = 4 * D + 128
NORM_EPS = 1e-6
GN_EPS = 64e-5
DECAY_SCALE = math.exp(-0.5)
CH = 64
DBG = {"stage": 9, "pairs": 8, "tiles": NT}


def tok_chunks(n=512):
    out = []
    t0 = 0
    while t0 < T:
        m = min(n, T - t0)
        out.append((t0, m))
        t0 += m
    return out


class Buf:
    __slots__ = ("name", "lw", "rd", "excl")

    def __init__(self, name="", excl=False):
        self.name = name
        self.lw = None
        self.rd = {}
        self.excl = excl


class Q:
    def __init__(self, name, eng, sem, self_sync=True):
        self.name = name
        self.eng = eng
        self.sem = sem
        self.cnt = 0
        self.seen = {}
        self.self_sync = self_sync
        self.key = name
        self.ring = []
        self.ring_i = 0


class PEProxy:
    def __init__(self, K, eng):
        self.K = K
        self.eng = eng
        self.partial = False

    def _pre(self, st_ap, out):
        K = self.K
        rg = (st_ap.base_partition(), st_ap.partition_size())
        bank = out.name
        last = K.pe_last
        if last is not None and last[0] != rg and (last[1] == bank or (last[0][1] < 128 and rg[1] < 128)):
            assert K.pe_last_tk is not None, "previous matmul needs a semaphore increment"
            K._wait(K.pe, K.pe_last_tk)
        K.pe_last = (rg, bank)
        self.partial = rg[1] < 128

    def matmul(self, out, lhsT, rhs, **kw):
        self._pre(lhsT, out)
        return self.eng.matmul(out, lhsT=lhsT, rhs=rhs, **kw)

    def transpose(self, out, in_, identity):
        self._pre(in_, out)
        return self.eng.transpose(out=out, in_=in_, identity=identity)


class KB:
    def __init__(self, nc, st):
        self.nc = nc
        self.st = st
        mk = lambda n: st.enter_context(nc.semaphore(n))
        self.pe = Q("pe", nc.tensor, mk("s_pe"), self_sync=False)
        self.act = Q("act", nc.scalar, mk("s_act"))
        self.dve = Q("dve", nc.vector, mk("s_dve"))
        self.pool = Q("pool", nc.gpsimd, mk("s_pool"))
        self.sp = Q("sp", nc.sync, mk("s_sp"))
        self.queues = [self.pe, self.act, self.dve, self.pool, self.sp]
        for q, n in ((self.sp, 24), (self.pool, 24)):
            for i in range(n):
                q.ring.append([mk(f"d_{q.name}{i}"), 0, f"d_{q.name}{i}"])
        self.dma_tickets = []
        self.nbuf = 0
        self.counting = False
        self.nops = 0
        self.pe_last = None
        self.pe_last_tk = None
        self.prox = PEProxy(self, nc.tensor)

    def sb(self, st, name, shape, dt):
        self.nbuf += 1
        return st.enter_context(self.nc.sbuf_tensor(f"{name}_{self.nbuf}", list(shape), dt))

    def psum(self, st, name, shape, dt):
        return st.enter_context(self.nc.psum_tensor(name, list(shape), dt))

    def _wait(self, q, tk):
        sem, val, key = tk
        if q.seen.get(key, 0) >= val:
            return
        q.eng.wait_ge(sem, val)
        q.seen[key] = val

    def _deps(self, q, r, w):
        deps = []
        for b in r:
            if b.lw is not None:
                deps.append(b.lw)
            if b.excl:
                for key, tk in b.rd.items():
                    if key != q.key:
                        deps.append(tk)
        for b in w:
            if b.lw is not None:
                deps.append(b.lw)
            for key, tk in b.rd.items():
                deps.append(tk)
        for tk in deps:
            if tk[2] == q.key and not q.self_sync:
                continue
            self._wait(q, tk)

    def _record(self, tk, r, w):
        for b in r:
            old = b.rd.get(tk[2])
            if old is None or old[1] < tk[1]:
                b.rd[tk[2]] = tk
        for b in w:
            b.lw = tk
            b.rd = {}

    def op(self, q, fn, r=(), w=(), inc=True):
        if self.counting:
            self.nops += 1
            if self.nops > DBG.get("maxops", 10 ** 9):
                return None
        self._deps(q, r, w)
        if q is self.pe:
            self.prox.partial = False
            ins = fn(self.prox)
            if self.prox.partial:
                inc = True
        else:
            ins = fn(q.eng)
        if inc:
            q.cnt += 1
            ins.then_inc(q.sem, 1)
            tk = (q.sem, q.cnt, q.key)
        else:
            tk = (q.sem, q.cnt + 1, q.key)
        if q is self.pe:
            self.pe_last_tk = tk if inc else None
        self._record(tk, r, w)
        return tk

    def dma(self, q, out, in_, r=(), w=()):
        self._deps(q, r, w)
        slot = q.ring[q.ring_i % len(q.ring)]
        q.ring_i += 1
        sem, n, key = slot
        if n > 0:
            self._wait(q, (sem, 16 * n, key))
        q.eng.dma_start(out=out, in_=in_).then_inc(sem, 16)
        slot[1] = n + 1
        tk = (sem, 16 * (n + 1), key)
        self._record(tk, r, w)
        self.dma_tickets.append(tk)
        return tk

    def barrier(self):
        tks = [(q.sem, q.cnt, q.key) for q in self.queues if q.cnt > 0]
        for q in self.queues:
            for slot in q.ring:
                if slot[1] > 0:
                    tks.append((slot[0], 16 * slot[1], slot[2]))
        for q in self.queues:
            for tk in tks:
                if tk[2] == q.key:
                    continue
                self._wait(q, tk)


def build_program(nlayers=DEPTH, dbg=False):
    nc = bass.Bass("TRN2", target_bir_lowering=False)
    dt_in = lambda name, shape: nc.dram_tensor(name, list(shape), F32, kind="ExternalInput").ap()
    h0 = dt_in("h0", [T, D])
    norm_pre = dt_in("norm_pre", [DEPTH, D])
    norm_post = dt_in("norm_post", [DEPTH, D])
    fox_w_in = dt_in("fox_w_in", [2, D, FOX_IN])
    fox_b_f = dt_in("fox_b_f", [2, NH])
    fox_w_out = dt_in("fox_w_out", [2, D, D])
    rwkv_w_in = dt_in("rwkv_w_in", [2, D, RWKV_IN])
    rwkv_mu = dt_in("rwkv_mu", [2, RWKV_IN])
    rwkv_w0 = dt_in("rwkv_w0", [2, D])
    rwkv_w_up = dt_in("rwkv_w_up", [2, 64, D])
    rwkv_a0 = dt_in("rwkv_a0", [2, D])
    rwkv_a_up = dt_in("rwkv_a_up", [2, 64, D])
    rwkv_k_k = dt_in("rwkv_k_k", [2, D])
    rwkv_k_a = dt_in("rwkv_k_a", [2, D])
    rwkv_r_k = dt_in("rwkv_r_k", [2, D])
    rwkv_ln_w = dt_in("rwkv_ln_w", [2, D])
    rwkv_ln_b = dt_in("rwkv_ln_b", [2, D])
    rwkv_w_out = dt_in("rwkv_w_out", [2, D, D])
    c_ident = dt_in("c_ident", [128, 128])
    c_tri = dt_in("c_tri", [128, 128])
    c_ones = dt_in("c_ones", [128, 128])
    c_m64 = dt_in("c_m64", [128, 128])
    c_scan = dt_in("c_scan", [128, 512])
    c_hind = dt_in("c_hind", [128, 2])
    c_m64x2 = dt_in("c_m64x2", [128, 256])
    c_mlow2 = dt_in("c_mlow2", [128, 128])
    c_idx2 = dt_in("c_idx2", [128, 128])
    c_bones = dt_in("c_bones", [128, 128])
    c_msc = dt_in("c_msc", [128, 512])
    c_mlow128 = dt_in("c_mlow128", [128, 256])
    c_id2 = dt_in("c_id2", [128, 256])
    y = nc.dram_tensor("y", [T, D], F32, kind="ExternalOutput").ap()

    with ExitStack() as st:
        K = KB(nc, st)
        pe, act, dve, pool, sp = K.pe, K.act, K.dve, K.pool, K.sp

        uT = K.sb(st, "uT", [128, 8, T + 2], BF16)
        uTb = Buf("uT")
        ogT = K.sb(st, "ogT", [128, 8, T], BF16)
        ogTb = Buf("ogT")
        ident = K.sb(st, "ident", [128, 128], BF16)
        identb = Buf()
        tri_f = K.sb(st, "tri_f", [128, 128], F32)
        ones_f = K.sb(st, "ones_f", [128, 128], F32)
        tri_b = K.sb(st, "tri_b", [128, 128], BF16)
        constb = Buf()
        gpre = K.sb(st, "gpre", [128, D], F32)
        gpost = K.sb(st, "gpost", [128, D], F32)
        gb = Buf()
        hbufs = [(K.sb(st, f"hb{i}", [128, D], F32), Buf()) for i in range(2)]
        hbufs2 = hbufs
        un = K.sb(st, "un", [128, D], BF16)
        unb = Buf()
        un2 = K.sb(st, "un2", [128, D], BF16)
        uns = [(un, unb), (un2, Buf())]
        sss = [(K.sb(st, f"ss{i}", [128, 4], F32), Buf()) for i in range(2)]
        mt = K.sb(st, "mt", [128, D], F32)
        mtb = Buf()
        ss = K.sb(st, "ss", [128, 4], F32)
        ssb = Buf()
        epsb = K.sb(st, "epsb", [128, 4], F32)
        K.op(dve, lambda e: e.memset(epsb[:, 0:1], NORM_EPS), w=[ssb])
        K.op(dve, lambda e: e.memset(epsb[:, 1:2], 1.0), w=[ssb])
        K.op(dve, lambda e: e.memset(epsb[:, 2:3], GN_EPS), w=[ssb])
        psA = [(K.psum(st, f"psA{i}", [128, 512], F32), Buf(excl=True)) for i in range(6)]
        psT = (K.psum(st, "psT", [128, 8, 128], BF16), Buf(excl=True))
        psT2 = (K.psum(st, "psT2", [128, 8, 128], BF16), Buf(excl=True))
        psTs = [psT, psT2]
        hB = [Buf(f"h{i}") for i in range(NT)]

        K.dma(pool, ident[:], c_ident[:, :], w=[identb])
        K.dma(pool, tri_b[:], c_tri[:, :], w=[constb])
        K.dma(sp, tri_f[:], c_tri[:, :], w=[constb])
        K.dma(sp, ones_f[:], c_ones[:, :], w=[constb])
        K.op(dve, lambda e: e.memset(uT[:, :, 0:1], 0.0), w=[uTb])

        def bcast_row(ap_row, n):
            return ap_row.partition_broadcast(128)

        def phase_prenorm(layer):
            K.dma(sp, gpre[:], bcast_row(norm_pre[layer, :], D), w=[gb])
            K.dma(sp, gpost[:], bcast_row(norm_post[layer, :], D), w=[gb])
            for i in range(NT):
                ht, htb = hbufs[i % 2]
                ss_, ssb_ = sss[i % 2]
                un_, unb_ = uns[i % 2]
                pT = psTs[i % 2]
                src = h0 if layer == 0 else y
                K.dma(sp, ht[:], src[128 * i:128 * i + 128, :], r=[hB[i]], w=[htb])
                K.op(act, lambda e: e.activation(out=mt[:], in_=ht[:], func=AF.Square, accum_out=ss_[:, 0:1]),
                     r=[htb], w=[mtb, ssb_])
                K.op(act, lambda e: e.activation(out=ss_[:, 1:2], in_=ss_[:, 0:1], func=AF.Ln, bias=epsb[:, 0:1], scale=1.0 / D),
                     r=[ssb_, ssb], w=[ssb_])
                K.op(act, lambda e: e.activation(out=ss_[:, 2:3], in_=ss_[:, 1:2], func=AF.Exp, scale=-0.5),
                     r=[ssb_], w=[ssb_])
                K.op(dve, lambda e: e.scalar_tensor_tensor(out=un_[:], in0=ht[:], scalar=ss_[:, 2:3], in1=gpre[:],
                                                           op0=ALU.mult, op1=ALU.mult), r=[htb, ssb_, gb], w=[unb_])
                for c in range(8):
                    K.op(pe, lambda e: e.transpose(out=pT[0][:, c, :], in_=un_[:, 128 * c:128 * c + 128],
                                                   identity=ident[:]),
                         r=[unb_, identb], w=[pT[1]], inc=(c == 7))
                if i % 2 == 0:
                    K.op(act, lambda e: e.copy(out=uT[:, :, 1 + 128 * i:1 + 128 * i + 128], in_=pT[0][:, :, :]),
                         r=[pT[1]], w=[uTb])
                else:
                    K.op(dve, lambda e: e.tensor_copy(out=uT[:, :, 1 + 128 * i:1 + 128 * i + 128], in_=pT[0][:, :, :]),
                         r=[pT[1]], w=[uTb])

        def phase_post(layer, w_out_ap, last):
            with ExitStack() as pst:
                wout = K.sb(pst, "wout", [128, 8, D], BF16)
                woutb = Buf()
                K.dma(pool, wout[:], w_out_ap.rearrange("(c p) n -> p c n", p=128), w=[woutb])
                for i in range(NT):
                    pms = [psA[0], psA[1]] if i % 2 == 0 else [psA[2], psA[3]]
                    ss_, ssb_ = sss[i % 2]
                    for hf in range(2):
                        for c in range(8):
                            K.op(pe, lambda e: e.matmul(pms[hf][0][:, :], lhsT=ogT[:, c, 128 * i:128 * i + 128],
                                                        rhs=wout[:, c, 512 * hf:512 * hf + 512],
                                                        start=(c == 0), stop=(c == 7)),
                                 r=[ogTb, woutb], w=[pms[hf][1]], inc=(c == 7))
                    for hf in range(2):
                        K.op(act, lambda e: e.activation(out=un[:, 512 * hf:512 * hf + 512], in_=pms[hf][0][:, :], func=AF.Square,
                                                         accum_out=ss_[:, hf:hf + 1]), r=[pms[hf][1]], w=[unb, ssb_])
                    K.op(dve, lambda e: e.tensor_tensor(out=ss_[:, 2:3], in0=ss_[:, 0:1], in1=ss_[:, 1:2], op=ALU.add),
                         r=[ssb_], w=[ssb_])
                    K.op(act, lambda e: e.activation(out=ss_[:, 3:4], in_=ss_[:, 2:3], func=AF.Ln, bias=epsb[:, 0:1], scale=1.0 / D),
                         r=[ssb_, ssb], w=[ssb_])
                    K.op(act, lambda e: e.activation(out=ss_[:, 3:4], in_=ss_[:, 3:4], func=AF.Exp, scale=-0.5),
                         r=[ssb_], w=[ssb_])
                    ht, htb = hbufs2[i % 2]
                    src = h0 if layer == 0 else y
                    K.dma(sp, ht[:], src[128 * i:128 * i + 128, :], r=[hB[i]], w=[htb])
                    for hf in range(2):
                        K.op(dve, lambda e: e.scalar_tensor_tensor(out=mt[:, 512 * hf:512 * hf + 512], in0=pms[hf][0][:, :], scalar=ss_[:, 3:4],
                                                                   in1=gpost[:, 512 * hf:512 * hf + 512], op0=ALU.mult, op1=ALU.mult),
                             r=[pms[hf][1], ssb_, gb], w=[mtb])
                    K.op(dve, lambda e: e.tensor_tensor(out=ht[:], in0=ht[:], in1=mt[:], op=ALU.add),
                         r=[htb, mtb], w=[htb])
                    K.dma(sp, y[128 * i:128 * i + 128, :], ht[:], r=[htb], w=[hB[i]])
                K.barrier()

        def fox_layer(layer):
            j = layer // 2
            win = fox_w_in[j].rearrange("(c p) n -> p c n", p=128)
            with ExitStack() as ls:
                wb = [[(K.sb(ls, f"fw{s}{t}", [128, 8, 256], BF16), Buf()) for t in range(4)] for s in range(2)]
                wf = K.sb(ls, "fwf", [128, 8, 16], BF16)
                wfb = Buf()
                qT = K.sb(ls, "qT", [128, 2, T], BF16)
                kT = K.sb(ls, "kT", [128, 2, T], BF16)
                sgT = K.sb(ls, "sgT", [128, 2, T], BF16)
                qTb, kTb, sgTb = Buf(), Buf(), Buf()
                vaug = K.sb(ls, "vaug", [128, NT, 4, 128], BF16)
                vaugb = Buf()
                bft = K.sb(ls, "bft", [128, NH], F32)
                lf = K.sb(ls, "lf", [128, NT, NH], F32)
                lfb = Buf()
                cum = K.sb(ls, "cum", [128, NT, NH], F32)
                carry = K.sb(ls, "carry", [128, NT, NH], F32)
                cumb = Buf()
                bias = [(K.sb(ls, f"bias{i}", [128, NT], F32), Buf()) for i in range(8)]
                PT = [(K.sb(ls, f"PT{i}", [128, 512], BF16), Buf()) for i in range(3)]
                rs = [(K.sb(ls, f"rs{i}", [128, 512], F32), Buf()) for i in range(2)]
                tmpo = [(K.sb(ls, f"tmpo{i}", [128, 512], F32), Buf()) for i in range(2)]

                def load_group(g, s):
                    for t in range(4):
                        K.dma(pool, wb[s][t][0][:], win[:, :, 1024 * t + 256 * g:1024 * t + 256 * g + 256],
                              w=[wb[s][t][1]])
                K.dma(pool, wf[:], win[:, :, 4096:4112], w=[wfb])
                load_group(0, 0)
                K.dma(sp, bft[:], fox_b_f[j, :].partition_broadcast(128), w=[lfb])
                K.op(dve, lambda e: e.memset(vaug[:], 1.0), w=[vaugb])

                pf = psA[4]
                for i in range(NT):
                    for c in range(8):
                        K.op(pe, lambda e: e.matmul(pf[0][:, 16 * i:16 * i + 16], lhsT=uT[:, c, 1 + 128 * i:1 + 128 * i + 128],
                                                    rhs=wf[:, c, :], start=(c == 0), stop=(c == 7)),
                             r=[uTb, wfb], w=[pf[1]], inc=(c == 7))
                for i in range(NT):
                    K.op(dve, lambda e: e.tensor_tensor(out=lf[:, i, :], in0=pf[0][:, 16 * i:16 * i + 16], in1=bft[:],
                                                        op=ALU.add), r=[pf[1], lfb], w=[lfb])
                lf2 = lf[:].rearrange("p a b -> p (a b)")
                K.op(act, lambda e: e.activation(out=lf2, in_=lf2, func=AF.Exp, scale=-1.0), r=[lfb], w=[lfb])
                K.op(act, lambda e: e.activation(out=lf2, in_=lf2, func=AF.Ln, bias=epsb[:, 1:2], scale=1.0), r=[lfb, ssb], w=[lfb])
                K.op(dve, lambda e: e.tensor_scalar(out=lf2, in0=lf2, scalar1=-1.0, scalar2=None, op0=ALU.mult),
                     r=[lfb], w=[lfb])
                pc, pl = psA[2], psA[3]
                K.op(dve, lambda e: e.memset(carry[:, 0, :], 0.0), w=[cumb])
                for i in range(NT):
                    for jj in range(i):
                        K.op(pe, lambda e: e.matmul(pc[0][:, 16 * i:16 * i + 16], lhsT=ones_f[:], rhs=lf[:, jj, :],
                                                    start=(jj == 0), stop=(jj == i - 1)),
                             r=[lfb, constb], w=[pc[1]], inc=(jj == i - 1))
                    K.op(pe, lambda e: e.matmul(pl[0][:, 16 * i:16 * i + 16], lhsT=tri_f[:], rhs=lf[:, i, :],
                                                start=True, stop=True), r=[lfb, constb], w=[pl[1]])
                K.op(dve, lambda e: e.tensor_copy(out=carry[:, 1:NT, :].rearrange("p a b -> p (a b)"),
                                                  in_=pc[0][:, 16:16 * NT]), r=[pc[1]], w=[cumb])
                K.op(dve, lambda e: e.tensor_tensor(out=cum[:].rearrange("p a b -> p (a b)"),
                                                    in0=pl[0][:, 0:16 * NT],
                                                    in1=carry[:].rearrange("p a b -> p (a b)"), op=ALU.add),
                     r=[pl[1], cumb], w=[cumb])

                chunks = tok_chunks(512)
                pcount = [0]

                def nextps():
                    p = psA[pcount[0] % 2]
                    pcount[0] += 1
                    return p

                stc = [0]
                otc = [0]
                ptc = [0]
                bc = [0]
                for g in range(4):
                    s = g % 2
                    if g + 1 < 4:
                        load_group(g + 1, (g + 1) % 2)
                    wq, wk, wv, wg = [wb[s][t] for t in range(4)]
                    for pp in range(2):
                        for (t0, n) in chunks:
                            for (wt, dst, dstb, kind) in ((wq, qT, qTb, 0), (wk, kT, kTb, 1), (wg, sgT, sgTb, 2)):
                                p = nextps()
                                for c in range(8):
                                    K.op(pe, lambda e: e.matmul(p[0][:, 0:n], lhsT=wt[0][:, c, 128 * pp:128 * pp + 128],
                                                                rhs=uT[:, c, 1 + t0:1 + t0 + n],
                                                                start=(c == 0), stop=(c == 7)),
                                         r=[uTb, wt[1]], w=[p[1]], inc=(c == 7))
                                if kind == 0:
                                    K.op(act, lambda e: e.activation(out=dst[:, pp, t0:t0 + n], in_=p[0][:, 0:n],
                                                                     func=AF.Copy, scale=HD ** -0.5),
                                         r=[p[1]], w=[dstb])
                                elif kind == 1:
                                    K.op(dve, lambda e: e.tensor_copy(out=dst[:, pp, t0:t0 + n], in_=p[0][:, 0:n]),
                                         r=[p[1]], w=[dstb])
                                else:
                                    K.op(act, lambda e: e.activation(out=dst[:, pp, t0:t0 + n], in_=p[0][:, 0:n],
                                                                     func=AF.Silu), r=[p[1]], w=[dstb])
                    for i in range(NT):
                        p = nextps()
                        for c in range(8):
                            K.op(pe, lambda e: e.matmul(p[0][:, 0:256], lhsT=uT[:, c, 1 + 128 * i:1 + 128 * i + 128],
                                                        rhs=wv[0][:, c, :], start=(c == 0), stop=(c == 7)),
                                 r=[uTb, wv[1]], w=[p[1]], inc=(c == 7))
                        pv = p[0][:, 0:256].rearrange("p (a b c) -> p a b c", a=2, b=2)
                        K.op(dve, lambda e: e.tensor_copy(out=vaug[:, i, 0:4:2, 0:64], in_=pv[:, :, 0, :]),
                             r=[p[1]], w=[vaugb])
                        K.op(dve, lambda e: e.tensor_copy(out=vaug[:, i, 1:4:2, 64:128], in_=pv[:, :, 1, :]),
                             r=[p[1]], w=[vaugb])
                    items = []
                    for hh in range(4):
                        for cidx, (q0, qn) in enumerate(chunks):
                            cx = dict(hh=hh, q0=q0, qn=qn, i0=q0 // 128, ni=qn // 128, started=False)
                            cx["jmax"] = cx["i0"] + cx["ni"] - 1
                            for jk in range(cx["jmax"] + 1):
                                items.append((cx, jk))

                    def emit_st(it):
                        cx, jk = it
                        hh = cx["hh"]
                        h = 4 * g + hh
                        pp, half = hh // 2, hh % 2
                        lo = 64 * half
                        q0, qn, i0, ni = cx["q0"], cx["qn"], cx["i0"], cx["ni"]
                        if not cx["started"]:
                            cx["started"] = True
                            base = 4 * (bc[0] % 2)
                            bc[0] += 1
                            btabs = []
                            for ii in range(ni):
                                bt = bias[base + ii]
                                i = i0 + ii
                                K.op(dve, lambda e: e.tensor_scalar(out=bt[0][:, :], in0=cum[:, :, h], scalar1=-1.0,
                                                                    scalar2=carry[:, i, h:h + 1], op0=ALU.mult,
                                                                    op1=ALU.add), r=[cumb], w=[bt[1]])
                                btabs.append(bt)
                            cx["btabs"] = btabs
                            cx["ot"] = psA[4 + otc[0] % 2]
                            otc[0] += 1
                        qs = max(q0, 128 * jk)
                        n = q0 + qn - qs
                        stp = psA[2 + stc[0] % 2]
                        stc[0] += 1
                        K.op(pe, lambda e: e.matmul(stp[0][:, 0:n], lhsT=kT[lo:lo + 64, pp, 128 * jk:128 * jk + 128],
                                                    rhs=qT[lo:lo + 64, pp, qs:qs + n], start=True, stop=True),
                             r=[kTb, qTb], w=[stp[1]])
                        return stp

                    def emit_rest(it, stp):
                        cx, jk = it
                        hh = cx["hh"]
                        pp, half = hh // 2, hh % 2
                        olo, slo = (0, 64) if half == 0 else (64, 0)
                        q0, qn, i0, ni, jmax = cx["q0"], cx["qn"], cx["i0"], cx["ni"], cx["jmax"]
                        btabs, ot = cx["btabs"], cx["ot"]
                        qs = max(q0, 128 * jk)
                        n = q0 + qn - qs
                        pt = PT[ptc[0] % 3]
                        ptc[0] += 1
                        for ii in range((qs - q0) // 128, ni):
                            i = i0 + ii
                            co = 128 * i - qs
                            K.op(act, lambda e: e.activation(out=pt[0][:, co:co + 128], in_=stp[0][:, co:co + 128],
                                                             func=AF.Exp, bias=btabs[ii][0][:, jk:jk + 1], scale=1.0),
                                 r=[stp[1], btabs[ii][1]], w=[pt[1]])
                        if jk >= i0:
                            K.op(dve, lambda e: e.tensor_tensor(out=pt[0][:, 0:128], in0=pt[0][:, 0:128],
                                                                in1=tri_b[:], op=ALU.mult),
                                 r=[pt[1], constb], w=[pt[1]])
                        K.op(pe, lambda e: e.matmul(ot[0][:, qs - q0:qs - q0 + n], lhsT=vaug[:, jk, hh, :],
                                                    rhs=pt[0][:, 0:n], start=(jk == 0), stop=(jk == jmax),
                                                    skip_group_check=True),
                             r=[vaugb, pt[1]], w=[ot[1]])
                        if jk == jmax:
                            r_ = rs[otc[0] % 2]
                            tm = tmpo[otc[0] % 2]
                            K.op(dve, lambda e: e.reciprocal(out=r_[0][olo:olo + 64, 0:qn], in_=ot[0][slo:slo + 64, 0:qn]),
                                 r=[ot[1]], w=[r_[1]])
                            K.op(dve, lambda e: e.tensor_tensor(out=tm[0][olo:olo + 64, 0:qn], in0=ot[0][olo:olo + 64, 0:qn],
                                                                in1=r_[0][olo:olo + 64, 0:qn], op=ALU.mult),
                                 r=[ot[1], r_[1]], w=[tm[1]])
                            K.op(dve, lambda e: e.tensor_tensor(out=ogT[olo:olo + 64, 2 * g + pp, q0:q0 + qn],
                                                                in0=tm[0][olo:olo + 64, 0:qn],
                                                                in1=sgT[olo:olo + 64, pp, q0:q0 + qn], op=ALU.mult),
                                 r=[tm[1], sgTb], w=[ogTb])

                    nxt = emit_st(items[0])
                    for n_ in range(len(items)):
                        cur_st = nxt
                        if n_ + 1 < len(items):
                            nxt = emit_st(items[n_ + 1])
                        emit_rest(items[n_], cur_st)
                K.barrier()


        def rwkv_layer(layer):
            j = layer // 2
            win = rwkv_w_in[j].rearrange("(c p) n -> p c n", p=128)
            with ExitStack() as ls:
                SB = lambda n, shp, dt: K.sb(ls, n, shp, dt)
                wadT = SB("wadT", [128, T], BF16); wadTb = Buf()
                wup = SB("wup", [128, D], BF16); wupb = Buf()
                vecs = SB("vecs", [128, 8, 8], F32); vecb = Buf()
                mub = (SB("mub", [128, 2, 128], F32), Buf())
                wraw = (SB("wraw", [128, 8, 128], F32), Buf())
                bones = SB("bones", [128, 128], F32)
                msc = SB("msc", [128, 2, 256], BF16)
                mlow2 = SB("mlow2", [128, 2, 128], BF16)
                id2 = SB("id2", [128, 2, 128], BF16)
                hind = SB("hind", [128, 2], BF16)
                cb2 = Buf()

                class Ctx:
                    pass

                ctxs = []
                for ci in range(2):
                    C = Ctx()
                    C.ci = ci
                    F_ = lambda n: (SB(f"{n}{ci}", [128, 128], F32), Buf())
                    B_ = lambda n, shp: (SB(f"{n}{ci}", shp, BF16), Buf())
                    for n in ("rf", "kf", "sgw", "av", "lw", "cm", "cmx", "E1", "E2", "E3", "kkr", "sq", "hsn", "kk",
                              "t1", "kp", "bb", "ke3", "be3", "gs", "ytile", "yn"):
                        setattr(C, n, F_(n))
                    C.sets = []
                    for si in range(2):
                        S = Ctx()
                        S.sg = B_(f"sg{si}", [128, 128]); S.vtok = B_(f"vtok{si}", [128, 128])
                        S.AR = B_(f"AR{si}", [128, 2, 128]); S.BK = B_(f"BK{si}", [128, 2, 128])
                        S.ARz = B_(f"ARz{si}", [128, 2, 2, 128]); S.BKz = B_(f"BKz{si}", [128, 2, 128])
                        S.kbhat = B_(f"kbhat{si}", [128, 2, 128])
                        S.MB = B_(f"MB{si}", [128, 2, 256]); S.MK = B_(f"MK{si}", [128, 2, 256])
                        S.TT = B_(f"TTf{si}", [128, 2, 128])
                        S.GC = (SB(f"GC{ci}{si}", [128, 2], F32), Buf())
                        S.bon = (SB(f"bon{ci}{si}", [128, 2], F32), Buf())
                        K.op(dve, lambda e: e.memset(S.ARz[0][:], 0.0), w=[S.ARz[1]])
                        K.op(dve, lambda e: e.memset(S.BKz[0][:], 0.0), w=[S.BKz[1]])
                        C.sets.append(S)
                    C.khT = B_("khT", [128, 128]); C.bhT = B_("bhT", [128, 128]); C.rkb = B_("rkb", [128, 128])
                    C.PQT = [B_(f"PQT{i}", [128, 2, 384]) for i in range(2)]
                    C.Xs = B_("Xs", [128, 128]); C.Us = B_("Us", [128, 128]); C.ybf = B_("ybf", [128, 128])
                    C.st6 = (SB(f"st6{ci}", [128, 2, 6], F32), Buf())
                    C.mv = (SB(f"mv{ci}", [128, 2, 2], F32), Buf())
                    C.rstd = (SB(f"rstd{ci}", [128, 2], F32), Buf())
                    C.Hs = SB(f"Hs{ci}", [128, 128], F32); C.Hb = SB(f"Hb{ci}", [128, 128], BF16)
                    C.Hsb = Buf(); C.Hbb = Buf()
                    C.wcp = [(SB(f"wcp{ci}{t_}", [128, 16, 128], BF16), Buf()) for t_ in range(4)]
                    C.lnw = SB(f"lnw{ci}", [128, 128], F32); C.lnb = SB(f"lnb{ci}", [128, 128], F32); C.lnbuf = Buf()
                    C.bX, C.bY, C.bZ = psA[3 * ci], psA[3 * ci + 1], psA[3 * ci + 2]
                    C.front_done = 0
                    C.back_done = 0
                    ctxs.append(C)

                K.dma(sp, bones[:], c_bones[:, :], w=[cb2])
                K.dma(pool, msc[:].rearrange("p a b -> p (a b)"), c_msc[:, :], w=[cb2])
                K.dma(pool, mlow2[:].rearrange("p a b -> p (a b)"), c_mlow128[:, :], w=[cb2])
                K.dma(pool, id2[:].rearrange("p a b -> p (a b)"), c_id2[:, :], w=[cb2])
                K.dma(pool, hind[:], c_hind[:, :], w=[cb2])
                K.dma(pool, wup[0:64, :], rwkv_w_up[j], w=[wupb])
                K.dma(pool, wup[64:128, :], rwkv_a_up[j], w=[wupb])
                with nc.allow_non_contiguous_dma(reason="tiny per-feature vectors"):
                    for vi, src in enumerate((rwkv_w0, rwkv_a0, rwkv_k_k, rwkv_k_a, rwkv_r_k)):
                        K.dma(sp, vecs[:, vi, :], src[j, :].rearrange("(c p) -> p c", p=128), w=[vecb])
                K.op(dve, lambda e: e.tensor_scalar(out=vecs[:, 5, :], in0=vecs[:, 3, :], scalar1=-1.0, scalar2=1.0,
                                                    op0=ALU.mult, op1=ALU.add), r=[vecb], w=[vecb])
                K.op(dve, lambda e: e.tensor_scalar(out=vecs[:, 6, :], in0=vecs[:, 0, :], scalar1=-1.0, scalar2=None,
                                                    op0=ALU.mult), r=[vecb], w=[vecb])
                K.op(dve, lambda e: e.tensor_scalar(out=vecs[:, 7, :], in0=vecs[:, 1, :], scalar1=-1.0, scalar2=None,
                                                    op0=ALU.mult), r=[vecb], w=[vecb])

                def load_w(col0, dst):
                    mb, rw = mub, wraw
                    K.dma(sp, mb[0][:, 0, :], rwkv_mu[j, col0:col0 + 128].partition_broadcast(128), w=[mb[1]])
                    K.dma(sp, rw[0][:], win[:, :, col0:col0 + 128], w=[rw[1]])
                    K.op(dve, lambda e: e.tensor_scalar(out=mb[0][:, 1, :], in0=mb[0][:, 0, :], scalar1=-1.0, scalar2=1.0,
                                                        op0=ALU.mult, op1=ALU.add), r=[mb[1]], w=[mb[1]])
                    for c in range(8):
                        K.op(dve, lambda e: e.tensor_tensor(out=dst[0][:, c, :], in0=rw[0][:, c, :], in1=mb[0][:, 1, :],
                                                            op=ALU.mult), r=[rw[1], mb[1]], w=[dst[1]])
                        K.op(dve, lambda e: e.tensor_tensor(out=dst[0][:, 8 + c, :], in0=rw[0][:, c, :], in1=mb[0][:, 0, :],
                                                            op=ALU.mult), r=[rw[1], mb[1]], w=[dst[1]])

                def proj_fm(out_ps, outb, wt, t0, n):
                    for c in range(16):
                        rhs = uT[:, c, 1 + t0:1 + t0 + n] if c < 8 else uT[:, c - 8, t0:t0 + n]
                        K.op(pe, lambda e: e.matmul(out_ps, lhsT=wt[0][:, c, :], rhs=rhs, start=(c == 0), stop=(c == 15)),
                             r=[uTb, wt[1]], w=[outb], inc=(c == 15))

                wwa = ctxs[0].wcp[0]
                load_w(4096, wwa)
                for ci_, (t0, n) in enumerate(tok_chunks(512)):
                    p_ = psA[ci_ % 2]
                    proj_fm(p_[0][:, 0:n], p_[1], wwa, t0, n)
                    K.op(act, lambda e: e.activation(out=wadT[0:64, t0:t0 + n], in_=p_[0][0:64, 0:n], func=AF.Tanh),
                         r=[p_[1]], w=[wadTb])
                    K.op(act, lambda e: e.copy(out=wadT[64:128, t0:t0 + n], in_=p_[0][64:128, 0:n]), r=[p_[1]], w=[wadTb])

                v3 = lambda t_: t_[0][:].rearrange("p (c s) -> p c s", c=2)
                flat = lambda ap: ap.rearrange("p a b -> p (a b)")
                one_b = epsb[:, 1:2]

                def sigmoid_chain(C, src_ps, srcb, bias_ap, dst, extra_r=()):
                    if bias_ap is None:
                        K.op(act, lambda e: e.activation(out=dst[0][:], in_=src_ps, func=AF.Exp, scale=-1.0),
                             r=[srcb] + list(extra_r), w=[dst[1]])
                    else:
                        K.op(act, lambda e: e.activation(out=dst[0][:], in_=src_ps, func=AF.Exp, bias=bias_ap, scale=-1.0),
                             r=[srcb] + list(extra_r), w=[dst[1]])
                    K.op(act, lambda e: e.activation(out=dst[0][:], in_=dst[0][:], func=AF.Ln, bias=one_b, scale=1.0),
                         r=[dst[1], ssb], w=[dst[1]])
                    K.op(act, lambda e: e.activation(out=dst[0][:], in_=dst[0][:], func=AF.Exp, scale=-1.0),
                         r=[dst[1]], w=[dst[1]])

                def front(C, p):
                    vcol = lambda vi: vecs[:, vi, p:p + 1]
                    wr, wk, wv, wg = C.wcp
                    bX, bY = C.bX, C.bY
                    for i in range(NT):
                        while C.back_done < i - 1:
                            yield False
                        S = C.sets[i % 2]
                        t0 = 128 * i
                        rf, kf, sgw, av, lw, cm, cmx, E1, E2, E3 = C.rf, C.kf, C.sgw, C.av, C.lw, C.cm, C.cmx, C.E1, C.E2, C.E3
                        kkr, sq, hsn, kk, t1, kp, bb, ke3, be3, gs = C.kkr, C.sq, C.hsn, C.kk, C.t1, C.kp, C.bb, C.ke3, C.be3, C.gs
                        AR, BK, MB, MK = S.AR, S.BK, S.MB, S.MK
                        proj_fm(bX[0][:, 0:128], bX[1], wr, t0, 128)
                        proj_fm(bX[0][:, 128:256], bX[1], wk, t0, 128)
                        yield True
                        proj_fm(bX[0][:, 256:384], bX[1], wg, t0, 128)
                        for c in range(16):
                            lhsT = uT[:, c, 1 + t0:1 + t0 + 128] if c < 8 else uT[:, c - 8, t0:t0 + 128]
                            K.op(pe, lambda e: e.matmul(bX[0][:, 384:512], lhsT=lhsT, rhs=wv[0][:, c, :],
                                                        start=(c == 0), stop=(c == 15)),
                                 r=[uTb, wv[1]], w=[bX[1]], inc=(c == 15))
                        K.op(pe, lambda e: e.matmul(bY[0][:, 0:128], lhsT=wup[0:64, 128 * p:128 * p + 128],
                                                    rhs=wadT[0:64, t0:t0 + 128], start=True, stop=True),
                             r=[wupb, wadTb], w=[bY[1]])
                        K.op(pe, lambda e: e.matmul(bY[0][:, 128:256], lhsT=wup[64:128, 128 * p:128 * p + 128],
                                                    rhs=wadT[64:128, t0:t0 + 128], start=True, stop=True),
                             r=[wupb, wadTb], w=[bY[1]])
                        yield True
                        K.op(act, lambda e: e.copy(out=rf[0][:], in_=bX[0][:, 0:128]), r=[bX[1]], w=[rf[1]])
                        K.op(act, lambda e: e.copy(out=kf[0][:], in_=bX[0][:, 128:256]), r=[bX[1]], w=[kf[1]])
                        sigmoid_chain(C, bX[0][:, 256:384], bX[1], None, gs)
                        K.op(dve, lambda e: e.tensor_tensor(out=S.sg[0][:], in0=bX[0][:, 256:384], in1=gs[0][:], op=ALU.mult),
                             r=[bX[1], gs[1]], w=[S.sg[1]])
                        K.op(dve, lambda e: e.tensor_copy(out=S.vtok[0][:], in_=bX[0][:, 384:512]), r=[bX[1]], w=[S.vtok[1]])
                        yield True
                        sigmoid_chain(C, bY[0][:, 0:128], bY[1], vcol(6), sgw, extra_r=[vecb])
                        sigmoid_chain(C, bY[0][:, 128:256], bY[1], vcol(7), av, extra_r=[vecb])
                        K.op(dve, lambda e: e.tensor_scalar(out=lw[0][:], in0=sgw[0][:], scalar1=-DECAY_SCALE, scalar2=None,
                                                            op0=ALU.mult), r=[sgw[1]], w=[lw[1]])
                        K.op(dve, lambda e: e.tensor_tensor_scan(out=cm[0][:], data0=ones_f[:], data1=lw[0][:], initial=0.0,
                                                                 op0=ALU.mult, op1=ALU.add), r=[lw[1], constb], w=[cm[1]])
                        K.op(dve, lambda e: e.tensor_tensor(out=cmx[0][:], in0=cm[0][:], in1=lw[0][:], op=ALU.subtract),
                             r=[cm[1], lw[1]], w=[cmx[1]])
                        yield True
                        K.op(act, lambda e: e.activation(out=E1[0][:], in_=cm[0][:], func=AF.Exp), r=[cm[1]], w=[E1[1]])
                        K.op(act, lambda e: e.activation(out=E2[0][:], in_=cmx[0][:], func=AF.Exp), r=[cmx[1]], w=[E2[1]])
                        K.op(act, lambda e: e.activation(out=E3[0][:], in_=cm[0][:], func=AF.Exp, scale=-1.0),
                             r=[cm[1]], w=[E3[1]])
                        K.op(act, lambda e: e.activation(out=S.GC[0][:, 0:1], in_=cm[0][:, 127:128], func=AF.Exp),
                             r=[cm[1]], w=[S.GC[1]])
                        K.op(dve, lambda e: e.tensor_scalar(out=kkr[0][:], in0=kf[0][:], scalar1=vcol(2), scalar2=None,
                                                            op0=ALU.mult), r=[kf[1], vecb], w=[kkr[1]])
                        K.op(pool, lambda e: e.tensor_tensor(out=sq[0][:], in0=kkr[0][:], in1=kkr[0][:], op=ALU.mult),
                             r=[kkr[1]], w=[sq[1]])
                        K.op(pe, lambda e: e.matmul(bY[0][:, 256:384], lhsT=bones[:], rhs=sq[0][:], start=True, stop=True),
                             r=[cb2, sq[1]], w=[bY[1]])
                        yield True
                        K.op(dve, lambda e: e.tensor_scalar(out=hsn[0][:], in0=bY[0][:, 256:384], scalar1=1e-24, scalar2=None,
                                                            op0=ALU.max), r=[bY[1]], w=[hsn[1]])
                        K.op(act, lambda e: e.activation(out=hsn[0][:], in_=hsn[0][:], func=AF.Ln), r=[hsn[1]], w=[hsn[1]])
                        K.op(act, lambda e: e.activation(out=hsn[0][:], in_=hsn[0][:], func=AF.Exp, scale=-0.5),
                             r=[hsn[1]], w=[hsn[1]])
                        K.op(dve, lambda e: e.tensor_scalar(out=t1[0][:], in0=av[0][:], scalar1=vcol(3), scalar2=vcol(5),
                                                            op0=ALU.mult, op1=ALU.add), r=[av[1], vecb], w=[t1[1]])
                        K.op(pool, lambda e: e.tensor_tensor(out=kp[0][:], in0=kf[0][:], in1=t1[0][:], op=ALU.mult),
                             r=[kf[1], t1[1]], w=[kp[1]])
                        K.op(dve, lambda e: e.tensor_tensor(out=AR[0][:, 1, :], in0=rf[0][:], in1=E1[0][:], op=ALU.mult),
                             r=[rf[1], E1[1]], w=[AR[1]])
                        K.op(pool, lambda e: e.tensor_tensor(out=ke3[0][:], in0=kp[0][:], in1=E3[0][:], op=ALU.mult),
                             r=[kp[1], E3[1]], w=[ke3[1]])
                        yield True
                        K.op(dve, lambda e: e.tensor_tensor(out=kk[0][:], in0=kkr[0][:], in1=hsn[0][:], op=ALU.mult),
                             r=[kkr[1], hsn[1]], w=[kk[1]])
                        K.op(pool, lambda e: e.tensor_tensor(out=bb[0][:], in0=kk[0][:], in1=av[0][:], op=ALU.mult),
                             r=[kk[1], av[1]], w=[bb[1]])
                        K.op(dve, lambda e: e.scalar_tensor_tensor(out=AR[0][:, 0, :], in0=kk[0][:], scalar=-1.0, in1=E2[0][:],
                                                                   op0=ALU.mult, op1=ALU.mult), r=[kk[1], E2[1]], w=[AR[1]])
                        K.op(pool, lambda e: e.tensor_tensor(out=be3[0][:], in0=bb[0][:], in1=E3[0][:], op=ALU.mult),
                             r=[bb[1], E3[1]], w=[be3[1]])
                        K.op(act, lambda e: e.copy(out=BK[0][:, 1, :], in_=ke3[0][:]), r=[ke3[1]], w=[BK[1]])
                        K.op(act, lambda e: e.copy(out=BK[0][:, 0, :], in_=be3[0][:]), r=[be3[1]], w=[BK[1]])
                        yield True
                        K.op(pool, lambda e: e.tensor_copy(out=S.ARz[0][0:64, 0, :, :], in_=AR[0][0:64, :, :]), r=[AR[1]], w=[S.ARz[1]])
                        K.op(pool, lambda e: e.tensor_copy(out=S.ARz[0][64:128, 1, :, :], in_=AR[0][64:128, :, :]), r=[AR[1]], w=[S.ARz[1]])
                        K.op(pool, lambda e: e.tensor_copy(out=S.BKz[0][0:64, 0, :], in_=BK[0][0:64, 0, :]), r=[BK[1]], w=[S.BKz[1]])
                        K.op(pool, lambda e: e.tensor_copy(out=S.BKz[0][64:128, 1, :], in_=BK[0][64:128, 0, :]), r=[BK[1]], w=[S.BKz[1]])
                        K.op(dve, lambda e: e.tensor_scalar(out=C.khT[0][:], in0=ke3[0][:], scalar1=S.GC[0][:, 0:1], scalar2=None,
                                                            op0=ALU.mult), r=[ke3[1], S.GC[1]], w=[C.khT[1]])
                        K.op(dve, lambda e: e.tensor_scalar(out=C.bhT[0][:], in0=be3[0][:], scalar1=S.GC[0][:, 0:1], scalar2=None,
                                                            op0=ALU.mult), r=[be3[1], S.GC[1]], w=[C.bhT[1]])
                        K.op(dve, lambda e: e.scalar_tensor_tensor(out=C.rkb[0][:], in0=rf[0][:], scalar=vcol(4), in1=kp[0][:],
                                                                   op0=ALU.mult, op1=ALU.mult), r=[rf[1], kp[1], vecb], w=[C.rkb[1]])
                        K.op(pe, lambda e: e.matmul(bY[0][:, 384:386], lhsT=C.rkb[0][:], rhs=hind[:], start=True, stop=True),
                             r=[C.rkb[1], cb2], w=[bY[1]])
                        K.op(pe, lambda e: e.transpose(out=psT[0][:, 4 + 2 * C.ci, :], in_=C.khT[0][:], identity=ident[:]),
                             r=[C.khT[1], identb], w=[psT[1]], inc=False)
                        K.op(pe, lambda e: e.transpose(out=psT[0][:, 5 + 2 * C.ci, :], in_=C.bhT[0][:], identity=ident[:]),
                             r=[C.bhT[1], identb], w=[psT[1]])
                        yield True
                        K.op(dve, lambda e: e.tensor_copy(out=S.bon[0][:], in_=bY[0][:, 384:386]), r=[bY[1]], w=[S.bon[1]])
                        K.op(act, lambda e: e.copy(out=S.kbhat[0][:], in_=psT[0][:, 4 + 2 * C.ci:6 + 2 * C.ci, :]), r=[psT[1]], w=[S.kbhat[1]])
                        arz = S.ARz[0][:].rearrange("p h s t -> p (h s t)")
                        K.op(pe, lambda e: e.matmul(bX[0][:, 0:512], lhsT=BK[0][:, 0, :], rhs=arz, start=True, stop=True),
                             r=[BK[1], S.ARz[1]], w=[bX[1]])
                        K.op(pe, lambda e: e.matmul(bY[0][:, 0:512], lhsT=BK[0][:, 1, :], rhs=arz, start=True, stop=True),
                             r=[BK[1], S.ARz[1]], w=[bY[1]])
                        yield True
                        P0 = C.PQT[0]
                        K.op(dve, lambda e: e.tensor_tensor(out=MB[0][:].rearrange("p h c -> p (h c)"), in0=bX[0][:, 0:512],
                                                            in1=msc[:].rearrange("p h c -> p (h c)"), op=ALU.mult),
                             r=[bX[1], cb2], w=[MB[1]])
                        K.op(pe, lambda e: e.matmul(bX[0][:, 0:256], lhsT=AR[0][:, 0, :], rhs=S.BKz[0][:].rearrange("p h s -> p (h s)"),
                                                    start=True, stop=True), r=[AR[1], S.BKz[1]], w=[bX[1]])
                        K.op(dve, lambda e: e.tensor_tensor(out=MK[0][:].rearrange("p h c -> p (h c)"), in0=bY[0][:, 0:512],
                                                            in1=msc[:].rearrange("p h c -> p (h c)"), op=ALU.mult),
                             r=[bY[1], cb2], w=[MK[1]])
                        yield True
                        K.op(dve, lambda e: e.tensor_tensor(out=P0[0][:, :, 0:128], in0=bX[0][:, 0:256].rearrange("p (h s) -> p h s", h=2),
                                                            in1=mlow2[:], op=ALU.mult), r=[bX[1], cb2], w=[P0[1]])
                        K.op(pool, lambda e: e.tensor_copy(out=P0[0][:, :, 128:256], in_=MB[0][:, :, 0:128]), r=[MB[1]], w=[P0[1]])
                        K.op(pool, lambda e: e.tensor_copy(out=P0[0][:, :, 256:384], in_=id2[:]), r=[cb2], w=[P0[1]])
                        yield True
                        for stp_ in range(1, 8):
                            prev, cur = C.PQT[(stp_ - 1) % 2], C.PQT[stp_ % 2]
                            last = (stp_ == 7)
                            for hd in range(2):
                                bk = bX if hd == 0 else bY
                                Pm = prev[0][:, hd, 0:128]
                                Qm = prev[0][:, hd, 128:256]
                                if not last:
                                    K.op(pe, lambda e: e.matmul(bk[0][:, 0:128], lhsT=Qm, rhs=Pm, start=True, stop=True),
                                         r=[prev[1]], w=[bk[1]], inc=False)
                                    K.op(pe, lambda e: e.matmul(bk[0][:, 128:384], lhsT=Pm, rhs=prev[0][:, hd, 128:384], start=True, stop=False),
                                         r=[prev[1]], w=[bk[1]], inc=False)
                                else:
                                    K.op(pe, lambda e: e.matmul(bk[0][:, 256:384], lhsT=Pm, rhs=prev[0][:, hd, 256:384], start=True, stop=False),
                                         r=[prev[1]], w=[bk[1]], inc=False)
                                K.op(pe, lambda e: e.matmul(bk[0][:, 256:384], lhsT=ident[:], rhs=prev[0][:, hd, 256:384], start=False, stop=True),
                                     r=[prev[1], identb], w=[bk[1]])
                            yield True
                            for hd in range(2):
                                bk = bX if hd == 0 else bY
                                if not last:
                                    dst_, src_ = cur[0][:, hd, :], bk[0][:, 0:384]
                                    dstb_ = cur[1]
                                else:
                                    dst_, src_ = S.TT[0][:, hd, :], bk[0][:, 256:384]
                                    dstb_ = S.TT[1]
                                if hd == 0:
                                    K.op(act, lambda e: e.copy(out=dst_, in_=src_), r=[bk[1]], w=[dstb_])
                                else:
                                    K.op(dve, lambda e: e.tensor_copy(out=dst_, in_=src_), r=[bk[1]], w=[dstb_])
                            yield True
                        C.front_done = i + 1
                        yield True

                def back(C, p):
                    bZ = C.bZ
                    Hs, Hb, Hsb, Hbb = C.Hs, C.Hb, C.Hsb, C.Hbb
                    Xs, Us, ytile, yn, ybf = C.Xs, C.Us, C.ytile, C.yn, C.ybf
                    for i in range(NT):
                        while C.front_done < i + 1:
                            yield False
                        S = C.sets[i % 2]
                        AR, MB, MK, vtok, kbhat, GC, Tf = S.AR, S.MB, S.MK, S.vtok, S.kbhat, S.GC, S.TT
                        t0 = 128 * i
                        hc = lambda hd: slice(64 * hd, 64 * hd + 64)
                        K.op(pe, lambda e: e.matmul(bZ[0][:, 0:128], lhsT=AR[0][:, 0, :], rhs=Hb[:], start=True, stop=False, skip_group_check=True),
                             r=[AR[1], Hbb], w=[bZ[1]], inc=False)
                        for hd in range(2):
                            K.op(pe, lambda e: e.matmul(bZ[0][:, hc(hd)], lhsT=MK[0][:, hd, 0:128], rhs=vtok[0][:, hc(hd)],
                                                        start=False, stop=(hd == 1), skip_group_check=True),
                                 r=[MK[1], vtok[1]], w=[bZ[1]], inc=(hd == 1))
                        yield True
                        K.op(act, lambda e: e.copy(out=Xs[0][:], in_=bZ[0][:, 0:128]), r=[bZ[1]], w=[Xs[1]])
                        yield True
                        for hd in range(2):
                            K.op(pe, lambda e: e.matmul(bZ[0][:, 128 + 64 * hd:128 + 64 * hd + 64], lhsT=Tf[0][:, hd, :], rhs=Xs[0][:, hc(hd)],
                                                        start=True, stop=True), r=[Tf[1], Xs[1]], w=[bZ[1]], inc=(hd == 1))
                        yield True
                        K.op(dve, lambda e: e.tensor_copy(out=Us[0][:], in_=bZ[0][:, 128:256]), r=[bZ[1]], w=[Us[1]])
                        yield True
                        K.op(pe, lambda e: e.matmul(bZ[0][:, 256:384], lhsT=AR[0][:, 1, :], rhs=Hb[:], start=True, stop=False, skip_group_check=True),
                             r=[AR[1], Hbb], w=[bZ[1]], inc=False)
                        for hd in range(2):
                            o_ = bZ[0][:, 256 + 64 * hd:256 + 64 * hd + 64]
                            K.op(pe, lambda e: e.matmul(o_, lhsT=MB[0][:, hd, 128:256], rhs=Us[0][:, hc(hd)], start=False, stop=False,
                                                        skip_group_check=True), r=[MB[1], Us[1]], w=[bZ[1]], inc=False)
                            K.op(pe, lambda e: e.matmul(o_, lhsT=MK[0][:, hd, 128:256], rhs=vtok[0][:, hc(hd)], start=False, stop=(hd == 1),
                                                        skip_group_check=True), r=[MK[1], vtok[1]], w=[bZ[1]], inc=False)
                        for hd in range(2):
                            o2 = bZ[0][:, 384 + 64 * hd:384 + 64 * hd + 64]
                            K.op(pe, lambda e: e.matmul(o2, lhsT=kbhat[0][:, 1, :], rhs=Us[0][:, hc(hd)], start=True, stop=False),
                                 r=[kbhat[1], Us[1]], w=[bZ[1]], inc=False)
                            K.op(pe, lambda e: e.matmul(o2, lhsT=kbhat[0][:, 0, :], rhs=vtok[0][:, hc(hd)], start=False, stop=True),
                                 r=[kbhat[1], vtok[1]], w=[bZ[1]], inc=(hd == 1))
                        yield True
                        for hd in range(2):
                            lo = 64 * hd
                            K.op(dve, lambda e: e.scalar_tensor_tensor(out=Hs[lo:lo + 64, hc(hd)], in0=Hs[lo:lo + 64, hc(hd)], scalar=GC[0][lo:lo + 64, 0:1],
                                                                       in1=bZ[0][lo:lo + 64, 384 + 64 * hd:384 + 64 * hd + 64],
                                                                       op0=ALU.mult, op1=ALU.add), r=[Hsb, GC[1], bZ[1]], w=[Hsb])
                        K.op(dve, lambda e: e.tensor_copy(out=Hb[:], in_=Hs[:]), r=[Hsb], w=[Hbb])
                        K.op(dve, lambda e: e.tensor_copy(out=ytile[0][:], in_=bZ[0][:, 256:384]), r=[bZ[1]], w=[ytile[1]])
                        yield True
                        for hf in range(2):
                            K.op(dve, lambda e: e.bn_stats(out=C.st6[0][:, hf, :], in_=ytile[0][:, 64 * hf:64 * hf + 64]), r=[ytile[1]], w=[C.st6[1]])
                            K.op(dve, lambda e: e.bn_aggr(out=C.mv[0][:, hf, :], in_=C.st6[0][:, hf, :]), r=[C.st6[1]], w=[C.mv[1]])
                        yield True
                        K.op(act, lambda e: e.activation(out=C.rstd[0][:], in_=C.mv[0][:, :, 1], func=AF.Ln, bias=epsb[:, 2:3], scale=1.0),
                             r=[C.mv[1], ssb], w=[C.rstd[1]])
                        K.op(act, lambda e: e.activation(out=C.rstd[0][:], in_=C.rstd[0][:], func=AF.Exp, scale=-0.5), r=[C.rstd[1]], w=[C.rstd[1]])
                        yield True
                        for hf in range(2):
                            cs = slice(64 * hf, 64 * hf + 64)
                            K.op(dve, lambda e: e.tensor_scalar(out=yn[0][:, cs], in0=ytile[0][:, cs], scalar1=C.mv[0][:, hf, 0:1],
                                                                scalar2=C.rstd[0][:, hf:hf + 1], op0=ALU.subtract, op1=ALU.mult),
                                 r=[ytile[1], C.mv[1], C.rstd[1]], w=[yn[1]])
                        K.op(dve, lambda e: e.tensor_tensor(out=yn[0][:], in0=yn[0][:], in1=C.lnw[:], op=ALU.mult),
                             r=[yn[1], C.lnbuf], w=[yn[1]])
                        K.op(dve, lambda e: e.tensor_tensor(out=yn[0][:], in0=yn[0][:], in1=C.lnb[:], op=ALU.add),
                             r=[yn[1], C.lnbuf], w=[yn[1]])
                        for hf in range(2):
                            cs = slice(64 * hf, 64 * hf + 64)
                            K.op(dve, lambda e: e.scalar_tensor_tensor(out=ybf[0][:, cs], in0=vtok[0][:, cs], scalar=S.bon[0][:, hf:hf + 1],
                                                                       in1=yn[0][:, cs], op0=ALU.mult, op1=ALU.add),
                                 r=[vtok[1], S.bon[1], yn[1]], w=[ybf[1]])
                        yield True
                        K.op(pe, lambda e: e.transpose(out=psT[0][:, 2 + C.ci, :], in_=ybf[0][:], identity=ident[:]), r=[ybf[1], identb], w=[psT[1]])
                        yield True
                        K.op(dve, lambda e: e.tensor_tensor(out=ogT[:, p, t0:t0 + 128], in0=psT[0][:, 2 + C.ci, :], in1=S.sg[0][:], op=ALU.mult),
                             r=[psT[1], S.sg[1]], w=[ogTb])
                        C.back_done = i + 1
                        yield True

                def pair_stream(C):
                    for p in range(C.ci, 8, 2):
                        for t_ in range(4):
                            load_w(1024 * t_ + 128 * p, C.wcp[t_])
                            yield True
                        K.dma(sp, C.lnw[:], rwkv_ln_w[j, 128 * p:128 * p + 128].partition_broadcast(128), w=[C.lnbuf])
                        K.dma(sp, C.lnb[:], rwkv_ln_b[j, 128 * p:128 * p + 128].partition_broadcast(128), w=[C.lnbuf])
                        K.op(dve, lambda e: e.memset(C.Hs[:], 0.0), w=[C.Hsb])
                        K.op(dve, lambda e: e.memset(C.Hb[:], 0.0), w=[C.Hbb])
                        C.front_done = 0
                        C.back_done = 0
                        gens = [front(C, p), back(C, p)]
                        while gens:
                            progressed = False
                            for g_ in list(gens):
                                try:
                                    if next(g_):
                                        progressed = True
                                except StopIteration:
                                    gens.remove(g_)
                                    progressed = True
                            yield progressed

                streams = [pair_stream(C) for C in ctxs]
                while streams:
                    for s_ in list(streams):
                        try:
                            next(s_)
                        except StopIteration:
                            streams.remove(s_)
                K.barrier()


        for layer in range(nlayers):
            phase_prenorm(layer)
            if layer % 2 == 0:
                fox_layer(layer)
                phase_post(layer, fox_w_out[layer // 2], layer == nlayers - 1)
            else:
                rwkv_layer(layer)
                phase_post(layer, rwkv_w_out[layer // 2], layer == nlayers - 1)
            K.barrier()
        K.barrier()
    return nc


_CACHE = {}


def _consts():
    idx = np.arange(128)
    tri = (idx[:, None] <= idx[None, :]).astype(np.float32)
    ident = np.eye(128, dtype=np.float32)
    ones = np.ones((128, 128), np.float32)
    m64 = np.zeros((128, 128), np.float32)
    s = idx[:, None] % 64
    t = idx[None, :] % 64
    m64[:, 0:64] = (s < t)[:, 0:64]
    m64[:, 64:128] = (s <= t)[:, 64:128]
    scan = np.ones((128, 512), np.float32)
    scan[:, ::64] = 0.0
    hind = np.zeros((128, 2), np.float32)
    hind[0:64, 0] = 1.0
    hind[64:128, 1] = 1.0
    m64x2 = np.concatenate([m64, m64], axis=1)
    r64 = idx[:, None] % 64
    c64 = np.arange(64)[None, :]
    mlow = (c64 < r64).astype(np.float32)
    mlow2 = np.concatenate([mlow, mlow], axis=1)
    i64 = (c64 == r64).astype(np.float32)
    idx2 = np.concatenate([i64, i64], axis=1)
    bones = ((idx[:, None] // 64) == (idx[None, :] // 64)).astype(np.float32)
    strict = (idx[:, None] < idx[None, :]).astype(np.float32)
    msc1 = np.concatenate([strict, tri], axis=1)
    msc = np.concatenate([msc1, msc1], axis=1)
    mlow128 = (idx[None, :] < idx[:, None]).astype(np.float32)
    return dict(c_ident=ident, c_tri=tri, c_ones=ones, c_m64=m64, c_scan=scan, c_hind=hind,
                c_m64x2=m64x2, c_mlow2=mlow2, c_idx2=idx2, c_bones=bones,
                c_msc=msc, c_mlow128=np.concatenate([mlow128, mlow128], axis=1),
                c_id2=np.concatenate([ident, ident], axis=1))


def kernel(x, meta_tokens, norm_pre, norm_post, fox_w_in, fox_b_f, fox_w_out,
           rwkv_w_in, rwkv_mu, rwkv_w0, rwkv_w_up, rwkv_a0, rwkv_a_up, rwkv_k_k,
           rwkv_k_a, rwkv_r_k, rwkv_ln_w, rwkv_ln_b, rwkv_w_out, _nlayers=DEPTH):
    f = lambda a: np.ascontiguousarray(np.asarray(a, dtype=np.float32))
    x = f(x)
    B = x.shape[0]
    meta = f(meta_tokens)
    h0 = np.zeros((B, T, D), np.float32)
    h0[:, :NMETA] = meta[None]
    h0[:, NMETA:NMETA + SEQ] = x
    shared = dict(
        norm_pre=f(norm_pre), norm_post=f(norm_post), fox_w_in=f(fox_w_in), fox_b_f=f(fox_b_f),
        fox_w_out=f(fox_w_out), rwkv_w_in=f(rwkv_w_in), rwkv_mu=f(rwkv_mu), rwkv_w0=f(rwkv_w0),
        rwkv_w_up=f(rwkv_w_up), rwkv_a0=f(rwkv_a0), rwkv_a_up=f(rwkv_a_up), rwkv_k_k=f(rwkv_k_k),
        rwkv_k_a=f(rwkv_k_a), rwkv_r_k=f(rwkv_r_k).reshape(2, D), rwkv_ln_w=f(rwkv_ln_w),
        rwkv_ln_b=f(rwkv_ln_b), rwkv_w_out=f(rwkv_w_out))
    shared.update(_consts())
    key = _nlayers
    if key not in _CACHE:
        _CACHE[key] = build_program(_nlayers)
    nc = _CACHE[key]
    in_maps = []
    for b in range(B):
        m = dict(shared)
        m["h0"] = h0[b]
        in_maps.append(m)
    res = run_bass_kernel_spmd(nc, in_maps, core_ids=list(range(B)))
    out = np.stack([np.asarray(r["y"])[NMETA:NMETA + SEQ] for r in res.results], axis=0)
    return out.astype(np.float32)
```

```python
import math
from contextlib import ExitStack

import numpy as np
import concourse.bass as bass
import concourse.mybir as mybir
from concourse.bass_utils import run_bass_kernel_spmd

F32 = mybir.dt.float32
BF16 = mybir.dt.bfloat16
AF = mybir.ActivationFunctionType
ALU = mybir.AluOpType
AX = mybir.AxisListType

D = 1024
SEQ = 2048
NMETA = 16
NT = 17
T = NT * 128
DEPTH = 4
NH = 16
HD = 64
FOX_IN = 4 * D + NH
RWKV_IN = 4 * D + 128
NORM_EPS = 1e-6
GN_EPS = 64e-5
DECAY_SCALE = math.exp(-0.5)
CH = 64
DBG = {"stage": 9, "pairs": 8, "tiles": NT}


def tok_chunks(n=512):
    out = []
    t0 = 0
    while t0 < T:
        m = min(n, T - t0)
        out.append((t0, m))
        t0 += m
    return out


class Buf:
    __slots__ = ("name", "lw", "rd", "excl")

    def __init__(self, name="", excl=False):
        self.name = name
        self.lw = None
        self.rd = {}
        self.excl = excl


class Q:
    def __init__(self, name, eng, sem, self_sync=True):
        self.name = name
        self.eng = eng
        self.sem = sem
        self.cnt = 0
        self.seen = {}
        self.self_sync = self_sync
        self.key = name
        self.ring = []
        self.ring_i = 0


class PEProxy:
    def __init__(self, K, eng):
        self.K = K
        self.eng = eng
        self.partial = False

    def _pre(self, st_ap, out):
        K = self.K
        rg = (st_ap.base_partition(), st_ap.partition_size())
        bank = out.name
        last = K.pe_last
        if last is not None and last[0] != rg and (last[1] == bank or (last[0][1] < 128 and rg[1] < 128)):
            assert K.pe_last_tk is not None, "previous matmul needs a semaphore increment"
            K._wait(K.pe, K.pe_last_tk)
        K.pe_last = (rg, bank)
        self.partial = rg[1] < 128

    def matmul(self, out, lhsT, rhs, **kw):
        self._pre(lhsT, out)
        return self.eng.matmul(out, lhsT=lhsT, rhs=rhs, **kw)

    def transpose(self, out, in_, identity):
        self._pre(in_, out)
        return self.eng.transpose(out=out, in_=in_, identity=identity)


class KB:
    def __init__(self, nc, st):
        self.nc = nc
        self.st = st
        mk = lambda n: st.enter_context(nc.semaphore(n))
        self.pe = Q("pe", nc.tensor, mk("s_pe"), self_sync=False)
        self.act = Q("act", nc.scalar, mk("s_act"))
        self.dve = Q("dve", nc.vector, mk("s_dve"))
        self.pool = Q("pool", nc.gpsimd, mk("s_pool"))
        self.sp = Q("sp", nc.sync, mk("s_sp"))
        self.queues = [self.pe, self.act, self.dve, self.pool, self.sp]
        for q, n in ((self.sp, 24), (self.pool, 24)):
            for i in range(n):
                q.ring.append([mk(f"d_{q.name}{i}"), 0, f"d_{q.name}{i}"])
        self.dma_tickets = []
        self.nbuf = 0
        self.counting = False
        self.nops = 0
        self.pe_last = None
        self.pe_last_tk = None
        self.prox = PEProxy(self, nc.tensor)

    def sb(self, st, name, shape, dt):
        self.nbuf += 1
        return st.enter_context(self.nc.sbuf_tensor(f"{name}_{self.nbuf}", list(shape), dt))

    def psum(self, st, name, shape, dt):
        return st.enter_context(self.nc.psum_tensor(name, list(shape), dt))

    def _wait(self, q, tk):
        sem, val, key = tk
        if q.seen.get(key, 0) >= val:
            return
        q.eng.wait_ge(sem, val)
        q.seen[key] = val

    def _deps(self, q, r, w):
        deps = []
        for b in r:
            if b.lw is not None:
                deps.append(b.lw)
            if b.excl:
                for key, tk in b.rd.items():
                    if key != q.key:
                        deps.append(tk)
        for b in w:
            if b.lw is not None:
                deps.append(b.lw)
            for key, tk in b.rd.items():
                deps.append(tk)
        for tk in deps:
            if tk[2] == q.key and not q.self_sync:
                continue
            self._wait(q, tk)

    def _record(self, tk, r, w):
        for b in r:
            old = b.rd.get(tk[2])
            if old is None or old[1] < tk[1]:
                b.rd[tk[2]] = tk
        for b in w:
            b.lw = tk
            b.rd = {}

    def op(self, q, fn, r=(), w=(), inc=True):
        if self.counting:
            self.nops += 1
            if self.nops > DBG.get("maxops", 10 ** 9):
                return None
        self._deps(q, r, w)
        if q is self.pe:
            self.prox.partial = False
            ins = fn(self.prox)
            if self.prox.partial:
                inc = True
        else:
            ins = fn(q.eng)
        if inc:
            q.cnt += 1
            ins.then_inc(q.sem, 1)
            tk = (q.sem, q.cnt, q.key)
        else:
            tk = (q.sem, q.cnt + 1, q.key)
        if q is self.pe:
            self.pe_last_tk = tk if inc else None
        self._record(tk, r, w)
        return tk

    def dma(self, q, out, in_, r=(), w=()):
        self._deps(q, r, w)
        slot = q.ring[q.ring_i % len(q.ring)]
        q.ring_i += 1
        sem, n, key = slot
        if n > 0:
            self._wait(q, (sem, 16 * n, key))
        q.eng.dma_start(out=out, in_=in_).then_inc(sem, 16)
        slot[1] = n + 1
        tk = (sem, 16 * (n + 1), key)
        self._record(tk, r, w)
        self.dma_tickets.append(tk)
        return tk

    def barrier(self):
        tks = [(q.sem, q.cnt, q.key) for q in self.queues if q.cnt > 0]
        for q in self.queues:
            for slot in q.ring:
                if slot[1] > 0:
                    tks.append((slot[0], 16 * slot[1], slot[2]))
        for q in self.queues:
            for tk in tks:
                if tk[2] == q.key:
                    continue
                self._wait(q, tk)


def build_program(nlayers=DEPTH, dbg=False):
    nc = bass.Bass("TRN2", target_bir_lowering=False)
    dt_in = lambda name, shape: nc.dram_tensor(name, list(shape), F32, kind="ExternalInput").ap()
    h0 = dt_in("h0", [T, D])
    norm_pre = dt_in("norm_pre", [DEPTH, D])
    norm_post = dt_in("norm_post", [DEPTH, D])
    fox_w_in = dt_in("fox_w_in", [2, D, FOX_IN])
    fox_b_f = dt_in("fox_b_f", [2, NH])
    fox_w_out = dt_in("fox_w_out", [2, D, D])
    rwkv_w_in = dt_in("rwkv_w_in", [2, D, RWKV_IN])
    rwkv_mu = dt_in("rwkv_mu", [2, RWKV_IN])
    rwkv_w0 = dt_in("rwkv_w0", [2, D])
    rwkv_w_up = dt_in("rwkv_w_up", [2, 64, D])
    rwkv_a0 = dt_in("rwkv_a0", [2, D])
    rwkv_a_up = dt_in("rwkv_a_up", [2, 64, D])
    rwkv_k_k = dt_in("rwkv_k_k", [2, D])
    rwkv_k_a = dt_in("rwkv_k_a", [2, D])
    rwkv_r_k = dt_in("rwkv_r_k", [2, D])
    rwkv_ln_w = dt_in("rwkv_ln_w", [2, D])
    rwkv_ln_b = dt_in("rwkv_ln_b", [2, D])
    rwkv_w_out = dt_in("rwkv_w_out", [2, D, D])
    c_ident = dt_in("c_ident", [128, 128])
    c_tri = dt_in("c_tri", [128, 128])
    c_ones = dt_in("c_ones", [128, 128])
    c_m64 = dt_in("c_m64", [128, 128])
    c_scan = dt_in("c_scan", [128, 512])
    c_hind = dt_in("c_hind", [128, 2])
    c_m64x2 = dt_in("c_m64x2", [128, 256])
    c_mlow2 = dt_in("c_mlow2", [128, 128])
    c_idx2 = dt_in("c_idx2", [128, 128])
    c_bones = dt_in("c_bones", [128, 128])
    c_msc = dt_in("c_msc", [128, 512])
    c_mlow128 = dt_in("c_mlow128", [128, 256])
    c_id2 = dt_in("c_id2", [128, 256])
    y = nc.dram_tensor("y", [T, D], F32, kind="ExternalOutput").ap()

    with ExitStack() as st:
        K = KB(nc, st)
        pe, act, dve, pool, sp = K.pe, K.act, K.dve, K.pool, K.sp

        uT = K.sb(st, "uT", [128, 8, T + 2], BF16)
        uTb = Buf("uT")
        ogT = K.sb(st, "ogT", [128, 8, T], BF16)
        ogTb = Buf("ogT")
        ident = K.sb(st, "ident", [128, 128], BF16)
        identb = Buf()
        tri_f = K.sb(st, "tri_f", [128, 128], F32)
        ones_f = K.sb(st, "ones_f", [128, 128], F32)
        tri_b = K.sb(st, "tri_b", [128, 128], BF16)
        constb = Buf()
        gpre = K.sb(st, "gpre", [128, D], F32)
        gpost = K.sb(st, "gpost", [128, D], F32)
        gb = Buf()
        hbufs = [(K.sb(st, f"hb{i}", [128, D], F32), Buf()) for i in range(2)]
        hbufs2 = hbufs
        un = K.sb(st, "un", [128, D], BF16)
        unb = Buf()
        un2 = K.sb(st, "un2", [128, D], BF16)
        uns = [(un, unb), (un2, Buf())]
        sss = [(K.sb(st, f"ss{i}", [128, 4], F32), Buf()) for i in range(2)]
        mt = K.sb(st, "mt", [128, D], F32)
        mtb = Buf()
        ss = K.sb(st, "ss", [128, 4], F32)
        ssb = Buf()
        epsb = K.sb(st, "epsb", [128, 4], F32)
        K.op(dve, lambda e: e.memset(epsb[:, 0:1], NORM_EPS), w=[ssb])
        K.op(dve, lambda e: e.memset(epsb[:, 1:2], 1.0), w=[ssb])
        K.op(dve, lambda e: e.memset(epsb[:, 2:3], GN_EPS), w=[ssb])
        psA = [(K.psum(st, f"psA{i}", [128, 512], F32), Buf(excl=True)) for i in range(6)]
        psT = (K.psum(st, "psT", [128, 8, 128], BF16), Buf(excl=True))
        psT2 = (K.psum(st, "psT2", [128, 8, 128], BF16), Buf(excl=True))
        psTs = [psT, psT2]
        hB = [Buf(f"h{i}") for i in range(NT)]

        K.dma(pool, ident[:], c_ident[:, :], w=[identb])
        K.dma(pool, tri_b[:], c_tri[:, :], w=[constb])
        K.dma(sp, tri_f[:], c_tri[:, :], w=[constb])
        K.dma(sp, ones_f[:], c_ones[:, :], w=[constb])
        K.op(dve, lambda e: e.memset(uT[:, :, 0:1], 0.0), w=[uTb])

        def bcast_row(ap_row, n):
            return ap_row.partition_broadcast(128)

        def phase_prenorm(layer):
            K.dma(sp, gpre[:], bcast_row(norm_pre[layer, :], D), w=[gb])
            K.dma(sp, gpost[:], bcast_row(norm_post[layer, :], D), w=[gb])
            for i in range(NT):
                ht, htb = hbufs[i % 2]
                ss_, ssb_ = sss[i % 2]
                un_, unb_ = uns[i % 2]
                pT = psTs[i % 2]
                src = h0 if layer == 0 else y
                K.dma(sp, ht[:], src[128 * i:128 * i + 128, :], r=[hB[i]], w=[htb])
                K.op(act, lambda e: e.activation(out=mt[:], in_=ht[:], func=AF.Square, accum_out=ss_[:, 0:1]),
                     r=[htb], w=[mtb, ssb_])
                K.op(act, lambda e: e.activation(out=ss_[:, 1:2], in_=ss_[:, 0:1], func=AF.Ln, bias=epsb[:, 0:1], scale=1.0 / D),
                     r=[ssb_, ssb], w=[ssb_])
                K.op(act, lambda e: e.activation(out=ss_[:, 2:3], in_=ss_[:, 1:2], func=AF.Exp, scale=-0.5),
                     r=[ssb_], w=[ssb_])
                K.op(dve, lambda e: e.scalar_tensor_tensor(out=un_[:], in0=ht[:], scalar=ss_[:, 2:3], in1=gpre[:],
                                                           op0=ALU.mult, op1=ALU.mult), r=[htb, ssb_, gb], w=[unb_])
                for c in range(8):
                    K.op(pe, lambda e: e.transpose(out=pT[0][:, c, :], in_=un_[:, 128 * c:128 * c + 128],
                                                   identity=ident[:]),
                         r=[unb_, identb], w=[pT[1]], inc=(c == 7))
                if i % 2 == 0:
                    K.op(act, lambda e: e.copy(out=uT[:, :, 1 + 128 * i:1 + 128 * i + 128], in_=pT[0][:, :, :]),
                         r=[pT[1]], w=[uTb])
                else:
                    K.op(dve, lambda e: e.tensor_copy(out=uT[:, :, 1 + 128 * i:1 + 128 * i + 128], in_=pT[0][:, :, :]),
                         r=[pT[1]], w=[uTb])

        def phase_post(layer, w_out_ap, last):
            with ExitStack() as pst:
                wout = K.sb(pst, "wout", [128, 8, D], BF16)
                woutb = Buf()
                K.dma(pool, wout[:], w_out_ap.rearrange("(c p) n -> p c n", p=128), w=[woutb])
                for i in range(NT):
                    pms = [psA[0], psA[1]] if i % 2 == 0 else [psA[2], psA[3]]
                    ss_, ssb_ = sss[i % 2]
                    for hf in range(2):
                        for c in range(8):
                            K.op(pe, lambda e: e.matmul(pms[hf][0][:, :], lhsT=ogT[:, c, 128 * i:128 * i + 128],
                                                        rhs=wout[:, c, 512 * hf:512 * hf + 512],
                                                        start=(c == 0), stop=(c == 7)),
                                 r=[ogTb, woutb], w=[pms[hf][1]], inc=(c == 7))
                    for hf in range(2):
                        K.op(act, lambda e: e.activation(out=un[:, 512 * hf:512 * hf + 512], in_=pms[hf][0][:, :], func=AF.Square,
                                                         accum_out=ss_[:, hf:hf + 1]), r=[pms[hf][1]], w=[unb, ssb_])
                    K.op(dve, lambda e: e.tensor_tensor(out=ss_[:, 2:3], in0=ss_[:, 0:1], in1=ss_[:, 1:2], op=ALU.add),
                         r=[ssb_], w=[ssb_])
                    K.op(act, lambda e: e.activation(out=ss_[:, 3:4], in_=ss_[:, 2:3], func=AF.Ln, bias=epsb[:, 0:1], scale=1.0 / D),
                         r=[ssb_, ssb], w=[ssb_])
                    K.op(act, lambda e: e.activation(out=ss_[:, 3:4], in_=ss_[:, 3:4], func=AF.Exp, scale=-0.5),
                         r=[ssb_], w=[ssb_])
                    ht, htb = hbufs2[i % 2]
                    src = h0 if layer == 0 else y
                    K.dma(sp, ht[:], src[128 * i:128 * i + 128, :], r=[hB[i]], w=[htb])
                    for hf in range(2):
                        K.op(dve, lambda e: e.scalar_tensor_tensor(out=mt[:, 512 * hf:512 * hf + 512], in0=pms[hf][0][:, :], scalar=ss_[:, 3:4],
                                                                   in1=gpost[:, 512 * hf:512 * hf + 512], op0=ALU.mult, op1=ALU.mult),
                             r=[pms[hf][1], ssb_, gb], w=[mtb])
                    K.op(dve, lambda e: e.tensor_tensor(out=ht[:], in0=ht[:], in1=mt[:], op=ALU.add),
                         r=[htb, mtb], w=[htb])
                    K.dma(sp, y[128 * i:128 * i + 128, :], ht[:], r=[htb], w=[hB[i]])
                K.barrier()

        def fox_layer(layer):
            j = layer // 2
            win = fox_w_in[j].rearrange("(c p) n -> p c n", p=128)
            with ExitStack() as ls:
                wb = [[(K.sb(ls, f"fw{s}{t}", [128, 8, 256], BF16), Buf()) for t in range(4)] for s in range(2)]
                wf = K.sb(ls, "fwf", [128, 8, 16], BF16)
                wfb = Buf()
                qT = K.sb(ls, "qT", [128, 2, T], BF16)
                kT = K.sb(ls, "kT", [128, 2, T], BF16)
                sgT = K.sb(ls, "sgT", [128, 2, T], BF16)
                qTb, kTb, sgTb = Buf(), Buf(), Buf()
                vaug = K.sb(ls, "vaug", [128, NT, 4, 128], BF16)
                vaugb = Buf()
                bft = K.sb(ls, "bft", [128, NH], F32)
                lf = K.sb(ls, "lf", [128, NT, NH], F32)
                lfb = Buf()
                cum = K.sb(ls, "cum", [128, NT, NH], F32)
                carry = K.sb(ls, "carry", [128, NT, NH], F32)
                cumb = Buf()
                bias = [(K.sb(ls, f"bias{i}", [128, NT], F32), Buf()) for i in range(8)]
                PT = [(K.sb(ls, f"PT{i}", [128, 512], BF16), Buf()) for i in range(3)]
                rs = [(K.sb(ls, f"rs{i}", [128, 512], F32), Buf()) for i in range(2)]
                tmpo = [(K.sb(ls, f"tmpo{i}", [128, 512], F32), Buf()) for i in range(2)]

                def load_group(g, s):
                    for t in range(4):
                        K.dma(pool, wb[s][t][0][:], win[:, :, 1024 * t + 256 * g:1024 * t + 256 * g + 256],
                              w=[wb[s][t][1]])
                K.dma(pool, wf[:], win[:, :, 4096:4112], w=[wfb])
                load_group(0, 0)
                K.dma(sp, bft[:], fox_b_f[j, :].partition_broadcast(128), w=[lfb])
                K.op(dve, lambda e: e.memset(vaug[:], 1.0), w=[vaugb])

                pf = psA[4]
                for i in range(NT):
                    for c in range(8):
                        K.op(pe, lambda e: e.matmul(pf[0][:, 16 * i:16 * i + 16], lhsT=uT[:, c, 1 + 128 * i:1 + 128 * i + 128],
                                                    rhs=wf[:, c, :], start=(c == 0), stop=(c == 7)),
                             r=[uTb, wfb], w=[pf[1]], inc=(c == 7))
                for i in range(NT):
                    K.op(dve, lambda e: e.tensor_tensor(out=lf[:, i, :], in0=pf[0][:, 16 * i:16 * i + 16], in1=bft[:],
                                                        op=ALU.add), r=[pf[1], lfb], w=[lfb])
                lf2 = lf[:].rearrange("p a b -> p (a b)")
                K.op(act, lambda e: e.activation(out=lf2, in_=lf2, func=AF.Exp, scale=-1.0), r=[lfb], w=[lfb])
                K.op(act, lambda e: e.activation(out=lf2, in_=lf2, func=AF.Ln, bias=epsb[:, 1:2], scale=1.0), r=[lfb, ssb], w=[lfb])
                K.op(dve, lambda e: e.tensor_scalar(out=lf2, in0=lf2, scalar1=-1.0, scalar2=None, op0=ALU.mult),
                     r=[lfb], w=[lfb])
                pc, pl = psA[2], psA[3]
                K.op(dve, lambda e: e.memset(carry[:, 0, :], 0.0), w=[cumb])
                for i in range(NT):
                    for jj in range(i):
                        K.op(pe, lambda e: e.matmul(pc[0][:, 16 * i:16 * i + 16], lhsT=ones_f[:], rhs=lf[:, jj, :],
                                                    start=(jj == 0), stop=(jj == i - 1)),
                             r=[lfb, constb], w=[pc[1]], inc=(jj == i - 1))
                    K.op(pe, lambda e: e.matmul(pl[0][:, 16 * i:16 * i + 16], lhsT=tri_f[:], rhs=lf[:, i, :],
                                                start=True, stop=True), r=[lfb, constb], w=[pl[1]])
                K.op(dve, lambda e: e.tensor_copy(out=carry[:, 1:NT, :].rearrange("p a b -> p (a b)"),
                                                  in_=pc[0][:, 16:16 * NT]), r=[pc[1]], w=[cumb])
                K.op(dve, lambda e: e.tensor_tensor(out=cum[:].rearrange("p a b -> p (a b)"),
                                                    in0=pl[0][:, 0:16 * NT],
                                                    in1=carry[:].rearrange("p a b -> p (a b)"), op=ALU.add),
                     r=[pl[1], cumb], w=[cumb])

                chunks = tok_chunks(512)
                pcount = [0]

                def nextps():
                    p = psA[pcount[0] % 2]
                    pcount[0] += 1
                    return p

                stc = [0]
                otc = [0]
                ptc = [0]
                bc = [0]
                for g in range(4):
                    s = g % 2
                    if g + 1 < 4:
                        load_group(g + 1, (g + 1) % 2)
                    wq, wk, wv, wg = [wb[s][t] for t in range(4)]
                    for pp in range(2):
                        for (t0, n) in chunks:
                            for (wt, dst, dstb, kind) in ((wq, qT, qTb, 0), (wk, kT, kTb, 1), (wg, sgT, sgTb, 2)):
                                p = nextps()
                                for c in range(8):
                                    K.op(pe, lambda e: e.matmul(p[0][:, 0:n], lhsT=wt[0][:, c, 128 * pp:128 * pp + 128],
                                                                rhs=uT[:, c, 1 + t0:1 + t0 + n],
                                                                start=(c == 0), stop=(c == 7)),
                                         r=[uTb, wt[1]], w=[p[1]], inc=(c == 7))
                                if kind == 0:
                                    K.op(act, lambda e: e.activation(out=dst[:, pp, t0:t0 + n], in_=p[0][:, 0:n],
                                                                     func=AF.Copy, scale=HD ** -0.5),
                                         r=[p[1]], w=[dstb])
                                elif kind == 1:
                                    K.op(dve, lambda e: e.tensor_copy(out=dst[:, pp, t0:t0 + n], in_=p[0][:, 0:n]),
                                         r=[p[1]], w=[dstb])
                                else:
                                    K.op(act, lambda e: e.activation(out=dst[:, pp, t0:t0 + n], in_=p[0][:, 0:n],
                                                                     func=AF.Silu), r=[p[1]], w=[dstb])
                    for i in range(NT):
                        p = nextps()
                        for c in range(8):
                            K.op(pe, lambda e: e.matmul(p[0][:, 0:256], lhsT=uT[:, c, 1 + 128 * i:1 + 128 * i + 128],
                                                        rhs=wv[0][:, c, :], start=(c == 0), stop=(c == 7)),
                                 r=[uTb, wv[1]], w=[p[1]], inc=(c == 7))
                        pv = p[0][:, 0:256].rearrange("p (a b c) -> p a b c", a=2, b=2)
                        K.op(dve, lambda e: e.tensor_copy(out=vaug[:, i, 0:4:2, 0:64], in_=pv[:, :, 0, :]),
                             r=[p[1]], w=[vaugb])
                        K.op(dve, lambda e: e.tensor_copy(out=vaug[:, i, 1:4:2, 64:128], in_=pv[:, :, 1, :]),
                             r=[p[1]], w=[vaugb])
                    items = []
                    for hh in range(4):
                        for cidx, (q0, qn) in enumerate(chunks):
                            cx = dict(hh=hh, q0=q0, qn=qn, i0=q0 // 128, ni=qn // 128, started=False)
                            cx["jmax"] = cx["i0"] + cx["ni"] - 1
                            for jk in range(cx["jmax"] + 1):
                                items.append((cx, jk))

                    def emit_st(it):
                        cx, jk = it
                        hh = cx["hh"]
                        h = 4 * g + hh
                        pp, half = hh // 2, hh % 2
                        lo = 64 * half
                        q0, qn, i0, ni = cx["q0"], cx["qn"], cx["i0"], cx["ni"]
                        if not cx["started"]:
                            cx["started"] = True
                            base = 4 * (bc[0] % 2)
                            bc[0] += 1
                            btabs = []
                            for ii in range(ni):
                                bt = bias[base + ii]
                                i = i0 + ii
                                K.op(dve, lambda e: e.tensor_scalar(out=bt[0][:, :], in0=cum[:, :, h], scalar1=-1.0,
                                                                    scalar2=carry[:, i, h:h + 1], op0=ALU.mult,
                                                                    op1=ALU.add), r=[cumb], w=[bt[1]])
                                btabs.append(bt)
                            cx["btabs"] = btabs
                            cx["ot"] = psA[4 + otc[0] % 2]
                            otc[0] += 1
                        qs = max(q0, 128 * jk)
                        n = q0 + qn - qs
                        stp = psA[2 + stc[0] % 2]
                        stc[0] += 1
                        K.op(pe, lambda e: e.matmul(stp[0][:, 0:n], lhsT=kT[lo:lo + 64, pp, 128 * jk:128 * jk + 128],
                                                    rhs=qT[lo:lo + 64, pp, qs:qs + n], start=True, stop=True),
                             r=[kTb, qTb], w=[stp[1]])
                        return stp

                    def emit_rest(it, stp):
                        cx, jk = it
                        hh = cx["hh"]
                        pp, half = hh // 2, hh % 2
                        olo, slo = (0, 64) if half == 0 else (64, 0)
                        q0, qn, i0, ni, jmax = cx["q0"], cx["qn"], cx["i0"], cx["ni"], cx["jmax"]
                        btabs, ot = cx["btabs"], cx["ot"]
                        qs = max(q0, 128 * jk)
                        n = q0 + qn - qs
                        pt = PT[ptc[0] % 3]
                        ptc[0] += 1
                        for ii in range((qs - q0) // 128, ni):
                            i = i0 + ii
                            co = 128 * i - qs
                            K.op(act, lambda e: e.activation(out=pt[0][:, co:co + 128], in_=stp[0][:, co:co + 128],
                                                             func=AF.Exp, bias=btabs[ii][0][:, jk:jk + 1], scale=1.0),
                                 r=[stp[1], btabs[ii][1]], w=[pt[1]])
                        if jk >= i0:
                            K.op(dve, lambda e: e.tensor_tensor(out=pt[0][:, 0:128], in0=pt[0][:, 0:128],
                                                                in1=tri_b[:], op=ALU.mult),
                                 r=[pt[1], constb], w=[pt[1]])
                        K.op(pe, lambda e: e.matmul(ot[0][:, qs - q0:qs - q0 + n], lhsT=vaug[:, jk, hh, :],
                                                    rhs=pt[0][:, 0:n], start=(jk == 0), stop=(jk == jmax),
                                                    skip_group_check=True),
                             r=[vaugb, pt[1]], w=[ot[1]])
                        if jk == jmax:
                            r_ = rs[otc[0] % 2]
                            tm = tmpo[otc[0] % 2]
                            K.op(dve, lambda e: e.reciprocal(out=r_[0][olo:olo + 64, 0:qn], in_=ot[0][slo:slo + 64, 0:qn]),
                                 r=[ot[1]], w=[r_[1]])
                            K.op(dve, lambda e: e.tensor_tensor(out=tm[0][olo:olo + 64, 0:qn], in0=ot[0][olo:olo + 64, 0:qn],
                                                                in1=r_[0][olo:olo + 64, 0:qn], op=ALU.mult),
                                 r=[ot[1], r_[1]], w=[tm[1]])
                            K.op(dve, lambda e: e.tensor_tensor(out=ogT[olo:olo + 64, 2 * g + pp, q0:q0 + qn],
                                                                in0=tm[0][olo:olo + 64, 0:qn],
                                                                in1=sgT[olo:olo + 64, pp, q0:q0 + qn], op=ALU.mult),
                                 r=[tm[1], sgTb], w=[ogTb])

                    nxt = emit_st(items[0])
                    for n_ in range(len(items)):
                        cur_st = nxt
                        if n_ + 1 < len(items):
                            nxt = emit_st(items[n_ + 1])
                        emit_rest(items[n_], cur_st)
                K.barrier()


        def rwkv_layer(layer):
            j = layer // 2
            win = rwkv_w_in[j].rearrange("(c p) n -> p c n", p=128)
            with ExitStack() as ls:
                SB = lambda n, shp, dt: K.sb(ls, n, shp, dt)
                wadT = SB("wadT", [128, T], BF16); wadTb = Buf()
                wup = SB("wup", [128, D], BF16); wupb = Buf()
                vecs = SB("vecs", [128, 8, 8], F32); vecb = Buf()
                mub = (SB("mub", [128, 2, 128], F32), Buf())
                wraw = (SB("wraw", [128, 8, 128], F32), Buf())
                bones = SB("bones", [128, 128], F32)
                msc = SB("msc", [128, 2, 256], BF16)
                mlow2 = SB("mlow2", [128, 2, 128], BF16)
                id2 = SB("id2", [128, 2, 128], BF16)
                hind = SB("hind", [128, 2], BF16)
                cb2 = Buf()

                class Ctx:
                    pass

                ctxs = []
                for ci in range(2):
                    C = Ctx()
                    C.ci = ci
                    F_ = lambda n: (SB(f"{n}{ci}", [128, 128], F32), Buf())
                    B_ = lambda n, shp: (SB(f"{n}{ci}", shp, BF16), Buf())
                    for n in ("rf", "kf", "sgw", "av", "lw", "cm", "cmx", "E1", "E2", "E3", "kkr", "sq", "hsn", "kk",
                              "t1", "kp", "bb", "ke3", "be3", "gs", "ytile", "yn"):
                        setattr(C, n, F_(n))
                    C.sets = []
                    for si in range(2):
                        S = Ctx()
                        S.sg = B_(f"sg{si}", [128, 128]); S.vtok = B_(f"vtok{si}", [128, 128])
                        S.AR = B_(f"AR{si}", [128, 2, 128]); S.BK = B_(f"BK{si}", [128, 2, 128])
                        S.ARz = B_(f"ARz{si}", [128, 2, 2, 128]); S.BKz = B_(f"BKz{si}", [128, 2, 128])
                        S.kbhat = B_(f"kbhat{si}", [128, 2, 128])
                        S.MB = B_(f"MB{si}", [128, 2, 256]); S.MK = B_(f"MK{si}", [128, 2, 256])
                        S.TT = B_(f"TTf{si}", [128, 2, 128])
                        S.GC = (SB(f"GC{ci}{si}", [128, 2], F32), Buf())
                        S.bon = (SB(f"bon{ci}{si}", [128, 2], F32), Buf())
                        K.op(dve, lambda e: e.memset(S.ARz[0][:], 0.0), w=[S.ARz[1]])
                        K.op(dve, lambda e: e.memset(S.BKz[0][:], 0.0), w=[S.BKz[1]])
                        C.sets.append(S)
                    C.khT = B_("khT", [128, 128]); C.bhT = B_("bhT", [128, 128]); C.rkb = B_("rkb", [128, 128])
                    C.PQT = [B_(f"PQT{i}", [128, 2, 384]) for i in range(2)]
                    C.PQ0 = B_("PQ0", [128, 2, 384])
                    C.Xs = B_("Xs", [128, 128]); C.Us = B_("Us", [128, 128]); C.ybf = B_("ybf", [128, 128])
                    C.st6 = (SB(f"st6{ci}", [128, 2, 6], F32), Buf())
                    C.mv = (SB(f"mv{ci}", [128, 2, 2], F32), Buf())
                    C.rstd = (SB(f"rstd{ci}", [128, 2], F32), Buf())
                    C.Hs = SB(f"Hs{ci}", [128, 128], F32); C.Hb = SB(f"Hb{ci}", [128, 128], BF16)
                    C.Hsb = Buf(); C.Hbb = Buf()
                    C.wcp = [(SB(f"wcp{ci}{t_}", [128, 16, 128], BF16), Buf()) for t_ in range(4)]
                    C.lnw = SB(f"lnw{ci}", [128, 128], F32); C.lnb = SB(f"lnb{ci}", [128, 128], F32); C.lnbuf = Buf()
                    C.bX, C.bY, C.bZ = psA[3 * ci], psA[3 * ci + 1], psA[3 * ci + 2]
                    C.front_done = 0
                    C.back_done = 0
                    ctxs.append(C)

                K.dma(sp, bones[:], c_bones[:, :], w=[cb2])
                K.dma(pool, msc[:].rearrange("p a b -> p (a b)"), c_msc[:, :], w=[cb2])
                K.dma(pool, mlow2[:].rearrange("p a b -> p (a b)"), c_mlow128[:, :], w=[cb2])
                K.dma(pool, id2[:].rearrange("p a b -> p (a b)"), c_id2[:, :], w=[cb2])
                for C in ctxs:
                    K.op(act, lambda e: e.copy(out=C.PQ0[0][:, :, 256:384], in_=id2[:]), r=[cb2], w=[C.PQ0[1]])
                K.dma(pool, hind[:], c_hind[:, :], w=[cb2])
                K.dma(pool, wup[0:64, :], rwkv_w_up[j], w=[wupb])
                K.dma(pool, wup[64:128, :], rwkv_a_up[j], w=[wupb])
                with nc.allow_non_contiguous_dma(reason="tiny per-feature vectors"):
                    for vi, src in enumerate((rwkv_w0, rwkv_a0, rwkv_k_k, rwkv_k_a, rwkv_r_k)):
                        K.dma(sp, vecs[:, vi, :], src[j, :].rearrange("(c p) -> p c", p=128), w=[vecb])
                K.op(dve, lambda e: e.tensor_scalar(out=vecs[:, 5, :], in0=vecs[:, 3, :], scalar1=-1.0, scalar2=1.0,
                                                    op0=ALU.mult, op1=ALU.add), r=[vecb], w=[vecb])
                K.op(dve, lambda e: e.tensor_scalar(out=vecs[:, 6, :], in0=vecs[:, 0, :], scalar1=-1.0, scalar2=None,
                                                    op0=ALU.mult), r=[vecb], w=[vecb])
                K.op(dve, lambda e: e.tensor_scalar(out=vecs[:, 7, :], in0=vecs[:, 1, :], scalar1=-1.0, scalar2=None,
                                                    op0=ALU.mult), r=[vecb], w=[vecb])

                def load_w(col0, dst):
                    mb, rw = mub, wraw
                    K.dma(sp, mb[0][:, 0, :], rwkv_mu[j, col0:col0 + 128].partition_broadcast(128), w=[mb[1]])
                    K.dma(sp, rw[0][:], win[:, :, col0:col0 + 128], w=[rw[1]])
                    K.op(dve, lambda e: e.tensor_scalar(out=mb[0][:, 1, :], in0=mb[0][:, 0, :], scalar1=-1.0, scalar2=1.0,
                                                        op0=ALU.mult, op1=ALU.add), r=[mb[1]], w=[mb[1]])
                    for c in range(8):
                        K.op(dve, lambda e: e.tensor_tensor(out=dst[0][:, c, :], in0=rw[0][:, c, :], in1=mb[0][:, 1, :],
                                                            op=ALU.mult), r=[rw[1], mb[1]], w=[dst[1]])
                        K.op(dve, lambda e: e.tensor_tensor(out=dst[0][:, 8 + c, :], in0=rw[0][:, c, :], in1=mb[0][:, 0, :],
                                                            op=ALU.mult), r=[rw[1], mb[1]], w=[dst[1]])

                def proj_fm(out_ps, outb, wt, t0, n):
                    for c in range(16):
                        rhs = uT[:, c, 1 + t0:1 + t0 + n] if c < 8 else uT[:, c - 8, t0:t0 + n]
                        K.op(pe, lambda e: e.matmul(out_ps, lhsT=wt[0][:, c, :], rhs=rhs, start=(c == 0), stop=(c == 15)),
                             r=[uTb, wt[1]], w=[outb], inc=(c == 15))

                wwa = ctxs[0].wcp[0]
                load_w(4096, wwa)
                for ci_, (t0, n) in enumerate(tok_chunks(512)):
                    p_ = psA[ci_ % 2]
                    proj_fm(p_[0][:, 0:n], p_[1], wwa, t0, n)
                    K.op(act, lambda e: e.activation(out=wadT[0:64, t0:t0 + n], in_=p_[0][0:64, 0:n], func=AF.Tanh),
                         r=[p_[1]], w=[wadTb])
                    K.op(act, lambda e: e.copy(out=wadT[64:128, t0:t0 + n], in_=p_[0][64:128, 0:n]), r=[p_[1]], w=[wadTb])

                v3 = lambda t_: t_[0][:].rearrange("p (c s) -> p c s", c=2)
                flat = lambda ap: ap.rearrange("p a b -> p (a b)")
                one_b = epsb[:, 1:2]

                def sigmoid_chain(C, src_ps, srcb, bias_ap, dst, extra_r=()):
                    if bias_ap is None:
                        K.op(act, lambda e: e.activation(out=dst[0][:], in_=src_ps, func=AF.Exp, scale=-1.0),
                             r=[srcb] + list(extra_r), w=[dst[1]])
                    else:
                        K.op(act, lambda e: e.activation(out=dst[0][:], in_=src_ps, func=AF.Exp, bias=bias_ap, scale=-1.0),
                             r=[srcb] + list(extra_r), w=[dst[1]])
                    K.op(act, lambda e: e.activation(out=dst[0][:], in_=dst[0][:], func=AF.Ln, bias=one_b, scale=1.0),
                         r=[dst[1], ssb], w=[dst[1]])
                    K.op(act, lambda e: e.activation(out=dst[0][:], in_=dst[0][:], func=AF.Exp, scale=-1.0),
                         r=[dst[1]], w=[dst[1]])

                def front(C, p):
                    vcol = lambda vi: vecs[:, vi, p:p + 1]
                    wr, wk, wv, wg = C.wcp
                    bX, bY = C.bX, C.bY
                    for i in range(NT):
                        while C.back_done < i - 1:
                            yield False
                        S = C.sets[i % 2]
                        t0 = 128 * i
                        rf, kf, sgw, av, lw, cm, cmx, E1, E2, E3 = C.rf, C.kf, C.sgw, C.av, C.lw, C.cm, C.cmx, C.E1, C.E2, C.E3
                        kkr, sq, hsn, kk, t1, kp, bb, ke3, be3, gs = C.kkr, C.sq, C.hsn, C.kk, C.t1, C.kp, C.bb, C.ke3, C.be3, C.gs
                        AR, BK, MB, MK = S.AR, S.BK, S.MB, S.MK
                        proj_fm(bX[0][:, 0:128], bX[1], wr, t0, 128)
                        proj_fm(bX[0][:, 128:256], bX[1], wk, t0, 128)
                        proj_fm(bX[0][:, 256:384], bX[1], wg, t0, 128)
                        for c in range(16):
                            lhsT = uT[:, c, 1 + t0:1 + t0 + 128] if c < 8 else uT[:, c - 8, t0:t0 + 128]
                            K.op(pe, lambda e: e.matmul(bX[0][:, 384:512], lhsT=lhsT, rhs=wv[0][:, c, :],
                                                        start=(c == 0), stop=(c == 15)),
                                 r=[uTb, wv[1]], w=[bX[1]], inc=(c == 15))
                        K.op(pe, lambda e: e.matmul(bY[0][:, 0:128], lhsT=wup[0:64, 128 * p:128 * p + 128],
                                                    rhs=wadT[0:64, t0:t0 + 128], start=True, stop=True),
                             r=[wupb, wadTb], w=[bY[1]])
                        K.op(pe, lambda e: e.matmul(bY[0][:, 128:256], lhsT=wup[64:128, 128 * p:128 * p + 128],
                                                    rhs=wadT[64:128, t0:t0 + 128], start=True, stop=True),
                             r=[wupb, wadTb], w=[bY[1]])
                        yield True
                        K.op(act, lambda e: e.copy(out=rf[0][:], in_=bX[0][:, 0:128]), r=[bX[1]], w=[rf[1]])
                        K.op(act, lambda e: e.copy(out=kf[0][:], in_=bX[0][:, 128:256]), r=[bX[1]], w=[kf[1]])
                        sigmoid_chain(C, bX[0][:, 256:384], bX[1], None, gs)
                        K.op(dve, lambda e: e.tensor_tensor(out=S.sg[0][:], in0=bX[0][:, 256:384], in1=gs[0][:], op=ALU.mult),
                             r=[bX[1], gs[1]], w=[S.sg[1]])
                        K.op(dve, lambda e: e.tensor_copy(out=S.vtok[0][:], in_=bX[0][:, 384:512]), r=[bX[1]], w=[S.vtok[1]])
                        yield True
                        sigmoid_chain(C, bY[0][:, 0:128], bY[1], vcol(6), sgw, extra_r=[vecb])
                        sigmoid_chain(C, bY[0][:, 128:256], bY[1], vcol(7), av, extra_r=[vecb])
                        K.op(dve, lambda e: e.tensor_scalar(out=lw[0][:], in0=sgw[0][:], scalar1=-DECAY_SCALE, scalar2=None,
                                                            op0=ALU.mult), r=[sgw[1]], w=[lw[1]])
                        K.op(dve, lambda e: e.tensor_tensor_scan(out=cm[0][:], data0=ones_f[:], data1=lw[0][:], initial=0.0,
                                                                 op0=ALU.mult, op1=ALU.add), r=[lw[1], constb], w=[cm[1]])
                        K.op(dve, lambda e: e.tensor_tensor(out=cmx[0][:], in0=cm[0][:], in1=lw[0][:], op=ALU.subtract),
                             r=[cm[1], lw[1]], w=[cmx[1]])
                        yield True
                        K.op(act, lambda e: e.activation(out=E1[0][:], in_=cm[0][:], func=AF.Exp), r=[cm[1]], w=[E1[1]])
                        K.op(act, lambda e: e.activation(out=E2[0][:], in_=cmx[0][:], func=AF.Exp), r=[cmx[1]], w=[E2[1]])
                        K.op(act, lambda e: e.activation(out=E3[0][:], in_=cm[0][:], func=AF.Exp, scale=-1.0),
                             r=[cm[1]], w=[E3[1]])
                        K.op(act, lambda e: e.activation(out=S.GC[0][:, 0:1], in_=cm[0][:, 127:128], func=AF.Exp),
                             r=[cm[1]], w=[S.GC[1]])
                        K.op(dve, lambda e: e.tensor_scalar(out=kkr[0][:], in0=kf[0][:], scalar1=vcol(2), scalar2=None,
                                                            op0=ALU.mult), r=[kf[1], vecb], w=[kkr[1]])
                        K.op(dve, lambda e: e.tensor_tensor(out=sq[0][:], in0=kkr[0][:], in1=kkr[0][:], op=ALU.mult),
                             r=[kkr[1]], w=[sq[1]])
                        K.op(pe, lambda e: e.matmul(bY[0][:, 256:384], lhsT=bones[:], rhs=sq[0][:], start=True, stop=True),
                             r=[cb2, sq[1]], w=[bY[1]])
                        yield True
                        K.op(dve, lambda e: e.tensor_scalar(out=hsn[0][:], in0=bY[0][:, 256:384], scalar1=1e-24, scalar2=None,
                                                            op0=ALU.max), r=[bY[1]], w=[hsn[1]])
                        K.op(act, lambda e: e.activation(out=hsn[0][:], in_=hsn[0][:], func=AF.Ln), r=[hsn[1]], w=[hsn[1]])
                        K.op(act, lambda e: e.activation(out=hsn[0][:], in_=hsn[0][:], func=AF.Exp, scale=-0.5),
                             r=[hsn[1]], w=[hsn[1]])
                        K.op(dve, lambda e: e.tensor_scalar(out=t1[0][:], in0=av[0][:], scalar1=vcol(3), scalar2=vcol(5),
                                                            op0=ALU.mult, op1=ALU.add), r=[av[1], vecb], w=[t1[1]])
                        K.op(dve, lambda e: e.tensor_tensor(out=kp[0][:], in0=kf[0][:], in1=t1[0][:], op=ALU.mult),
                             r=[kf[1], t1[1]], w=[kp[1]])
                        K.op(dve, lambda e: e.tensor_tensor(out=AR[0][:, 1, :], in0=rf[0][:], in1=E1[0][:], op=ALU.mult),
                             r=[rf[1], E1[1]], w=[AR[1]])
                        K.op(dve, lambda e: e.tensor_tensor(out=ke3[0][:], in0=kp[0][:], in1=E3[0][:], op=ALU.mult),
                             r=[kp[1], E3[1]], w=[ke3[1]])
                        yield True
                        K.op(dve, lambda e: e.tensor_tensor(out=kk[0][:], in0=kkr[0][:], in1=hsn[0][:], op=ALU.mult),
                             r=[kkr[1], hsn[1]], w=[kk[1]])
                        K.op(dve, lambda e: e.tensor_tensor(out=bb[0][:], in0=kk[0][:], in1=av[0][:], op=ALU.mult),
                             r=[kk[1], av[1]], w=[bb[1]])
                        K.op(dve, lambda e: e.scalar_tensor_tensor(out=AR[0][:, 0, :], in0=kk[0][:], scalar=-1.0, in1=E2[0][:],
                                                                   op0=ALU.mult, op1=ALU.mult), r=[kk[1], E2[1]], w=[AR[1]])
                        K.op(dve, lambda e: e.tensor_tensor(out=be3[0][:], in0=bb[0][:], in1=E3[0][:], op=ALU.mult),
                             r=[bb[1], E3[1]], w=[be3[1]])
                        K.op(act, lambda e: e.copy(out=BK[0][:, 1, :], in_=ke3[0][:]), r=[ke3[1]], w=[BK[1]])
                        K.op(act, lambda e: e.copy(out=BK[0][:, 0, :], in_=be3[0][:]), r=[be3[1]], w=[BK[1]])
                        yield True
                        K.op(act, lambda e: e.copy(out=S.ARz[0][0:64, 0, :, :], in_=AR[0][0:64, :, :]), r=[AR[1]], w=[S.ARz[1]])
                        K.op(act, lambda e: e.copy(out=S.ARz[0][64:128, 1, :, :], in_=AR[0][64:128, :, :]), r=[AR[1]], w=[S.ARz[1]])
                        K.op(act, lambda e: e.copy(out=S.BKz[0][0:64, 0, :], in_=BK[0][0:64, 0, :]), r=[BK[1]], w=[S.BKz[1]])
                        K.op(act, lambda e: e.copy(out=S.BKz[0][64:128, 1, :], in_=BK[0][64:128, 0, :]), r=[BK[1]], w=[S.BKz[1]])
                        K.op(dve, lambda e: e.tensor_scalar(out=C.khT[0][:], in0=ke3[0][:], scalar1=S.GC[0][:, 0:1], scalar2=None,
                                                            op0=ALU.mult), r=[ke3[1], S.GC[1]], w=[C.khT[1]])
                        K.op(dve, lambda e: e.tensor_scalar(out=C.bhT[0][:], in0=be3[0][:], scalar1=S.GC[0][:, 0:1], scalar2=None,
                                                            op0=ALU.mult), r=[be3[1], S.GC[1]], w=[C.bhT[1]])
                        K.op(dve, lambda e: e.scalar_tensor_tensor(out=C.rkb[0][:], in0=rf[0][:], scalar=vcol(4), in1=kp[0][:],
                                                                   op0=ALU.mult, op1=ALU.mult), r=[rf[1], kp[1], vecb], w=[C.rkb[1]])
                        K.op(pe, lambda e: e.matmul(bY[0][:, 384:386], lhsT=C.rkb[0][:], rhs=hind[:], start=True, stop=True),
                             r=[C.rkb[1], cb2], w=[bY[1]])
                        K.op(pe, lambda e: e.transpose(out=psT[0][:, 4 + 2 * C.ci, :], in_=C.khT[0][:], identity=ident[:]),
                             r=[C.khT[1], identb], w=[psT[1]], inc=False)
                        K.op(pe, lambda e: e.transpose(out=psT[0][:, 5 + 2 * C.ci, :], in_=C.bhT[0][:], identity=ident[:]),
                             r=[C.bhT[1], identb], w=[psT[1]])
                        yield True
                        K.op(dve, lambda e: e.tensor_copy(out=S.bon[0][:], in_=bY[0][:, 384:386]), r=[bY[1]], w=[S.bon[1]])
                        K.op(act, lambda e: e.copy(out=S.kbhat[0][:], in_=psT[0][:, 4 + 2 * C.ci:6 + 2 * C.ci, :]), r=[psT[1]], w=[S.kbhat[1]])
                        arz = S.ARz[0][:].rearrange("p h s t -> p (h s t)")
                        K.op(pe, lambda e: e.matmul(bX[0][:, 0:512], lhsT=BK[0][:, 0, :], rhs=arz, start=True, stop=True),
                             r=[BK[1], S.ARz[1]], w=[bX[1]])
                        K.op(pe, lambda e: e.matmul(bY[0][:, 0:512], lhsT=BK[0][:, 1, :], rhs=arz, start=True, stop=True),
                             r=[BK[1], S.ARz[1]], w=[bY[1]])
                        yield True
                        P0 = C.PQ0
                        K.op(dve, lambda e: e.tensor_tensor(out=MB[0][:].rearrange("p h c -> p (h c)"), in0=bX[0][:, 0:512],
                                                            in1=msc[:].rearrange("p h c -> p (h c)"), op=ALU.mult),
                             r=[bX[1], cb2], w=[MB[1]])
                        K.op(pe, lambda e: e.matmul(bX[0][:, 0:256], lhsT=AR[0][:, 0, :], rhs=S.BKz[0][:].rearrange("p h s -> p (h s)"),
                                                    start=True, stop=True), r=[AR[1], S.BKz[1]], w=[bX[1]])
                        K.op(dve, lambda e: e.tensor_tensor(out=MK[0][:].rearrange("p h c -> p (h c)"), in0=bY[0][:, 0:512],
                                                            in1=msc[:].rearrange("p h c -> p (h c)"), op=ALU.mult),
                             r=[bY[1], cb2], w=[MK[1]])
                        yield True
                        K.op(dve, lambda e: e.tensor_tensor(out=P0[0][:, :, 0:128], in0=bX[0][:, 0:256].rearrange("p (h s) -> p h s", h=2),
                                                            in1=mlow2[:], op=ALU.mult), r=[bX[1], cb2], w=[P0[1]])
                        K.op(act, lambda e: e.copy(out=P0[0][:, :, 128:256], in_=MB[0][:, :, 0:128]), r=[MB[1]], w=[P0[1]])
                        yield True
                        for stp_ in range(1, 8):
                            prev = C.PQ0 if stp_ == 1 else C.PQT[(stp_ - 1) % 2]
                            cur = C.PQT[stp_ % 2]
                            last = (stp_ == 7)
                            for hd in range(2):
                                bk = bX if hd == 0 else bY
                                Pm = prev[0][:, hd, 0:128]
                                Qm = prev[0][:, hd, 128:256]
                                if not last:
                                    K.op(pe, lambda e: e.matmul(bk[0][:, 0:128], lhsT=Qm, rhs=Pm, start=True, stop=True),
                                         r=[prev[1]], w=[bk[1]], inc=False)
                                    K.op(pe, lambda e: e.matmul(bk[0][:, 128:384], lhsT=Pm, rhs=prev[0][:, hd, 128:384], start=True, stop=False),
                                         r=[prev[1]], w=[bk[1]], inc=False)
                                else:
                                    K.op(pe, lambda e: e.matmul(bk[0][:, 256:384], lhsT=Pm, rhs=prev[0][:, hd, 256:384], start=True, stop=False),
                                         r=[prev[1]], w=[bk[1]], inc=False)
                                K.op(pe, lambda e: e.matmul(bk[0][:, 256:384], lhsT=ident[:], rhs=prev[0][:, hd, 256:384], start=False, stop=True),
                                     r=[prev[1], identb], w=[bk[1]])
                            yield True
                            for hd in range(2):
                                bk = bX if hd == 0 else bY
                                if not last:
                                    dst_, src_ = cur[0][:, hd, :], bk[0][:, 0:384]
                                    dstb_ = cur[1]
                                else:
                                    dst_, src_ = S.TT[0][:, hd, :], bk[0][:, 256:384]
                                    dstb_ = S.TT[1]
                                if hd == 0:
                                    K.op(act, lambda e: e.copy(out=dst_, in_=src_), r=[bk[1]], w=[dstb_])
                                else:
                                    K.op(dve, lambda e: e.tensor_copy(out=dst_, in_=src_), r=[bk[1]], w=[dstb_])
                            yield True
                        C.front_done = i + 1
                        yield True

                def back(C, p):
                    bZ = C.bZ
                    Hs, Hb, Hsb, Hbb = C.Hs, C.Hb, C.Hsb, C.Hbb
                    Xs, Us, ytile, yn, ybf = C.Xs, C.Us, C.ytile, C.yn, C.ybf
                    for i in range(NT):
                        while C.front_done < i + 1:
                            yield False
                        S = C.sets[i % 2]
                        AR, MB, MK, vtok, kbhat, GC, Tf = S.AR, S.MB, S.MK, S.vtok, S.kbhat, S.GC, S.TT
                        t0 = 128 * i
                        hc = lambda hd: slice(64 * hd, 64 * hd + 64)
                        K.op(pe, lambda e: e.matmul(bZ[0][:, 0:128], lhsT=AR[0][:, 0, :], rhs=Hb[:], start=True, stop=False, skip_group_check=True),
                             r=[AR[1], Hbb], w=[bZ[1]], inc=False)
                        for hd in range(2):
                            K.op(pe, lambda e: e.matmul(bZ[0][:, hc(hd)], lhsT=MK[0][:, hd, 0:128], rhs=vtok[0][:, hc(hd)],
                                                        start=False, stop=(hd == 1), skip_group_check=True),
                                 r=[MK[1], vtok[1]], w=[bZ[1]], inc=(hd == 1))
                        yield True
                        K.op(act, lambda e: e.copy(out=Xs[0][:], in_=bZ[0][:, 0:128]), r=[bZ[1]], w=[Xs[1]])
                        yield True
                        for hd in range(2):
                            K.op(pe, lambda e: e.matmul(bZ[0][:, 128 + 64 * hd:128 + 64 * hd + 64], lhsT=Tf[0][:, hd, :], rhs=Xs[0][:, hc(hd)],
                                                        start=True, stop=True), r=[Tf[1], Xs[1]], w=[bZ[1]], inc=(hd == 1))
                        yield True
                        K.op(dve, lambda e: e.tensor_copy(out=Us[0][:], in_=bZ[0][:, 128:256]), r=[bZ[1]], w=[Us[1]])
                        yield True
                        K.op(pe, lambda e: e.matmul(bZ[0][:, 256:384], lhsT=AR[0][:, 1, :], rhs=Hb[:], start=True, stop=False, skip_group_check=True),
                             r=[AR[1], Hbb], w=[bZ[1]], inc=False)
                        for hd in range(2):
                            o_ = bZ[0][:, 256 + 64 * hd:256 + 64 * hd + 64]
                            K.op(pe, lambda e: e.matmul(o_, lhsT=MB[0][:, hd, 128:256], rhs=Us[0][:, hc(hd)], start=False, stop=False,
                                                        skip_group_check=True), r=[MB[1], Us[1]], w=[bZ[1]], inc=False)
                            K.op(pe, lambda e: e.matmul(o_, lhsT=MK[0][:, hd, 128:256], rhs=vtok[0][:, hc(hd)], start=False, stop=(hd == 1),
                                                        skip_group_check=True), r=[MK[1], vtok[1]], w=[bZ[1]], inc=False)
                        for hd in range(2):
                            o2 = bZ[0][:, 384 + 64 * hd:384 + 64 * hd + 64]
                            K.op(pe, lambda e: e.matmul(o2, lhsT=kbhat[0][:, 1, :], rhs=Us[0][:, hc(hd)], start=True, stop=False),
                                 r=[kbhat[1], Us[1]], w=[bZ[1]], inc=False)
                            K.op(pe, lambda e: e.matmul(o2, lhsT=kbhat[0][:, 0, :], rhs=vtok[0][:, hc(hd)], start=False, stop=True),
                                 r=[kbhat[1], vtok[1]], w=[bZ[1]], inc=(hd == 1))
                        yield True
                        for hd in range(2):
                            lo = 64 * hd
                            K.op(dve, lambda e: e.scalar_tensor_tensor(out=Hs[lo:lo + 64, hc(hd)], in0=Hs[lo:lo + 64, hc(hd)], scalar=GC[0][lo:lo + 64, 0:1],
                                                                       in1=bZ[0][lo:lo + 64, 384 + 64 * hd:384 + 64 * hd + 64],
                                                                       op0=ALU.mult, op1=ALU.add), r=[Hsb, GC[1], bZ[1]], w=[Hsb])
                        K.op(dve, lambda e: e.tensor_copy(out=Hb[:], in_=Hs[:]), r=[Hsb], w=[Hbb])
                        K.op(dve, lambda e: e.tensor_copy(out=ytile[0][:], in_=bZ[0][:, 256:384]), r=[bZ[1]], w=[ytile[1]])
                        yield True
                        for hf in range(2):
                            K.op(dve, lambda e: e.bn_stats(out=C.st6[0][:, hf, :], in_=ytile[0][:, 64 * hf:64 * hf + 64]), r=[ytile[1]], w=[C.st6[1]])
                            K.op(dve, lambda e: e.bn_aggr(out=C.mv[0][:, hf, :], in_=C.st6[0][:, hf, :]), r=[C.st6[1]], w=[C.mv[1]])
                        yield True
                        K.op(act, lambda e: e.activation(out=C.rstd[0][:], in_=C.mv[0][:, :, 1], func=AF.Ln, bias=epsb[:, 2:3], scale=1.0),
                             r=[C.mv[1], ssb], w=[C.rstd[1]])
                        K.op(act, lambda e: e.activation(out=C.rstd[0][:], in_=C.rstd[0][:], func=AF.Exp, scale=-0.5), r=[C.rstd[1]], w=[C.rstd[1]])
                        yield True
                        for hf in range(2):
                            cs = slice(64 * hf, 64 * hf + 64)
                            K.op(dve, lambda e: e.tensor_scalar(out=yn[0][:, cs], in0=ytile[0][:, cs], scalar1=C.mv[0][:, hf, 0:1],
                                                                scalar2=C.rstd[0][:, hf:hf + 1], op0=ALU.subtract, op1=ALU.mult),
                                 r=[ytile[1], C.mv[1], C.rstd[1]], w=[yn[1]])
                        K.op(dve, lambda e: e.tensor_tensor(out=yn[0][:], in0=yn[0][:], in1=C.lnw[:], op=ALU.mult),
                             r=[yn[1], C.lnbuf], w=[yn[1]])
                        K.op(dve, lambda e: e.tensor_tensor(out=yn[0][:], in0=yn[0][:], in1=C.lnb[:], op=ALU.add),
                             r=[yn[1], C.lnbuf], w=[yn[1]])
                        for hf in range(2):
                            cs = slice(64 * hf, 64 * hf + 64)
                            K.op(dve, lambda e: e.scalar_tensor_tensor(out=ybf[0][:, cs], in0=vtok[0][:, cs], scalar=S.bon[0][:, hf:hf + 1],
                                                                       in1=yn[0][:, cs], op0=ALU.mult, op1=ALU.add),
                                 r=[vtok[1], S.bon[1], yn[1]], w=[ybf[1]])
                        yield True
                        K.op(pe, lambda e: e.transpose(out=psT[0][:, 2 + C.ci, :], in_=ybf[0][:], identity=ident[:]), r=[ybf[1], identb], w=[psT[1]])
                        yield True
                        K.op(dve, lambda e: e.tensor_tensor(out=ogT[:, p, t0:t0 + 128], in0=psT[0][:, 2 + C.ci, :], in1=S.sg[0][:], op=ALU.mult),
                             r=[psT[1], S.sg[1]], w=[ogTb])
                        C.back_done = i + 1
                        yield True

                def pair_stream(C):
                    for p in range(C.ci, 8, 2):
                        for t_ in range(4):
                            load_w(1024 * t_ + 128 * p, C.wcp[t_])
                            yield True
                        K.dma(sp, C.lnw[:], rwkv_ln_w[j, 128 * p:128 * p + 128].partition_broadcast(128), w=[C.lnbuf])
                        K.dma(sp, C.lnb[:], rwkv_ln_b[j, 128 * p:128 * p + 128].partition_broadcast(128), w=[C.lnbuf])
                        K.op(dve, lambda e: e.memset(C.Hs[:], 0.0), w=[C.Hsb])
                        K.op(dve, lambda e: e.memset(C.Hb[:], 0.0), w=[C.Hbb])
                        C.front_done = 0
                        C.back_done = 0
                        gens = [front(C, p), back(C, p)]
                        while gens:
                            progressed = False
                            for g_ in list(gens):
                                try:
                                    if next(g_):
                                        progressed = True
                                except StopIteration:
                                    gens.remove(g_)
                                    progressed = True
                            yield progressed

                streams = [pair_stream(C) for C in ctxs]
                while streams:
                    for s_ in list(streams):
                        try:
                            next(s_)
                        except StopIteration:
                            streams.remove(s_)
                K.barrier()


        for layer in range(nlayers):
            phase_prenorm(layer)
            if layer % 2 == 0:
                fox_layer(layer)
                phase_post(layer, fox_w_out[layer // 2], layer == nlayers - 1)
            else:
                rwkv_layer(layer)
                phase_post(layer, rwkv_w_out[layer // 2], layer == nlayers - 1)
            K.barrier()
        K.barrier()
    return nc


_CACHE = {}


def _consts():
    idx = np.arange(128)
    tri = (idx[:, None] <= idx[None, :]).astype(np.float32)
    ident = np.eye(128, dtype=np.float32)
    ones = np.ones((128, 128), np.float32)
    m64 = np.zeros((128, 128), np.float32)
    s = idx[:, None] % 64
    t = idx[None, :] % 64
    m64[:, 0:64] = (s < t)[:, 0:64]
    m64[:, 64:128] = (s <= t)[:, 64:128]
    scan = np.ones((128, 512), np.float32)
    scan[:, ::64] = 0.0
    hind = np.zeros((128, 2), np.float32)
    hind[0:64, 0] = 1.0
    hind[64:128, 1] = 1.0
    m64x2 = np.concatenate([m64, m64], axis=1)
    r64 = idx[:, None] % 64
    c64 = np.arange(64)[None, :]
    mlow = (c64 < r64).astype(np.float32)
    mlow2 = np.concatenate([mlow, mlow], axis=1)
    i64 = (c64 == r64).astype(np.float32)
    idx2 = np.concatenate([i64, i64], axis=1)
    bones = ((idx[:, None] // 64) == (idx[None, :] // 64)).astype(np.float32)
    strict = (idx[:, None] < idx[None, :]).astype(np.float32)
    msc1 = np.concatenate([strict, tri], axis=1)
    msc = np.concatenate([msc1, msc1], axis=1)
    mlow128 = (idx[None, :] < idx[:, None]).astype(np.float32)
    return dict(c_ident=ident, c_tri=tri, c_ones=ones, c_m64=m64, c_scan=scan, c_hind=hind,
                c_m64x2=m64x2, c_mlow2=mlow2, c_idx2=idx2, c_bones=bones,
                c_msc=msc, c_mlow128=np.concatenate([mlow128, mlow128], axis=1),
                c_id2=np.concatenate([ident, ident], axis=1))


def kernel(x, meta_tokens, norm_pre, norm_post, fox_w_in, fox_b_f, fox_w_out,
           rwkv_w_in, rwkv_mu, rwkv_w0, rwkv_w_up, rwkv_a0, rwkv_a_up, rwkv_k_k,
           rwkv_k_a, rwkv_r_k, rwkv_ln_w, rwkv_ln_b, rwkv_w_out, _nlayers=DEPTH):
    f = lambda a: np.ascontiguousarray(np.asarray(a, dtype=np.float32))
    x = f(x)
    B = x.shape[0]
    meta = f(meta_tokens)
    h0 = np.zeros((B, T, D), np.float32)
    h0[:, :NMETA] = meta[None]
    h0[:, NMETA:NMETA + SEQ] = x
    shared = dict(
        norm_pre=f(norm_pre), norm_post=f(norm_post), fox_w_in=f(fox_w_in), fox_b_f=f(fox_b_f),
        fox_w_out=f(fox_w_out), rwkv_w_in=f(rwkv_w_in), rwkv_mu=f(rwkv_mu), rwkv_w0=f(rwkv_w0),
        rwkv_w_up=f(rwkv_w_up), rwkv_a0=f(rwkv_a0), rwkv_a_up=f(rwkv_a_up), rwkv_k_k=f(rwkv_k_k),
        rwkv_k_a=f(rwkv_k_a), rwkv_r_k=f(rwkv_r_k).reshape(2, D), rwkv_ln_w=f(rwkv_ln_w),
        rwkv_ln_b=f(rwkv_ln_b), rwkv_w_out=f(rwkv_w_out))
    shared.update(_consts())
    key = _nlayers
    if key not in _CACHE:
        _CACHE[key] = build_program(_nlayers)
    nc = _CACHE[key]
    in_maps = []
    for b in range(B):
        m = dict(shared)
        m["h0"] = h0[b]
        in_maps.append(m)
    res = run_bass_kernel_spmd(nc, in_maps, core_ids=list(range(B)))
    out = np.stack([np.asarray(r["y"])[NMETA:NMETA + SEQ] for r in res.results], axis=0)
    return out.astype(np.float32)
```

```python
import math
from contextlib import ExitStack

import numpy as np
import concourse.bass as bass
import concourse.mybir as mybir
from concourse.bass_utils import run_bass_kernel_spmd

F32 = mybir.dt.float32
BF16 = mybir.dt.bfloat16
AF = mybir.ActivationFunctionType
ALU = mybir.AluOpType
AX = mybir.AxisListType

D = 1024
SEQ = 2048
NMETA = 16
NT = 17
T = NT * 128
DEPTH = 4
NH = 16
HD = 64
FOX_IN = 4 * D + NH
RWKV_IN = 4 * D + 128
NORM_EPS = 1e-6
GN_EPS = 64e-5
DECAY_SCALE = math.exp(-0.5)
CH = 64
DBG = {"stage": 9, "pairs": 8, "tiles": NT}


def tok_chunks(n=512):
    out = []
    t0 = 0
    while t0 < T:
        m = min(n, T - t0)
        out.append((t0, m))
        t0 += m
    return out


class Buf:
    __slots__ = ("name", "lw", "rd", "excl")

    def __init__(self, name="", excl=False):
        self.name = name
        self.lw = None
        self.rd = {}
        self.excl = excl


class Q:
    def __init__(self, name, eng, sem, self_sync=True):
        self.name = name
        self.eng = eng
        self.sem = sem
        self.cnt = 0
        self.seen = {}
        self.self_sync = self_sync
        self.key = name
        self.ring = []
        self.ring_i = 0


class PEProxy:
    def __init__(self, K, eng):
        self.K = K
        self.eng = eng
        self.partial = False

    def _pre(self, st_ap, out):
        K = self.K
        rg = (st_ap.base_partition(), st_ap.partition_size())
        bank = out.name
        last = K.pe_last
        if last is not None and last[0] != rg and (last[1] == bank or (last[0][1] < 128 and rg[1] < 128)):
            assert K.pe_last_tk is not None, "previous matmul needs a semaphore increment"
            K._wait(K.pe, K.pe_last_tk)
        K.pe_last = (rg, bank)
        self.partial = rg[1] < 128

    def matmul(self, out, lhsT, rhs, **kw):
        self._pre(lhsT, out)
        return self.eng.matmul(out, lhsT=lhsT, rhs=rhs, **kw)

    def transpose(self, out, in_, identity):
        self._pre(in_, out)
        return self.eng.transpose(out=out, in_=in_, identity=identity)


class KB:
    def __init__(self, nc, st):
        self.nc = nc
        self.st = st
        mk = lambda n: st.enter_context(nc.semaphore(n))
        self.pe = Q("pe", nc.tensor, mk("s_pe"), self_sync=False)
        self.act = Q("act", nc.scalar, mk("s_act"))
        self.dve = Q("dve", nc.vector, mk("s_dve"))
        self.pool = Q("pool", nc.gpsimd, mk("s_pool"))
        self.sp = Q("sp", nc.sync, mk("s_sp"))
        self.queues = [self.pe, self.act, self.dve, self.pool, self.sp]
        for q, n in ((self.sp, 24), (self.pool, 24)):
            for i in range(n):
                q.ring.append([mk(f"d_{q.name}{i}"), 0, f"d_{q.name}{i}"])
        self.dma_tickets = []
        self.nbuf = 0
        self.counting = False
        self.nops = 0
        self.pe_last = None
        self.pe_last_tk = None
        self.prox = PEProxy(self, nc.tensor)

    def sb(self, st, name, shape, dt):
        self.nbuf += 1
        return st.enter_context(self.nc.sbuf_tensor(f"{name}_{self.nbuf}", list(shape), dt))

    def psum(self, st, name, shape, dt):
        return st.enter_context(self.nc.psum_tensor(name, list(shape), dt))

    def _wait(self, q, tk):
        sem, val, key = tk
        if q.seen.get(key, 0) >= val:
            return
        q.eng.wait_ge(sem, val)
        q.seen[key] = val

    def _deps(self, q, r, w, defer=False):
        deps = []
        for b in r:
            if b.lw is not None:
                deps.append(b.lw)
            if b.excl:
                for key, tk in b.rd.items():
                    if key != q.key:
                        deps.append(tk)
        for b in w:
            if b.lw is not None:
                deps.append(b.lw)
            for key, tk in b.rd.items():
                deps.append(tk)
        need = {}
        for tk in deps:
            if tk[2] == q.key and not q.self_sync:
                continue
            if q.seen.get(tk[2], 0) >= tk[1]:
                continue
            if tk[2] not in need or need[tk[2]][1] < tk[1]:
                need[tk[2]] = tk
        pend = list(need.values())
        last = pend.pop() if (defer and pend) else None
        for tk in pend:
            self._wait(q, tk)
        return last

    def _record(self, tk, r, w):
        for b in r:
            old = b.rd.get(tk[2])
            if old is None or old[1] < tk[1]:
                b.rd[tk[2]] = tk
        for b in w:
            b.lw = tk
            b.rd = {}

    def op(self, q, fn, r=(), w=(), inc=True):
        if self.counting:
            self.nops += 1
            if self.nops > DBG.get("maxops", 10 ** 9):
                return None
        last = self._deps(q, r, w, defer=(q is not self.pe))
        if q is self.pe:
            self.prox.partial = False
            ins = fn(self.prox)
            if self.prox.partial:
                inc = True
        else:
            ins = fn(q.eng)
            if last is not None:
                ins._wait_ge(last[0], last[1])
                q.seen[last[2]] = last[1]
        if inc:
            q.cnt += 1
            ins.then_inc(q.sem, 1)
            tk = (q.sem, q.cnt, q.key)
        else:
            tk = (q.sem, q.cnt + 1, q.key)
        if q is self.pe:
            self.pe_last_tk = tk if inc else None
        self._record(tk, r, w)
        return tk

    def dma(self, q, out, in_, r=(), w=()):
        self._deps(q, r, w)
        slot = q.ring[q.ring_i % len(q.ring)]
        q.ring_i += 1
        sem, n, key = slot
        if n > 0:
            self._wait(q, (sem, 16 * n, key))
        q.eng.dma_start(out=out, in_=in_).then_inc(sem, 16)
        slot[1] = n + 1
        tk = (sem, 16 * (n + 1), key)
        self._record(tk, r, w)
        self.dma_tickets.append(tk)
        return tk

    def barrier(self):
        tks = [(q.sem, q.cnt, q.key) for q in self.queues if q.cnt > 0]
        for q in self.queues:
            for slot in q.ring:
                if slot[1] > 0:
                    tks.append((slot[0], 16 * slot[1], slot[2]))
        for q in self.queues:
            for tk in tks:
                if tk[2] == q.key:
                    continue
                self._wait(q, tk)


def build_program(nlayers=DEPTH, dbg=False):
    nc = bass.Bass("TRN2", target_bir_lowering=False)
    dt_in = lambda name, shape: nc.dram_tensor(name, list(shape), F32, kind="ExternalInput").ap()
    h0 = dt_in("h0", [T, D])
    norm_pre = dt_in("norm_pre", [DEPTH, D])
    norm_post = dt_in("norm_post", [DEPTH, D])
    fox_w_in = dt_in("fox_w_in", [2, D, FOX_IN])
    fox_b_f = dt_in("fox_b_f", [2, NH])
    fox_w_out = dt_in("fox_w_out", [2, D, D])
    rwkv_w_in = dt_in("rwkv_w_in", [2, D, RWKV_IN])
    rwkv_mu = dt_in("rwkv_mu", [2, RWKV_IN])
    rwkv_w0 = dt_in("rwkv_w0", [2, D])
    rwkv_w_up = dt_in("rwkv_w_up", [2, 64, D])
    rwkv_a0 = dt_in("rwkv_a0", [2, D])
    rwkv_a_up = dt_in("rwkv_a_up", [2, 64, D])
    rwkv_k_k = dt_in("rwkv_k_k", [2, D])
    rwkv_k_a = dt_in("rwkv_k_a", [2, D])
    rwkv_r_k = dt_in("rwkv_r_k", [2, D])
    rwkv_ln_w = dt_in("rwkv_ln_w", [2, D])
    rwkv_ln_b = dt_in("rwkv_ln_b", [2, D])
    rwkv_w_out = dt_in("rwkv_w_out", [2, D, D])
    c_ident = dt_in("c_ident", [128, 128])
    c_tri = dt_in("c_tri", [128, 128])
    c_ones = dt_in("c_ones", [128, 128])
    c_m64 = dt_in("c_m64", [128, 128])
    c_scan = dt_in("c_scan", [128, 512])
    c_hind = dt_in("c_hind", [128, 2])
    c_m64x2 = dt_in("c_m64x2", [128, 256])
    c_mlow2 = dt_in("c_mlow2", [128, 128])
    c_idx2 = dt_in("c_idx2", [128, 128])
    c_bones = dt_in("c_bones", [128, 128])
    c_msc = dt_in("c_msc", [128, 512])
    c_mlow128 = dt_in("c_mlow128", [128, 256])
    c_id2 = dt_in("c_id2", [128, 256])
    y = nc.dram_tensor("y", [T, D], F32, kind="ExternalOutput").ap()

    with ExitStack() as st:
        K = KB(nc, st)
        pe, act, dve, pool, sp = K.pe, K.act, K.dve, K.pool, K.sp

        uT = K.sb(st, "uT", [128, 8, T + 2], BF16)
        uTb = Buf("uT")
        ogT = K.sb(st, "ogT", [128, 8, T], BF16)
        ogTb = Buf("ogT")
        ident = K.sb(st, "ident", [128, 128], BF16)
        identb = Buf()
        tri_f = K.sb(st, "tri_f", [128, 128], F32)
        ones_f = K.sb(st, "ones_f", [128, 128], F32)
        tri_b = K.sb(st, "tri_b", [128, 128], BF16)
        constb = Buf()
        gpre = K.sb(st, "gpre", [128, D], F32)
        gpost = K.sb(st, "gpost", [128, D], F32)
        gb = Buf()
        hbufs = [(K.sb(st, f"hb{i}", [128, D], F32), Buf()) for i in range(2)]
        hbufs2 = hbufs
        un = K.sb(st, "un", [128, D], BF16)
        unb = Buf()
        un2 = K.sb(st, "un2", [128, D], BF16)
        uns = [(un, unb), (un2, Buf())]
        sss = [(K.sb(st, f"ss{i}", [128, 4], F32), Buf()) for i in range(2)]
        mt = K.sb(st, "mt", [128, D], F32)
        mtb = Buf()
        ss = K.sb(st, "ss", [128, 4], F32)
        ssb = Buf()
        epsb = K.sb(st, "epsb", [128, 4], F32)
        K.op(dve, lambda e: e.memset(epsb[:, 0:1], NORM_EPS), w=[ssb])
        K.op(dve, lambda e: e.memset(epsb[:, 1:2], 1.0), w=[ssb])
        K.op(dve, lambda e: e.memset(epsb[:, 2:3], GN_EPS), w=[ssb])
        psA = [(K.psum(st, f"psA{i}", [128, 512], F32), Buf(excl=True)) for i in range(6)]
        psT = (K.psum(st, "psT", [128, 8, 128], BF16), Buf(excl=True))
        psT2 = (K.psum(st, "psT2", [128, 8, 128], BF16), Buf(excl=True))
        psTs = [psT, psT2]
        hB = [Buf(f"h{i}") for i in range(NT)]

        K.dma(pool, ident[:], c_ident[:, :], w=[identb])
        K.dma(pool, tri_b[:], c_tri[:, :], w=[constb])
        K.dma(sp, tri_f[:], c_tri[:, :], w=[constb])
        K.dma(sp, ones_f[:], c_ones[:, :], w=[constb])
        K.op(dve, lambda e: e.memset(uT[:, :, 0:1], 0.0), w=[uTb])

        def bcast_row(ap_row, n):
            return ap_row.partition_broadcast(128)

        def phase_prenorm(layer):
            K.dma(sp, gpre[:], bcast_row(norm_pre[layer, :], D), w=[gb])
            K.dma(sp, gpost[:], bcast_row(norm_post[layer, :], D), w=[gb])
            for i in range(NT):
                ht, htb = hbufs[i % 2]
                ss_, ssb_ = sss[i % 2]
                un_, unb_ = uns[i % 2]
                pT = psTs[i % 2]
                src = h0 if layer == 0 else y
                K.dma(sp, ht[:], src[128 * i:128 * i + 128, :], r=[hB[i]], w=[htb])
                K.op(act, lambda e: e.activation(out=mt[:], in_=ht[:], func=AF.Square, accum_out=ss_[:, 0:1]),
                     r=[htb], w=[mtb, ssb_])
                K.op(act, lambda e: e.activation(out=ss_[:, 1:2], in_=ss_[:, 0:1], func=AF.Ln, bias=epsb[:, 0:1], scale=1.0 / D),
                     r=[ssb_, ssb], w=[ssb_])
                K.op(act, lambda e: e.activation(out=ss_[:, 2:3], in_=ss_[:, 1:2], func=AF.Exp, scale=-0.5),
                     r=[ssb_], w=[ssb_])
                K.op(dve, lambda e: e.scalar_tensor_tensor(out=un_[:], in0=ht[:], scalar=ss_[:, 2:3], in1=gpre[:],
                                                           op0=ALU.mult, op1=ALU.mult), r=[htb, ssb_, gb], w=[unb_])
                for c in range(8):
                    K.op(pe, lambda e: e.transpose(out=pT[0][:, c, :], in_=un_[:, 128 * c:128 * c + 128],
                                                   identity=ident[:]),
                         r=[unb_, identb], w=[pT[1]], inc=(c == 7))
                if i % 2 == 0:
                    K.op(act, lambda e: e.copy(out=uT[:, :, 1 + 128 * i:1 + 128 * i + 128], in_=pT[0][:, :, :]),
                         r=[pT[1]], w=[uTb])
                else:
                    K.op(dve, lambda e: e.tensor_copy(out=uT[:, :, 1 + 128 * i:1 + 128 * i + 128], in_=pT[0][:, :, :]),
                         r=[pT[1]], w=[uTb])

        def phase_post(layer, w_out_ap, last):
            with ExitStack() as pst:
                wout = K.sb(pst, "wout", [128, 8, D], BF16)
                woutb = Buf()
                K.dma(pool, wout[:], w_out_ap.rearrange("(c p) n -> p c n", p=128), w=[woutb])
                for i in range(NT):
                    pms = [psA[0], psA[1]] if i % 2 == 0 else [psA[2], psA[3]]
                    ss_, ssb_ = sss[i % 2]
                    for hf in range(2):
                        for c in range(8):
                            K.op(pe, lambda e: e.matmul(pms[hf][0][:, :], lhsT=ogT[:, c, 128 * i:128 * i + 128],
                                                        rhs=wout[:, c, 512 * hf:512 * hf + 512],
                                                        start=(c == 0), stop=(c == 7)),
                                 r=[ogTb, woutb], w=[pms[hf][1]], inc=(c == 7))
                    for hf in range(2):
                        K.op(act, lambda e: e.activation(out=un[:, 512 * hf:512 * hf + 512], in_=pms[hf][0][:, :], func=AF.Square,
                                                         accum_out=ss_[:, hf:hf + 1]), r=[pms[hf][1]], w=[unb, ssb_])
                    K.op(dve, lambda e: e.tensor_tensor(out=ss_[:, 2:3], in0=ss_[:, 0:1], in1=ss_[:, 1:2], op=ALU.add),
                         r=[ssb_], w=[ssb_])
                    K.op(act, lambda e: e.activation(out=ss_[:, 3:4], in_=ss_[:, 2:3], func=AF.Ln, bias=epsb[:, 0:1], scale=1.0 / D),
                         r=[ssb_, ssb], w=[ssb_])
                    K.op(act, lambda e: e.activation(out=ss_[:, 3:4], in_=ss_[:, 3:4], func=AF.Exp, scale=-0.5),
                         r=[ssb_], w=[ssb_])
                    ht, htb = hbufs2[i % 2]
                    src = h0 if layer == 0 else y
                    K.dma(sp, ht[:], src[128 * i:128 * i + 128, :], r=[hB[i]], w=[htb])
                    for hf in range(2):
                        K.op(dve, lambda e: e.scalar_tensor_tensor(out=mt[:, 512 * hf:512 * hf + 512], in0=pms[hf][0][:, :], scalar=ss_[:, 3:4],
                                                                   in1=gpost[:, 512 * hf:512 * hf + 512], op0=ALU.mult, op1=ALU.mult),
                             r=[pms[hf][1], ssb_, gb], w=[mtb])
                    K.op(dve, lambda e: e.tensor_tensor(out=ht[:], in0=ht[:], in1=mt[:], op=ALU.add),
                         r=[htb, mtb], w=[htb])
                    K.dma(sp, y[128 * i:128 * i + 128, :], ht[:], r=[htb], w=[hB[i]])
                K.barrier()

        def fox_layer(layer):
            j = layer // 2
            win = fox_w_in[j].rearrange("(c p) n -> p c n", p=128)
            with ExitStack() as ls:
                wb = [[(K.sb(ls, f"fw{s}{t}", [128, 8, 256], BF16), Buf()) for t in range(4)] for s in range(2)]
                wf = K.sb(ls, "fwf", [128, 8, 16], BF16)
                wfb = Buf()
                qT = K.sb(ls, "qaug", [128, 4, T], BF16)
                kT = K.sb(ls, "kaug", [128, 4, T], BF16)
                negcum = K.sb(ls, "negcum", [128, NT, NH], F32)
                sgT = K.sb(ls, "sgT", [128, 2, T], BF16)
                qTb, kTb, sgTb = Buf(), Buf(), Buf()
                vaug = K.sb(ls, "vaug", [128, NT, 4, 128], BF16)
                vaugb = Buf()
                bft = K.sb(ls, "bft", [128, NH], F32)
                lf = K.sb(ls, "lf", [128, NT, NH], F32)
                lfb = Buf()
                cum = K.sb(ls, "cum", [128, NT, NH], F32)
                carry = K.sb(ls, "carry", [128, NT, NH], F32)
                cumb = Buf()
                bias = [(K.sb(ls, f"bias{i}", [128, NT], F32), Buf()) for i in range(8)]
                PT = [(K.sb(ls, f"PT{i}", [128, 512], BF16), Buf()) for i in range(3)]
                rs = [(K.sb(ls, f"rs{i}", [128, 512], F32), Buf()) for i in range(2)]
                tmpo = [(K.sb(ls, f"tmpo{i}", [128, 512], F32), Buf()) for i in range(2)]

                def load_group(g, s):
                    for t in range(4):
                        K.dma(pool, wb[s][t][0][:], win[:, :, 1024 * t + 256 * g:1024 * t + 256 * g + 256],
                              w=[wb[s][t][1]])
                K.dma(pool, wf[:], win[:, :, 4096:4112], w=[wfb])
                load_group(0, 0)
                K.dma(sp, bft[:], fox_b_f[j, :].partition_broadcast(128), w=[lfb])
                K.op(dve, lambda e: e.memset(vaug[:], 1.0), w=[vaugb])

                pf = psA[4]
                for i in range(NT):
                    for c in range(8):
                        K.op(pe, lambda e: e.matmul(pf[0][:, 16 * i:16 * i + 16], lhsT=uT[:, c, 1 + 128 * i:1 + 128 * i + 128],
                                                    rhs=wf[:, c, :], start=(c == 0), stop=(c == 7)),
                             r=[uTb, wfb], w=[pf[1]], inc=(c == 7))
                for i in range(NT):
                    K.op(dve, lambda e: e.tensor_tensor(out=lf[:, i, :], in0=pf[0][:, 16 * i:16 * i + 16], in1=bft[:],
                                                        op=ALU.add), r=[pf[1], lfb], w=[lfb])
                lf2 = lf[:].rearrange("p a b -> p (a b)")
                K.op(act, lambda e: e.activation(out=lf2, in_=lf2, func=AF.Exp, scale=-1.0), r=[lfb], w=[lfb])
                K.op(act, lambda e: e.activation(out=lf2, in_=lf2, func=AF.Ln, bias=epsb[:, 1:2], scale=1.0), r=[lfb, ssb], w=[lfb])
                K.op(dve, lambda e: e.tensor_scalar(out=lf2, in0=lf2, scalar1=-1.0, scalar2=None, op0=ALU.mult),
                     r=[lfb], w=[lfb])
                pc, pl = psA[2], psA[3]
                K.op(dve, lambda e: e.memset(carry[:, 0, :], 0.0), w=[cumb])
                for i in range(NT):
                    for jj in range(i):
                        K.op(pe, lambda e: e.matmul(pc[0][:, 16 * i:16 * i + 16], lhsT=ones_f[:], rhs=lf[:, jj, :],
                                                    start=(jj == 0), stop=(jj == i - 1)),
                             r=[lfb, constb], w=[pc[1]], inc=(jj == i - 1))
                    K.op(pe, lambda e: e.matmul(pl[0][:, 16 * i:16 * i + 16], lhsT=tri_f[:], rhs=lf[:, i, :],
                                                start=True, stop=True), r=[lfb, constb], w=[pl[1]])
                K.op(dve, lambda e: e.tensor_copy(out=carry[:, 1:NT, :].rearrange("p a b -> p (a b)"),
                                                  in_=pc[0][:, 16:16 * NT]), r=[pc[1]], w=[cumb])
                K.op(dve, lambda e: e.tensor_tensor(out=cum[:].rearrange("p a b -> p (a b)"),
                                                    in0=pl[0][:, 0:16 * NT],
                                                    in1=carry[:].rearrange("p a b -> p (a b)"), op=ALU.add),
                     r=[pl[1], cumb], w=[cumb])

                K.op(dve, lambda e: e.tensor_scalar(out=negcum[:].rearrange("p a b -> p (a b)"),
                                                    in0=cum[:].rearrange("p a b -> p (a b)"), scalar1=-1.0, scalar2=None,
                                                    op0=ALU.mult), r=[cumb], w=[cumb])
                K.op(dve, lambda e: e.memset(kT[64:65, :, :], 1.0), w=[kTb])
                chunks = tok_chunks(512)
                pcount = [0]

                def nextps():
                    p = psA[pcount[0] % 2]
                    pcount[0] += 1
                    return p

                stc = [0]
                otc = [0]
                ptc = [0]
                bc = [0]
                for g in range(4):
                    s = g % 2
                    if g + 1 < 4:
                        load_group(g + 1, (g + 1) % 2)
                    wq, wk, wv, wg = [wb[s][t] for t in range(4)]
                    for pp in range(2):
                        for (t0, n) in chunks:
                            for (wt, dst, dstb, kind) in ((wq, qT, qTb, 0), (wk, kT, kTb, 1), (wg, sgT, sgTb, 2)):
                                p = nextps()
                                for c in range(8):
                                    K.op(pe, lambda e: e.matmul(p[0][:, 0:n], lhsT=wt[0][:, c, 128 * pp:128 * pp + 128],
                                                                rhs=uT[:, c, 1 + t0:1 + t0 + n],
                                                                start=(c == 0), stop=(c == 7)),
                                         r=[uTb, wt[1]], w=[p[1]], inc=(c == 7))
                                if kind == 0:
                                    for hf_ in range(2):
                                        K.op(act, lambda e: e.activation(out=dst[0:64, 2 * pp + hf_, t0:t0 + n],
                                                                         in_=p[0][64 * hf_:64 * hf_ + 64, 0:n],
                                                                         func=AF.Copy, scale=HD ** -0.5),
                                             r=[p[1]], w=[dstb])
                                elif kind == 1:
                                    for hf_ in range(2):
                                        K.op(dve, lambda e: e.tensor_copy(out=dst[0:64, 2 * pp + hf_, t0:t0 + n],
                                                                          in_=p[0][64 * hf_:64 * hf_ + 64, 0:n]),
                                             r=[p[1]], w=[dstb])
                                else:
                                    K.op(act, lambda e: e.activation(out=dst[:, pp, t0:t0 + n], in_=p[0][:, 0:n],
                                                                     func=AF.Silu), r=[p[1]], w=[dstb])
                    for hh_ in range(4):
                        for i_ in range(NT):
                            K.op(dve, lambda e: e.tensor_scalar(out=qT[64:65, hh_, 128 * i_:128 * i_ + 128], in0=ones_f[64:65, 0:128],
                                                                scalar1=carry[64:65, i_, 4 * g + hh_:4 * g + hh_ + 1], scalar2=None,
                                                                op0=ALU.mult), r=[cumb, constb], w=[qTb])
                    for i in range(NT):
                        p = nextps()
                        for c in range(8):
                            K.op(pe, lambda e: e.matmul(p[0][:, 0:256], lhsT=uT[:, c, 1 + 128 * i:1 + 128 * i + 128],
                                                        rhs=wv[0][:, c, :], start=(c == 0), stop=(c == 7)),
                                 r=[uTb, wv[1]], w=[p[1]], inc=(c == 7))
                        pv = p[0][:, 0:256].rearrange("p (a b c) -> p a b c", a=2, b=2)
                        K.op(dve, lambda e: e.tensor_copy(out=vaug[:, i, 0:4:2, 0:64], in_=pv[:, :, 0, :]),
                             r=[p[1]], w=[vaugb])
                        K.op(dve, lambda e: e.tensor_copy(out=vaug[:, i, 1:4:2, 64:128], in_=pv[:, :, 1, :]),
                             r=[p[1]], w=[vaugb])
                    items = []
                    for hh in range(4):
                        for cidx, (q0, qn) in enumerate(chunks):
                            cx = dict(hh=hh, q0=q0, qn=qn, i0=q0 // 128, ni=qn // 128, started=False)
                            cx["jmax"] = cx["i0"] + cx["ni"] - 1
                            for jk in range(cx["jmax"] + 1):
                                items.append((cx, jk))

                    def emit_st(it):
                        cx, jk = it
                        hh = cx["hh"]
                        h = 4 * g + hh
                        pp, half = hh // 2, hh % 2
                        lo = 64 * half
                        q0, qn, i0, ni = cx["q0"], cx["qn"], cx["i0"], cx["ni"]
                        if not cx["started"]:
                            cx["started"] = True
                            cx["ot"] = psA[4 + otc[0] % 2]
                            otc[0] += 1
                        qs = max(q0, 128 * jk)
                        n = q0 + qn - qs
                        stp = psA[2 + stc[0] % 2]
                        stc[0] += 1
                        K.op(pe, lambda e: e.matmul(stp[0][:, 0:n], lhsT=kT[0:65, hh, 128 * jk:128 * jk + 128],
                                                    rhs=qT[0:65, hh, qs:qs + n], start=True, stop=True),
                             r=[kTb, qTb], w=[stp[1]])
                        return stp

                    def emit_rest(it, stp):
                        cx, jk = it
                        hh = cx["hh"]
                        pp, half = hh // 2, hh % 2
                        olo, slo = (0, 64) if half == 0 else (64, 0)
                        q0, qn, i0, ni, jmax = cx["q0"], cx["qn"], cx["i0"], cx["ni"], cx["jmax"]
                        ot = cx["ot"]
                        qs = max(q0, 128 * jk)
                        n = q0 + qn - qs
                        pt = PT[ptc[0] % 3]
                        ptc[0] += 1
                        h_ = 4 * g + hh
                        K.op(act, lambda e: e.activation(out=pt[0][:, 0:n], in_=stp[0][:, 0:n], func=AF.Exp,
                                                         bias=negcum[:, jk, h_:h_ + 1], scale=1.0),
                             r=[stp[1], cumb], w=[pt[1]])
                        if jk >= i0:
                            K.op(dve, lambda e: e.tensor_tensor(out=pt[0][:, 0:128], in0=pt[0][:, 0:128],
                                                                in1=tri_b[:], op=ALU.mult),
                                 r=[pt[1], constb], w=[pt[1]])
                        K.op(pe, lambda e: e.matmul(ot[0][:, qs - q0:qs - q0 + n], lhsT=vaug[:, jk, hh, :],
                                                    rhs=pt[0][:, 0:n], start=(jk == 0), stop=(jk == jmax),
                                                    skip_group_check=True),
                             r=[vaugb, pt[1]], w=[ot[1]])
                        if jk == jmax:
                            r_ = rs[otc[0] % 2]
                            tm = tmpo[otc[0] % 2]
                            K.op(dve, lambda e: e.reciprocal(out=r_[0][olo:olo + 64, 0:qn], in_=ot[0][slo:slo + 64, 0:qn]),
                                 r=[ot[1]], w=[r_[1]])
                            K.op(dve, lambda e: e.tensor_tensor(out=tm[0][olo:olo + 64, 0:qn], in0=ot[0][olo:olo + 64, 0:qn],
                                                                in1=r_[0][olo:olo + 64, 0:qn], op=ALU.mult),
                                 r=[ot[1], r_[1]], w=[tm[1]])
                            K.op(dve, lambda e: e.tensor_tensor(out=ogT[olo:olo + 64, 2 * g + pp, q0:q0 + qn],
                                                                in0=tm[0][olo:olo + 64, 0:qn],
                                                                in1=sgT[olo:olo + 64, pp, q0:q0 + qn], op=ALU.mult),
                                 r=[tm[1], sgTb], w=[ogTb])

                    nxt = emit_st(items[0])
                    for n_ in range(len(items)):
                        cur_st = nxt
                        if n_ + 1 < len(items):
                            nxt = emit_st(items[n_ + 1])
                        emit_rest(items[n_], cur_st)
                K.barrier()


        def rwkv_layer(layer):
            j = layer // 2
            win = rwkv_w_in[j].rearrange("(c p) n -> p c n", p=128)
            with ExitStack() as ls:
                SB = lambda n, shp, dt: K.sb(ls, n, shp, dt)
                wadT = SB("wadT", [128, T], BF16); wadTb = Buf()
                wup = SB("wup", [128, D], BF16); wupb = Buf()
                vecs = SB("vecs", [128, 8, 8], F32); vecb = Buf()
                mub = (SB("mub", [128, 2, 128], F32), Buf())
                wraw = (SB("wraw", [128, 8, 128], F32), Buf())
                bones = SB("bones", [128, 128], F32)
                msc = SB("msc", [128, 2, 256], BF16)
                mlow2 = SB("mlow2", [128, 2, 128], BF16)
                id2 = SB("id2", [128, 2, 128], BF16)
                hind = SB("hind", [128, 2], BF16)
                cb2 = Buf()

                class Ctx:
                    pass

                ctxs = []
                for ci in range(2):
                    C = Ctx()
                    C.ci = ci
                    F_ = lambda n: (SB(f"{n}{ci}", [128, 128], F32), Buf())
                    B_ = lambda n, shp: (SB(f"{n}{ci}", shp, BF16), Buf())
                    for n in ("rf", "kf", "sgw", "av", "lw", "cm", "cmx", "E1", "E2", "E3", "kkr", "sq", "hsn", "kk",
                              "t1", "kp", "bb", "ke3", "be3", "gs", "ytile"):
                        setattr(C, n, F_(n))
                    C.PS = []
                    for si in range(3):
                        S = Ctx()
                        S.sg = B_(f"sg{si}", [128, 128]); S.vtok = B_(f"vtok{si}", [128, 128])
                        S.AR = B_(f"AR{si}", [128, 2, 128]); S.BK = B_(f"BK{si}", [128, 2, 128])
                        S.ARz = B_(f"ARz{si}", [128, 2, 2, 128]); S.BKz = B_(f"BKz{si}", [128, 2, 128])
                        S.kbhat = B_(f"kbhat{si}", [128, 2, 128])
                        S.GC = (SB(f"GC{ci}{si}", [128, 2], F32), Buf())
                        S.bon = (SB(f"bon{ci}{si}", [128, 2], F32), Buf())
                        K.op(dve, lambda e: e.memset(S.ARz[0][:], 0.0), w=[S.ARz[1]])
                        K.op(dve, lambda e: e.memset(S.BKz[0][:], 0.0), w=[S.BKz[1]])
                        C.PS.append(S)
                    C.IS = []
                    for si in range(2):
                        S = Ctx()
                        S.MB = B_(f"MB{si}", [128, 2, 256]); S.MK = B_(f"MK{si}", [128, 2, 256])
                        S.TT = B_(f"TTf{si}", [128, 2, 128])
                        C.IS.append(S)
                    C.zlock = None
                    C.khT = B_("khT", [128, 128]); C.bhT = B_("bhT", [128, 128]); C.rkb = B_("rkb", [128, 128])
                    C.PQT = [B_(f"PQT{i}", [128, 2, 384]) for i in range(2)]
                    C.PQ0 = B_("PQ0", [128, 2, 384])
                    C.Xs = B_("Xs", [128, 128]); C.Us = B_("Us", [128, 128]); C.ybf = B_("ybf", [128, 128])
                    C.st6 = (SB(f"st6{ci}", [128, 2, 6], F32), Buf())
                    C.mv = (SB(f"mv{ci}", [128, 2, 2], F32), Buf())
                    C.rstd = (SB(f"rstd{ci}", [128, 2], F32), Buf())
                    C.Hs = SB(f"Hs{ci}", [128, 128], F32); C.Hb = SB(f"Hb{ci}", [128, 128], BF16)
                    C.Hsb = Buf(); C.Hbb = Buf()
                    C.wcp = [(SB(f"wcp{ci}{t_}", [128, 16, 128], BF16), Buf()) for t_ in range(4)]
                    C.lnw = SB(f"lnw{ci}", [128, 128], F32); C.lnb = SB(f"lnb{ci}", [128, 128], F32); C.lnbuf = Buf()
                    C.bX, C.bY, C.bZ = psA[3 * ci], psA[3 * ci + 1], psA[3 * ci + 2]
                    C.yn = C.ytile
                    C.prep_done = 0
                    C.inv_done = 0
                    C.back_done = 0
                    ctxs.append(C)

                K.dma(sp, bones[:], c_bones[:, :], w=[cb2])
                K.dma(pool, msc[:].rearrange("p a b -> p (a b)"), c_msc[:, :], w=[cb2])
                K.dma(pool, mlow2[:].rearrange("p a b -> p (a b)"), c_mlow128[:, :], w=[cb2])
                K.dma(pool, id2[:].rearrange("p a b -> p (a b)"), c_id2[:, :], w=[cb2])
                for C in ctxs:
                    K.op(act, lambda e: e.copy(out=C.PQ0[0][:, :, 256:384], in_=id2[:]), r=[cb2], w=[C.PQ0[1]])
                K.dma(pool, hind[:], c_hind[:, :], w=[cb2])
                K.dma(pool, wup[0:64, :], rwkv_w_up[j], w=[wupb])
                K.dma(pool, wup[64:128, :], rwkv_a_up[j], w=[wupb])
                with nc.allow_non_contiguous_dma(reason="tiny per-feature vectors"):
                    for vi, src in enumerate((rwkv_w0, rwkv_a0, rwkv_k_k, rwkv_k_a, rwkv_r_k)):
                        K.dma(sp, vecs[:, vi, :], src[j, :].rearrange("(c p) -> p c", p=128), w=[vecb])
                K.op(dve, lambda e: e.tensor_scalar(out=vecs[:, 5, :], in0=vecs[:, 3, :], scalar1=-1.0, scalar2=1.0,
                                                    op0=ALU.mult, op1=ALU.add), r=[vecb], w=[vecb])
                K.op(dve, lambda e: e.tensor_scalar(out=vecs[:, 6, :], in0=vecs[:, 0, :], scalar1=-1.0, scalar2=None,
                                                    op0=ALU.mult), r=[vecb], w=[vecb])
                K.op(dve, lambda e: e.tensor_scalar(out=vecs[:, 7, :], in0=vecs[:, 1, :], scalar1=-1.0, scalar2=None,
                                                    op0=ALU.mult), r=[vecb], w=[vecb])

                def load_w(col0, dst):
                    mb, rw = mub, wraw
                    K.dma(sp, mb[0][:, 0, :], rwkv_mu[j, col0:col0 + 128].partition_broadcast(128), w=[mb[1]])
                    K.dma(sp, rw[0][:], win[:, :, col0:col0 + 128], w=[rw[1]])
                    K.op(dve, lambda e: e.tensor_scalar(out=mb[0][:, 1, :], in0=mb[0][:, 0, :], scalar1=-1.0, scalar2=1.0,
                                                        op0=ALU.mult, op1=ALU.add), r=[mb[1]], w=[mb[1]])
                    for c in range(8):
                        K.op(dve, lambda e: e.tensor_tensor(out=dst[0][:, c, :], in0=rw[0][:, c, :], in1=mb[0][:, 1, :],
                                                            op=ALU.mult), r=[rw[1], mb[1]], w=[dst[1]])
                        K.op(dve, lambda e: e.tensor_tensor(out=dst[0][:, 8 + c, :], in0=rw[0][:, c, :], in1=mb[0][:, 0, :],
                                                            op=ALU.mult), r=[rw[1], mb[1]], w=[dst[1]])

                def proj_fm(out_ps, outb, wt, t0, n):
                    for c in range(16):
                        rhs = uT[:, c, 1 + t0:1 + t0 + n] if c < 8 else uT[:, c - 8, t0:t0 + n]
                        K.op(pe, lambda e: e.matmul(out_ps, lhsT=wt[0][:, c, :], rhs=rhs, start=(c == 0), stop=(c == 15)),
                             r=[uTb, wt[1]], w=[outb], inc=(c == 15))

                wwa = ctxs[0].wcp[0]
                load_w(4096, wwa)
                for ci_, (t0, n) in enumerate(tok_chunks(512)):
                    p_ = psA[ci_ % 2]
                    proj_fm(p_[0][:, 0:n], p_[1], wwa, t0, n)
                    K.op(act, lambda e: e.activation(out=wadT[0:64, t0:t0 + n], in_=p_[0][0:64, 0:n], func=AF.Tanh),
                         r=[p_[1]], w=[wadTb])
                    K.op(act, lambda e: e.copy(out=wadT[64:128, t0:t0 + n], in_=p_[0][64:128, 0:n]), r=[p_[1]], w=[wadTb])

                v3 = lambda t_: t_[0][:].rearrange("p (c s) -> p c s", c=2)
                flat = lambda ap: ap.rearrange("p a b -> p (a b)")
                one_b = epsb[:, 1:2]

                def sigmoid_chain(C, src_ps, srcb, bias_ap, dst, extra_r=()):
                    if bias_ap is None:
                        K.op(act, lambda e: e.activation(out=dst[0][:], in_=src_ps, func=AF.Exp, scale=-1.0),
                             r=[srcb] + list(extra_r), w=[dst[1]])
                    else:
                        K.op(act, lambda e: e.activation(out=dst[0][:], in_=src_ps, func=AF.Exp, bias=bias_ap, scale=-1.0),
                             r=[srcb] + list(extra_r), w=[dst[1]])
                    K.op(act, lambda e: e.activation(out=dst[0][:], in_=dst[0][:], func=AF.Ln, bias=one_b, scale=1.0),
                         r=[dst[1], ssb], w=[dst[1]])
                    K.op(act, lambda e: e.activation(out=dst[0][:], in_=dst[0][:], func=AF.Exp, scale=-1.0),
                         r=[dst[1]], w=[dst[1]])

                def prep(C, p):
                    vcol = lambda vi: vecs[:, vi, p:p + 1]
                    wr, wk, wv, wg = C.wcp
                    bX = bY = C.bZ
                    for i in range(NT):
                        while C.back_done < i - 2:
                            yield False
                        S = C.PS[i % 3]
                        while C.zlock is not None and C.zlock != "prep":
                            yield False
                        C.zlock = "prep"
                        t0 = 128 * i
                        rf, kf, sgw, av, lw, cm, cmx, E1, E2, E3 = C.rf, C.kf, C.sgw, C.av, C.lw, C.cm, C.cmx, C.E1, C.E2, C.E3
                        kkr, sq, hsn, kk, t1, kp, bb, ke3, be3, gs = C.kkr, C.sq, C.hsn, C.kk, C.t1, C.kp, C.bb, C.ke3, C.be3, C.gs
                        AR, BK = S.AR, S.BK
                        proj_fm(bX[0][:, 0:128], bX[1], wr, t0, 128)
                        proj_fm(bX[0][:, 128:256], bX[1], wk, t0, 128)
                        proj_fm(bX[0][:, 256:384], bX[1], wg, t0, 128)
                        for c in range(16):
                            lhsT = uT[:, c, 1 + t0:1 + t0 + 128] if c < 8 else uT[:, c - 8, t0:t0 + 128]
                            K.op(pe, lambda e: e.matmul(bX[0][:, 384:512], lhsT=lhsT, rhs=wv[0][:, c, :],
                                                        start=(c == 0), stop=(c == 15)),
                                 r=[uTb, wv[1]], w=[bX[1]], inc=(c == 15))
                        yield True
                        K.op(act, lambda e: e.copy(out=rf[0][:], in_=bX[0][:, 0:128]), r=[bX[1]], w=[rf[1]])
                        K.op(act, lambda e: e.copy(out=kf[0][:], in_=bX[0][:, 128:256]), r=[bX[1]], w=[kf[1]])
                        sigmoid_chain(C, bX[0][:, 256:384], bX[1], None, gs)
                        K.op(dve, lambda e: e.tensor_tensor(out=S.sg[0][:], in0=bX[0][:, 256:384], in1=gs[0][:], op=ALU.mult),
                             r=[bX[1], gs[1]], w=[S.sg[1]])
                        K.op(dve, lambda e: e.tensor_copy(out=S.vtok[0][:], in_=bX[0][:, 384:512]), r=[bX[1]], w=[S.vtok[1]])
                        K.op(pe, lambda e: e.matmul(bY[0][:, 0:128], lhsT=wup[0:64, 128 * p:128 * p + 128],
                                                    rhs=wadT[0:64, t0:t0 + 128], start=True, stop=True),
                             r=[wupb, wadTb], w=[bY[1]])
                        K.op(pe, lambda e: e.matmul(bY[0][:, 128:256], lhsT=wup[64:128, 128 * p:128 * p + 128],
                                                    rhs=wadT[64:128, t0:t0 + 128], start=True, stop=True),
                             r=[wupb, wadTb], w=[bY[1]])
                        yield True
                        sigmoid_chain(C, bY[0][:, 0:128], bY[1], vcol(6), sgw, extra_r=[vecb])
                        sigmoid_chain(C, bY[0][:, 128:256], bY[1], vcol(7), av, extra_r=[vecb])
                        K.op(dve, lambda e: e.tensor_scalar(out=lw[0][:], in0=sgw[0][:], scalar1=-DECAY_SCALE, scalar2=None,
                                                            op0=ALU.mult), r=[sgw[1]], w=[lw[1]])
                        K.op(dve, lambda e: e.tensor_tensor_scan(out=cm[0][:], data0=ones_f[:], data1=lw[0][:], initial=0.0,
                                                                 op0=ALU.mult, op1=ALU.add), r=[lw[1], constb], w=[cm[1]])
                        K.op(dve, lambda e: e.tensor_tensor(out=cmx[0][:], in0=cm[0][:], in1=lw[0][:], op=ALU.subtract),
                             r=[cm[1], lw[1]], w=[cmx[1]])
                        yield True
                        K.op(act, lambda e: e.activation(out=E1[0][:], in_=cm[0][:], func=AF.Exp), r=[cm[1]], w=[E1[1]])
                        K.op(act, lambda e: e.activation(out=E2[0][:], in_=cmx[0][:], func=AF.Exp), r=[cmx[1]], w=[E2[1]])
                        K.op(act, lambda e: e.activation(out=E3[0][:], in_=cm[0][:], func=AF.Exp, scale=-1.0),
                             r=[cm[1]], w=[E3[1]])
                        K.op(act, lambda e: e.activation(out=S.GC[0][:, 0:1], in_=cm[0][:, 127:128], func=AF.Exp),
                             r=[cm[1]], w=[S.GC[1]])
                        K.op(dve, lambda e: e.tensor_scalar(out=kkr[0][:], in0=kf[0][:], scalar1=vcol(2), scalar2=None,
                                                            op0=ALU.mult), r=[kf[1], vecb], w=[kkr[1]])
                        K.op(dve, lambda e: e.tensor_tensor(out=sq[0][:], in0=kkr[0][:], in1=kkr[0][:], op=ALU.mult),
                             r=[kkr[1]], w=[sq[1]])
                        K.op(pe, lambda e: e.matmul(bY[0][:, 256:384], lhsT=bones[:], rhs=sq[0][:], start=True, stop=True),
                             r=[cb2, sq[1]], w=[bY[1]])
                        yield True
                        K.op(dve, lambda e: e.tensor_scalar(out=hsn[0][:], in0=bY[0][:, 256:384], scalar1=1e-24, scalar2=None,
                                                            op0=ALU.max), r=[bY[1]], w=[hsn[1]])
                        K.op(act, lambda e: e.activation(out=hsn[0][:], in_=hsn[0][:], func=AF.Ln), r=[hsn[1]], w=[hsn[1]])
                        K.op(act, lambda e: e.activation(out=hsn[0][:], in_=hsn[0][:], func=AF.Exp, scale=-0.5),
                             r=[hsn[1]], w=[hsn[1]])
                        K.op(dve, lambda e: e.tensor_scalar(out=t1[0][:], in0=av[0][:], scalar1=vcol(3), scalar2=vcol(5),
                                                            op0=ALU.mult, op1=ALU.add), r=[av[1], vecb], w=[t1[1]])
                        K.op(dve, lambda e: e.tensor_tensor(out=kp[0][:], in0=kf[0][:], in1=t1[0][:], op=ALU.mult),
                             r=[kf[1], t1[1]], w=[kp[1]])
                        K.op(dve, lambda e: e.tensor_tensor(out=AR[0][:, 1, :], in0=rf[0][:], in1=E1[0][:], op=ALU.mult),
                             r=[rf[1], E1[1]], w=[AR[1]])
                        K.op(dve, lambda e: e.tensor_tensor(out=ke3[0][:], in0=kp[0][:], in1=E3[0][:], op=ALU.mult),
                             r=[kp[1], E3[1]], w=[ke3[1]])
                        yield True
                        K.op(dve, lambda e: e.tensor_tensor(out=kk[0][:], in0=kkr[0][:], in1=hsn[0][:], op=ALU.mult),
                             r=[kkr[1], hsn[1]], w=[kk[1]])
                        K.op(dve, lambda e: e.tensor_tensor(out=bb[0][:], in0=kk[0][:], in1=av[0][:], op=ALU.mult),
                             r=[kk[1], av[1]], w=[bb[1]])
                        K.op(dve, lambda e: e.scalar_tensor_tensor(out=AR[0][:, 0, :], in0=kk[0][:], scalar=-1.0, in1=E2[0][:],
                                                                   op0=ALU.mult, op1=ALU.mult), r=[kk[1], E2[1]], w=[AR[1]])
                        K.op(dve, lambda e: e.tensor_tensor(out=be3[0][:], in0=bb[0][:], in1=E3[0][:], op=ALU.mult),
                             r=[bb[1], E3[1]], w=[be3[1]])
                        K.op(act, lambda e: e.copy(out=BK[0][:, 1, :], in_=ke3[0][:]), r=[ke3[1]], w=[BK[1]])
                        K.op(act, lambda e: e.copy(out=BK[0][:, 0, :], in_=be3[0][:]), r=[be3[1]], w=[BK[1]])
                        yield True
                        K.op(act, lambda e: e.copy(out=S.ARz[0][0:64, 0, :, :], in_=AR[0][0:64, :, :]), r=[AR[1]], w=[S.ARz[1]])
                        K.op(act, lambda e: e.copy(out=S.ARz[0][64:128, 1, :, :], in_=AR[0][64:128, :, :]), r=[AR[1]], w=[S.ARz[1]])
                        K.op(act, lambda e: e.copy(out=S.BKz[0][0:64, 0, :], in_=BK[0][0:64, 0, :]), r=[BK[1]], w=[S.BKz[1]])
                        K.op(act, lambda e: e.copy(out=S.BKz[0][64:128, 1, :], in_=BK[0][64:128, 0, :]), r=[BK[1]], w=[S.BKz[1]])
                        K.op(dve, lambda e: e.tensor_scalar(out=C.khT[0][:], in0=ke3[0][:], scalar1=S.GC[0][:, 0:1], scalar2=None,
                                                            op0=ALU.mult), r=[ke3[1], S.GC[1]], w=[C.khT[1]])
                        K.op(dve, lambda e: e.tensor_scalar(out=C.bhT[0][:], in0=be3[0][:], scalar1=S.GC[0][:, 0:1], scalar2=None,
                                                            op0=ALU.mult), r=[be3[1], S.GC[1]], w=[C.bhT[1]])
                        K.op(dve, lambda e: e.scalar_tensor_tensor(out=C.rkb[0][:], in0=rf[0][:], scalar=vcol(4), in1=kp[0][:],
                                                                   op0=ALU.mult, op1=ALU.mult), r=[rf[1], kp[1], vecb], w=[C.rkb[1]])
                        K.op(pe, lambda e: e.matmul(bY[0][:, 384:386], lhsT=C.rkb[0][:], rhs=hind[:], start=True, stop=True),
                             r=[C.rkb[1], cb2], w=[bY[1]])
                        K.op(pe, lambda e: e.transpose(out=psT[0][:, 4 + 2 * C.ci, :], in_=C.khT[0][:], identity=ident[:]),
                             r=[C.khT[1], identb], w=[psT[1]], inc=False)
                        K.op(pe, lambda e: e.transpose(out=psT[0][:, 5 + 2 * C.ci, :], in_=C.bhT[0][:], identity=ident[:]),
                             r=[C.bhT[1], identb], w=[psT[1]])
                        yield True
                        K.op(dve, lambda e: e.tensor_copy(out=S.bon[0][:], in_=bY[0][:, 384:386]), r=[bY[1]], w=[S.bon[1]])
                        K.op(act, lambda e: e.copy(out=S.kbhat[0][:], in_=psT[0][:, 4 + 2 * C.ci:6 + 2 * C.ci, :]), r=[psT[1]], w=[S.kbhat[1]])
                        C.zlock = None
                        C.prep_done = i + 1
                        yield True

                def inv(C, p):
                    bX, bY = C.bX, C.bY
                    for i in range(NT):
                        while C.prep_done < i + 1 or C.back_done < i - 1:
                            yield False
                        S = C.PS[i % 3]
                        I_ = C.IS[i % 2]
                        AR, BK, MB, MK = S.AR, S.BK, I_.MB, I_.MK
                        arz = S.ARz[0][:].rearrange("p h s t -> p (h s t)")
                        K.op(pe, lambda e: e.matmul(bX[0][:, 0:512], lhsT=BK[0][:, 0, :], rhs=arz, start=True, stop=True),
                             r=[BK[1], S.ARz[1]], w=[bX[1]])
                        K.op(pe, lambda e: e.matmul(bY[0][:, 0:512], lhsT=BK[0][:, 1, :], rhs=arz, start=True, stop=True),
                             r=[BK[1], S.ARz[1]], w=[bY[1]])
                        yield True
                        P0 = C.PQ0
                        K.op(dve, lambda e: e.tensor_tensor(out=MB[0][:].rearrange("p h c -> p (h c)"), in0=bX[0][:, 0:512],
                                                            in1=msc[:].rearrange("p h c -> p (h c)"), op=ALU.mult),
                             r=[bX[1], cb2], w=[MB[1]])
                        K.op(pe, lambda e: e.matmul(bX[0][:, 0:256], lhsT=AR[0][:, 0, :], rhs=S.BKz[0][:].rearrange("p h s -> p (h s)"),
                                                    start=True, stop=True), r=[AR[1], S.BKz[1]], w=[bX[1]])
                        K.op(dve, lambda e: e.tensor_tensor(out=MK[0][:].rearrange("p h c -> p (h c)"), in0=bY[0][:, 0:512],
                                                            in1=msc[:].rearrange("p h c -> p (h c)"), op=ALU.mult),
                             r=[bY[1], cb2], w=[MK[1]])
                        yield True
                        K.op(dve, lambda e: e.tensor_tensor(out=P0[0][:, :, 0:128], in0=bX[0][:, 0:256].rearrange("p (h s) -> p h s", h=2),
                                                            in1=mlow2[:], op=ALU.mult), r=[bX[1], cb2], w=[P0[1]])
                        K.op(act, lambda e: e.copy(out=P0[0][:, :, 128:256], in_=MB[0][:, :, 0:128]), r=[MB[1]], w=[P0[1]])
                        yield True
                        for stp_ in range(1, 8):
                            prev = C.PQ0 if stp_ == 1 else C.PQT[(stp_ - 1) % 2]
                            cur = C.PQT[stp_ % 2]
                            last = (stp_ == 7)
                            for hd in range(2):
                                bk = bX if hd == 0 else bY
                                Pm = prev[0][:, hd, 0:128]
                                Qm = prev[0][:, hd, 128:256]
                                if not last:
                                    K.op(pe, lambda e: e.matmul(bk[0][:, 0:128], lhsT=Qm, rhs=Pm, start=True, stop=True),
                                         r=[prev[1]], w=[bk[1]], inc=False)
                                    K.op(pe, lambda e: e.matmul(bk[0][:, 128:384], lhsT=Pm, rhs=prev[0][:, hd, 128:384], start=True, stop=False),
                                         r=[prev[1]], w=[bk[1]], inc=False)
                                else:
                                    K.op(pe, lambda e: e.matmul(bk[0][:, 256:384], lhsT=Pm, rhs=prev[0][:, hd, 256:384], start=True, stop=False),
                                         r=[prev[1]], w=[bk[1]], inc=False)
                                K.op(pe, lambda e: e.matmul(bk[0][:, 256:384], lhsT=ident[:], rhs=prev[0][:, hd, 256:384], start=False, stop=True),
                                     r=[prev[1], identb], w=[bk[1]])
                            yield True
                            for hd in range(2):
                                bk = bX if hd == 0 else bY
                                if not last:
                                    dst_, src_ = cur[0][:, hd, :], bk[0][:, 0:384]
                                    dstb_ = cur[1]
                                else:
                                    dst_, src_ = I_.TT[0][:, hd, :], bk[0][:, 256:384]
                                    dstb_ = I_.TT[1]
                                if hd == 0:
                                    K.op(act, lambda e: e.copy(out=dst_, in_=src_), r=[bk[1]], w=[dstb_])
                                else:
                                    K.op(dve, lambda e: e.tensor_copy(out=dst_, in_=src_), r=[bk[1]], w=[dstb_])
                            yield True
                        C.inv_done = i + 1
                        yield True

                def back(C, p):
                    bZ = C.bZ
                    Hs, Hb, Hsb, Hbb = C.Hs, C.Hb, C.Hsb, C.Hbb
                    Xs, Us, ytile, yn, ybf = C.Xs, C.Us, C.ytile, C.yn, C.ybf
                    for i in range(NT):
                        while C.inv_done < i + 1:
                            yield False
                        S = C.PS[i % 3]
                        I_ = C.IS[i % 2]
                        AR, MB, MK, vtok, kbhat, GC, Tf = S.AR, I_.MB, I_.MK, S.vtok, S.kbhat, S.GC, I_.TT
                        while C.zlock is not None and C.zlock != "back":
                            yield False
                        C.zlock = "back"
                        t0 = 128 * i
                        hc = lambda hd: slice(64 * hd, 64 * hd + 64)
                        K.op(pe, lambda e: e.matmul(bZ[0][:, 0:128], lhsT=AR[0][:, 0, :], rhs=Hb[:], start=True, stop=False, skip_group_check=True),
                             r=[AR[1], Hbb], w=[bZ[1]], inc=False)
                        for hd in range(2):
                            K.op(pe, lambda e: e.matmul(bZ[0][:, hc(hd)], lhsT=MK[0][:, hd, 0:128], rhs=vtok[0][:, hc(hd)],
                                                        start=False, stop=(hd == 1), skip_group_check=True),
                                 r=[MK[1], vtok[1]], w=[bZ[1]], inc=(hd == 1))
                        yield True
                        K.op(act, lambda e: e.copy(out=Xs[0][:], in_=bZ[0][:, 0:128]), r=[bZ[1]], w=[Xs[1]])
                        yield True
                        for hd in range(2):
                            K.op(pe, lambda e: e.matmul(bZ[0][:, 128 + 64 * hd:128 + 64 * hd + 64], lhsT=Tf[0][:, hd, :], rhs=Xs[0][:, hc(hd)],
                                                        start=True, stop=True), r=[Tf[1], Xs[1]], w=[bZ[1]], inc=(hd == 1))
                        yield True
                        K.op(dve, lambda e: e.tensor_copy(out=Us[0][:], in_=bZ[0][:, 128:256]), r=[bZ[1]], w=[Us[1]])
                        yield True
                        K.op(pe, lambda e: e.matmul(bZ[0][:, 256:384], lhsT=AR[0][:, 1, :], rhs=Hb[:], start=True, stop=False, skip_group_check=True),
                             r=[AR[1], Hbb], w=[bZ[1]], inc=False)
                        for hd in range(2):
                            o_ = bZ[0][:, 256 + 64 * hd:256 + 64 * hd + 64]
                            K.op(pe, lambda e: e.matmul(o_, lhsT=MB[0][:, hd, 128:256], rhs=Us[0][:, hc(hd)], start=False, stop=False,
                                                        skip_group_check=True), r=[MB[1], Us[1]], w=[bZ[1]], inc=False)
                            K.op(pe, lambda e: e.matmul(o_, lhsT=MK[0][:, hd, 128:256], rhs=vtok[0][:, hc(hd)], start=False, stop=(hd == 1),
                                                        skip_group_check=True), r=[MK[1], vtok[1]], w=[bZ[1]], inc=False)
                        for hd in range(2):
                            o2 = bZ[0][:, 384 + 64 * hd:384 + 64 * hd + 64]
                            K.op(pe, lambda e: e.matmul(o2, lhsT=kbhat[0][:, 1, :], rhs=Us[0][:, hc(hd)], start=True, stop=False),
                                 r=[kbhat[1], Us[1]], w=[bZ[1]], inc=False)
                            K.op(pe, lambda e: e.matmul(o2, lhsT=kbhat[0][:, 0, :], rhs=vtok[0][:, hc(hd)], start=False, stop=True),
                                 r=[kbhat[1], vtok[1]], w=[bZ[1]], inc=(hd == 1))
                        yield True
                        for hd in range(2):
                            lo = 64 * hd
                            K.op(dve, lambda e: e.scalar_tensor_tensor(out=Hs[lo:lo + 64, hc(hd)], in0=Hs[lo:lo + 64, hc(hd)], scalar=GC[0][lo:lo + 64, 0:1],
                                                                       in1=bZ[0][lo:lo + 64, 384 + 64 * hd:384 + 64 * hd + 64],
                                                                       op0=ALU.mult, op1=ALU.add), r=[Hsb, GC[1], bZ[1]], w=[Hsb])
                        K.op(dve, lambda e: e.tensor_copy(out=Hb[:], in_=Hs[:]), r=[Hsb], w=[Hbb])
                        K.op(dve, lambda e: e.tensor_copy(out=ytile[0][:], in_=bZ[0][:, 256:384]), r=[bZ[1]], w=[ytile[1]])
                        C.zlock = None
                        yield True
                        for hf in range(2):
                            K.op(dve, lambda e: e.bn_stats(out=C.st6[0][:, hf, :], in_=ytile[0][:, 64 * hf:64 * hf + 64]), r=[ytile[1]], w=[C.st6[1]])
                            K.op(dve, lambda e: e.bn_aggr(out=C.mv[0][:, hf, :], in_=C.st6[0][:, hf, :]), r=[C.st6[1]], w=[C.mv[1]])
                        yield True
                        K.op(act, lambda e: e.activation(out=C.rstd[0][:], in_=C.mv[0][:, :, 1], func=AF.Ln, bias=epsb[:, 2:3], scale=1.0),
                             r=[C.mv[1], ssb], w=[C.rstd[1]])
                        K.op(act, lambda e: e.activation(out=C.rstd[0][:], in_=C.rstd[0][:], func=AF.Exp, scale=-0.5), r=[C.rstd[1]], w=[C.rstd[1]])
                        yield True
                        for hf in range(2):
                            cs = slice(64 * hf, 64 * hf + 64)
                            K.op(dve, lambda e: e.tensor_scalar(out=yn[0][:, cs], in0=ytile[0][:, cs], scalar1=C.mv[0][:, hf, 0:1],
                                                                scalar2=C.rstd[0][:, hf:hf + 1], op0=ALU.subtract, op1=ALU.mult),
                                 r=[ytile[1], C.mv[1], C.rstd[1]], w=[yn[1]])
                        K.op(dve, lambda e: e.tensor_tensor(out=yn[0][:], in0=yn[0][:], in1=C.lnw[:], op=ALU.mult),
                             r=[yn[1], C.lnbuf], w=[yn[1]])
                        K.op(dve, lambda e: e.tensor_tensor(out=yn[0][:], in0=yn[0][:], in1=C.lnb[:], op=ALU.add),
                             r=[yn[1], C.lnbuf], w=[yn[1]])
                        for hf in range(2):
                            cs = slice(64 * hf, 64 * hf + 64)
                            K.op(dve, lambda e: e.scalar_tensor_tensor(out=ybf[0][:, cs], in0=vtok[0][:, cs], scalar=S.bon[0][:, hf:hf + 1],
                                                                       in1=yn[0][:, cs], op0=ALU.mult, op1=ALU.add),
                                 r=[vtok[1], S.bon[1], yn[1]], w=[ybf[1]])
                        yield True
                        K.op(pe, lambda e: e.transpose(out=psT[0][:, 2 + C.ci, :], in_=ybf[0][:], identity=ident[:]), r=[ybf[1], identb], w=[psT[1]])
                        yield True
                        K.op(dve, lambda e: e.tensor_tensor(out=ogT[:, p, t0:t0 + 128], in0=psT[0][:, 2 + C.ci, :], in1=S.sg[0][:], op=ALU.mult),
                             r=[psT[1], S.sg[1]], w=[ogTb])
                        C.back_done = i + 1
                        yield True

                def pair_stream(C):
                    for p in range(C.ci, 8, 2):
                        for t_ in range(4):
                            load_w(1024 * t_ + 128 * p, C.wcp[t_])
                            yield True
                        K.dma(sp, C.lnw[:], rwkv_ln_w[j, 128 * p:128 * p + 128].partition_broadcast(128), w=[C.lnbuf])
                        K.dma(sp, C.lnb[:], rwkv_ln_b[j, 128 * p:128 * p + 128].partition_broadcast(128), w=[C.lnbuf])
                        K.op(dve, lambda e: e.memset(C.Hs[:], 0.0), w=[C.Hsb])
                        K.op(dve, lambda e: e.memset(C.Hb[:], 0.0), w=[C.Hbb])
                        C.prep_done = 0
                        C.inv_done = 0
                        C.back_done = 0
                        C.zlock = None
                        gens = [prep(C, p), inv(C, p), back(C, p)]
                        while gens:
                            progressed = False
                            for g_ in list(gens):
                                try:
                                    if next(g_):
                                        progressed = True
                                except StopIteration:
                                    gens.remove(g_)
                                    progressed = True
                            yield progressed

                streams = [pair_stream(C) for C in ctxs]
                while streams:
                    for s_ in list(streams):
                        try:
                            next(s_)
                        except StopIteration:
                            streams.remove(s_)
                K.barrier()


        for layer in range(nlayers):
            phase_prenorm(layer)
            if layer % 2 == 0:
                fox_layer(layer)
                phase_post(layer, fox_w_out[layer // 2], layer == nlayers - 1)
            else:
                rwkv_layer(layer)
                phase_post(layer, rwkv_w_out[layer // 2], layer == nlayers - 1)
            K.barrier()
        K.barrier()
    return nc


_CACHE = {}


def _consts():
    idx = np.arange(128)
    tri = (idx[:, None] <= idx[None, :]).astype(np.float32)
    ident = np.eye(128, dtype=np.float32)
    ones = np.ones((128, 128), np.float32)
    m64 = np.zeros((128, 128), np.float32)
    s = idx[:, None] % 64
    t = idx[None, :] % 64
    m64[:, 0:64] = (s < t)[:, 0:64]
    m64[:, 64:128] = (s <= t)[:, 64:128]
    scan = np.ones((128, 512), np.float32)
    scan[:, ::64] = 0.0
    hind = np.zeros((128, 2), np.float32)
    hind[0:64, 0] = 1.0
    hind[64:128, 1] = 1.0
    m64x2 = np.concatenate([m64, m64], axis=1)
    r64 = idx[:, None] % 64
    c64 = np.arange(64)[None, :]
    mlow = (c64 < r64).astype(np.float32)
    mlow2 = np.concatenate([mlow, mlow], axis=1)
    i64 = (c64 == r64).astype(np.float32)
    idx2 = np.concatenate([i64, i64], axis=1)
    bones = ((idx[:, None] // 64) == (idx[None, :] // 64)).astype(np.float32)
    strict = (idx[:, None] < idx[None, :]).astype(np.float32)
    msc1 = np.concatenate([strict, tri], axis=1)
    msc = np.concatenate([msc1, msc1], axis=1)
    mlow128 = (idx[None, :] < idx[:, None]).astype(np.float32)
    return dict(c_ident=ident, c_tri=tri, c_ones=ones, c_m64=m64, c_scan=scan, c_hind=hind,
                c_m64x2=m64x2, c_mlow2=mlow2, c_idx2=idx2, c_bones=bones,
                c_msc=msc, c_mlow128=np.concatenate([mlow128, mlow128], axis=1),
                c_id2=np.concatenate([ident, ident], axis=1))


def kernel(x, meta_tokens, norm_pre, norm_post, fox_w_in, fox_b_f, fox_w_out,
           rwkv_w_in, rwkv_mu, rwkv_w0, rwkv_w_up, rwkv_a0, rwkv_a_up, rwkv_k_k,
           rwkv_k_a, rwkv_r_k, rwkv_ln_w, rwkv_ln_b, rwkv_w_out, _nlayers=DEPTH):
    f = lambda a: np.ascontiguousarray(np.asarray(a, dtype=np.float32))
    x = f(x)
    B = x.shape[0]
    meta = f(meta_tokens)
    h0 = np.zeros((B, T, D), np.float32)
    h0[:, :NMETA] = meta[None]
    h0[:, NMETA:NMETA + SEQ] = x
    shared = dict(
        norm_pre=f(norm_pre), norm_post=f(norm_post), fox_w_in=f(fox_w_in), fox_b_f=f(fox_b_f),
        fox_w_out=f(fox_w_out), rwkv_w_in=f(rwkv_w_in), rwkv_mu=f(rwkv_mu), rwkv_w0=f(rwkv_w0),
        rwkv_w_up=f(rwkv_w_up), rwkv_a0=f(rwkv_a0), rwkv_a_up=f(rwkv_a_up), rwkv_k_k=f(rwkv_k_k),
        rwkv_k_a=f(rwkv_k_a), rwkv_r_k=f(rwkv_r_k).reshape(2, D), rwkv_ln_w=f(rwkv_ln_w),
        rwkv_ln_b=f(rwkv_ln_b), rwkv_w_out=f(rwkv_w_out))
    shared.update(_consts())
    key = _nlayers
    if key not in _CACHE:
        _CACHE[key] = build_program(_nlayers)
    nc = _CACHE[key]
    in_maps = []
    for b in range(B):
        m = dict(shared)
        m["h0"] = h0[b]
        in_maps.append(m)
    res = run_bass_kernel_spmd(nc, in_maps, core_ids=list(range(B)))
    out = np.stack([np.asarray(r["y"])[NMETA:NMETA + SEQ] for r in res.results], axis=0)
    return out.astype(np.float32)
```

```python
import math
from contextlib import ExitStack

import numpy as np
import concourse.bass as bass
import concourse.mybir as mybir
from concourse.bass_utils import run_bass_kernel_spmd

F32 = mybir.dt.float32
BF16 = mybir.dt.bfloat16
AF = mybir.ActivationFunctionType
ALU = mybir.AluOpType
AX = mybir.AxisListType

D = 1024
SEQ = 2048
NMETA = 16
NT = 17
T = NT * 128
DEPTH = 4
NH = 16
HD = 64
FOX_IN = 4 * D + NH
RWKV_IN = 4 * D + 128
NORM_EPS = 1e-6
GN_EPS = 64e-5
DECAY_SCALE = math.exp(-0.5)
CH = 64
DBG = {"stage": 9, "pairs": 8, "tiles": NT}


def tok_chunks(n=512):
    out = []
    t0 = 0
    while t0 < T:
        m = min(n, T - t0)
        out.append((t0, m))
        t0 += m
    return out


class Buf:
    __slots__ = ("name", "lw", "rd", "excl")

    def __init__(self, name="", excl=False):
        self.name = name
        self.lw = None
        self.rd = {}
        self.excl = excl


class Q:
    def __init__(self, name, eng, sem, self_sync=True):
        self.name = name
        self.eng = eng
        self.sem = sem
        self.cnt = 0
        self.seen = {}
        self.self_sync = self_sync
        self.key = name
        self.ring = []
        self.ring_i = 0


class PEProxy:
    def __init__(self, K, eng):
        self.K = K
        self.eng = eng
        self.partial = False

    def _pre(self, st_ap, out):
        K = self.K
        rg = (st_ap.base_partition(), st_ap.partition_size())
        bank = out.name
        last = K.pe_last
        if last is not None and last[0] != rg and (last[1] == bank or (last[0][1] < 128 and rg[1] < 128)):
            assert K.pe_last_tk is not None, "previous matmul needs a semaphore increment"
            K._wait(K.pe, K.pe_last_tk)
        K.pe_last = (rg, bank)
        self.partial = rg[1] < 128

    def matmul(self, out, lhsT, rhs, **kw):
        self._pre(lhsT, out)
        return self.eng.matmul(out, lhsT=lhsT, rhs=rhs, **kw)

    def transpose(self, out, in_, identity):
        self._pre(in_, out)
        return self.eng.transpose(out=out, in_=in_, identity=identity)


class KB:
    def __init__(self, nc, st):
        self.nc = nc
        self.st = st
        mk = lambda n: st.enter_context(nc.semaphore(n))
        self.pe = Q("pe", nc.tensor, mk("s_pe"), self_sync=False)
        self.act = Q("act", nc.scalar, mk("s_act"))
        self.dve = Q("dve", nc.vector, mk("s_dve"))
        self.pool = Q("pool", nc.gpsimd, mk("s_pool"))
        self.sp = Q("sp", nc.sync, mk("s_sp"))
        self.queues = [self.pe, self.act, self.dve, self.pool, self.sp]
        for q, n in ((self.sp, 24), (self.pool, 24)):
            for i in range(n):
                q.ring.append([mk(f"d_{q.name}{i}"), 0, f"d_{q.name}{i}"])
        self.dma_tickets = []
        self.nbuf = 0
        self.counting = False
        self.nops = 0
        self.pe_last = None
        self.pe_last_tk = None
        self.snap = {}
        self.prox = PEProxy(self, nc.tensor)

    def sb(self, st, name, shape, dt):
        self.nbuf += 1
        return st.enter_context(self.nc.sbuf_tensor(f"{name}_{self.nbuf}", list(shape), dt))

    def psum(self, st, name, shape, dt):
        return st.enter_context(self.nc.psum_tensor(name, list(shape), dt))

    def _learn(self, q, tk):
        q.seen[tk[2]] = max(q.seen.get(tk[2], 0), tk[1])
        sn = self.snap.get((tk[2], tk[1]))
        if sn:
            for k2, v2 in sn.items():
                if q.seen.get(k2, 0) < v2:
                    q.seen[k2] = v2

    def _wait(self, q, tk):
        sem, val, key = tk
        if q.seen.get(key, 0) >= val:
            return
        q.eng.wait_ge(sem, val)
        self._learn(q, tk)

    def _deps(self, q, r, w, defer=False):
        deps = []
        for b in r:
            if b.lw is not None:
                deps.append(b.lw)
            if b.excl:
                for key, tk in b.rd.items():
                    if key != q.key:
                        deps.append(tk)
        for b in w:
            if b.lw is not None:
                deps.append(b.lw)
            for key, tk in b.rd.items():
                deps.append(tk)
        need = {}
        for tk in deps:
            if tk[2] == q.key and not q.self_sync:
                continue
            if q.seen.get(tk[2], 0) >= tk[1]:
                continue
            if tk[2] not in need or need[tk[2]][1] < tk[1]:
                need[tk[2]] = tk
        pend = list(need.values())
        last = pend.pop() if (defer and pend) else None
        for tk in pend:
            self._wait(q, tk)
        return last

    def _record(self, tk, r, w):
        for b in r:
            old = b.rd.get(tk[2])
            if old is None or old[1] < tk[1]:
                b.rd[tk[2]] = tk
        for b in w:
            b.lw = tk
            b.rd = {}

    def op(self, q, fn, r=(), w=(), inc=True):
        if self.counting:
            self.nops += 1
            if self.nops > DBG.get("maxops", 10 ** 9):
                return None
        last = self._deps(q, r, w, defer=(q is not self.pe))
        if q is self.pe:
            self.prox.partial = False
            ins = fn(self.prox)
            if self.prox.partial:
                inc = True
        else:
            ins = fn(q.eng)
            if last is not None:
                ins._wait_ge(last[0], last[1])
                self._learn(q, last)
        if inc:
            q.cnt += 1
            ins.then_inc(q.sem, 1)
            tk = (q.sem, q.cnt, q.key)
            self.snap[(q.key, q.cnt)] = dict(q.seen)
        else:
            tk = (q.sem, q.cnt + 1, q.key)
        if q is self.pe:
            self.pe_last_tk = tk if inc else None
        self._record(tk, r, w)
        return tk

    def dma(self, q, out, in_, r=(), w=()):
        self._deps(q, r, w)
        slot = q.ring[q.ring_i % len(q.ring)]
        q.ring_i += 1
        sem, n, key = slot
        if n > 0:
            self._wait(q, (sem, 16 * n, key))
        q.eng.dma_start(out=out, in_=in_).then_inc(sem, 16)
        slot[1] = n + 1
        tk = (sem, 16 * (n + 1), key)
        self.snap[(key, 16 * (n + 1))] = dict(q.seen)
        self._record(tk, r, w)
        self.dma_tickets.append(tk)
        return tk

    def barrier(self):
        tks = [(q.sem, q.cnt, q.key) for q in self.queues if q.cnt > 0]
        for q in self.queues:
            for slot in q.ring:
                if slot[1] > 0:
                    tks.append((slot[0], 16 * slot[1], slot[2]))
        for q in self.queues:
            for tk in tks:
                if tk[2] == q.key:
                    continue
                self._wait(q, tk)


def build_program(nlayers=DEPTH, dbg=False):
    nc = bass.Bass("TRN2", target_bir_lowering=False)
    dt_in = lambda name, shape: nc.dram_tensor(name, list(shape), F32, kind="ExternalInput").ap()
    h0 = dt_in("h0", [T, D])
    norm_pre = dt_in("norm_pre", [DEPTH, D])
    norm_post = dt_in("norm_post", [DEPTH, D])
    fox_w_in = dt_in("fox_w_in", [2, D, FOX_IN])
    fox_b_f = dt_in("fox_b_f", [2, NH])
    fox_w_out = dt_in("fox_w_out", [2, D, D])
    rwkv_w_in = dt_in("rwkv_w_in", [2, D, RWKV_IN])
    rwkv_mu = dt_in("rwkv_mu", [2, RWKV_IN])
    rwkv_w0 = dt_in("rwkv_w0", [2, D])
    rwkv_w_up = dt_in("rwkv_w_up", [2, 64, D])
    rwkv_a0 = dt_in("rwkv_a0", [2, D])
    rwkv_a_up = dt_in("rwkv_a_up", [2, 64, D])
    rwkv_k_k = dt_in("rwkv_k_k", [2, D])
    rwkv_k_a = dt_in("rwkv_k_a", [2, D])
    rwkv_r_k = dt_in("rwkv_r_k", [2, D])
    rwkv_ln_w = dt_in("rwkv_ln_w", [2, D])
    rwkv_ln_b = dt_in("rwkv_ln_b", [2, D])
    rwkv_w_out = dt_in("rwkv_w_out", [2, D, D])
    c_ident = dt_in("c_ident", [128, 128])
    c_tri = dt_in("c_tri", [128, 128])
    c_ones = dt_in("c_ones", [128, 128])
    c_m64 = dt_in("c_m64", [128, 128])
    c_scan = dt_in("c_scan", [128, 512])
    c_hind = dt_in("c_hind", [128, 2])
    c_m64x2 = dt_in("c_m64x2", [128, 256])
    c_mlow2 = dt_in("c_mlow2", [128, 128])
    c_idx2 = dt_in("c_idx2", [128, 128])
    c_bones = dt_in("c_bones", [128, 128])
    c_msc = dt_in("c_msc", [128, 512])
    c_mlow128 = dt_in("c_mlow128", [128, 256])
    c_id2 = dt_in("c_id2", [128, 256])
    y = nc.dram_tensor("y", [T, D], F32, kind="ExternalOutput").ap()

    with ExitStack() as st:
        K = KB(nc, st)
        pe, act, dve, pool, sp = K.pe, K.act, K.dve, K.pool, K.sp

        uT = K.sb(st, "uT", [128, 8, T + 2], BF16)
        uTb = Buf("uT")
        ogT = K.sb(st, "ogT", [128, 8, T], BF16)
        ogTb = Buf("ogT")
        ident = K.sb(st, "ident", [128, 128], BF16)
        identb = Buf()
        tri_f = K.sb(st, "tri_f", [128, 128], F32)
        ones_f = K.sb(st, "ones_f", [128, 128], F32)
        tri_b = K.sb(st, "tri_b", [128, 128], BF16)
        constb = Buf()
        gpre = K.sb(st, "gpre", [128, D], F32)
        gpost = K.sb(st, "gpost", [128, D], F32)
        gb = Buf()
        hbufs = [(K.sb(st, f"hb{i}", [128, D], F32), Buf()) for i in range(2)]
        hbufs2 = hbufs
        un = K.sb(st, "un", [128, D], BF16)
        unb = Buf()
        un2 = K.sb(st, "un2", [128, D], BF16)
        uns = [(un, unb), (un2, Buf())]
        sss = [(K.sb(st, f"ss{i}", [128, 4], F32), Buf()) for i in range(2)]
        mt = K.sb(st, "mt", [128, D], F32)
        mtb = Buf()
        ss = K.sb(st, "ss", [128, 4], F32)
        ssb = Buf()
        epsb = K.sb(st, "epsb", [128, 4], F32)
        K.op(dve, lambda e: e.memset(epsb[:, 0:1], NORM_EPS), w=[ssb])
        K.op(dve, lambda e: e.memset(epsb[:, 1:2], 1.0), w=[ssb])
        K.op(dve, lambda e: e.memset(epsb[:, 2:3], GN_EPS), w=[ssb])
        psA = [(K.psum(st, f"psA{i}", [128, 512], F32), Buf(excl=True)) for i in range(6)]
        psT = (K.psum(st, "psT", [128, 8, 128], BF16), Buf(excl=True))
        psT2 = (K.psum(st, "psT2", [128, 8, 128], BF16), Buf(excl=True))
        psTs = [psT, psT2]
        hB = [Buf(f"h{i}") for i in range(NT)]

        K.dma(pool, ident[:], c_ident[:, :], w=[identb])
        K.dma(pool, tri_b[:], c_tri[:, :], w=[constb])
        K.dma(sp, tri_f[:], c_tri[:, :], w=[constb])
        K.dma(sp, ones_f[:], c_ones[:, :], w=[constb])
        K.op(dve, lambda e: e.memset(uT[:, :, 0:1], 0.0), w=[uTb])

        def bcast_row(ap_row, n):
            return ap_row.partition_broadcast(128)

        def phase_prenorm(layer):
            K.dma(sp, gpre[:], bcast_row(norm_pre[layer, :], D), w=[gb])
            K.dma(sp, gpost[:], bcast_row(norm_post[layer, :], D), w=[gb])
            for i in range(NT):
                ht, htb = hbufs[i % 2]
                ss_, ssb_ = sss[i % 2]
                un_, unb_ = uns[i % 2]
                pT = psTs[i % 2]
                src = h0 if layer == 0 else y
                K.dma(sp, ht[:], src[128 * i:128 * i + 128, :], r=[hB[i]], w=[htb])
                K.op(act, lambda e: e.activation(out=mt[:], in_=ht[:], func=AF.Square, accum_out=ss_[:, 0:1]),
                     r=[htb], w=[mtb, ssb_])
                K.op(act, lambda e: e.activation(out=ss_[:, 1:2], in_=ss_[:, 0:1], func=AF.Ln, bias=epsb[:, 0:1], scale=1.0 / D),
                     r=[ssb_, ssb], w=[ssb_])
                K.op(act, lambda e: e.activation(out=ss_[:, 2:3], in_=ss_[:, 1:2], func=AF.Exp, scale=-0.5),
                     r=[ssb_], w=[ssb_])
                K.op(dve, lambda e: e.scalar_tensor_tensor(out=un_[:], in0=ht[:], scalar=ss_[:, 2:3], in1=gpre[:],
                                                           op0=ALU.mult, op1=ALU.mult), r=[htb, ssb_, gb], w=[unb_])
                for c in range(8):
                    K.op(pe, lambda e: e.transpose(out=pT[0][:, c, :], in_=un_[:, 128 * c:128 * c + 128],
                                                   identity=ident[:]),
                         r=[unb_, identb], w=[pT[1]], inc=(c == 7))
                if i % 2 == 0:
                    K.op(act, lambda e: e.copy(out=uT[:, :, 1 + 128 * i:1 + 128 * i + 128], in_=pT[0][:, :, :]),
                         r=[pT[1]], w=[uTb])
                else:
                    K.op(dve, lambda e: e.tensor_copy(out=uT[:, :, 1 + 128 * i:1 + 128 * i + 128], in_=pT[0][:, :, :]),
                         r=[pT[1]], w=[uTb])

        def phase_post(layer, w_out_ap, last):
            with ExitStack() as pst:
                wout = K.sb(pst, "wout", [128, 8, D], BF16)
                woutb = Buf()
                K.dma(pool, wout[:], w_out_ap.rearrange("(c p) n -> p c n", p=128), w=[woutb])
                for i in range(NT):
                    pms = [psA[0], psA[1]] if i % 2 == 0 else [psA[2], psA[3]]
                    ss_, ssb_ = sss[i % 2]
                    for hf in range(2):
                        for c in range(8):
                            K.op(pe, lambda e: e.matmul(pms[hf][0][:, :], lhsT=ogT[:, c, 128 * i:128 * i + 128],
                                                        rhs=wout[:, c, 512 * hf:512 * hf + 512],
                                                        start=(c == 0), stop=(c == 7)),
                                 r=[ogTb, woutb], w=[pms[hf][1]], inc=(c == 7))
                    for hf in range(2):
                        K.op(act, lambda e: e.activation(out=un[:, 512 * hf:512 * hf + 512], in_=pms[hf][0][:, :], func=AF.Square,
                                                         accum_out=ss_[:, hf:hf + 1]), r=[pms[hf][1]], w=[unb, ssb_])
                    K.op(dve, lambda e: e.tensor_tensor(out=ss_[:, 2:3], in0=ss_[:, 0:1], in1=ss_[:, 1:2], op=ALU.add),
                         r=[ssb_], w=[ssb_])
                    K.op(act, lambda e: e.activation(out=ss_[:, 3:4], in_=ss_[:, 2:3], func=AF.Ln, bias=epsb[:, 0:1], scale=1.0 / D),
                         r=[ssb_, ssb], w=[ssb_])
                    K.op(act, lambda e: e.activation(out=ss_[:, 3:4], in_=ss_[:, 3:4], func=AF.Exp, scale=-0.5),
                         r=[ssb_], w=[ssb_])
                    ht, htb = hbufs2[i % 2]
                    src = h0 if layer == 0 else y
                    K.dma(sp, ht[:], src[128 * i:128 * i + 128, :], r=[hB[i]], w=[htb])
                    for hf in range(2):
                        K.op(dve, lambda e: e.scalar_tensor_tensor(out=mt[:, 512 * hf:512 * hf + 512], in0=pms[hf][0][:, :], scalar=ss_[:, 3:4],
                                                                   in1=gpost[:, 512 * hf:512 * hf + 512], op0=ALU.mult, op1=ALU.mult),
                             r=[pms[hf][1], ssb_, gb], w=[mtb])
                    K.op(dve, lambda e: e.tensor_tensor(out=ht[:], in0=ht[:], in1=mt[:], op=ALU.add),
                         r=[htb, mtb], w=[htb])
                    K.dma(sp, y[128 * i:128 * i + 128, :], ht[:], r=[htb], w=[hB[i]])
                K.barrier()

        def fox_layer(layer):
            j = layer // 2
            win = fox_w_in[j].rearrange("(c p) n -> p c n", p=128)
            with ExitStack() as ls:
                wb = [[(K.sb(ls, f"fw{s}{t}", [128, 8, 256], BF16), Buf()) for t in range(4)] for s in range(2)]
                wf = K.sb(ls, "fwf", [128, 8, 16], BF16)
                wfb = Buf()
                qT = K.sb(ls, "qaug", [128, 4, T], BF16)
                kT = K.sb(ls, "kaug", [128, 4, T], BF16)
                negcum = K.sb(ls, "negcum", [128, NT, NH], F32)
                sgT = K.sb(ls, "sgT", [128, 2, T], BF16)
                qTb, kTb, sgTb = Buf(), Buf(), Buf()
                vaug = K.sb(ls, "vaug", [128, NT, 4, 128], BF16)
                vaugb = Buf()
                bft = K.sb(ls, "bft", [128, NH], F32)
                lf = K.sb(ls, "lf", [128, NT, NH], F32)
                lfb = Buf()
                cum = K.sb(ls, "cum", [128, NT, NH], F32)
                carry = K.sb(ls, "carry", [128, NT, NH], F32)
                cumb = Buf()
                bias = [(K.sb(ls, f"bias{i}", [128, NT], F32), Buf()) for i in range(8)]
                PT = [(K.sb(ls, f"PT{i}", [128, 512], BF16), Buf()) for i in range(3)]
                rs = [(K.sb(ls, f"rs{i}", [128, 512], F32), Buf()) for i in range(2)]
                tmpo = [(K.sb(ls, f"tmpo{i}", [128, 512], F32), Buf()) for i in range(2)]

                def load_group(g, s):
                    for t in range(4):
                        K.dma(pool, wb[s][t][0][:], win[:, :, 1024 * t + 256 * g:1024 * t + 256 * g + 256],
                              w=[wb[s][t][1]])
                K.dma(pool, wf[:], win[:, :, 4096:4112], w=[wfb])
                load_group(0, 0)
                K.dma(sp, bft[:], fox_b_f[j, :].partition_broadcast(128), w=[lfb])
                K.op(dve, lambda e: e.memset(vaug[:], 1.0), w=[vaugb])

                pf = psA[4]
                for i in range(NT):
                    for c in range(8):
                        K.op(pe, lambda e: e.matmul(pf[0][:, 16 * i:16 * i + 16], lhsT=uT[:, c, 1 + 128 * i:1 + 128 * i + 128],
                                                    rhs=wf[:, c, :], start=(c == 0), stop=(c == 7)),
                             r=[uTb, wfb], w=[pf[1]], inc=(c == 7))
                for i in range(NT):
                    K.op(dve, lambda e: e.tensor_tensor(out=lf[:, i, :], in0=pf[0][:, 16 * i:16 * i + 16], in1=bft[:],
                                                        op=ALU.add), r=[pf[1], lfb], w=[lfb])
                lf2 = lf[:].rearrange("p a b -> p (a b)")
                K.op(act, lambda e: e.activation(out=lf2, in_=lf2, func=AF.Exp, scale=-1.0), r=[lfb], w=[lfb])
                K.op(act, lambda e: e.activation(out=lf2, in_=lf2, func=AF.Ln, bias=epsb[:, 1:2], scale=1.0), r=[lfb, ssb], w=[lfb])
                K.op(dve, lambda e: e.tensor_scalar(out=lf2, in0=lf2, scalar1=-1.0, scalar2=None, op0=ALU.mult),
                     r=[lfb], w=[lfb])
                pc, pl = psA[2], psA[3]
                K.op(dve, lambda e: e.memset(carry[:, 0, :], 0.0), w=[cumb])
                for i in range(NT):
                    for jj in range(i):
                        K.op(pe, lambda e: e.matmul(pc[0][:, 16 * i:16 * i + 16], lhsT=ones_f[:], rhs=lf[:, jj, :],
                                                    start=(jj == 0), stop=(jj == i - 1)),
                             r=[lfb, constb], w=[pc[1]], inc=(jj == i - 1))
                    K.op(pe, lambda e: e.matmul(pl[0][:, 16 * i:16 * i + 16], lhsT=tri_f[:], rhs=lf[:, i, :],
                                                start=True, stop=True), r=[lfb, constb], w=[pl[1]])
                K.op(dve, lambda e: e.tensor_copy(out=carry[:, 1:NT, :].rearrange("p a b -> p (a b)"),
                                                  in_=pc[0][:, 16:16 * NT]), r=[pc[1]], w=[cumb])
                K.op(dve, lambda e: e.tensor_tensor(out=cum[:].rearrange("p a b -> p (a b)"),
                                                    in0=pl[0][:, 0:16 * NT],
                                                    in1=carry[:].rearrange("p a b -> p (a b)"), op=ALU.add),
                     r=[pl[1], cumb], w=[cumb])

                K.op(dve, lambda e: e.tensor_scalar(out=negcum[:].rearrange("p a b -> p (a b)"),
                                                    in0=cum[:].rearrange("p a b -> p (a b)"), scalar1=-1.0, scalar2=None,
                                                    op0=ALU.mult), r=[cumb], w=[cumb])
                K.op(dve, lambda e: e.memset(kT[64:65, :, :], 1.0), w=[kTb])
                chunks = tok_chunks(512)
                pcount = [0]

                def nextps():
                    p = psA[pcount[0] % 2]
                    pcount[0] += 1
                    return p

                stc = [0]
                otc = [0]
                ptc = [0]
                bc = [0]
                for g in range(4):
                    s = g % 2
                    if g + 1 < 4:
                        load_group(g + 1, (g + 1) % 2)
                    wq, wk, wv, wg = [wb[s][t] for t in range(4)]
                    for pp in range(2):
                        for (t0, n) in chunks:
                            for (wt, dst, dstb, kind) in ((wq, qT, qTb, 0), (wk, kT, kTb, 1), (wg, sgT, sgTb, 2)):
                                p = nextps()
                                for c in range(8):
                                    K.op(pe, lambda e: e.matmul(p[0][:, 0:n], lhsT=wt[0][:, c, 128 * pp:128 * pp + 128],
                                                                rhs=uT[:, c, 1 + t0:1 + t0 + n],
                                                                start=(c == 0), stop=(c == 7)),
                                         r=[uTb, wt[1]], w=[p[1]], inc=(c == 7))
                                if kind == 0:
                                    for hf_ in range(2):
                                        K.op(act, lambda e: e.activation(out=dst[0:64, 2 * pp + hf_, t0:t0 + n],
                                                                         in_=p[0][64 * hf_:64 * hf_ + 64, 0:n],
                                                                         func=AF.Copy, scale=HD ** -0.5),
                                             r=[p[1]], w=[dstb])
                                elif kind == 1:
                                    for hf_ in range(2):
                                        K.op(dve, lambda e: e.tensor_copy(out=dst[0:64, 2 * pp + hf_, t0:t0 + n],
                                                                          in_=p[0][64 * hf_:64 * hf_ + 64, 0:n]),
                                             r=[p[1]], w=[dstb])
                                else:
                                    K.op(act, lambda e: e.activation(out=dst[:, pp, t0:t0 + n], in_=p[0][:, 0:n],
                                                                     func=AF.Silu), r=[p[1]], w=[dstb])
                    for hh_ in range(4):
                        for i_ in range(NT):
                            K.op(dve, lambda e: e.tensor_scalar(out=qT[64:65, hh_, 128 * i_:128 * i_ + 128], in0=ones_f[64:65, 0:128],
                                                                scalar1=carry[64:65, i_, 4 * g + hh_:4 * g + hh_ + 1], scalar2=None,
                                                                op0=ALU.mult), r=[cumb, constb], w=[qTb])
                    for i in range(NT):
                        p = nextps()
                        for c in range(8):
                            K.op(pe, lambda e: e.matmul(p[0][:, 0:256], lhsT=uT[:, c, 1 + 128 * i:1 + 128 * i + 128],
                                                        rhs=wv[0][:, c, :], start=(c == 0), stop=(c == 7)),
                                 r=[uTb, wv[1]], w=[p[1]], inc=(c == 7))
                        pv = p[0][:, 0:256].rearrange("p (a b c) -> p a b c", a=2, b=2)
                        K.op(dve, lambda e: e.tensor_copy(out=vaug[:, i, 0:4:2, 0:64], in_=pv[:, :, 0, :]),
                             r=[p[1]], w=[vaugb])
                        K.op(dve, lambda e: e.tensor_copy(out=vaug[:, i, 1:4:2, 64:128], in_=pv[:, :, 1, :]),
                             r=[p[1]], w=[vaugb])
                    items = []
                    for hh in range(4):
                        for cidx, (q0, qn) in enumerate(chunks):
                            cx = dict(hh=hh, q0=q0, qn=qn, i0=q0 // 128, ni=qn // 128, started=False)
                            cx["jmax"] = cx["i0"] + cx["ni"] - 1
                            for jk in range(cx["jmax"] + 1):
                                items.append((cx, jk))

                    def emit_st(it):
                        cx, jk = it
                        hh = cx["hh"]
                        h = 4 * g + hh
                        pp, half = hh // 2, hh % 2
                        lo = 64 * half
                        q0, qn, i0, ni = cx["q0"], cx["qn"], cx["i0"], cx["ni"]
                        if not cx["started"]:
                            cx["started"] = True
                            cx["ot"] = psA[4 + otc[0] % 2]
                            otc[0] += 1
                        qs = max(q0, 128 * jk)
                        n = q0 + qn - qs
                        stp = psA[2 + stc[0] % 2]
                        stc[0] += 1
                        K.op(pe, lambda e: e.matmul(stp[0][:, 0:n], lhsT=kT[0:65, hh, 128 * jk:128 * jk + 128],
                                                    rhs=qT[0:65, hh, qs:qs + n], start=True, stop=True),
                             r=[kTb, qTb], w=[stp[1]])
                        return stp

                    def emit_rest(it, stp):
                        cx, jk = it
                        hh = cx["hh"]
                        pp, half = hh // 2, hh % 2
                        olo, slo = (0, 64) if half == 0 else (64, 0)
                        q0, qn, i0, ni, jmax = cx["q0"], cx["qn"], cx["i0"], cx["ni"], cx["jmax"]
                        ot = cx["ot"]
                        qs = max(q0, 128 * jk)
                        n = q0 + qn - qs
                        pt = PT[ptc[0] % 3]
                        ptc[0] += 1
                        h_ = 4 * g + hh
                        K.op(act, lambda e: e.activation(out=pt[0][:, 0:n], in_=stp[0][:, 0:n], func=AF.Exp,
                                                         bias=negcum[:, jk, h_:h_ + 1], scale=1.0),
                             r=[stp[1], cumb], w=[pt[1]])
                        if jk >= i0:
                            K.op(dve, lambda e: e.tensor_tensor(out=pt[0][:, 0:128], in0=pt[0][:, 0:128],
                                                                in1=tri_b[:], op=ALU.mult),
                                 r=[pt[1], constb], w=[pt[1]])
                        K.op(pe, lambda e: e.matmul(ot[0][:, qs - q0:qs - q0 + n], lhsT=vaug[:, jk, hh, :],
                                                    rhs=pt[0][:, 0:n], start=(jk == 0), stop=(jk == jmax),
                                                    skip_group_check=True),
                             r=[vaugb, pt[1]], w=[ot[1]])
                        if jk == jmax:
                            r_ = rs[otc[0] % 2]
                            tm = tmpo[otc[0] % 2]
                            K.op(dve, lambda e: e.reciprocal(out=r_[0][olo:olo + 64, 0:qn], in_=ot[0][slo:slo + 64, 0:qn]),
                                 r=[ot[1]], w=[r_[1]])
                            K.op(dve, lambda e: e.tensor_tensor(out=tm[0][olo:olo + 64, 0:qn], in0=ot[0][olo:olo + 64, 0:qn],
                                                                in1=r_[0][olo:olo + 64, 0:qn], op=ALU.mult),
                                 r=[ot[1], r_[1]], w=[tm[1]])
                            K.op(dve, lambda e: e.tensor_tensor(out=ogT[olo:olo + 64, 2 * g + pp, q0:q0 + qn],
                                                                in0=tm[0][olo:olo + 64, 0:qn],
                                                                in1=sgT[olo:olo + 64, pp, q0:q0 + qn], op=ALU.mult),
                                 r=[tm[1], sgTb], w=[ogTb])

                    nxt = emit_st(items[0])
                    for n_ in range(len(items)):
                        cur_st = nxt
                        if n_ + 1 < len(items):
                            nxt = emit_st(items[n_ + 1])
                        emit_rest(items[n_], cur_st)
                K.barrier()


        def rwkv_layer(layer):
            j = layer // 2
            win = rwkv_w_in[j].rearrange("(c p) n -> p c n", p=128)
            with ExitStack() as ls:
                SB = lambda n, shp, dt: K.sb(ls, n, shp, dt)
                wadT = SB("wadT", [128, T], BF16); wadTb = Buf()
                wup = SB("wup", [128, D], BF16); wupb = Buf()
                vecs = SB("vecs", [128, 8, 8], F32); vecb = Buf()
                mub = (SB("mub", [128, 2, 128], F32), Buf())
                wraw = (SB("wraw", [128, 8, 128], F32), Buf())
                bones = SB("bones", [128, 128], F32)
                msc = SB("msc", [128, 2, 256], BF16)
                mlow2 = SB("mlow2", [128, 2, 128], BF16)
                id2 = SB("id2", [128, 2, 128], BF16)
                hind = SB("hind", [128, 2], BF16)
                cb2 = Buf()

                class Ctx:
                    pass

                ctxs = []
                for ci in range(2):
                    C = Ctx()
                    C.ci = ci
                    F_ = lambda n: (SB(f"{n}{ci}", [128, 128], F32), Buf())
                    B_ = lambda n, shp: (SB(f"{n}{ci}", shp, BF16), Buf())
                    for n in ("rf", "kf", "sgw", "av", "lw", "cm", "cmx", "E1", "E2", "E3", "kkr", "sq", "hsn", "kk",
                              "t1", "kp", "bb", "ke3", "be3", "gs", "ytile"):
                        setattr(C, n, F_(n))
                    C.PS = []
                    for si in range(3):
                        S = Ctx()
                        S.sg = B_(f"sg{si}", [128, 128]); S.vtok = B_(f"vtok{si}", [128, 128])
                        S.AR = B_(f"AR{si}", [128, 2, 128]); S.BK = B_(f"BK{si}", [128, 2, 128])
                        S.ARz = B_(f"ARz{si}", [128, 2, 2, 128]); S.BKz = B_(f"BKz{si}", [128, 2, 128])
                        S.kbhat = B_(f"kbhat{si}", [128, 2, 128])
                        S.GC = (SB(f"GC{ci}{si}", [128, 2], F32), Buf())
                        S.bon = (SB(f"bon{ci}{si}", [128, 2], F32), Buf())
                        K.op(dve, lambda e: e.memset(S.ARz[0][:], 0.0), w=[S.ARz[1]])
                        K.op(dve, lambda e: e.memset(S.BKz[0][:], 0.0), w=[S.BKz[1]])
                        C.PS.append(S)
                    C.IS = []
                    for si in range(2):
                        S = Ctx()
                        S.MB = B_(f"MB{si}", [128, 2, 256]); S.MK = B_(f"MK{si}", [128, 2, 256])
                        S.TT = B_(f"TTf{si}", [128, 2, 128])
                        C.IS.append(S)
                    C.zlock = None
                    C.khT = B_("khT", [128, 128]); C.bhT = B_("bhT", [128, 128]); C.rkb = B_("rkb", [128, 128])
                    C.PQT = [B_(f"PQT{i}", [128, 2, 384]) for i in range(2)]
                    C.PQ0 = B_("PQ0", [128, 2, 384])
                    C.Xs = B_("Xs", [128, 128]); C.Us = B_("Us", [128, 128]); C.ybf = B_("ybf", [128, 128])
                    C.st6 = (SB(f"st6{ci}", [128, 2, 6], F32), Buf())
                    C.mv = (SB(f"mv{ci}", [128, 2, 2], F32), Buf())
                    C.rstd = (SB(f"rstd{ci}", [128, 2], F32), Buf())
                    C.Hs = SB(f"Hs{ci}", [128, 128], F32); C.Hb = SB(f"Hb{ci}", [128, 128], BF16)
                    C.Hsb = Buf(); C.Hbb = Buf()
                    C.wcp = [(SB(f"wcp{ci}{t_}", [128, 16, 128], BF16), Buf()) for t_ in range(4)]
                    C.lnw = SB(f"lnw{ci}", [128, 128], F32); C.lnb = SB(f"lnb{ci}", [128, 128], F32); C.lnbuf = Buf()
                    C.bX, C.bY, C.bZ = psA[3 * ci], psA[3 * ci + 1], psA[3 * ci + 2]
                    C.yn = C.ytile
                    C.prep_done = 0
                    C.inv_done = 0
                    C.back_done = 0
                    ctxs.append(C)

                K.dma(sp, bones[:], c_bones[:, :], w=[cb2])
                K.dma(pool, msc[:].rearrange("p a b -> p (a b)"), c_msc[:, :], w=[cb2])
                K.dma(pool, mlow2[:].rearrange("p a b -> p (a b)"), c_mlow128[:, :], w=[cb2])
                K.dma(pool, id2[:].rearrange("p a b -> p (a b)"), c_id2[:, :], w=[cb2])
                for C in ctxs:
                    K.op(act, lambda e: e.copy(out=C.PQ0[0][:, :, 256:384], in_=id2[:]), r=[cb2], w=[C.PQ0[1]])
                K.dma(pool, hind[:], c_hind[:, :], w=[cb2])
                K.dma(pool, wup[0:64, :], rwkv_w_up[j], w=[wupb])
                K.dma(pool, wup[64:128, :], rwkv_a_up[j], w=[wupb])
                with nc.allow_non_contiguous_dma(reason="tiny per-feature vectors"):
                    for vi, src in enumerate((rwkv_w0, rwkv_a0, rwkv_k_k, rwkv_k_a, rwkv_r_k)):
                        K.dma(sp, vecs[:, vi, :], src[j, :].rearrange("(c p) -> p c", p=128), w=[vecb])
                K.op(dve, lambda e: e.tensor_scalar(out=vecs[:, 5, :], in0=vecs[:, 3, :], scalar1=-1.0, scalar2=1.0,
                                                    op0=ALU.mult, op1=ALU.add), r=[vecb], w=[vecb])
                K.op(dve, lambda e: e.tensor_scalar(out=vecs[:, 6, :], in0=vecs[:, 0, :], scalar1=-1.0, scalar2=None,
                                                    op0=ALU.mult), r=[vecb], w=[vecb])
                K.op(dve, lambda e: e.tensor_scalar(out=vecs[:, 7, :], in0=vecs[:, 1, :], scalar1=-1.0, scalar2=None,
                                                    op0=ALU.mult), r=[vecb], w=[vecb])

                def load_w(col0, dst):
                    mb, rw = mub, wraw
                    K.dma(sp, mb[0][:, 0, :], rwkv_mu[j, col0:col0 + 128].partition_broadcast(128), w=[mb[1]])
                    K.dma(sp, rw[0][:], win[:, :, col0:col0 + 128], w=[rw[1]])
                    K.op(dve, lambda e: e.tensor_scalar(out=mb[0][:, 1, :], in0=mb[0][:, 0, :], scalar1=-1.0, scalar2=1.0,
                                                        op0=ALU.mult, op1=ALU.add), r=[mb[1]], w=[mb[1]])
                    for c in range(8):
                        K.op(dve, lambda e: e.tensor_tensor(out=dst[0][:, c, :], in0=rw[0][:, c, :], in1=mb[0][:, 1, :],
                                                            op=ALU.mult), r=[rw[1], mb[1]], w=[dst[1]])
                        K.op(dve, lambda e: e.tensor_tensor(out=dst[0][:, 8 + c, :], in0=rw[0][:, c, :], in1=mb[0][:, 0, :],
                                                            op=ALU.mult), r=[rw[1], mb[1]], w=[dst[1]])

                def proj_fm(out_ps, outb, wt, t0, n):
                    for c in range(16):
                        rhs = uT[:, c, 1 + t0:1 + t0 + n] if c < 8 else uT[:, c - 8, t0:t0 + n]
                        K.op(pe, lambda e: e.matmul(out_ps, lhsT=wt[0][:, c, :], rhs=rhs, start=(c == 0), stop=(c == 15)),
                             r=[uTb, wt[1]], w=[outb], inc=(c == 15))

                wwa = ctxs[0].wcp[0]
                load_w(4096, wwa)
                for ci_, (t0, n) in enumerate(tok_chunks(512)):
                    p_ = psA[ci_ % 2]
                    proj_fm(p_[0][:, 0:n], p_[1], wwa, t0, n)
                    K.op(act, lambda e: e.activation(out=wadT[0:64, t0:t0 + n], in_=p_[0][0:64, 0:n], func=AF.Tanh),
                         r=[p_[1]], w=[wadTb])
                    K.op(act, lambda e: e.copy(out=wadT[64:128, t0:t0 + n], in_=p_[0][64:128, 0:n]), r=[p_[1]], w=[wadTb])

                v3 = lambda t_: t_[0][:].rearrange("p (c s) -> p c s", c=2)
                flat = lambda ap: ap.rearrange("p a b -> p (a b)")
                one_b = epsb[:, 1:2]

                def sigmoid_chain(C, src_ps, srcb, bias_ap, dst, extra_r=()):
                    if bias_ap is None:
                        K.op(act, lambda e: e.activation(out=dst[0][:], in_=src_ps, func=AF.Exp, scale=-1.0),
                             r=[srcb] + list(extra_r), w=[dst[1]])
                    else:
                        K.op(act, lambda e: e.activation(out=dst[0][:], in_=src_ps, func=AF.Exp, bias=bias_ap, scale=-1.0),
                             r=[srcb] + list(extra_r), w=[dst[1]])
                    K.op(act, lambda e: e.activation(out=dst[0][:], in_=dst[0][:], func=AF.Ln, bias=one_b, scale=1.0),
                         r=[dst[1], ssb], w=[dst[1]])
                    K.op(act, lambda e: e.activation(out=dst[0][:], in_=dst[0][:], func=AF.Exp, scale=-1.0),
                         r=[dst[1]], w=[dst[1]])

                def prep(C, p):
                    vcol = lambda vi: vecs[:, vi, p:p + 1]
                    wr, wk, wv, wg = C.wcp
                    bX = bY = C.bZ
                    for i in range(NT):
                        while C.back_done < i - 2:
                            yield False
                        S = C.PS[i % 3]
                        while C.zlock is not None and C.zlock != "prep":
                            yield False
                        C.zlock = "prep"
                        t0 = 128 * i
                        rf, kf, sgw, av, lw, cm, cmx, E1, E2, E3 = C.rf, C.kf, C.sgw, C.av, C.lw, C.cm, C.cmx, C.E1, C.E2, C.E3
                        kkr, sq, hsn, kk, t1, kp, bb, ke3, be3, gs = C.kkr, C.sq, C.hsn, C.kk, C.t1, C.kp, C.bb, C.ke3, C.be3, C.gs
                        AR, BK = S.AR, S.BK
                        proj_fm(bX[0][:, 0:128], bX[1], wr, t0, 128)
                        proj_fm(bX[0][:, 128:256], bX[1], wk, t0, 128)
                        proj_fm(bX[0][:, 256:384], bX[1], wg, t0, 128)
                        for c in range(16):
                            lhsT = uT[:, c, 1 + t0:1 + t0 + 128] if c < 8 else uT[:, c - 8, t0:t0 + 128]
                            K.op(pe, lambda e: e.matmul(bX[0][:, 384:512], lhsT=lhsT, rhs=wv[0][:, c, :],
                                                        start=(c == 0), stop=(c == 15)),
                                 r=[uTb, wv[1]], w=[bX[1]], inc=(c == 15))
                        yield True
                        K.op(act, lambda e: e.copy(out=rf[0][:], in_=bX[0][:, 0:128]), r=[bX[1]], w=[rf[1]])
                        K.op(act, lambda e: e.copy(out=kf[0][:], in_=bX[0][:, 128:256]), r=[bX[1]], w=[kf[1]])
                        sigmoid_chain(C, bX[0][:, 256:384], bX[1], None, gs)
                        K.op(dve, lambda e: e.tensor_tensor(out=S.sg[0][:], in0=bX[0][:, 256:384], in1=gs[0][:], op=ALU.mult),
                             r=[bX[1], gs[1]], w=[S.sg[1]])
                        K.op(dve, lambda e: e.tensor_copy(out=S.vtok[0][:], in_=bX[0][:, 384:512]), r=[bX[1]], w=[S.vtok[1]])
                        K.op(pe, lambda e: e.matmul(bY[0][:, 0:128], lhsT=wup[0:64, 128 * p:128 * p + 128],
                                                    rhs=wadT[0:64, t0:t0 + 128], start=True, stop=True),
                             r=[wupb, wadTb], w=[bY[1]])
                        K.op(pe, lambda e: e.matmul(bY[0][:, 128:256], lhsT=wup[64:128, 128 * p:128 * p + 128],
                                                    rhs=wadT[64:128, t0:t0 + 128], start=True, stop=True),
                             r=[wupb, wadTb], w=[bY[1]])
                        yield True
                        sigmoid_chain(C, bY[0][:, 0:128], bY[1], vcol(6), sgw, extra_r=[vecb])
                        sigmoid_chain(C, bY[0][:, 128:256], bY[1], vcol(7), av, extra_r=[vecb])
                        K.op(dve, lambda e: e.tensor_scalar(out=lw[0][:], in0=sgw[0][:], scalar1=-DECAY_SCALE, scalar2=None,
                                                            op0=ALU.mult), r=[sgw[1]], w=[lw[1]])
                        K.op(dve, lambda e: e.tensor_tensor_scan(out=cm[0][:], data0=ones_f[:], data1=lw[0][:], initial=0.0,
                                                                 op0=ALU.mult, op1=ALU.add), r=[lw[1], constb], w=[cm[1]])
                        K.op(dve, lambda e: e.tensor_tensor(out=cmx[0][:], in0=cm[0][:], in1=lw[0][:], op=ALU.subtract),
                             r=[cm[1], lw[1]], w=[cmx[1]])
                        yield True
                        K.op(act, lambda e: e.activation(out=E1[0][:], in_=cm[0][:], func=AF.Exp), r=[cm[1]], w=[E1[1]])
                        K.op(act, lambda e: e.activation(out=E2[0][:], in_=cmx[0][:], func=AF.Exp), r=[cmx[1]], w=[E2[1]])
                        K.op(act, lambda e: e.activation(out=E3[0][:], in_=cm[0][:], func=AF.Exp, scale=-1.0),
                             r=[cm[1]], w=[E3[1]])
                        K.op(act, lambda e: e.activation(out=S.GC[0][:, 0:1], in_=cm[0][:, 127:128], func=AF.Exp),
                             r=[cm[1]], w=[S.GC[1]])
                        K.op(dve, lambda e: e.tensor_scalar(out=kkr[0][:], in0=kf[0][:], scalar1=vcol(2), scalar2=None,
                                                            op0=ALU.mult), r=[kf[1], vecb], w=[kkr[1]])
                        K.op(dve, lambda e: e.tensor_tensor(out=sq[0][:], in0=kkr[0][:], in1=kkr[0][:], op=ALU.mult),
                             r=[kkr[1]], w=[sq[1]])
                        K.op(pe, lambda e: e.matmul(bY[0][:, 256:384], lhsT=bones[:], rhs=sq[0][:], start=True, stop=True),
                             r=[cb2, sq[1]], w=[bY[1]])
                        yield True
                        K.op(dve, lambda e: e.tensor_scalar(out=hsn[0][:], in0=bY[0][:, 256:384], scalar1=1e-24, scalar2=None,
                                                            op0=ALU.max), r=[bY[1]], w=[hsn[1]])
                        K.op(act, lambda e: e.activation(out=hsn[0][:], in_=hsn[0][:], func=AF.Ln), r=[hsn[1]], w=[hsn[1]])
                        K.op(act, lambda e: e.activation(out=hsn[0][:], in_=hsn[0][:], func=AF.Exp, scale=-0.5),
                             r=[hsn[1]], w=[hsn[1]])
                        K.op(dve, lambda e: e.tensor_scalar(out=t1[0][:], in0=av[0][:], scalar1=vcol(3), scalar2=vcol(5),
                                                            op0=ALU.mult, op1=ALU.add), r=[av[1], vecb], w=[t1[1]])
                        K.op(dve, lambda e: e.tensor_tensor(out=kp[0][:], in0=kf[0][:], in1=t1[0][:], op=ALU.mult),
                             r=[kf[1], t1[1]], w=[kp[1]])
                        K.op(dve, lambda e: e.tensor_tensor(out=AR[0][:, 1, :], in0=rf[0][:], in1=E1[0][:], op=ALU.mult),
                             r=[rf[1], E1[1]], w=[AR[1]])
                        K.op(dve, lambda e: e.tensor_tensor(out=ke3[0][:], in0=kp[0][:], in1=E3[0][:], op=ALU.mult),
                             r=[kp[1], E3[1]], w=[ke3[1]])
                        yield True
                        K.op(dve, lambda e: e.tensor_tensor(out=kk[0][:], in0=kkr[0][:], in1=hsn[0][:], op=ALU.mult),
                             r=[kkr[1], hsn[1]], w=[kk[1]])
                        K.op(dve, lambda e: e.tensor_tensor(out=bb[0][:], in0=kk[0][:], in1=av[0][:], op=ALU.mult),
                             r=[kk[1], av[1]], w=[bb[1]])
                        K.op(dve, lambda e: e.scalar_tensor_tensor(out=AR[0][:, 0, :], in0=kk[0][:], scalar=-1.0, in1=E2[0][:],
                                                                   op0=ALU.mult, op1=ALU.mult), r=[kk[1], E2[1]], w=[AR[1]])
                        K.op(dve, lambda e: e.tensor_tensor(out=be3[0][:], in0=bb[0][:], in1=E3[0][:], op=ALU.mult),
                             r=[bb[1], E3[1]], w=[be3[1]])
                        K.op(act, lambda e: e.copy(out=BK[0][:, 1, :], in_=ke3[0][:]), r=[ke3[1]], w=[BK[1]])
                        K.op(act, lambda e: e.copy(out=BK[0][:, 0, :], in_=be3[0][:]), r=[be3[1]], w=[BK[1]])
                        yield True
                        K.op(act, lambda e: e.copy(out=S.ARz[0][0:64, 0, :, :], in_=AR[0][0:64, :, :]), r=[AR[1]], w=[S.ARz[1]])
                        K.op(act, lambda e: e.copy(out=S.ARz[0][64:128, 1, :, :], in_=AR[0][64:128, :, :]), r=[AR[1]], w=[S.ARz[1]])
                        K.op(act, lambda e: e.copy(out=S.BKz[0][0:64, 0, :], in_=BK[0][0:64, 0, :]), r=[BK[1]], w=[S.BKz[1]])
                        K.op(act, lambda e: e.copy(out=S.BKz[0][64:128, 1, :], in_=BK[0][64:128, 0, :]), r=[BK[1]], w=[S.BKz[1]])
                        K.op(dve, lambda e: e.tensor_scalar(out=C.khT[0][:], in0=ke3[0][:], scalar1=S.GC[0][:, 0:1], scalar2=None,
                                                            op0=ALU.mult), r=[ke3[1], S.GC[1]], w=[C.khT[1]])
                        K.op(dve, lambda e: e.tensor_scalar(out=C.bhT[0][:], in0=be3[0][:], scalar1=S.GC[0][:, 0:1], scalar2=None,
                                                            op0=ALU.mult), r=[be3[1], S.GC[1]], w=[C.bhT[1]])
                        K.op(dve, lambda e: e.scalar_tensor_tensor(out=C.rkb[0][:], in0=rf[0][:], scalar=vcol(4), in1=kp[0][:],
                                                                   op0=ALU.mult, op1=ALU.mult), r=[rf[1], kp[1], vecb], w=[C.rkb[1]])
                        K.op(pe, lambda e: e.matmul(bY[0][:, 384:386], lhsT=C.rkb[0][:], rhs=hind[:], start=True, stop=True),
                             r=[C.rkb[1], cb2], w=[bY[1]])
                        K.op(pe, lambda e: e.transpose(out=psT[0][:, 4 + 2 * C.ci, :], in_=C.khT[0][:], identity=ident[:]),
                             r=[C.khT[1], identb], w=[psT[1]], inc=False)
                        K.op(pe, lambda e: e.transpose(out=psT[0][:, 5 + 2 * C.ci, :], in_=C.bhT[0][:], identity=ident[:]),
                             r=[C.bhT[1], identb], w=[psT[1]])
                        yield True
                        K.op(dve, lambda e: e.tensor_copy(out=S.bon[0][:], in_=bY[0][:, 384:386]), r=[bY[1]], w=[S.bon[1]])
                        K.op(act, lambda e: e.copy(out=S.kbhat[0][:], in_=psT[0][:, 4 + 2 * C.ci:6 + 2 * C.ci, :]), r=[psT[1]], w=[S.kbhat[1]])
                        C.zlock = None
                        C.prep_done = i + 1
                        yield True

                def inv(C, p):
                    bX, bY = C.bX, C.bY
                    for i in range(NT):
                        while C.prep_done < i + 1 or C.back_done < i - 1:
                            yield False
                        S = C.PS[i % 3]
                        I_ = C.IS[i % 2]
                        AR, BK, MB, MK = S.AR, S.BK, I_.MB, I_.MK
                        arz = S.ARz[0][:].rearrange("p h s t -> p (h s t)")
                        K.op(pe, lambda e: e.matmul(bX[0][:, 0:512], lhsT=BK[0][:, 0, :], rhs=arz, start=True, stop=True),
                             r=[BK[1], S.ARz[1]], w=[bX[1]])
                        K.op(pe, lambda e: e.matmul(bY[0][:, 0:512], lhsT=BK[0][:, 1, :], rhs=arz, start=True, stop=True),
                             r=[BK[1], S.ARz[1]], w=[bY[1]])
                        yield True
                        P0 = C.PQ0
                        K.op(dve, lambda e: e.tensor_tensor(out=MB[0][:].rearrange("p h c -> p (h c)"), in0=bX[0][:, 0:512],
                                                            in1=msc[:].rearrange("p h c -> p (h c)"), op=ALU.mult),
                             r=[bX[1], cb2], w=[MB[1]])
                        K.op(pe, lambda e: e.matmul(bX[0][:, 0:256], lhsT=AR[0][:, 0, :], rhs=S.BKz[0][:].rearrange("p h s -> p (h s)"),
                                                    start=True, stop=True), r=[AR[1], S.BKz[1]], w=[bX[1]])
                        K.op(dve, lambda e: e.tensor_tensor(out=MK[0][:].rearrange("p h c -> p (h c)"), in0=bY[0][:, 0:512],
                                                            in1=msc[:].rearrange("p h c -> p (h c)"), op=ALU.mult),
                             r=[bY[1], cb2], w=[MK[1]])
                        yield True
                        K.op(dve, lambda e: e.tensor_tensor(out=P0[0][:, :, 0:128], in0=bX[0][:, 0:256].rearrange("p (h s) -> p h s", h=2),
                                                            in1=mlow2[:], op=ALU.mult), r=[bX[1], cb2], w=[P0[1]])
                        K.op(act, lambda e: e.copy(out=P0[0][:, :, 128:256], in_=MB[0][:, :, 0:128]), r=[MB[1]], w=[P0[1]])
                        yield True
                        for stp_ in range(1, 8):
                            prev = C.PQ0 if stp_ == 1 else C.PQT[(stp_ - 1) % 2]
                            cur = C.PQT[stp_ % 2]
                            last = (stp_ == 7)
                            for hd in range(2):
                                bk = bX if hd == 0 else bY
                                Pm = prev[0][:, hd, 0:128]
                                Qm = prev[0][:, hd, 128:256]
                                if not last:
                                    K.op(pe, lambda e: e.matmul(bk[0][:, 0:128], lhsT=Qm, rhs=Pm, start=True, stop=True),
                                         r=[prev[1]], w=[bk[1]], inc=False)
                                    K.op(pe, lambda e: e.matmul(bk[0][:, 128:384], lhsT=Pm, rhs=prev[0][:, hd, 128:384], start=True, stop=False),
                                         r=[prev[1]], w=[bk[1]], inc=False)
                                else:
                                    K.op(pe, lambda e: e.matmul(bk[0][:, 256:384], lhsT=Pm, rhs=prev[0][:, hd, 256:384], start=True, stop=False),
                                         r=[prev[1]], w=[bk[1]], inc=False)
                                K.op(pe, lambda e: e.matmul(bk[0][:, 256:384], lhsT=ident[:], rhs=prev[0][:, hd, 256:384], start=False, stop=True),
                                     r=[prev[1], identb], w=[bk[1]])
                            yield True
                            for hd in range(2):
                                bk = bX if hd == 0 else bY
                                if not last:
                                    dst_, src_ = cur[0][:, hd, :], bk[0][:, 0:384]
                                    dstb_ = cur[1]
                                else:
                                    dst_, src_ = I_.TT[0][:, hd, :], bk[0][:, 256:384]
                                    dstb_ = I_.TT[1]
                                if hd == 0:
                                    K.op(act, lambda e: e.copy(out=dst_, in_=src_), r=[bk[1]], w=[dstb_])
                                else:
                                    K.op(dve, lambda e: e.tensor_copy(out=dst_, in_=src_), r=[bk[1]], w=[dstb_])
                            yield True
                        C.inv_done = i + 1
                        yield True

                def back(C, p):
                    bZ = C.bZ
                    Hs, Hb, Hsb, Hbb = C.Hs, C.Hb, C.Hsb, C.Hbb
                    Xs, Us, ytile, yn, ybf = C.Xs, C.Us, C.ytile, C.yn, C.ybf
                    for i in range(NT):
                        while C.inv_done < i + 1:
                            yield False
                        S = C.PS[i % 3]
                        I_ = C.IS[i % 2]
                        AR, MB, MK, vtok, kbhat, GC, Tf = S.AR, I_.MB, I_.MK, S.vtok, S.kbhat, S.GC, I_.TT
                        while C.zlock is not None and C.zlock != "back":
                            yield False
                        C.zlock = "back"
                        t0 = 128 * i
                        hc = lambda hd: slice(64 * hd, 64 * hd + 64)
                        K.op(pe, lambda e: e.matmul(bZ[0][:, 0:128], lhsT=AR[0][:, 0, :], rhs=Hb[:], start=True, stop=False, skip_group_check=True),
                             r=[AR[1], Hbb], w=[bZ[1]], inc=False)
                        for hd in range(2):
                            K.op(pe, lambda e: e.matmul(bZ[0][:, hc(hd)], lhsT=MK[0][:, hd, 0:128], rhs=vtok[0][:, hc(hd)],
                                                        start=False, stop=(hd == 1), skip_group_check=True),
                                 r=[MK[1], vtok[1]], w=[bZ[1]], inc=(hd == 1))
                        yield True
                        K.op(act, lambda e: e.copy(out=Xs[0][:], in_=bZ[0][:, 0:128]), r=[bZ[1]], w=[Xs[1]])
                        yield True
                        for hd in range(2):
                            K.op(pe, lambda e: e.matmul(bZ[0][:, 128 + 64 * hd:128 + 64 * hd + 64], lhsT=Tf[0][:, hd, :], rhs=Xs[0][:, hc(hd)],
                                                        start=True, stop=True), r=[Tf[1], Xs[1]], w=[bZ[1]], inc=(hd == 1))
                        yield True
                        K.op(dve, lambda e: e.tensor_copy(out=Us[0][:], in_=bZ[0][:, 128:256]), r=[bZ[1]], w=[Us[1]])
                        yield True
                        K.op(pe, lambda e: e.matmul(bZ[0][:, 256:384], lhsT=AR[0][:, 1, :], rhs=Hb[:], start=True, stop=False, skip_group_check=True),
                             r=[AR[1], Hbb], w=[bZ[1]], inc=False)
                        for hd in range(2):
                            o_ = bZ[0][:, 256 + 64 * hd:256 + 64 * hd + 64]
                            K.op(pe, lambda e: e.matmul(o_, lhsT=MB[0][:, hd, 128:256], rhs=Us[0][:, hc(hd)], start=False, stop=False,
                                                        skip_group_check=True), r=[MB[1], Us[1]], w=[bZ[1]], inc=False)
                            K.op(pe, lambda e: e.matmul(o_, lhsT=MK[0][:, hd, 128:256], rhs=vtok[0][:, hc(hd)], start=False, stop=(hd == 1),
                                                        skip_group_check=True), r=[MK[1], vtok[1]], w=[bZ[1]], inc=False)
                        for hd in range(2):
                            o2 = bZ[0][:, 384 + 64 * hd:384 + 64 * hd + 64]
                            K.op(pe, lambda e: e.matmul(o2, lhsT=kbhat[0][:, 1, :], rhs=Us[0][:, hc(hd)], start=True, stop=False),
                                 r=[kbhat[1], Us[1]], w=[bZ[1]], inc=False)
                            K.op(pe, lambda e: e.matmul(o2, lhsT=kbhat[0][:, 0, :], rhs=vtok[0][:, hc(hd)], start=False, stop=True),
                                 r=[kbhat[1], vtok[1]], w=[bZ[1]], inc=(hd == 1))
                        yield True
                        for hd in range(2):
                            lo = 64 * hd
                            K.op(dve, lambda e: e.scalar_tensor_tensor(out=Hs[lo:lo + 64, hc(hd)], in0=Hs[lo:lo + 64, hc(hd)], scalar=GC[0][lo:lo + 64, 0:1],
                                                                       in1=bZ[0][lo:lo + 64, 384 + 64 * hd:384 + 64 * hd + 64],
                                                                       op0=ALU.mult, op1=ALU.add), r=[Hsb, GC[1], bZ[1]], w=[Hsb])
                        K.op(dve, lambda e: e.tensor_copy(out=Hb[:], in_=Hs[:]), r=[Hsb], w=[Hbb])
                        K.op(dve, lambda e: e.tensor_copy(out=ytile[0][:], in_=bZ[0][:, 256:384]), r=[bZ[1]], w=[ytile[1]])
                        C.zlock = None
                        yield True
                        for hf in range(2):
                            K.op(dve, lambda e: e.bn_stats(out=C.st6[0][:, hf, :], in_=ytile[0][:, 64 * hf:64 * hf + 64]), r=[ytile[1]], w=[C.st6[1]])
                            K.op(dve, lambda e: e.bn_aggr(out=C.mv[0][:, hf, :], in_=C.st6[0][:, hf, :]), r=[C.st6[1]], w=[C.mv[1]])
                        yield True
                        K.op(act, lambda e: e.activation(out=C.rstd[0][:], in_=C.mv[0][:, :, 1], func=AF.Ln, bias=epsb[:, 2:3], scale=1.0),
                             r=[C.mv[1], ssb], w=[C.rstd[1]])
                        K.op(act, lambda e: e.activation(out=C.rstd[0][:], in_=C.rstd[0][:], func=AF.Exp, scale=-0.5), r=[C.rstd[1]], w=[C.rstd[1]])
                        yield True
                        for hf in range(2):
                            cs = slice(64 * hf, 64 * hf + 64)
                            K.op(dve, lambda e: e.tensor_scalar(out=yn[0][:, cs], in0=ytile[0][:, cs], scalar1=C.mv[0][:, hf, 0:1],
                                                                scalar2=C.rstd[0][:, hf:hf + 1], op0=ALU.subtract, op1=ALU.mult),
                                 r=[ytile[1], C.mv[1], C.rstd[1]], w=[yn[1]])
                        K.op(dve, lambda e: e.tensor_tensor(out=yn[0][:], in0=yn[0][:], in1=C.lnw[:], op=ALU.mult),
                             r=[yn[1], C.lnbuf], w=[yn[1]])
                        K.op(dve, lambda e: e.tensor_tensor(out=yn[0][:], in0=yn[0][:], in1=C.lnb[:], op=ALU.add),
                             r=[yn[1], C.lnbuf], w=[yn[1]])
                        for hf in range(2):
                            cs = slice(64 * hf, 64 * hf + 64)
                            K.op(dve, lambda e: e.scalar_tensor_tensor(out=ybf[0][:, cs], in0=vtok[0][:, cs], scalar=S.bon[0][:, hf:hf + 1],
                                                                       in1=yn[0][:, cs], op0=ALU.mult, op1=ALU.add),
                                 r=[vtok[1], S.bon[1], yn[1]], w=[ybf[1]])
                        yield True
                        K.op(pe, lambda e: e.transpose(out=psT[0][:, 2 + C.ci, :], in_=ybf[0][:], identity=ident[:]), r=[ybf[1], identb], w=[psT[1]])
                        yield True
                        K.op(dve, lambda e: e.tensor_tensor(out=ogT[:, p, t0:t0 + 128], in0=psT[0][:, 2 + C.ci, :], in1=S.sg[0][:], op=ALU.mult),
                             r=[psT[1], S.sg[1]], w=[ogTb])
                        C.back_done = i + 1
                        yield True

                def pair_stream(C):
                    for p in range(C.ci, 8, 2):
                        for t_ in range(4):
                            load_w(1024 * t_ + 128 * p, C.wcp[t_])
                            yield True
                        K.dma(sp, C.lnw[:], rwkv_ln_w[j, 128 * p:128 * p + 128].partition_broadcast(128), w=[C.lnbuf])
                        K.dma(sp, C.lnb[:], rwkv_ln_b[j, 128 * p:128 * p + 128].partition_broadcast(128), w=[C.lnbuf])
                        K.op(dve, lambda e: e.memset(C.Hs[:], 0.0), w=[C.Hsb])
                        K.op(dve, lambda e: e.memset(C.Hb[:], 0.0), w=[C.Hbb])
                        C.prep_done = 0
                        C.inv_done = 0
                        C.back_done = 0
                        C.zlock = None
                        gens = [prep(C, p), inv(C, p), back(C, p)]
                        while gens:
                            progressed = False
                            for g_ in list(gens):
                                try:
                                    if next(g_):
                                        progressed = True
                                except StopIteration:
                                    gens.remove(g_)
                                    progressed = True
                            yield progressed

                streams = [pair_stream(C) for C in ctxs]
                while streams:
                    for s_ in list(streams):
                        try:
                            next(s_)
                        except StopIteration:
                            streams.remove(s_)
                K.barrier()


        for layer in range(nlayers):
            phase_prenorm(layer)
            if layer % 2 == 0:
                fox_layer(layer)
                phase_post(layer, fox_w_out[layer // 2], layer == nlayers - 1)
            else:
                rwkv_layer(layer)
                phase_post(layer, rwkv_w_out[layer // 2], layer == nlayers - 1)
            K.barrier()
        K.barrier()
    return nc


_CACHE = {}


def _consts():
    idx = np.arange(128)
    tri = (idx[:, None] <= idx[None, :]).astype(np.float32)
    ident = np.eye(128, dtype=np.float32)
    ones = np.ones((128, 128), np.float32)
    m64 = np.zeros((128, 128), np.float32)
    s = idx[:, None] % 64
    t = idx[None, :] % 64
    m64[:, 0:64] = (s < t)[:, 0:64]
    m64[:, 64:128] = (s <= t)[:, 64:128]
    scan = np.ones((128, 512), np.float32)
    scan[:, ::64] = 0.0
    hind = np.zeros((128, 2), np.float32)
    hind[0:64, 0] = 1.0
    hind[64:128, 1] = 1.0
    m64x2 = np.concatenate([m64, m64], axis=1)
    r64 = idx[:, None] % 64
    c64 = np.arange(64)[None, :]
    mlow = (c64 < r64).astype(np.float32)
    mlow2 = np.concatenate([mlow, mlow], axis=1)
    i64 = (c64 == r64).astype(np.float32)
    idx2 = np.concatenate([i64, i64], axis=1)
    bones = ((idx[:, None] // 64) == (idx[None, :] // 64)).astype(np.float32)
    strict = (idx[:, None] < idx[None, :]).astype(np.float32)
    msc1 = np.concatenate([strict, tri], axis=1)
    msc = np.concatenate([msc1, msc1], axis=1)
    mlow128 = (idx[None, :] < idx[:, None]).astype(np.float32)
    return dict(c_ident=ident, c_tri=tri, c_ones=ones, c_m64=m64, c_scan=scan, c_hind=hind,
                c_m64x2=m64x2, c_mlow2=mlow2, c_idx2=idx2, c_bones=bones,
                c_msc=msc, c_mlow128=np.concatenate([mlow128, mlow128], axis=1),
                c_id2=np.concatenate([ident, ident], axis=1))


def kernel(x, meta_tokens, norm_pre, norm_post, fox_w_in, fox_b_f, fox_w_out,
           rwkv_w_in, rwkv_mu, rwkv_w0, rwkv_w_up, rwkv_a0, rwkv_a_up, rwkv_k_k,
           rwkv_k_a, rwkv_r_k, rwkv_ln_w, rwkv_ln_b, rwkv_w_out, _nlayers=DEPTH):
    f = lambda a: np.ascontiguousarray(np.asarray(a, dtype=np.float32))
    x = f(x)
    B = x.shape[0]
    meta = f(meta_tokens)
    h0 = np.zeros((B, T, D), np.float32)
    h0[:, :NMETA] = meta[None]
    h0[:, NMETA:NMETA + SEQ] = x
    shared = dict(
        norm_pre=f(norm_pre), norm_post=f(norm_post), fox_w_in=f(fox_w_in), fox_b_f=f(fox_b_f),
        fox_w_out=f(fox_w_out), rwkv_w_in=f(rwkv_w_in), rwkv_mu=f(rwkv_mu), rwkv_w0=f(rwkv_w0),
        rwkv_w_up=f(rwkv_w_up), rwkv_a0=f(rwkv_a0), rwkv_a_up=f(rwkv_a_up), rwkv_k_k=f(rwkv_k_k),
        rwkv_k_a=f(rwkv_k_a), rwkv_r_k=f(rwkv_r_k).reshape(2, D), rwkv_ln_w=f(rwkv_ln_w),
        rwkv_ln_b=f(rwkv_ln_b), rwkv_w_out=f(rwkv_w_out))
    shared.update(_consts())
    key = _nlayers
    if key not in _CACHE:
        _CACHE[key] = build_program(_nlayers)
    nc = _CACHE[key]
    in_maps = []
    for b in range(B):
        m = dict(shared)
        m["h0"] = h0[b]
        in_maps.append(m)
    res = run_bass_kernel_spmd(nc, in_maps, core_ids=list(range(B)))
    out = np.stack([np.asarray(r["y"])[NMETA:NMETA + SEQ] for r in res.results], axis=0)
    return out.astype(np.float32)
```

```python
import math
from contextlib import ExitStack

import numpy as np
import concourse.bass as bass
import concourse.mybir as mybir
from concourse.bass_utils import run_bass_kernel_spmd

F32 = mybir.dt.float32
BF16 = mybir.dt.bfloat16
AF = mybir.ActivationFunctionType
ALU = mybir.AluOpType
AX = mybir.AxisListType

D = 1024
SEQ = 2048
NMETA = 16
NT = 17
T = NT * 128
DEPTH = 4
NH = 16
HD = 64
FOX_IN = 4 * D + NH
RWKV_IN = 4 * D + 128
NORM_EPS = 1e-6
GN_EPS = 64e-5
DECAY_SCALE = math.exp(-0.5)
CH = 64
DBG = {"stage": 9, "pairs": 8, "tiles": NT}


def tok_chunks(n=512):
    out = []
    t0 = 0
    while t0 < T:
        m = min(n, T - t0)
        out.append((t0, m))
        t0 += m
    return out


class Buf:
    __slots__ = ("name", "lw", "rd", "excl")

    def __init__(self, name="", excl=False):
        self.name = name
        self.lw = None
        self.rd = {}
        self.excl = excl


class Q:
    def __init__(self, name, eng, sem, self_sync=True):
        self.name = name
        self.eng = eng
        self.sem = sem
        self.cnt = 0
        self.seen = {}
        self.self_sync = self_sync
        self.key = name
        self.ring = []
        self.ring_i = 0


class PEProxy:
    def __init__(self, K, eng):
        self.K = K
        self.eng = eng
        self.partial = False

    def _pre(self, st_ap, out):
        K = self.K
        rg = (st_ap.base_partition(), st_ap.partition_size())
        bank = out.name
        last = K.pe_last
        if last is not None and last[0] != rg and (last[1] == bank or (last[0][1] < 128 and rg[1] < 128)):
            assert K.pe_last_tk is not None, "previous matmul needs a semaphore increment"
            K._wait(K.pe, K.pe_last_tk)
        K.pe_last = (rg, bank)
        self.partial = rg[1] < 128

    def matmul(self, out, lhsT, rhs, **kw):
        self._pre(lhsT, out)
        return self.eng.matmul(out, lhsT=lhsT, rhs=rhs, **kw)

    def transpose(self, out, in_, identity):
        self._pre(in_, out)
        return self.eng.transpose(out=out, in_=in_, identity=identity)


class KB:
    def __init__(self, nc, st):
        self.nc = nc
        self.st = st
        mk = lambda n: st.enter_context(nc.semaphore(n))
        self.pe = Q("pe", nc.tensor, mk("s_pe"), self_sync=False)
        self.act = Q("act", nc.scalar, mk("s_act"))
        self.dve = Q("dve", nc.vector, mk("s_dve"))
        self.pool = Q("pool", nc.gpsimd, mk("s_pool"))
        self.sp = Q("sp", nc.sync, mk("s_sp"))
        self.queues = [self.pe, self.act, self.dve, self.pool, self.sp]
        for q, n in ((self.sp, 24), (self.pool, 24)):
            for i in range(n):
                q.ring.append([mk(f"d_{q.name}{i}"), 0, f"d_{q.name}{i}"])
        self.dma_tickets = []
        self.nbuf = 0
        self.counting = False
        self.nops = 0
        self.pe_last = None
        self.pe_last_tk = None
        self.snap = {}
        self.prox = PEProxy(self, nc.tensor)

    def sb(self, st, name, shape, dt):
        self.nbuf += 1
        return st.enter_context(self.nc.sbuf_tensor(f"{name}_{self.nbuf}", list(shape), dt))

    def psum(self, st, name, shape, dt):
        return st.enter_context(self.nc.psum_tensor(name, list(shape), dt))

    def _learn(self, q, tk):
        q.seen[tk[2]] = max(q.seen.get(tk[2], 0), tk[1])
        sn = self.snap.get((tk[2], tk[1]))
        if sn:
            for k2, v2 in sn.items():
                if q.seen.get(k2, 0) < v2:
                    q.seen[k2] = v2

    def _wait(self, q, tk):
        sem, val, key = tk
        if q.seen.get(key, 0) >= val:
            return
        q.eng.wait_ge(sem, val)
        self._learn(q, tk)

    def _deps(self, q, r, w, defer=False):
        deps = []
        for b in r:
            if b.lw is not None:
                deps.append(b.lw)
            if b.excl:
                for key, tk in b.rd.items():
                    if key != q.key:
                        deps.append(tk)
        for b in w:
            if b.lw is not None:
                deps.append(b.lw)
            for key, tk in b.rd.items():
                deps.append(tk)
        need = {}
        for tk in deps:
            if tk[2] == q.key and not q.self_sync:
                continue
            if q.seen.get(tk[2], 0) >= tk[1]:
                continue
            if tk[2] not in need or need[tk[2]][1] < tk[1]:
                need[tk[2]] = tk
        pend = list(need.values())
        last = pend.pop() if (defer and pend) else None
        for tk in pend:
            self._wait(q, tk)
        return last

    def _record(self, tk, r, w):
        for b in r:
            old = b.rd.get(tk[2])
            if old is None or old[1] < tk[1]:
                b.rd[tk[2]] = tk
        for b in w:
            b.lw = tk
            b.rd = {}

    def op(self, q, fn, r=(), w=(), inc=True):
        if self.counting:
            self.nops += 1
            if self.nops > DBG.get("maxops", 10 ** 9):
                return None
        last = self._deps(q, r, w, defer=(q is not self.pe))
        if q is self.pe:
            self.prox.partial = False
            ins = fn(self.prox)
            if self.prox.partial:
                inc = True
        else:
            ins = fn(q.eng)
            if last is not None:
                ins._wait_ge(last[0], last[1])
                self._learn(q, last)
        if inc:
            q.cnt += 1
            ins.then_inc(q.sem, 1)
            tk = (q.sem, q.cnt, q.key)
            self.snap[(q.key, q.cnt)] = dict(q.seen)
        else:
            tk = (q.sem, q.cnt + 1, q.key)
        if q is self.pe:
            self.pe_last_tk = tk if inc else None
        self._record(tk, r, w)
        return tk

    def dma(self, q, out, in_, r=(), w=()):
        self._deps(q, r, w)
        slot = q.ring[q.ring_i % len(q.ring)]
        q.ring_i += 1
        sem, n, key = slot
        if n > 0:
            self._wait(q, (sem, 16 * n, key))
        q.eng.dma_start(out=out, in_=in_).then_inc(sem, 16)
        slot[1] = n + 1
        tk = (sem, 16 * (n + 1), key)
        self.snap[(key, 16 * (n + 1))] = dict(q.seen)
        self._record(tk, r, w)
        self.dma_tickets.append(tk)
        return tk

    def barrier(self):
        tks = [(q.sem, q.cnt, q.key) for q in self.queues if q.cnt > 0]
        for q in self.queues:
            for slot in q.ring:
                if slot[1] > 0:
                    tks.append((slot[0], 16 * slot[1], slot[2]))
        for q in self.queues:
            for tk in tks:
                if tk[2] == q.key:
                    continue
                self._wait(q, tk)


def build_program(nlayers=DEPTH, dbg=False):
    nc = bass.Bass("TRN2", target_bir_lowering=False)
    dt_in = lambda name, shape: nc.dram_tensor(name, list(shape), F32, kind="ExternalInput").ap()
    h0 = dt_in("h0", [T, D])
    norm_pre = dt_in("norm_pre", [DEPTH, D])
    norm_post = dt_in("norm_post", [DEPTH, D])
    fox_w_in = dt_in("fox_w_in", [2, D, FOX_IN])
    fox_b_f = dt_in("fox_b_f", [2, NH])
    fox_w_out = dt_in("fox_w_out", [2, D, D])
    rwkv_w_in = dt_in("rwkv_w_in", [2, D, RWKV_IN])
    rwkv_mu = dt_in("rwkv_mu", [2, RWKV_IN])
    rwkv_w0 = dt_in("rwkv_w0", [2, D])
    rwkv_w_up = dt_in("rwkv_w_up", [2, 64, D])
    rwkv_a0 = dt_in("rwkv_a0", [2, D])
    rwkv_a_up = dt_in("rwkv_a_up", [2, 64, D])
    rwkv_k_k = dt_in("rwkv_k_k", [2, D])
    rwkv_k_a = dt_in("rwkv_k_a", [2, D])
    rwkv_r_k = dt_in("rwkv_r_k", [2, D])
    rwkv_ln_w = dt_in("rwkv_ln_w", [2, D])
    rwkv_ln_b = dt_in("rwkv_ln_b", [2, D])
    rwkv_w_out = dt_in("rwkv_w_out", [2, D, D])
    c_ident = dt_in("c_ident", [128, 128])
    c_tri = dt_in("c_tri", [128, 128])
    c_ones = dt_in("c_ones", [128, 128])
    c_m64 = dt_in("c_m64", [128, 128])
    c_scan = dt_in("c_scan", [128, 512])
    c_hind = dt_in("c_hind", [128, 2])
    c_m64x2 = dt_in("c_m64x2", [128, 256])
    c_mlow2 = dt_in("c_mlow2", [128, 128])
    c_idx2 = dt_in("c_idx2", [128, 128])
    c_bones = dt_in("c_bones", [128, 128])
    c_msc = dt_in("c_msc", [128, 512])
    c_mlow128 = dt_in("c_mlow128", [128, 256])
    c_id2 = dt_in("c_id2", [128, 256])
    y = nc.dram_tensor("y", [T, D], F32, kind="ExternalOutput").ap()

    with ExitStack() as st:
        K = KB(nc, st)
        pe, act, dve, pool, sp = K.pe, K.act, K.dve, K.pool, K.sp

        uT = K.sb(st, "uT", [128, 8, T + 2], BF16)
        uTb = Buf("uT")
        ogT = K.sb(st, "ogT", [128, 8, T], BF16)
        ogTb = Buf("ogT")
        ident = K.sb(st, "ident", [128, 128], BF16)
        identb = Buf()
        tri_f = K.sb(st, "tri_f", [128, 128], F32)
        ones_f = K.sb(st, "ones_f", [128, 128], F32)
        tri_b = K.sb(st, "tri_b", [128, 128], BF16)
        constb = Buf()
        gpre = K.sb(st, "gpre", [128, D], F32)
        gpost = K.sb(st, "gpost", [128, D], F32)
        gb = Buf()
        hbufs = [(K.sb(st, f"hb{i}", [128, D], F32), Buf()) for i in range(2)]
        hbufs2 = hbufs
        un = K.sb(st, "un", [128, D], BF16)
        unb = Buf()
        un2 = K.sb(st, "un2", [128, D], BF16)
        uns = [(un, unb), (un2, Buf())]
        sss = [(K.sb(st, f"ss{i}", [128, 4], F32), Buf()) for i in range(2)]
        mt = K.sb(st, "mt", [128, D], F32)
        mtb = Buf()
        ss = K.sb(st, "ss", [128, 4], F32)
        ssb = Buf()
        epsb = K.sb(st, "epsb", [128, 4], F32)
        K.op(dve, lambda e: e.memset(epsb[:, 0:1], NORM_EPS), w=[ssb])
        K.op(dve, lambda e: e.memset(epsb[:, 1:2], 1.0), w=[ssb])
        K.op(dve, lambda e: e.memset(epsb[:, 2:3], GN_EPS), w=[ssb])
        psA = [(K.psum(st, f"psA{i}", [128, 512], F32), Buf(excl=True)) for i in range(6)]
        psT = (K.psum(st, "psT", [128, 8, 128], BF16), Buf(excl=True))
        psT2 = (K.psum(st, "psT2", [128, 8, 128], BF16), Buf(excl=True))
        psTs = [psT, psT2]
        hB = [Buf(f"h{i}") for i in range(NT)]

        K.dma(pool, ident[:], c_ident[:, :], w=[identb])
        K.dma(pool, tri_b[:], c_tri[:, :], w=[constb])
        K.dma(sp, tri_f[:], c_tri[:, :], w=[constb])
        K.dma(sp, ones_f[:], c_ones[:, :], w=[constb])
        K.op(dve, lambda e: e.memset(uT[:, :, 0:1], 0.0), w=[uTb])

        def bcast_row(ap_row, n):
            return ap_row.partition_broadcast(128)

        def phase_prenorm(layer):
            K.dma(sp, gpre[:], bcast_row(norm_pre[layer, :], D), w=[gb])
            K.dma(sp, gpost[:], bcast_row(norm_post[layer, :], D), w=[gb])
            for i in range(NT):
                ht, htb = hbufs[i % 2]
                ss_, ssb_ = sss[i % 2]
                un_, unb_ = uns[i % 2]
                pT = psTs[i % 2]
                src = h0 if layer == 0 else y
                K.dma(sp, ht[:], src[128 * i:128 * i + 128, :], r=[hB[i]], w=[htb])
                K.op(act, lambda e: e.activation(out=mt[:], in_=ht[:], func=AF.Square, accum_out=ss_[:, 0:1]),
                     r=[htb], w=[mtb, ssb_])
                K.op(act, lambda e: e.activation(out=ss_[:, 1:2], in_=ss_[:, 0:1], func=AF.Ln, bias=epsb[:, 0:1], scale=1.0 / D),
                     r=[ssb_, ssb], w=[ssb_])
                K.op(act, lambda e: e.activation(out=ss_[:, 2:3], in_=ss_[:, 1:2], func=AF.Exp, scale=-0.5),
                     r=[ssb_], w=[ssb_])
                K.op(dve, lambda e: e.scalar_tensor_tensor(out=un_[:], in0=ht[:], scalar=ss_[:, 2:3], in1=gpre[:],
                                                           op0=ALU.mult, op1=ALU.mult), r=[htb, ssb_, gb], w=[unb_])
                for c in range(8):
                    K.op(pe, lambda e: e.transpose(out=pT[0][:, c, :], in_=un_[:, 128 * c:128 * c + 128],
                                                   identity=ident[:]),
                         r=[unb_, identb], w=[pT[1]], inc=(c == 7))
                if i % 2 == 0:
                    K.op(act, lambda e: e.copy(out=uT[:, :, 1 + 128 * i:1 + 128 * i + 128], in_=pT[0][:, :, :]),
                         r=[pT[1]], w=[uTb])
                else:
                    K.op(dve, lambda e: e.tensor_copy(out=uT[:, :, 1 + 128 * i:1 + 128 * i + 128], in_=pT[0][:, :, :]),
                         r=[pT[1]], w=[uTb])

        def phase_post(layer, w_out_ap, last):
            with ExitStack() as pst:
                wout = K.sb(pst, "wout", [128, 8, D], BF16)
                woutb = Buf()
                K.dma(pool, wout[:], w_out_ap.rearrange("(c p) n -> p c n", p=128), w=[woutb])
                for i in range(NT):
                    pms = [psA[0], psA[1]] if i % 2 == 0 else [psA[2], psA[3]]
                    ss_, ssb_ = sss[i % 2]
                    for hf in range(2):
                        for c in range(8):
                            K.op(pe, lambda e: e.matmul(pms[hf][0][:, :], lhsT=ogT[:, c, 128 * i:128 * i + 128],
                                                        rhs=wout[:, c, 512 * hf:512 * hf + 512],
                                                        start=(c == 0), stop=(c == 7)),
                                 r=[ogTb, woutb], w=[pms[hf][1]], inc=(c == 7))
                    for hf in range(2):
                        K.op(act, lambda e: e.activation(out=un[:, 512 * hf:512 * hf + 512], in_=pms[hf][0][:, :], func=AF.Square,
                                                         accum_out=ss_[:, hf:hf + 1]), r=[pms[hf][1]], w=[unb, ssb_])
                    K.op(dve, lambda e: e.tensor_tensor(out=ss_[:, 2:3], in0=ss_[:, 0:1], in1=ss_[:, 1:2], op=ALU.add),
                         r=[ssb_], w=[ssb_])
                    K.op(act, lambda e: e.activation(out=ss_[:, 3:4], in_=ss_[:, 2:3], func=AF.Ln, bias=epsb[:, 0:1], scale=1.0 / D),
                         r=[ssb_, ssb], w=[ssb_])
                    K.op(act, lambda e: e.activation(out=ss_[:, 3:4], in_=ss_[:, 3:4], func=AF.Exp, scale=-0.5),
                         r=[ssb_], w=[ssb_])
                    ht, htb = hbufs2[i % 2]
                    src = h0 if layer == 0 else y
                    K.dma(sp, ht[:], src[128 * i:128 * i + 128, :], r=[hB[i]], w=[htb])
                    for hf in range(2):
                        K.op(dve, lambda e: e.scalar_tensor_tensor(out=mt[:, 512 * hf:512 * hf + 512], in0=pms[hf][0][:, :], scalar=ss_[:, 3:4],
                                                                   in1=gpost[:, 512 * hf:512 * hf + 512], op0=ALU.mult, op1=ALU.mult),
                             r=[pms[hf][1], ssb_, gb], w=[mtb])
                    K.op(dve, lambda e: e.tensor_tensor(out=ht[:], in0=ht[:], in1=mt[:], op=ALU.add),
                         r=[htb, mtb], w=[htb])
                    K.dma(sp, y[128 * i:128 * i + 128, :], ht[:], r=[htb], w=[hB[i]])
                K.barrier()

        def fox_layer(layer):
            j = layer // 2
            win = fox_w_in[j].rearrange("(c p) n -> p c n", p=128)
            with ExitStack() as ls:
                wb = [[(K.sb(ls, f"fw{s}{t}", [128, 8, 256], BF16), Buf()) for t in range(4)] for s in range(2)]
                wf = K.sb(ls, "fwf", [128, 8, 16], BF16)
                wfb = Buf()
                qT = K.sb(ls, "qaug", [128, 4, T], BF16)
                kT = K.sb(ls, "kaug", [128, 4, T], BF16)
                negcum = K.sb(ls, "negcum", [128, NT, NH], F32)
                sgT = K.sb(ls, "sgT", [128, 2, T], BF16)
                qTb, kTb, sgTb = Buf(), Buf(), Buf()
                vaug = K.sb(ls, "vaug", [128, NT, 4, 128], BF16)
                vaugb = Buf()
                bft = K.sb(ls, "bft", [128, NH], F32)
                lf = K.sb(ls, "lf", [128, NT, NH], F32)
                lfb = Buf()
                cum = K.sb(ls, "cum", [128, NT, NH], F32)
                carry = K.sb(ls, "carry", [128, NT, NH], F32)
                cumb = Buf()
                bias = [(K.sb(ls, f"bias{i}", [128, NT], F32), Buf()) for i in range(8)]
                PT = [(K.sb(ls, f"PT{i}", [128, 512], BF16), Buf()) for i in range(3)]
                rs = [(K.sb(ls, f"rs{i}", [128, 512], F32), Buf()) for i in range(2)]
                tmpo = [(K.sb(ls, f"tmpo{i}", [128, 512], F32), Buf()) for i in range(2)]

                def load_group(g, s):
                    for t in range(4):
                        K.dma(pool, wb[s][t][0][:], win[:, :, 1024 * t + 256 * g:1024 * t + 256 * g + 256],
                              w=[wb[s][t][1]])
                K.dma(pool, wf[:], win[:, :, 4096:4112], w=[wfb])
                load_group(0, 0)
                K.dma(sp, bft[:], fox_b_f[j, :].partition_broadcast(128), w=[lfb])
                K.op(dve, lambda e: e.memset(vaug[:], 1.0), w=[vaugb])

                pf = psA[4]
                for i in range(NT):
                    for c in range(8):
                        K.op(pe, lambda e: e.matmul(pf[0][:, 16 * i:16 * i + 16], lhsT=uT[:, c, 1 + 128 * i:1 + 128 * i + 128],
                                                    rhs=wf[:, c, :], start=(c == 0), stop=(c == 7)),
                             r=[uTb, wfb], w=[pf[1]], inc=(c == 7))
                for i in range(NT):
                    K.op(dve, lambda e: e.tensor_tensor(out=lf[:, i, :], in0=pf[0][:, 16 * i:16 * i + 16], in1=bft[:],
                                                        op=ALU.add), r=[pf[1], lfb], w=[lfb])
                lf2 = lf[:].rearrange("p a b -> p (a b)")
                K.op(act, lambda e: e.activation(out=lf2, in_=lf2, func=AF.Exp, scale=-1.0), r=[lfb], w=[lfb])
                K.op(act, lambda e: e.activation(out=lf2, in_=lf2, func=AF.Ln, bias=epsb[:, 1:2], scale=1.0), r=[lfb, ssb], w=[lfb])
                K.op(dve, lambda e: e.tensor_scalar(out=lf2, in0=lf2, scalar1=-1.0, scalar2=None, op0=ALU.mult),
                     r=[lfb], w=[lfb])
                pc, pl = psA[2], psA[3]
                K.op(dve, lambda e: e.memset(carry[:, 0, :], 0.0), w=[cumb])
                for i in range(NT):
                    for jj in range(i):
                        K.op(pe, lambda e: e.matmul(pc[0][:, 16 * i:16 * i + 16], lhsT=ones_f[:], rhs=lf[:, jj, :],
                                                    start=(jj == 0), stop=(jj == i - 1)),
                             r=[lfb, constb], w=[pc[1]], inc=(jj == i - 1))
                    K.op(pe, lambda e: e.matmul(pl[0][:, 16 * i:16 * i + 16], lhsT=tri_f[:], rhs=lf[:, i, :],
                                                start=True, stop=True), r=[lfb, constb], w=[pl[1]])
                K.op(dve, lambda e: e.tensor_copy(out=carry[:, 1:NT, :].rearrange("p a b -> p (a b)"),
                                                  in_=pc[0][:, 16:16 * NT]), r=[pc[1]], w=[cumb])
                K.op(dve, lambda e: e.tensor_tensor(out=cum[:].rearrange("p a b -> p (a b)"),
                                                    in0=pl[0][:, 0:16 * NT],
                                                    in1=carry[:].rearrange("p a b -> p (a b)"), op=ALU.add),
                     r=[pl[1], cumb], w=[cumb])

                K.op(dve, lambda e: e.tensor_scalar(out=negcum[:].rearrange("p a b -> p (a b)"),
                                                    in0=cum[:].rearrange("p a b -> p (a b)"), scalar1=-1.0, scalar2=None,
                                                    op0=ALU.mult), r=[cumb], w=[cumb])
                K.op(dve, lambda e: e.memset(kT[64:65, :, :], 1.0), w=[kTb])
                chunks = tok_chunks(512)
                pcount = [0]

                def nextps():
                    p = psA[pcount[0] % 2]
                    pcount[0] += 1
                    return p

                stc = [0]
                otc = [0]
                ptc = [0]
                bc = [0]
                for g in range(4):
                    s = g % 2
                    if g + 1 < 4:
                        load_group(g + 1, (g + 1) % 2)
                    wq, wk, wv, wg = [wb[s][t] for t in range(4)]
                    for pp in range(2):
                        for (t0, n) in chunks:
                            for (wt, dst, dstb, kind) in ((wq, qT, qTb, 0), (wk, kT, kTb, 1), (wg, sgT, sgTb, 2)):
                                p = nextps()
                                for c in range(8):
                                    K.op(pe, lambda e: e.matmul(p[0][:, 0:n], lhsT=wt[0][:, c, 128 * pp:128 * pp + 128],
                                                                rhs=uT[:, c, 1 + t0:1 + t0 + n],
                                                                start=(c == 0), stop=(c == 7)),
                                         r=[uTb, wt[1]], w=[p[1]], inc=(c == 7))
                                if kind == 0:
                                    for hf_ in range(2):
                                        K.op(act, lambda e: e.activation(out=dst[0:64, 2 * pp + hf_, t0:t0 + n],
                                                                         in_=p[0][64 * hf_:64 * hf_ + 64, 0:n],
                                                                         func=AF.Copy, scale=HD ** -0.5),
                                             r=[p[1]], w=[dstb])
                                elif kind == 1:
                                    for hf_ in range(2):
                                        K.op(dve, lambda e: e.tensor_copy(out=dst[0:64, 2 * pp + hf_, t0:t0 + n],
                                                                          in_=p[0][64 * hf_:64 * hf_ + 64, 0:n]),
                                             r=[p[1]], w=[dstb])
                                else:
                                    K.op(act, lambda e: e.activation(out=dst[:, pp, t0:t0 + n], in_=p[0][:, 0:n],
                                                                     func=AF.Silu), r=[p[1]], w=[dstb])
                    for hh_ in range(4):
                        for i_ in range(NT):
                            K.op(dve, lambda e: e.tensor_scalar(out=qT[64:65, hh_, 128 * i_:128 * i_ + 128], in0=ones_f[64:65, 0:128],
                                                                scalar1=carry[64:65, i_, 4 * g + hh_:4 * g + hh_ + 1], scalar2=None,
                                                                op0=ALU.mult), r=[cumb, constb], w=[qTb])
                    for i in range(NT):
                        p = nextps()
                        for c in range(8):
                            K.op(pe, lambda e: e.matmul(p[0][:, 0:256], lhsT=uT[:, c, 1 + 128 * i:1 + 128 * i + 128],
                                                        rhs=wv[0][:, c, :], start=(c == 0), stop=(c == 7)),
                                 r=[uTb, wv[1]], w=[p[1]], inc=(c == 7))
                        pv = p[0][:, 0:256].rearrange("p (a b c) -> p a b c", a=2, b=2)
                        K.op(dve, lambda e: e.tensor_copy(out=vaug[:, i, 0:4:2, 0:64], in_=pv[:, :, 0, :]),
                             r=[p[1]], w=[vaugb])
                        K.op(dve, lambda e: e.tensor_copy(out=vaug[:, i, 1:4:2, 64:128], in_=pv[:, :, 1, :]),
                             r=[p[1]], w=[vaugb])
                    items = []
                    for hh in range(4):
                        for cidx, (q0, qn) in enumerate(chunks):
                            cx = dict(hh=hh, q0=q0, qn=qn, i0=q0 // 128, ni=qn // 128, started=False)
                            cx["jmax"] = cx["i0"] + cx["ni"] - 1
                            for jk in range(cx["jmax"] + 1):
                                items.append((cx, jk))

                    def emit_st(it):
                        cx, jk = it
                        hh = cx["hh"]
                        h = 4 * g + hh
                        pp, half = hh // 2, hh % 2
                        lo = 64 * half
                        q0, qn, i0, ni = cx["q0"], cx["qn"], cx["i0"], cx["ni"]
                        if not cx["started"]:
                            cx["started"] = True
                            cx["ot"] = psA[4 + otc[0] % 2]
                            otc[0] += 1
                        qs = max(q0, 128 * jk)
                        n = q0 + qn - qs
                        stp = psA[2 + stc[0] % 2]
                        stc[0] += 1
                        K.op(pe, lambda e: e.matmul(stp[0][:, 0:n], lhsT=kT[0:65, hh, 128 * jk:128 * jk + 128],
                                                    rhs=qT[0:65, hh, qs:qs + n], start=True, stop=True),
                             r=[kTb, qTb], w=[stp[1]])
                        return stp

                    def emit_rest(it, stp):
                        cx, jk = it
                        hh = cx["hh"]
                        pp, half = hh // 2, hh % 2
                        olo, slo = (0, 64) if half == 0 else (64, 0)
                        q0, qn, i0, ni, jmax = cx["q0"], cx["qn"], cx["i0"], cx["ni"], cx["jmax"]
                        ot = cx["ot"]
                        qs = max(q0, 128 * jk)
                        n = q0 + qn - qs
                        pt = PT[ptc[0] % 3]
                        ptc[0] += 1
                        h_ = 4 * g + hh
                        K.op(act, lambda e: e.activation(out=pt[0][:, 0:n], in_=stp[0][:, 0:n], func=AF.Exp,
                                                         bias=negcum[:, jk, h_:h_ + 1], scale=1.0),
                             r=[stp[1], cumb], w=[pt[1]])
                        if jk >= i0:
                            K.op(dve, lambda e: e.tensor_tensor(out=pt[0][:, 0:128], in0=pt[0][:, 0:128],
                                                                in1=tri_b[:], op=ALU.mult),
                                 r=[pt[1], constb], w=[pt[1]])
                        K.op(pe, lambda e: e.matmul(ot[0][:, qs - q0:qs - q0 + n], lhsT=vaug[:, jk, hh, :],
                                                    rhs=pt[0][:, 0:n], start=(jk == 0), stop=(jk == jmax),
                                                    skip_group_check=True),
                             r=[vaugb, pt[1]], w=[ot[1]])
                        if jk == jmax:
                            r_ = rs[otc[0] % 2]
                            tm = tmpo[otc[0] % 2]
                            K.op(dve, lambda e: e.reciprocal(out=r_[0][olo:olo + 64, 0:qn], in_=ot[0][slo:slo + 64, 0:qn]),
                                 r=[ot[1]], w=[r_[1]])
                            K.op(dve, lambda e: e.tensor_tensor(out=tm[0][olo:olo + 64, 0:qn], in0=ot[0][olo:olo + 64, 0:qn],
                                                                in1=r_[0][olo:olo + 64, 0:qn], op=ALU.mult),
                                 r=[ot[1], r_[1]], w=[tm[1]])
                            K.op(dve, lambda e: e.tensor_tensor(out=ogT[olo:olo + 64, 2 * g + pp, q0:q0 + qn],
                                                                in0=tm[0][olo:olo + 64, 0:qn],
                                                                in1=sgT[olo:olo + 64, pp, q0:q0 + qn], op=ALU.mult),
                                 r=[tm[1], sgTb], w=[ogTb])

                    nxt = emit_st(items[0])
                    for n_ in range(len(items)):
                        cur_st = nxt
                        if n_ + 1 < len(items):
                            nxt = emit_st(items[n_ + 1])
                        emit_rest(items[n_], cur_st)
                K.barrier()


        def rwkv_layer(layer):
            j = layer // 2
            win = rwkv_w_in[j].rearrange("(c p) n -> p c n", p=128)
            with ExitStack() as ls:
                SB = lambda n, shp, dt: K.sb(ls, n, shp, dt)
                wadT = SB("wadT", [128, T], BF16); wadTb = Buf()
                wup = SB("wup", [128, D], BF16); wupb = Buf()
                vecs = SB("vecs", [128, 8, 8], F32); vecb = Buf()
                mub = (SB("mub", [128, 2, 128], F32), Buf())
                wraw = (SB("wraw", [128, 8, 128], F32), Buf())
                bones = SB("bones", [128, 128], F32)
                msc = SB("msc", [128, 2, 256], BF16)
                mlow2 = SB("mlow2", [128, 2, 128], BF16)
                id2 = SB("id2", [128, 2, 128], BF16)
                hind = SB("hind", [128, 2], BF16)
                cb2 = Buf()

                class Ctx:
                    pass

                ctxs = []
                for ci in range(2):
                    C = Ctx()
                    C.ci = ci
                    F_ = lambda n: (SB(f"{n}{ci}", [128, 128], F32), Buf())
                    B_ = lambda n, shp: (SB(f"{n}{ci}", shp, BF16), Buf())
                    for n in ("rf", "kf", "sgw", "av", "lw", "cm", "cmx", "E1", "E2", "E3", "kkr", "sq", "hsn", "kk",
                              "t1", "kp", "bb", "ke3", "be3", "gs", "ytile"):
                        setattr(C, n, F_(n))
                    C.PS = []
                    for si in range(3):
                        S = Ctx()
                        S.sg = B_(f"sg{si}", [128, 128]); S.vtok = B_(f"vtok{si}", [128, 128])
                        S.AR = B_(f"AR{si}", [128, 2, 128]); S.BK = B_(f"BK{si}", [128, 2, 128])
                        S.ARz = B_(f"ARz{si}", [128, 2, 2, 128]); S.BKz = B_(f"BKz{si}", [128, 2, 128])
                        S.kbhat = B_(f"kbhat{si}", [128, 2, 128])
                        S.GC = (SB(f"GC{ci}{si}", [128, 2], F32), Buf())
                        S.bon = (SB(f"bon{ci}{si}", [128, 2], F32), Buf())
                        K.op(dve, lambda e: e.memset(S.ARz[0][:], 0.0), w=[S.ARz[1]])
                        K.op(dve, lambda e: e.memset(S.BKz[0][:], 0.0), w=[S.BKz[1]])
                        C.PS.append(S)
                    C.IS = []
                    for si in range(2):
                        S = Ctx()
                        S.MB = B_(f"MB{si}", [128, 2, 256]); S.MK = B_(f"MK{si}", [128, 2, 256])
                        S.TT = B_(f"TTf{si}", [128, 2, 128])
                        C.IS.append(S)
                    C.zlock = None
                    C.khT = B_("khT", [128, 128]); C.bhT = B_("bhT", [128, 128]); C.rkb = B_("rkb", [128, 128])
                    C.PQT = [B_(f"PQT{i}", [128, 2, 384]) for i in range(2)]
                    C.PQ0 = B_("PQ0", [128, 2, 384])
                    C.Xs = B_("Xs", [128, 128]); C.Us = B_("Us", [128, 128]); C.ybf = B_("ybf", [128, 128])
                    C.st6 = (SB(f"st6{ci}", [128, 2, 6], F32), Buf())
                    C.mv = (SB(f"mv{ci}", [128, 2, 2], F32), Buf())
                    C.rstd = (SB(f"rstd{ci}", [128, 2], F32), Buf())
                    C.Hs = SB(f"Hs{ci}", [128, 128], F32); C.Hb = SB(f"Hb{ci}", [128, 128], BF16)
                    C.Hsb = Buf(); C.Hbb = Buf()
                    C.wcp = [(SB(f"wcp{ci}{t_}", [128, 16, 128], BF16), Buf()) for t_ in range(4)]
                    C.lnw = SB(f"lnw{ci}", [128, 128], F32); C.lnb = SB(f"lnb{ci}", [128, 128], F32); C.lnbuf = Buf()
                    C.bX, C.bY, C.bZ = psA[3 * ci], psA[3 * ci + 1], psA[3 * ci + 2]
                    C.yn = C.ytile
                    C.prep_done = 0
                    C.inv_done = 0
                    C.back_done = 0
                    ctxs.append(C)

                K.dma(sp, bones[:], c_bones[:, :], w=[cb2])
                K.dma(pool, msc[:].rearrange("p a b -> p (a b)"), c_msc[:, :], w=[cb2])
                K.dma(pool, mlow2[:].rearrange("p a b -> p (a b)"), c_mlow128[:, :], w=[cb2])
                K.dma(pool, id2[:].rearrange("p a b -> p (a b)"), c_id2[:, :], w=[cb2])
                for C in ctxs:
                    K.op(act, lambda e: e.copy(out=C.PQ0[0][:, :, 256:384], in_=id2[:]), r=[cb2], w=[C.PQ0[1]])
                K.dma(pool, hind[:], c_hind[:, :], w=[cb2])
                K.dma(pool, wup[0:64, :], rwkv_w_up[j], w=[wupb])
                K.dma(pool, wup[64:128, :], rwkv_a_up[j], w=[wupb])
                with nc.allow_non_contiguous_dma(reason="tiny per-feature vectors"):
                    for vi, src in enumerate((rwkv_w0, rwkv_a0, rwkv_k_k, rwkv_k_a, rwkv_r_k)):
                        K.dma(sp, vecs[:, vi, :], src[j, :].rearrange("(c p) -> p c", p=128), w=[vecb])
                K.op(dve, lambda e: e.tensor_scalar(out=vecs[:, 5, :], in0=vecs[:, 3, :], scalar1=-1.0, scalar2=1.0,
                                                    op0=ALU.mult, op1=ALU.add), r=[vecb], w=[vecb])
                K.op(dve, lambda e: e.tensor_scalar(out=vecs[:, 6, :], in0=vecs[:, 0, :], scalar1=-1.0, scalar2=None,
                                                    op0=ALU.mult), r=[vecb], w=[vecb])
                K.op(dve, lambda e: e.tensor_scalar(out=vecs[:, 7, :], in0=vecs[:, 1, :], scalar1=-1.0, scalar2=None,
                                                    op0=ALU.mult), r=[vecb], w=[vecb])

                def load_w(col0, dst):
                    mb, rw = mub, wraw
                    K.dma(sp, mb[0][:, 0, :], rwkv_mu[j, col0:col0 + 128].partition_broadcast(128), w=[mb[1]])
                    K.dma(sp, rw[0][:], win[:, :, col0:col0 + 128], w=[rw[1]])
                    K.op(dve, lambda e: e.tensor_scalar(out=mb[0][:, 1, :], in0=mb[0][:, 0, :], scalar1=-1.0, scalar2=1.0,
                                                        op0=ALU.mult, op1=ALU.add), r=[mb[1]], w=[mb[1]])
                    for c in range(8):
                        K.op(dve, lambda e: e.tensor_tensor(out=dst[0][:, c, :], in0=rw[0][:, c, :], in1=mb[0][:, 1, :],
                                                            op=ALU.mult), r=[rw[1], mb[1]], w=[dst[1]])
                        K.op(dve, lambda e: e.tensor_tensor(out=dst[0][:, 8 + c, :], in0=rw[0][:, c, :], in1=mb[0][:, 0, :],
                                                            op=ALU.mult), r=[rw[1], mb[1]], w=[dst[1]])

                def proj_fm(out_ps, outb, wt, t0, n):
                    for c in range(16):
                        rhs = uT[:, c, 1 + t0:1 + t0 + n] if c < 8 else uT[:, c - 8, t0:t0 + n]
                        K.op(pe, lambda e: e.matmul(out_ps, lhsT=wt[0][:, c, :], rhs=rhs, start=(c == 0), stop=(c == 15)),
                             r=[uTb, wt[1]], w=[outb], inc=(c == 15))

                wwa = ctxs[0].wcp[0]
                load_w(4096, wwa)
                for ci_, (t0, n) in enumerate(tok_chunks(512)):
                    p_ = psA[ci_ % 2]
                    proj_fm(p_[0][:, 0:n], p_[1], wwa, t0, n)
                    K.op(act, lambda e: e.activation(out=wadT[0:64, t0:t0 + n], in_=p_[0][0:64, 0:n], func=AF.Tanh),
                         r=[p_[1]], w=[wadTb])
                    K.op(act, lambda e: e.copy(out=wadT[64:128, t0:t0 + n], in_=p_[0][64:128, 0:n]), r=[p_[1]], w=[wadTb])

                v3 = lambda t_: t_[0][:].rearrange("p (c s) -> p c s", c=2)
                flat = lambda ap: ap.rearrange("p a b -> p (a b)")
                one_b = epsb[:, 1:2]

                def sigmoid_chain(C, src_ps, srcb, bias_ap, dst, extra_r=()):
                    if bias_ap is None:
                        K.op(act, lambda e: e.activation(out=dst[0][:], in_=src_ps, func=AF.Exp, scale=-1.0),
                             r=[srcb] + list(extra_r), w=[dst[1]])
                    else:
                        K.op(act, lambda e: e.activation(out=dst[0][:], in_=src_ps, func=AF.Exp, bias=bias_ap, scale=-1.0),
                             r=[srcb] + list(extra_r), w=[dst[1]])
                    K.op(act, lambda e: e.activation(out=dst[0][:], in_=dst[0][:], func=AF.Ln, bias=one_b, scale=1.0),
                         r=[dst[1], ssb], w=[dst[1]])
                    K.op(act, lambda e: e.activation(out=dst[0][:], in_=dst[0][:], func=AF.Exp, scale=-1.0),
                         r=[dst[1]], w=[dst[1]])

                def prep(C, p):
                    vcol = lambda vi: vecs[:, vi, p:p + 1]
                    wr, wk, wv, wg = C.wcp
                    bX = bY = C.bZ
                    for i in range(NT):
                        while C.back_done < i - 2:
                            yield False
                        S = C.PS[i % 3]
                        while C.zlock is not None and C.zlock != "prep":
                            yield False
                        C.zlock = "prep"
                        t0 = 128 * i
                        rf, kf, sgw, av, lw, cm, cmx, E1, E2, E3 = C.rf, C.kf, C.sgw, C.av, C.lw, C.cm, C.cmx, C.E1, C.E2, C.E3
                        kkr, sq, hsn, kk, t1, kp, bb, ke3, be3, gs = C.kkr, C.sq, C.hsn, C.kk, C.t1, C.kp, C.bb, C.ke3, C.be3, C.gs
                        AR, BK = S.AR, S.BK
                        proj_fm(bX[0][:, 0:128], bX[1], wr, t0, 128)
                        proj_fm(bX[0][:, 128:256], bX[1], wk, t0, 128)
                        proj_fm(bX[0][:, 256:384], bX[1], wg, t0, 128)
                        for c in range(16):
                            lhsT = uT[:, c, 1 + t0:1 + t0 + 128] if c < 8 else uT[:, c - 8, t0:t0 + 128]
                            K.op(pe, lambda e: e.matmul(bX[0][:, 384:512], lhsT=lhsT, rhs=wv[0][:, c, :],
                                                        start=(c == 0), stop=(c == 15)),
                                 r=[uTb, wv[1]], w=[bX[1]], inc=(c == 15))
                        yield True
                        K.op(act, lambda e: e.copy(out=rf[0][:], in_=bX[0][:, 0:128]), r=[bX[1]], w=[rf[1]])
                        K.op(act, lambda e: e.copy(out=kf[0][:], in_=bX[0][:, 128:256]), r=[bX[1]], w=[kf[1]])
                        sigmoid_chain(C, bX[0][:, 256:384], bX[1], None, gs)
                        K.op(dve, lambda e: e.tensor_tensor(out=S.sg[0][:], in0=bX[0][:, 256:384], in1=gs[0][:], op=ALU.mult),
                             r=[bX[1], gs[1]], w=[S.sg[1]])
                        K.op(dve, lambda e: e.tensor_copy(out=S.vtok[0][:], in_=bX[0][:, 384:512]), r=[bX[1]], w=[S.vtok[1]])
                        K.op(pe, lambda e: e.matmul(bY[0][:, 0:128], lhsT=wup[0:64, 128 * p:128 * p + 128],
                                                    rhs=wadT[0:64, t0:t0 + 128], start=True, stop=True),
                             r=[wupb, wadTb], w=[bY[1]])
                        K.op(pe, lambda e: e.matmul(bY[0][:, 128:256], lhsT=wup[64:128, 128 * p:128 * p + 128],
                                                    rhs=wadT[64:128, t0:t0 + 128], start=True, stop=True),
                             r=[wupb, wadTb], w=[bY[1]])
                        yield True
                        sigmoid_chain(C, bY[0][:, 0:128], bY[1], vcol(6), sgw, extra_r=[vecb])
                        sigmoid_chain(C, bY[0][:, 128:256], bY[1], vcol(7), av, extra_r=[vecb])
                        K.op(dve, lambda e: e.tensor_scalar(out=lw[0][:], in0=sgw[0][:], scalar1=-DECAY_SCALE, scalar2=None,
                                                            op0=ALU.mult), r=[sgw[1]], w=[lw[1]])
                        K.op(dve, lambda e: e.tensor_tensor_scan(out=cm[0][:], data0=ones_f[:], data1=lw[0][:], initial=0.0,
                                                                 op0=ALU.mult, op1=ALU.add), r=[lw[1], constb], w=[cm[1]])
                        K.op(dve, lambda e: e.tensor_tensor(out=cmx[0][:], in0=cm[0][:], in1=lw[0][:], op=ALU.subtract),
                             r=[cm[1], lw[1]], w=[cmx[1]])
                        yield True
                        K.op(act, lambda e: e.activation(out=E1[0][:], in_=cm[0][:], func=AF.Exp), r=[cm[1]], w=[E1[1]])
                        K.op(act, lambda e: e.activation(out=E2[0][:], in_=cmx[0][:], func=AF.Exp), r=[cmx[1]], w=[E2[1]])
                        K.op(act, lambda e: e.activation(out=E3[0][:], in_=cm[0][:], func=AF.Exp, scale=-1.0),
                             r=[cm[1]], w=[E3[1]])
                        K.op(act, lambda e: e.activation(out=S.GC[0][:, 0:1], in_=cm[0][:, 127:128], func=AF.Exp),
                             r=[cm[1]], w=[S.GC[1]])
                        K.op(dve, lambda e: e.tensor_scalar(out=kkr[0][:], in0=kf[0][:], scalar1=vcol(2), scalar2=None,
                                                            op0=ALU.mult), r=[kf[1], vecb], w=[kkr[1]])
                        K.op(dve, lambda e: e.tensor_tensor(out=sq[0][:], in0=kkr[0][:], in1=kkr[0][:], op=ALU.mult),
                             r=[kkr[1]], w=[sq[1]])
                        K.op(pe, lambda e: e.matmul(bY[0][:, 256:384], lhsT=bones[:], rhs=sq[0][:], start=True, stop=True),
                             r=[cb2, sq[1]], w=[bY[1]])
                        yield True
                        K.op(dve, lambda e: e.tensor_scalar(out=hsn[0][:], in0=bY[0][:, 256:384], scalar1=1e-24, scalar2=None,
                                                            op0=ALU.max), r=[bY[1]], w=[hsn[1]])
                        K.op(act, lambda e: e.activation(out=hsn[0][:], in_=hsn[0][:], func=AF.Ln), r=[hsn[1]], w=[hsn[1]])
                        K.op(act, lambda e: e.activation(out=hsn[0][:], in_=hsn[0][:], func=AF.Exp, scale=-0.5),
                             r=[hsn[1]], w=[hsn[1]])
                        K.op(dve, lambda e: e.tensor_scalar(out=t1[0][:], in0=av[0][:], scalar1=vcol(3), scalar2=vcol(5),
                                                            op0=ALU.mult, op1=ALU.add), r=[av[1], vecb], w=[t1[1]])
                        K.op(dve, lambda e: e.tensor_tensor(out=kp[0][:], in0=kf[0][:], in1=t1[0][:], op=ALU.mult),
                             r=[kf[1], t1[1]], w=[kp[1]])
                        K.op(dve, lambda e: e.tensor_tensor(out=AR[0][:, 1, :], in0=rf[0][:], in1=E1[0][:], op=ALU.mult),
                             r=[rf[1], E1[1]], w=[AR[1]])
                        K.op(dve, lambda e: e.tensor_tensor(out=ke3[0][:], in0=kp[0][:], in1=E3[0][:], op=ALU.mult),
                             r=[kp[1], E3[1]], w=[ke3[1]])
                        yield True
                        K.op(dve, lambda e: e.tensor_tensor(out=kk[0][:], in0=kkr[0][:], in1=hsn[0][:], op=ALU.mult),
                             r=[kkr[1], hsn[1]], w=[kk[1]])
                        K.op(dve, lambda e: e.tensor_tensor(out=bb[0][:], in0=kk[0][:], in1=av[0][:], op=ALU.mult),
                             r=[kk[1], av[1]], w=[bb[1]])
                        K.op(dve, lambda e: e.scalar_tensor_tensor(out=AR[0][:, 0, :], in0=kk[0][:], scalar=-1.0, in1=E2[0][:],
                                                                   op0=ALU.mult, op1=ALU.mult), r=[kk[1], E2[1]], w=[AR[1]])
                        K.op(dve, lambda e: e.tensor_tensor(out=be3[0][:], in0=bb[0][:], in1=E3[0][:], op=ALU.mult),
                             r=[bb[1], E3[1]], w=[be3[1]])
                        K.op(act, lambda e: e.copy(out=BK[0][:, 1, :], in_=ke3[0][:]), r=[ke3[1]], w=[BK[1]])
                        K.op(act, lambda e: e.copy(out=BK[0][:, 0, :], in_=be3[0][:]), r=[be3[1]], w=[BK[1]])
                        yield True
                        K.op(act, lambda e: e.copy(out=S.ARz[0][0:64, 0, :, :], in_=AR[0][0:64, :, :]), r=[AR[1]], w=[S.ARz[1]])
                        K.op(act, lambda e: e.copy(out=S.ARz[0][64:128, 1, :, :], in_=AR[0][64:128, :, :]), r=[AR[1]], w=[S.ARz[1]])
                        K.op(act, lambda e: e.copy(out=S.BKz[0][0:64, 0, :], in_=BK[0][0:64, 0, :]), r=[BK[1]], w=[S.BKz[1]])
                        K.op(act, lambda e: e.copy(out=S.BKz[0][64:128, 1, :], in_=BK[0][64:128, 0, :]), r=[BK[1]], w=[S.BKz[1]])
                        K.op(dve, lambda e: e.tensor_scalar(out=C.khT[0][:], in0=ke3[0][:], scalar1=S.GC[0][:, 0:1], scalar2=None,
                                                            op0=ALU.mult), r=[ke3[1], S.GC[1]], w=[C.khT[1]])
                        K.op(dve, lambda e: e.tensor_scalar(out=C.bhT[0][:], in0=be3[0][:], scalar1=S.GC[0][:, 0:1], scalar2=None,
                                                            op0=ALU.mult), r=[be3[1], S.GC[1]], w=[C.bhT[1]])
                        K.op(dve, lambda e: e.scalar_tensor_tensor(out=C.rkb[0][:], in0=rf[0][:], scalar=vcol(4), in1=kp[0][:],
                                                                   op0=ALU.mult, op1=ALU.mult), r=[rf[1], kp[1], vecb], w=[C.rkb[1]])
                        K.op(pe, lambda e: e.matmul(bY[0][:, 384:386], lhsT=C.rkb[0][:], rhs=hind[:], start=True, stop=True),
                             r=[C.rkb[1], cb2], w=[bY[1]])
                        K.op(pe, lambda e: e.transpose(out=psT[0][:, 4 + 2 * C.ci, :], in_=C.khT[0][:], identity=ident[:]),
                             r=[C.khT[1], identb], w=[psT[1]], inc=False)
                        K.op(pe, lambda e: e.transpose(out=psT[0][:, 5 + 2 * C.ci, :], in_=C.bhT[0][:], identity=ident[:]),
                             r=[C.bhT[1], identb], w=[psT[1]])
                        yield True
                        K.op(dve, lambda e: e.tensor_copy(out=S.bon[0][:], in_=bY[0][:, 384:386]), r=[bY[1]], w=[S.bon[1]])
                        K.op(act, lambda e: e.copy(out=S.kbhat[0][:], in_=psT[0][:, 4 + 2 * C.ci:6 + 2 * C.ci, :]), r=[psT[1]], w=[S.kbhat[1]])
                        C.zlock = None
                        C.prep_done = i + 1
                        yield True

                def inv(C, p):
                    bX, bY = C.bX, C.bY
                    for i in range(NT):
                        while C.prep_done < i + 1 or C.back_done < i - 1:
                            yield False
                        S = C.PS[i % 3]
                        I_ = C.IS[i % 2]
                        AR, BK, MB, MK = S.AR, S.BK, I_.MB, I_.MK
                        arz = S.ARz[0][:].rearrange("p h s t -> p (h s t)")
                        K.op(pe, lambda e: e.matmul(bX[0][:, 0:512], lhsT=BK[0][:, 0, :], rhs=arz, start=True, stop=True),
                             r=[BK[1], S.ARz[1]], w=[bX[1]])
                        K.op(pe, lambda e: e.matmul(bY[0][:, 0:512], lhsT=BK[0][:, 1, :], rhs=arz, start=True, stop=True),
                             r=[BK[1], S.ARz[1]], w=[bY[1]])
                        yield True
                        P0 = C.PQ0
                        K.op(dve, lambda e: e.tensor_tensor(out=MB[0][:].rearrange("p h c -> p (h c)"), in0=bX[0][:, 0:512],
                                                            in1=msc[:].rearrange("p h c -> p (h c)"), op=ALU.mult),
                             r=[bX[1], cb2], w=[MB[1]])
                        K.op(pe, lambda e: e.matmul(bX[0][:, 0:256], lhsT=AR[0][:, 0, :], rhs=S.BKz[0][:].rearrange("p h s -> p (h s)"),
                                                    start=True, stop=True), r=[AR[1], S.BKz[1]], w=[bX[1]])
                        K.op(dve, lambda e: e.tensor_tensor(out=MK[0][:].rearrange("p h c -> p (h c)"), in0=bY[0][:, 0:512],
                                                            in1=msc[:].rearrange("p h c -> p (h c)"), op=ALU.mult),
                             r=[bY[1], cb2], w=[MK[1]])
                        yield True
                        K.op(dve, lambda e: e.tensor_tensor(out=P0[0][:, :, 0:128], in0=bX[0][:, 0:256].rearrange("p (h s) -> p h s", h=2),
                                                            in1=mlow2[:], op=ALU.mult), r=[bX[1], cb2], w=[P0[1]])
                        K.op(act, lambda e: e.copy(out=P0[0][:, :, 128:256], in_=MB[0][:, :, 0:128]), r=[MB[1]], w=[P0[1]])
                        yield True
                        for stp_ in range(1, 8):
                            prev = C.PQ0 if stp_ == 1 else C.PQT[(stp_ - 1) % 2]
                            cur = C.PQT[stp_ % 2]
                            last = (stp_ == 7)
                            for hd in range(2):
                                bk = bX if hd == 0 else bY
                                Pm = prev[0][:, hd, 0:128]
                                Qm = prev[0][:, hd, 128:256]
                                if not last:
                                    K.op(pe, lambda e: e.matmul(bk[0][:, 0:128], lhsT=Qm, rhs=Pm, start=True, stop=True),
                                         r=[prev[1]], w=[bk[1]], inc=False)
                                    K.op(pe, lambda e: e.matmul(bk[0][:, 128:384], lhsT=Pm, rhs=prev[0][:, hd, 128:384], start=True, stop=False),
                                         r=[prev[1]], w=[bk[1]], inc=False)
                                else:
                                    K.op(pe, lambda e: e.matmul(bk[0][:, 256:384], lhsT=Pm, rhs=prev[0][:, hd, 256:384], start=True, stop=False),
                                         r=[prev[1]], w=[bk[1]], inc=False)
                                K.op(pe, lambda e: e.matmul(bk[0][:, 256:384], lhsT=ident[:], rhs=prev[0][:, hd, 256:384], start=False, stop=True),
                                     r=[prev[1], identb], w=[bk[1]])
                            yield True
                            for hd in range(2):
                                bk = bX if hd == 0 else bY
                                if not last:
                                    dst_, src_ = cur[0][:, hd, :], bk[0][:, 0:384]
                                    dstb_ = cur[1]
                                else:
                                    dst_, src_ = I_.TT[0][:, hd, :], bk[0][:, 256:384]
                                    dstb_ = I_.TT[1]
                                if hd == 0 or stp_ in (2, 4, 6):
                                    K.op(act, lambda e: e.copy(out=dst_, in_=src_), r=[bk[1]], w=[dstb_])
                                else:
                                    K.op(dve, lambda e: e.tensor_copy(out=dst_, in_=src_), r=[bk[1]], w=[dstb_])
                            yield True
                        C.inv_done = i + 1
                        yield True

                def back(C, p):
                    bZ = C.bZ
                    Hs, Hb, Hsb, Hbb = C.Hs, C.Hb, C.Hsb, C.Hbb
                    Xs, Us, ytile, yn, ybf = C.Xs, C.Us, C.ytile, C.yn, C.ybf
                    for i in range(NT):
                        while C.inv_done < i + 1:
                            yield False
                        S = C.PS[i % 3]
                        I_ = C.IS[i % 2]
                        AR, MB, MK, vtok, kbhat, GC, Tf = S.AR, I_.MB, I_.MK, S.vtok, S.kbhat, S.GC, I_.TT
                        while C.zlock is not None and C.zlock != "back":
                            yield False
                        C.zlock = "back"
                        t0 = 128 * i
                        hc = lambda hd: slice(64 * hd, 64 * hd + 64)
                        K.op(pe, lambda e: e.matmul(bZ[0][:, 0:128], lhsT=AR[0][:, 0, :], rhs=Hb[:], start=True, stop=False, skip_group_check=True),
                             r=[AR[1], Hbb], w=[bZ[1]], inc=False)
                        for hd in range(2):
                            K.op(pe, lambda e: e.matmul(bZ[0][:, hc(hd)], lhsT=MK[0][:, hd, 0:128], rhs=vtok[0][:, hc(hd)],
                                                        start=False, stop=(hd == 1), skip_group_check=True),
                                 r=[MK[1], vtok[1]], w=[bZ[1]], inc=(hd == 1))
                        yield True
                        K.op(act, lambda e: e.copy(out=Xs[0][:], in_=bZ[0][:, 0:128]), r=[bZ[1]], w=[Xs[1]])
                        yield True
                        for hd in range(2):
                            K.op(pe, lambda e: e.matmul(bZ[0][:, 128 + 64 * hd:128 + 64 * hd + 64], lhsT=Tf[0][:, hd, :], rhs=Xs[0][:, hc(hd)],
                                                        start=True, stop=True), r=[Tf[1], Xs[1]], w=[bZ[1]], inc=(hd == 1))
                        yield True
                        K.op(dve, lambda e: e.tensor_copy(out=Us[0][:], in_=bZ[0][:, 128:256]), r=[bZ[1]], w=[Us[1]])
                        yield True
                        K.op(pe, lambda e: e.matmul(bZ[0][:, 256:384], lhsT=AR[0][:, 1, :], rhs=Hb[:], start=True, stop=False, skip_group_check=True),
                             r=[AR[1], Hbb], w=[bZ[1]], inc=False)
                        for hd in range(2):
                            o_ = bZ[0][:, 256 + 64 * hd:256 + 64 * hd + 64]
                            K.op(pe, lambda e: e.matmul(o_, lhsT=MB[0][:, hd, 128:256], rhs=Us[0][:, hc(hd)], start=False, stop=False,
                                                        skip_group_check=True), r=[MB[1], Us[1]], w=[bZ[1]], inc=False)
                            K.op(pe, lambda e: e.matmul(o_, lhsT=MK[0][:, hd, 128:256], rhs=vtok[0][:, hc(hd)], start=False, stop=(hd == 1),
                                                        skip_group_check=True), r=[MK[1], vtok[1]], w=[bZ[1]], inc=False)
                        for hd in range(2):
                            o2 = bZ[0][:, 384 + 64 * hd:384 + 64 * hd + 64]
                            K.op(pe, lambda e: e.matmul(o2, lhsT=kbhat[0][:, 1, :], rhs=Us[0][:, hc(hd)], start=True, stop=False),
                                 r=[kbhat[1], Us[1]], w=[bZ[1]], inc=False)
                            K.op(pe, lambda e: e.matmul(o2, lhsT=kbhat[0][:, 0, :], rhs=vtok[0][:, hc(hd)], start=False, stop=True),
                                 r=[kbhat[1], vtok[1]], w=[bZ[1]], inc=(hd == 1))
                        yield True
                        for hd in range(2):
                            lo = 64 * hd
                            K.op(dve, lambda e: e.scalar_tensor_tensor(out=Hs[lo:lo + 64, hc(hd)], in0=Hs[lo:lo + 64, hc(hd)], scalar=GC[0][lo:lo + 64, 0:1],
                                                                       in1=bZ[0][lo:lo + 64, 384 + 64 * hd:384 + 64 * hd + 64],
                                                                       op0=ALU.mult, op1=ALU.add), r=[Hsb, GC[1], bZ[1]], w=[Hsb])
                        K.op(dve, lambda e: e.tensor_copy(out=Hb[:], in_=Hs[:]), r=[Hsb], w=[Hbb])
                        K.op(dve, lambda e: e.tensor_copy(out=ytile[0][:], in_=bZ[0][:, 256:384]), r=[bZ[1]], w=[ytile[1]])
                        C.zlock = None
                        yield True
                        for hf in range(2):
                            K.op(dve, lambda e: e.bn_stats(out=C.st6[0][:, hf, :], in_=ytile[0][:, 64 * hf:64 * hf + 64]), r=[ytile[1]], w=[C.st6[1]])
                            K.op(dve, lambda e: e.bn_aggr(out=C.mv[0][:, hf, :], in_=C.st6[0][:, hf, :]), r=[C.st6[1]], w=[C.mv[1]])
                        yield True
                        K.op(act, lambda e: e.activation(out=C.rstd[0][:], in_=C.mv[0][:, :, 1], func=AF.Ln, bias=epsb[:, 2:3], scale=1.0),
                             r=[C.mv[1], ssb], w=[C.rstd[1]])
                        K.op(act, lambda e: e.activation(out=C.rstd[0][:], in_=C.rstd[0][:], func=AF.Exp, scale=-0.5), r=[C.rstd[1]], w=[C.rstd[1]])
                        yield True
                        for hf in range(2):
                            cs = slice(64 * hf, 64 * hf + 64)
                            K.op(dve, lambda e: e.tensor_scalar(out=yn[0][:, cs], in0=ytile[0][:, cs], scalar1=C.mv[0][:, hf, 0:1],
                                                                scalar2=C.rstd[0][:, hf:hf + 1], op0=ALU.subtract, op1=ALU.mult),
                                 r=[ytile[1], C.mv[1], C.rstd[1]], w=[yn[1]])
                        K.op(dve, lambda e: e.tensor_tensor(out=yn[0][:], in0=yn[0][:], in1=C.lnw[:], op=ALU.mult),
                             r=[yn[1], C.lnbuf], w=[yn[1]])
                        K.op(dve, lambda e: e.tensor_tensor(out=yn[0][:], in0=yn[0][:], in1=C.lnb[:], op=ALU.add),
                             r=[yn[1], C.lnbuf], w=[yn[1]])
                        for hf in range(2):
                            cs = slice(64 * hf, 64 * hf + 64)
                            K.op(dve, lambda e: e.scalar_tensor_tensor(out=ybf[0][:, cs], in0=vtok[0][:, cs], scalar=S.bon[0][:, hf:hf + 1],
                                                                       in1=yn[0][:, cs], op0=ALU.mult, op1=ALU.add),
                                 r=[vtok[1], S.bon[1], yn[1]], w=[ybf[1]])
                        yield True
                        K.op(pe, lambda e: e.transpose(out=psT[0][:, 2 + C.ci, :], in_=ybf[0][:], identity=ident[:]), r=[ybf[1], identb], w=[psT[1]])
                        yield True
                        K.op(dve, lambda e: e.tensor_tensor(out=ogT[:, p, t0:t0 + 128], in0=psT[0][:, 2 + C.ci, :], in1=S.sg[0][:], op=ALU.mult),
                             r=[psT[1], S.sg[1]], w=[ogTb])
                        C.back_done = i + 1
                        yield True

                def pair_stream(C):
                    for p in range(C.ci, 8, 2):
                        for t_ in range(4):
                            load_w(1024 * t_ + 128 * p, C.wcp[t_])
                            yield True
                        K.dma(sp, C.lnw[:], rwkv_ln_w[j, 128 * p:128 * p + 128].partition_broadcast(128), w=[C.lnbuf])
                        K.dma(sp, C.lnb[:], rwkv_ln_b[j, 128 * p:128 * p + 128].partition_broadcast(128), w=[C.lnbuf])
                        K.op(dve, lambda e: e.memset(C.Hs[:], 0.0), w=[C.Hsb])
                        K.op(dve, lambda e: e.memset(C.Hb[:], 0.0), w=[C.Hbb])
                        C.prep_done = 0
                        C.inv_done = 0
                        C.back_done = 0
                        C.zlock = None
                        gens = [prep(C, p), inv(C, p), back(C, p)]
                        while gens:
                            progressed = False
                            for g_ in list(gens):
                                try:
                                    if next(g_):
                                        progressed = True
                                except StopIteration:
                                    gens.remove(g_)
                                    progressed = True
                            yield progressed

                streams = [pair_stream(C) for C in ctxs]
                while streams:
                    for s_ in list(streams):
                        try:
                            next(s_)
                        except StopIteration:
                            streams.remove(s_)
                K.barrier()


        for layer in range(nlayers):
            phase_prenorm(layer)
            if layer % 2 == 0:
                fox_layer(layer)
                phase_post(layer, fox_w_out[layer // 2], layer == nlayers - 1)
            else:
                rwkv_layer(layer)
                phase_post(layer, rwkv_w_out[layer // 2], layer == nlayers - 1)
            K.barrier()
        K.barrier()
    return nc


_CACHE = {}


def _consts():
    idx = np.arange(128)
    tri = (idx[:, None] <= idx[None, :]).astype(np.float32)
    ident = np.eye(128, dtype=np.float32)
    ones = np.ones((128, 128), np.float32)
    m64 = np.zeros((128, 128), np.float32)
    s = idx[:, None] % 64
    t = idx[None, :] % 64
    m64[:, 0:64] = (s < t)[:, 0:64]
    m64[:, 64:128] = (s <= t)[:, 64:128]
    scan = np.ones((128, 512), np.float32)
    scan[:, ::64] = 0.0
    hind = np.zeros((128, 2), np.float32)
    hind[0:64, 0] = 1.0
    hind[64:128, 1] = 1.0
    m64x2 = np.concatenate([m64, m64], axis=1)
    r64 = idx[:, None] % 64
    c64 = np.arange(64)[None, :]
    mlow = (c64 < r64).astype(np.float32)
    mlow2 = np.concatenate([mlow, mlow], axis=1)
    i64 = (c64 == r64).astype(np.float32)
    idx2 = np.concatenate([i64, i64], axis=1)
    bones = ((idx[:, None] // 64) == (idx[None, :] // 64)).astype(np.float32)
    strict = (idx[:, None] < idx[None, :]).astype(np.float32)
    msc1 = np.concatenate([strict, tri], axis=1)
    msc = np.concatenate([msc1, msc1], axis=1)
    mlow128 = (idx[None, :] < idx[:, None]).astype(np.float32)
    return dict(c_ident=ident, c_tri=tri, c_ones=ones, c_m64=m64, c_scan=scan, c_hind=hind,
                c_m64x2=m64x2, c_mlow2=mlow2, c_idx2=idx2, c_bones=bones,
                c_msc=msc, c_mlow128=np.concatenate([mlow128, mlow128], axis=1),
                c_id2=np.concatenate([ident, ident], axis=1))


def kernel(x, meta_tokens, norm_pre, norm_post, fox_w_in, fox_b_f, fox_w_out,
           rwkv_w_in, rwkv_mu, rwkv_w0, rwkv_w_up, rwkv_a0, rwkv_a_up, rwkv_k_k,
           rwkv_k_a, rwkv_r_k, rwkv_ln_w, rwkv_ln_b, rwkv_w_out, _nlayers=DEPTH):
    f = lambda a: np.ascontiguousarray(np.asarray(a, dtype=np.float32))
    x = f(x)
    B = x.shape[0]
    meta = f(meta_tokens)
    h0 = np.zeros((B, T, D), np.float32)
    h0[:, :NMETA] = meta[None]
    h0[:, NMETA:NMETA + SEQ] = x
    shared = dict(
        norm_pre=f(norm_pre), norm_post=f(norm_post), fox_w_in=f(fox_w_in), fox_b_f=f(fox_b_f),
        fox_w_out=f(fox_w_out), rwkv_w_in=f(rwkv_w_in), rwkv_mu=f(rwkv_mu), rwkv_w0=f(rwkv_w0),
        rwkv_w_up=f(rwkv_w_up), rwkv_a0=f(rwkv_a0), rwkv_a_up=f(rwkv_a_up), rwkv_k_k=f(rwkv_k_k),
        rwkv_k_a=f(rwkv_k_a), rwkv_r_k=f(rwkv_r_k).reshape(2, D), rwkv_ln_w=f(rwkv_ln_w),
        rwkv_ln_b=f(rwkv_ln_b), rwkv_w_out=f(rwkv_w_out))
    shared.update(_consts())
    key = _nlayers
    if key not in _CACHE:
        _CACHE[key] = build_program(_nlayers)
    nc = _CACHE[key]
    in_maps = []
    for b in range(B):
        m = dict(shared)
        m["h0"] = h0[b]
        in_maps.append(m)
    res = run_bass_kernel_spmd(nc, in_maps, core_ids=list(range(B)))
    out = np.stack([np.asarray(r["y"])[NMETA:NMETA + SEQ] for r in res.results], axis=0)
    return out.astype(np.float32)
```

```python
import math
from contextlib import ExitStack

import numpy as np
import concourse.bass as bass
import concourse.mybir as mybir
from concourse.bass_utils import run_bass_kernel_spmd

F32 = mybir.dt.float32
BF16 = mybir.dt.bfloat16
AF = mybir.ActivationFunctionType
ALU = mybir.AluOpType
AX = mybir.AxisListType

D = 1024
SEQ = 2048
NMETA = 16
NT = 17
T = NT * 128
DEPTH = 4
NH = 16
HD = 64
FOX_IN = 4 * D + NH
RWKV_IN = 4 * D + 128
NORM_EPS = 1e-6
GN_EPS = 64e-5
DECAY_SCALE = math.exp(-0.5)
CH = 64
DBG = {"stage": 9, "pairs": 8, "tiles": NT}


def tok_chunks(n=512):
    out = []
    t0 = 0
    while t0 < T:
        m = min(n, T - t0)
        out.append((t0, m))
        t0 += m
    return out


class Buf:
    __slots__ = ("name", "lw", "rd", "excl")

    def __init__(self, name="", excl=False):
        self.name = name
        self.lw = None
        self.rd = {}
        self.excl = excl


class Q:
    def __init__(self, name, eng, sem, self_sync=True):
        self.name = name
        self.eng = eng
        self.sem = sem
        self.cnt = 0
        self.seen = {}
        self.self_sync = self_sync
        self.key = name
        self.ring = []
        self.ring_i = 0


class PEProxy:
    def __init__(self, K, eng):
        self.K = K
        self.eng = eng
        self.partial = False

    def _pre(self, st_ap, out):
        K = self.K
        rg = (st_ap.base_partition(), st_ap.partition_size())
        bank = out.name
        last = K.pe_last
        if last is not None and last[0] != rg and (last[1] == bank or (last[0][1] < 128 and rg[1] < 128)):
            assert K.pe_last_tk is not None, "previous matmul needs a semaphore increment"
            K._wait(K.pe, K.pe_last_tk)
        K.pe_last = (rg, bank)
        self.partial = rg[1] < 128

    def matmul(self, out, lhsT, rhs, **kw):
        self._pre(lhsT, out)
        return self.eng.matmul(out, lhsT=lhsT, rhs=rhs, **kw)

    def transpose(self, out, in_, identity):
        self._pre(in_, out)
        return self.eng.transpose(out=out, in_=in_, identity=identity)


class KB:
    def __init__(self, nc, st):
        self.nc = nc
        self.st = st
        mk = lambda n: st.enter_context(nc.semaphore(n))
        self.pe = Q("pe", nc.tensor, mk("s_pe"), self_sync=False)
        self.act = Q("act", nc.scalar, mk("s_act"))
        self.dve = Q("dve", nc.vector, mk("s_dve"))
        self.pool = Q("pool", nc.gpsimd, mk("s_pool"))
        self.sp = Q("sp", nc.sync, mk("s_sp"))
        self.queues = [self.pe, self.act, self.dve, self.pool, self.sp]
        for q, n in ((self.sp, 24), (self.pool, 24)):
            for i in range(n):
                q.ring.append([mk(f"d_{q.name}{i}"), 0, f"d_{q.name}{i}"])
        self.dma_tickets = []
        self.nbuf = 0
        self.counting = False
        self.nops = 0
        self.pe_last = None
        self.pe_last_tk = None
        self.snap = {}
        self.prox = PEProxy(self, nc.tensor)

    def sb(self, st, name, shape, dt):
        self.nbuf += 1
        return st.enter_context(self.nc.sbuf_tensor(f"{name}_{self.nbuf}", list(shape), dt))

    def psum(self, st, name, shape, dt):
        return st.enter_context(self.nc.psum_tensor(name, list(shape), dt))

    def _learn(self, q, tk):
        q.seen[tk[2]] = max(q.seen.get(tk[2], 0), tk[1])
        sn = self.snap.get((tk[2], tk[1]))
        if sn:
            for k2, v2 in sn.items():
                if q.seen.get(k2, 0) < v2:
                    q.seen[k2] = v2

    def _wait(self, q, tk):
        sem, val, key = tk
        if q.seen.get(key, 0) >= val:
            return
        q.eng.wait_ge(sem, val)
        self._learn(q, tk)

    def _deps(self, q, r, w, defer=False):
        deps = []
        for b in r:
            if b.lw is not None:
                deps.append(b.lw)
            if b.excl:
                for key, tk in b.rd.items():
                    if key != q.key:
                        deps.append(tk)
        for b in w:
            if b.lw is not None:
                deps.append(b.lw)
            for key, tk in b.rd.items():
                deps.append(tk)
        need = {}
        for tk in deps:
            if tk[2] == q.key and not q.self_sync:
                continue
            if q.seen.get(tk[2], 0) >= tk[1]:
                continue
            if tk[2] not in need or need[tk[2]][1] < tk[1]:
                need[tk[2]] = tk
        pend = list(need.values())
        last = pend.pop() if (defer and pend) else None
        for tk in pend:
            self._wait(q, tk)
        return last

    def _record(self, tk, r, w):
        for b in r:
            old = b.rd.get(tk[2])
            if old is None or old[1] < tk[1]:
                b.rd[tk[2]] = tk
        for b in w:
            b.lw = tk
            b.rd = {}

    def op(self, q, fn, r=(), w=(), inc=True):
        if self.counting:
            self.nops += 1
            if self.nops > DBG.get("maxops", 10 ** 9):
                return None
        last = self._deps(q, r, w, defer=(q is not self.pe))
        if q is self.pe:
            self.prox.partial = False
            ins = fn(self.prox)
            if self.prox.partial:
                inc = True
        else:
            ins = fn(q.eng)
            if last is not None:
                ins._wait_ge(last[0], last[1])
                self._learn(q, last)
        if inc:
            q.cnt += 1
            ins.then_inc(q.sem, 1)
            tk = (q.sem, q.cnt, q.key)
            self.snap[(q.key, q.cnt)] = dict(q.seen)
        else:
            tk = (q.sem, q.cnt + 1, q.key)
        if q is self.pe:
            self.pe_last_tk = tk if inc else None
        self._record(tk, r, w)
        return tk

    def dma(self, q, out, in_, r=(), w=()):
        self._deps(q, r, w)
        slot = q.ring[q.ring_i % len(q.ring)]
        q.ring_i += 1
        sem, n, key = slot
        if n > 0:
            self._wait(q, (sem, 16 * n, key))
        q.eng.dma_start(out=out, in_=in_).then_inc(sem, 16)
        slot[1] = n + 1
        tk = (sem, 16 * (n + 1), key)
        self.snap[(key, 16 * (n + 1))] = dict(q.seen)
        self._record(tk, r, w)
        self.dma_tickets.append(tk)
        return tk

    def barrier(self):
        tks = [(q.sem, q.cnt, q.key) for q in self.queues if q.cnt > 0]
        for q in self.queues:
            for slot in q.ring:
                if slot[1] > 0:
                    tks.append((slot[0], 16 * slot[1], slot[2]))
        for q in self.queues:
            for tk in tks:
                if tk[2] == q.key:
                    continue
                self._wait(q, tk)


def build_program(nlayers=DEPTH, dbg=False):
    nc = bass.Bass("TRN2", target_bir_lowering=False)
    dt_in = lambda name, shape: nc.dram_tensor(name, list(shape), F32, kind="ExternalInput").ap()
    h0 = dt_in("h0", [T, D])
    norm_pre = dt_in("norm_pre", [DEPTH, D])
    norm_post = dt_in("norm_post", [DEPTH, D])
    fox_w_in = dt_in("fox_w_in", [2, D, FOX_IN])
    fox_b_f = dt_in("fox_b_f", [2, NH])
    fox_w_out = dt_in("fox_w_out", [2, D, D])
    rwkv_w_in = dt_in("rwkv_w_in", [2, D, RWKV_IN])
    rwkv_mu = dt_in("rwkv_mu", [2, RWKV_IN])
    rwkv_w0 = dt_in("rwkv_w0", [2, D])
    rwkv_w_up = dt_in("rwkv_w_up", [2, 64, D])
    rwkv_a0 = dt_in("rwkv_a0", [2, D])
    rwkv_a_up = dt_in("rwkv_a_up", [2, 64, D])
    rwkv_k_k = dt_in("rwkv_k_k", [2, D])
    rwkv_k_a = dt_in("rwkv_k_a", [2, D])
    rwkv_r_k = dt_in("rwkv_r_k", [2, D])
    rwkv_ln_w = dt_in("rwkv_ln_w", [2, D])
    rwkv_ln_b = dt_in("rwkv_ln_b", [2, D])
    rwkv_w_out = dt_in("rwkv_w_out", [2, D, D])
    c_ident = dt_in("c_ident", [128, 128])
    c_tri = dt_in("c_tri", [128, 128])
    c_ones = dt_in("c_ones", [128, 128])
    c_m64 = dt_in("c_m64", [128, 128])
    c_scan = dt_in("c_scan", [128, 512])
    c_hind = dt_in("c_hind", [128, 2])
    c_m64x2 = dt_in("c_m64x2", [128, 256])
    c_mlow2 = dt_in("c_mlow2", [128, 128])
    c_idx2 = dt_in("c_idx2", [128, 128])
    c_bones = dt_in("c_bones", [128, 128])
    c_msc = dt_in("c_msc", [128, 512])
    c_mlow128 = dt_in("c_mlow128", [128, 256])
    c_id2 = dt_in("c_id2", [128, 256])
    y = nc.dram_tensor("y", [T, D], F32, kind="ExternalOutput").ap()

    with ExitStack() as st:
        K = KB(nc, st)
        pe, act, dve, pool, sp = K.pe, K.act, K.dve, K.pool, K.sp

        uT = K.sb(st, "uT", [128, 8, T + 2], BF16)
        uTb = Buf("uT")
        ogT = K.sb(st, "ogT", [128, 8, T], BF16)
        ogTb = Buf("ogT")
        ident = K.sb(st, "ident", [128, 128], BF16)
        identb = Buf()
        tri_f = K.sb(st, "tri_f", [128, 128], F32)
        ones_f = K.sb(st, "ones_f", [128, 128], F32)
        tri_b = K.sb(st, "tri_b", [128, 128], BF16)
        constb = Buf()
        gpre = K.sb(st, "gpre", [128, D], F32)
        gpost = K.sb(st, "gpost", [128, D], F32)
        gb = Buf()
        hbufs = [(K.sb(st, f"hb{i}", [128, D], F32), Buf()) for i in range(2)]
        hbufs2 = hbufs
        un = K.sb(st, "un", [128, D], BF16)
        unb = Buf()
        un2 = K.sb(st, "un2", [128, D], BF16)
        uns = [(un, unb), (un2, Buf())]
        sss = [(K.sb(st, f"ss{i}", [128, 4], F32), Buf()) for i in range(2)]
        mt = K.sb(st, "mt", [128, D], F32)
        mtb = Buf()
        ss = K.sb(st, "ss", [128, 4], F32)
        ssb = Buf()
        epsb = K.sb(st, "epsb", [128, 4], F32)
        K.op(dve, lambda e: e.memset(epsb[:, 0:1], NORM_EPS), w=[ssb])
        K.op(dve, lambda e: e.memset(epsb[:, 1:2], 1.0), w=[ssb])
        K.op(dve, lambda e: e.memset(epsb[:, 2:3], GN_EPS), w=[ssb])
        psA = [(K.psum(st, f"psA{i}", [128, 512], F32), Buf(excl=True)) for i in range(6)]
        psT = (K.psum(st, "psT", [128, 8, 128], BF16), Buf(excl=True))
        psT2 = (K.psum(st, "psT2", [128, 8, 128], BF16), Buf(excl=True))
        psTs = [psT, psT2]
        hB = [Buf(f"h{i}") for i in range(NT)]

        K.dma(pool, ident[:], c_ident[:, :], w=[identb])
        K.dma(pool, tri_b[:], c_tri[:, :], w=[constb])
        K.dma(sp, tri_f[:], c_tri[:, :], w=[constb])
        K.dma(sp, ones_f[:], c_ones[:, :], w=[constb])
        K.op(dve, lambda e: e.memset(uT[:, :, 0:1], 0.0), w=[uTb])

        def bcast_row(ap_row, n):
            return ap_row.partition_broadcast(128)

        def phase_prenorm(layer):
            K.dma(sp, gpre[:], bcast_row(norm_pre[layer, :], D), w=[gb])
            K.dma(sp, gpost[:], bcast_row(norm_post[layer, :], D), w=[gb])
            for i in range(NT):
                ht, htb = hbufs[i % 2]
                ss_, ssb_ = sss[i % 2]
                un_, unb_ = uns[i % 2]
                pT = psTs[i % 2]
                src = h0 if layer == 0 else y
                K.dma(sp, ht[:], src[128 * i:128 * i + 128, :], r=[hB[i]], w=[htb])
                K.op(act, lambda e: e.activation(out=mt[:], in_=ht[:], func=AF.Square, accum_out=ss_[:, 0:1]),
                     r=[htb], w=[mtb, ssb_])
                K.op(act, lambda e: e.activation(out=ss_[:, 1:2], in_=ss_[:, 0:1], func=AF.Ln, bias=epsb[:, 0:1], scale=1.0 / D),
                     r=[ssb_, ssb], w=[ssb_])
                K.op(act, lambda e: e.activation(out=ss_[:, 2:3], in_=ss_[:, 1:2], func=AF.Exp, scale=-0.5),
                     r=[ssb_], w=[ssb_])
                K.op(dve, lambda e: e.scalar_tensor_tensor(out=un_[:], in0=ht[:], scalar=ss_[:, 2:3], in1=gpre[:],
                                                           op0=ALU.mult, op1=ALU.mult), r=[htb, ssb_, gb], w=[unb_])
                for c in range(8):
                    K.op(pe, lambda e: e.transpose(out=pT[0][:, c, :], in_=un_[:, 128 * c:128 * c + 128],
                                                   identity=ident[:]),
                         r=[unb_, identb], w=[pT[1]], inc=(c == 7))
                if i % 2 == 0:
                    K.op(act, lambda e: e.copy(out=uT[:, :, 1 + 128 * i:1 + 128 * i + 128], in_=pT[0][:, :, :]),
                         r=[pT[1]], w=[uTb])
                else:
                    K.op(dve, lambda e: e.tensor_copy(out=uT[:, :, 1 + 128 * i:1 + 128 * i + 128], in_=pT[0][:, :, :]),
                         r=[pT[1]], w=[uTb])

        def phase_post(layer, w_out_ap, last):
            with ExitStack() as pst:
                wout = K.sb(pst, "wout", [128, 8, D], BF16)
                woutb = Buf()
                K.dma(pool, wout[:], w_out_ap.rearrange("(c p) n -> p c n", p=128), w=[woutb])
                for i in range(NT):
                    pms = [psA[0], psA[1]] if i % 2 == 0 else [psA[2], psA[3]]
                    ss_, ssb_ = sss[i % 2]
                    for hf in range(2):
                        for c in range(8):
                            K.op(pe, lambda e: e.matmul(pms[hf][0][:, :], lhsT=ogT[:, c, 128 * i:128 * i + 128],
                                                        rhs=wout[:, c, 512 * hf:512 * hf + 512],
                                                        start=(c == 0), stop=(c == 7)),
                                 r=[ogTb, woutb], w=[pms[hf][1]], inc=(c == 7))
                    for hf in range(2):
                        K.op(act, lambda e: e.activation(out=un[:, 512 * hf:512 * hf + 512], in_=pms[hf][0][:, :], func=AF.Square,
                                                         accum_out=ss_[:, hf:hf + 1]), r=[pms[hf][1]], w=[unb, ssb_])
                    K.op(dve, lambda e: e.tensor_tensor(out=ss_[:, 2:3], in0=ss_[:, 0:1], in1=ss_[:, 1:2], op=ALU.add),
                         r=[ssb_], w=[ssb_])
                    K.op(act, lambda e: e.activation(out=ss_[:, 3:4], in_=ss_[:, 2:3], func=AF.Ln, bias=epsb[:, 0:1], scale=1.0 / D),
                         r=[ssb_, ssb], w=[ssb_])
                    K.op(act, lambda e: e.activation(out=ss_[:, 3:4], in_=ss_[:, 3:4], func=AF.Exp, scale=-0.5),
                         r=[ssb_], w=[ssb_])
                    ht, htb = hbufs2[i % 2]
                    src = h0 if layer == 0 else y
                    K.dma(sp, ht[:], src[128 * i:128 * i + 128, :], r=[hB[i]], w=[htb])
                    for hf in range(2):
                        K.op(dve, lambda e: e.scalar_tensor_tensor(out=mt[:, 512 * hf:512 * hf + 512], in0=pms[hf][0][:, :], scalar=ss_[:, 3:4],
                                                                   in1=gpost[:, 512 * hf:512 * hf + 512], op0=ALU.mult, op1=ALU.mult),
                             r=[pms[hf][1], ssb_, gb], w=[mtb])
                    K.op(dve, lambda e: e.tensor_tensor(out=ht[:], in0=ht[:], in1=mt[:], op=ALU.add),
                         r=[htb, mtb], w=[htb])
                    K.dma(sp, y[128 * i:128 * i + 128, :], ht[:], r=[htb], w=[hB[i]])
                K.barrier()

        def fox_layer(layer):
            j = layer // 2
            win = fox_w_in[j].rearrange("(c p) n -> p c n", p=128)
            with ExitStack() as ls:
                wb = [[(K.sb(ls, f"fw{s}{t}", [128, 8, 256], BF16), Buf()) for t in range(4)] for s in range(2)]
                wf = K.sb(ls, "fwf", [128, 8, 16], BF16)
                wfb = Buf()
                qT = K.sb(ls, "qaug", [128, 4, T], BF16)
                kT = K.sb(ls, "kaug", [128, 4, T], BF16)
                negcum = K.sb(ls, "negcum", [128, NT, NH], F32)
                sgT = K.sb(ls, "sgT", [128, 2, T], BF16)
                qTb, kTb, sgTb = Buf(), Buf(), Buf()
                vaug = K.sb(ls, "vaug", [128, NT, 4, 128], BF16)
                vaugb = Buf()
                bft = K.sb(ls, "bft", [128, NH], F32)
                lf = K.sb(ls, "lf", [128, NT, NH], F32)
                lfb = Buf()
                cum = K.sb(ls, "cum", [128, NT, NH], F32)
                carry = K.sb(ls, "carry", [128, NT, NH], F32)
                cumb = Buf()
                bias = [(K.sb(ls, f"bias{i}", [128, NT], F32), Buf()) for i in range(8)]
                PT = [(K.sb(ls, f"PT{i}", [128, 512], BF16), Buf()) for i in range(3)]
                rs = [(K.sb(ls, f"rs{i}", [128, 512], F32), Buf()) for i in range(2)]
                tmpo = [(K.sb(ls, f"tmpo{i}", [128, 512], F32), Buf()) for i in range(2)]

                def load_group(g, s):
                    for t in range(4):
                        K.dma(pool, wb[s][t][0][:], win[:, :, 1024 * t + 256 * g:1024 * t + 256 * g + 256],
                              w=[wb[s][t][1]])
                K.dma(pool, wf[:], win[:, :, 4096:4112], w=[wfb])
                load_group(0, 0)
                K.dma(sp, bft[:], fox_b_f[j, :].partition_broadcast(128), w=[lfb])
                K.op(dve, lambda e: e.memset(vaug[:], 1.0), w=[vaugb])

                pf = psA[4]
                for i in range(NT):
                    for c in range(8):
                        K.op(pe, lambda e: e.matmul(pf[0][:, 16 * i:16 * i + 16], lhsT=uT[:, c, 1 + 128 * i:1 + 128 * i + 128],
                                                    rhs=wf[:, c, :], start=(c == 0), stop=(c == 7)),
                             r=[uTb, wfb], w=[pf[1]], inc=(c == 7))
                for i in range(NT):
                    K.op(dve, lambda e: e.tensor_tensor(out=lf[:, i, :], in0=pf[0][:, 16 * i:16 * i + 16], in1=bft[:],
                                                        op=ALU.add), r=[pf[1], lfb], w=[lfb])
                lf2 = lf[:].rearrange("p a b -> p (a b)")
                K.op(act, lambda e: e.activation(out=lf2, in_=lf2, func=AF.Exp, scale=-1.0), r=[lfb], w=[lfb])
                K.op(act, lambda e: e.activation(out=lf2, in_=lf2, func=AF.Ln, bias=epsb[:, 1:2], scale=1.0), r=[lfb, ssb], w=[lfb])
                K.op(dve, lambda e: e.tensor_scalar(out=lf2, in0=lf2, scalar1=-1.0, scalar2=None, op0=ALU.mult),
                     r=[lfb], w=[lfb])
                pc, pl = psA[2], psA[3]
                K.op(dve, lambda e: e.memset(carry[:, 0, :], 0.0), w=[cumb])
                for i in range(NT):
                    for jj in range(i):
                        K.op(pe, lambda e: e.matmul(pc[0][:, 16 * i:16 * i + 16], lhsT=ones_f[:], rhs=lf[:, jj, :],
                                                    start=(jj == 0), stop=(jj == i - 1)),
                             r=[lfb, constb], w=[pc[1]], inc=(jj == i - 1))
                    K.op(pe, lambda e: e.matmul(pl[0][:, 16 * i:16 * i + 16], lhsT=tri_f[:], rhs=lf[:, i, :],
                                                start=True, stop=True), r=[lfb, constb], w=[pl[1]])
                K.op(dve, lambda e: e.tensor_copy(out=carry[:, 1:NT, :].rearrange("p a b -> p (a b)"),
                                                  in_=pc[0][:, 16:16 * NT]), r=[pc[1]], w=[cumb])
                K.op(dve, lambda e: e.tensor_tensor(out=cum[:].rearrange("p a b -> p (a b)"),
                                                    in0=pl[0][:, 0:16 * NT],
                                                    in1=carry[:].rearrange("p a b -> p (a b)"), op=ALU.add),
                     r=[pl[1], cumb], w=[cumb])

                K.op(dve, lambda e: e.tensor_scalar(out=negcum[:].rearrange("p a b -> p (a b)"),
                                                    in0=cum[:].rearrange("p a b -> p (a b)"), scalar1=-1.0, scalar2=None,
                                                    op0=ALU.mult), r=[cumb], w=[cumb])
                K.op(dve, lambda e: e.memset(kT[64:65, :, :], 1.0), w=[kTb])
                chunks = tok_chunks(512)
                pcount = [0]

                def nextps():
                    p = psA[pcount[0] % 2]
                    pcount[0] += 1
                    return p

                stc = [0]
                otc = [0]
                ptc = [0]
                bc = [0]
                for g in range(4):
                    s = g % 2
                    if g + 1 < 4:
                        load_group(g + 1, (g + 1) % 2)
                    wq, wk, wv, wg = [wb[s][t] for t in range(4)]
                    for pp in range(2):
                        for (t0, n) in chunks:
                            for (wt, dst, dstb, kind) in ((wq, qT, qTb, 0), (wk, kT, kTb, 1), (wg, sgT, sgTb, 2)):
                                p = nextps()
                                for c in range(8):
                                    K.op(pe, lambda e: e.matmul(p[0][:, 0:n], lhsT=wt[0][:, c, 128 * pp:128 * pp + 128],
                                                                rhs=uT[:, c, 1 + t0:1 + t0 + n],
                                                                start=(c == 0), stop=(c == 7)),
                                         r=[uTb, wt[1]], w=[p[1]], inc=(c == 7))
                                if kind == 0:
                                    for hf_ in range(2):
                                        K.op(act, lambda e: e.activation(out=dst[0:64, 2 * pp + hf_, t0:t0 + n],
                                                                         in_=p[0][64 * hf_:64 * hf_ + 64, 0:n],
                                                                         func=AF.Copy, scale=HD ** -0.5),
                                             r=[p[1]], w=[dstb])
                                elif kind == 1:
                                    for hf_ in range(2):
                                        K.op(dve, lambda e: e.tensor_copy(out=dst[0:64, 2 * pp + hf_, t0:t0 + n],
                                                                          in_=p[0][64 * hf_:64 * hf_ + 64, 0:n]),
                                             r=[p[1]], w=[dstb])
                                else:
                                    K.op(act, lambda e: e.activation(out=dst[:, pp, t0:t0 + n], in_=p[0][:, 0:n],
                                                                     func=AF.Silu), r=[p[1]], w=[dstb])
                    for hh_ in range(4):
                        for i_ in range(NT):
                            K.op(dve, lambda e: e.tensor_scalar(out=qT[64:65, hh_, 128 * i_:128 * i_ + 128], in0=ones_f[64:65, 0:128],
                                                                scalar1=carry[64:65, i_, 4 * g + hh_:4 * g + hh_ + 1], scalar2=None,
                                                                op0=ALU.mult), r=[cumb, constb], w=[qTb])
                    for i in range(NT):
                        p = nextps()
                        for c in range(8):
                            K.op(pe, lambda e: e.matmul(p[0][:, 0:256], lhsT=uT[:, c, 1 + 128 * i:1 + 128 * i + 128],
                                                        rhs=wv[0][:, c, :], start=(c == 0), stop=(c == 7)),
                                 r=[uTb, wv[1]], w=[p[1]], inc=(c == 7))
                        pv = p[0][:, 0:256].rearrange("p (a b c) -> p a b c", a=2, b=2)
                        K.op(dve, lambda e: e.tensor_copy(out=vaug[:, i, 0:4:2, 0:64], in_=pv[:, :, 0, :]),
                             r=[p[1]], w=[vaugb])
                        K.op(dve, lambda e: e.tensor_copy(out=vaug[:, i, 1:4:2, 64:128], in_=pv[:, :, 1, :]),
                             r=[p[1]], w=[vaugb])
                    items = []
                    for hh in range(4):
                        for cidx, (q0, qn) in enumerate(chunks):
                            cx = dict(hh=hh, q0=q0, qn=qn, i0=q0 // 128, ni=qn // 128, started=False)
                            cx["jmax"] = cx["i0"] + cx["ni"] - 1
                            for jk in range(cx["jmax"] + 1):
                                items.append((cx, jk))

                    def emit_st(it):
                        cx, jk = it
                        hh = cx["hh"]
                        h = 4 * g + hh
                        pp, half = hh // 2, hh % 2
                        lo = 64 * half
                        q0, qn, i0, ni = cx["q0"], cx["qn"], cx["i0"], cx["ni"]
                        if not cx["started"]:
                            cx["started"] = True
                            cx["ot"] = psA[4 + otc[0] % 2]
                            otc[0] += 1
                        qs = max(q0, 128 * jk)
                        n = q0 + qn - qs
                        stp = psA[2 + stc[0] % 2]
                        stc[0] += 1
                        K.op(pe, lambda e: e.matmul(stp[0][:, 0:n], lhsT=kT[0:65, hh, 128 * jk:128 * jk + 128],
                                                    rhs=qT[0:65, hh, qs:qs + n], start=True, stop=True),
                             r=[kTb, qTb], w=[stp[1]])
                        return stp

                    def emit_rest(it, stp):
                        cx, jk = it
                        hh = cx["hh"]
                        pp, half = hh // 2, hh % 2
                        olo, slo = (0, 64) if half == 0 else (64, 0)
                        q0, qn, i0, ni, jmax = cx["q0"], cx["qn"], cx["i0"], cx["ni"], cx["jmax"]
                        ot = cx["ot"]
                        qs = max(q0, 128 * jk)
                        n = q0 + qn - qs
                        pt = PT[ptc[0] % 3]
                        ptc[0] += 1
                        h_ = 4 * g + hh
                        K.op(act, lambda e: e.activation(out=pt[0][:, 0:n], in_=stp[0][:, 0:n], func=AF.Exp,
                                                         bias=negcum[:, jk, h_:h_ + 1], scale=1.0),
                             r=[stp[1], cumb], w=[pt[1]])
                        if jk >= i0:
                            K.op(dve, lambda e: e.tensor_tensor(out=pt[0][:, 0:128], in0=pt[0][:, 0:128],
                                                                in1=tri_b[:], op=ALU.mult),
                                 r=[pt[1], constb], w=[pt[1]])
                        K.op(pe, lambda e: e.matmul(ot[0][:, qs - q0:qs - q0 + n], lhsT=vaug[:, jk, hh, :],
                                                    rhs=pt[0][:, 0:n], start=(jk == 0), stop=(jk == jmax),
                                                    skip_group_check=True),
                             r=[vaugb, pt[1]], w=[ot[1]])
                        if jk == jmax:
                            r_ = rs[otc[0] % 2]
                            tm = tmpo[otc[0] % 2]
                            K.op(dve, lambda e: e.reciprocal(out=r_[0][olo:olo + 64, 0:qn], in_=ot[0][slo:slo + 64, 0:qn]),
                                 r=[ot[1]], w=[r_[1]])
                            K.op(dve, lambda e: e.tensor_tensor(out=tm[0][olo:olo + 64, 0:qn], in0=ot[0][olo:olo + 64, 0:qn],
                                                                in1=r_[0][olo:olo + 64, 0:qn], op=ALU.mult),
                                 r=[ot[1], r_[1]], w=[tm[1]])
                            K.op(dve, lambda e: e.tensor_tensor(out=ogT[olo:olo + 64, 2 * g + pp, q0:q0 + qn],
                                                                in0=tm[0][olo:olo + 64, 0:qn],
                                                                in1=sgT[olo:olo + 64, pp, q0:q0 + qn], op=ALU.mult),
                                 r=[tm[1], sgTb], w=[ogTb])

                    nxt = emit_st(items[0])
                    for n_ in range(len(items)):
                        cur_st = nxt
                        if n_ + 1 < len(items):
                            nxt = emit_st(items[n_ + 1])
                        emit_rest(items[n_], cur_st)
                K.barrier()


        def rwkv_layer(layer):
            j = layer // 2
            win = rwkv_w_in[j].rearrange("(c p) n -> p c n", p=128)
            with ExitStack() as ls:
                SB = lambda n, shp, dt: K.sb(ls, n, shp, dt)
                wadT = SB("wadT", [128, T], BF16); wadTb = Buf()
                wup = SB("wup", [128, D], BF16); wupb = Buf()
                vecs = SB("vecs", [128, 8, 8], F32); vecb = Buf()
                mub = (SB("mub", [128, 2, 128], F32), Buf())
                wraw = (SB("wraw", [128, 8, 128], F32), Buf())
                bones = SB("bones", [128, 128], F32)
                msc = SB("msc", [128, 2, 256], BF16)
                mlow2 = SB("mlow2", [128, 2, 128], BF16)
                id2 = SB("id2", [128, 2, 128], BF16)
                hind = SB("hind", [128, 2], BF16)
                cb2 = Buf()

                class Ctx:
                    pass

                ctxs = []
                for ci in range(2):
                    C = Ctx()
                    C.ci = ci
                    F_ = lambda n: (SB(f"{n}{ci}", [128, 128], F32), Buf())
                    B_ = lambda n, shp: (SB(f"{n}{ci}", shp, BF16), Buf())
                    for n in ("rf", "kf", "sgw", "av", "lw", "cm", "cmx", "E1", "E2", "E3", "kkr", "sq", "hsn", "kk",
                              "t1", "kp", "bb", "ke3", "be3", "gs", "ytile"):
                        setattr(C, n, F_(n))
                    C.PS = []
                    for si in range(3):
                        S = Ctx()
                        S.sg = B_(f"sg{si}", [128, 128]); S.vtok = B_(f"vtok{si}", [128, 128])
                        S.AR = B_(f"AR{si}", [128, 2, 128]); S.BK = B_(f"BK{si}", [128, 2, 128])
                        S.ARz = B_(f"ARz{si}", [128, 2, 2, 128]); S.BKz = B_(f"BKz{si}", [128, 2, 128])
                        S.kbhat = B_(f"kbhat{si}", [128, 2, 128])
                        S.GC = (SB(f"GC{ci}{si}", [128, 2], F32), Buf())
                        S.bon = (SB(f"bon{ci}{si}", [128, 2], F32), Buf())
                        K.op(dve, lambda e: e.memset(S.ARz[0][:], 0.0), w=[S.ARz[1]])
                        K.op(dve, lambda e: e.memset(S.BKz[0][:], 0.0), w=[S.BKz[1]])
                        C.PS.append(S)
                    C.IS = []
                    for si in range(2):
                        S = Ctx()
                        S.MB = B_(f"MB{si}", [128, 2, 256]); S.MK = B_(f"MK{si}", [128, 2, 256])
                        S.TT = B_(f"TTf{si}", [128, 2, 128])
                        C.IS.append(S)
                    C.zlock = None
                    C.khT = B_("khT", [128, 128]); C.bhT = B_("bhT", [128, 128]); C.rkb = B_("rkb", [128, 128])
                    C.PQT = [B_(f"PQT{i}", [128, 2, 384]) for i in range(2)]
                    C.PQ0 = B_("PQ0", [128, 2, 384])
                    C.Xs = B_("Xs", [128, 128]); C.Us = B_("Us", [128, 128]); C.ybf = B_("ybf", [128, 128])
                    C.st6 = (SB(f"st6{ci}", [128, 2, 6], F32), Buf())
                    C.mv = (SB(f"mv{ci}", [128, 2, 2], F32), Buf())
                    C.rstd = (SB(f"rstd{ci}", [128, 2], F32), Buf())
                    C.Hs = SB(f"Hs{ci}", [128, 128], F32); C.Hb = SB(f"Hb{ci}", [128, 128], BF16)
                    C.Hsb = Buf(); C.Hbb = Buf()
                    C.wcp = [(SB(f"wcp{ci}{t_}", [128, 16, 128], BF16), Buf()) for t_ in range(4)]
                    C.lnw = SB(f"lnw{ci}", [128, 128], F32); C.lnb = SB(f"lnb{ci}", [128, 128], F32); C.lnbuf = Buf()
                    C.bX, C.bY, C.bZ = psA[3 * ci], psA[3 * ci + 1], psA[3 * ci + 2]
                    C.yn = C.ytile
                    C.prep_done = 0
                    C.inv_done = 0
                    C.back_done = 0
                    ctxs.append(C)

                K.dma(sp, bones[:], c_bones[:, :], w=[cb2])
                K.dma(pool, msc[:].rearrange("p a b -> p (a b)"), c_msc[:, :], w=[cb2])
                K.dma(pool, mlow2[:].rearrange("p a b -> p (a b)"), c_mlow128[:, :], w=[cb2])
                K.dma(pool, id2[:].rearrange("p a b -> p (a b)"), c_id2[:, :], w=[cb2])
                for C in ctxs:
                    K.op(act, lambda e: e.copy(out=C.PQ0[0][:, :, 256:384], in_=id2[:]), r=[cb2], w=[C.PQ0[1]])
                K.dma(pool, hind[:], c_hind[:, :], w=[cb2])
                K.dma(pool, wup[0:64, :], rwkv_w_up[j], w=[wupb])
                K.dma(pool, wup[64:128, :], rwkv_a_up[j], w=[wupb])
                with nc.allow_non_contiguous_dma(reason="tiny per-feature vectors"):
                    for vi, src in enumerate((rwkv_w0, rwkv_a0, rwkv_k_k, rwkv_k_a, rwkv_r_k)):
                        K.dma(sp, vecs[:, vi, :], src[j, :].rearrange("(c p) -> p c", p=128), w=[vecb])
                K.op(dve, lambda e: e.tensor_scalar(out=vecs[:, 5, :], in0=vecs[:, 3, :], scalar1=-1.0, scalar2=1.0,
                                                    op0=ALU.mult, op1=ALU.add), r=[vecb], w=[vecb])
                K.op(dve, lambda e: e.tensor_scalar(out=vecs[:, 6, :], in0=vecs[:, 0, :], scalar1=-1.0, scalar2=None,
                                                    op0=ALU.mult), r=[vecb], w=[vecb])
                K.op(dve, lambda e: e.tensor_scalar(out=vecs[:, 7, :], in0=vecs[:, 1, :], scalar1=-1.0, scalar2=None,
                                                    op0=ALU.mult), r=[vecb], w=[vecb])

                def load_w(col0, dst):
                    mb, rw = mub, wraw
                    K.dma(sp, mb[0][:, 0, :], rwkv_mu[j, col0:col0 + 128].partition_broadcast(128), w=[mb[1]])
                    K.dma(sp, rw[0][:], win[:, :, col0:col0 + 128], w=[rw[1]])
                    K.op(dve, lambda e: e.tensor_scalar(out=mb[0][:, 1, :], in0=mb[0][:, 0, :], scalar1=-1.0, scalar2=1.0,
                                                        op0=ALU.mult, op1=ALU.add), r=[mb[1]], w=[mb[1]])
                    for c in range(8):
                        K.op(dve, lambda e: e.tensor_tensor(out=dst[0][:, c, :], in0=rw[0][:, c, :], in1=mb[0][:, 1, :],
                                                            op=ALU.mult), r=[rw[1], mb[1]], w=[dst[1]])
                        K.op(dve, lambda e: e.tensor_tensor(out=dst[0][:, 8 + c, :], in0=rw[0][:, c, :], in1=mb[0][:, 0, :],
                                                            op=ALU.mult), r=[rw[1], mb[1]], w=[dst[1]])

                def proj_fm(out_ps, outb, wt, t0, n):
                    for c in range(16):
                        rhs = uT[:, c, 1 + t0:1 + t0 + n] if c < 8 else uT[:, c - 8, t0:t0 + n]
                        K.op(pe, lambda e: e.matmul(out_ps, lhsT=wt[0][:, c, :], rhs=rhs, start=(c == 0), stop=(c == 15)),
                             r=[uTb, wt[1]], w=[outb], inc=(c == 15))

                wwa = ctxs[0].wcp[0]
                load_w(4096, wwa)
                for ci_, (t0, n) in enumerate(tok_chunks(512)):
                    p_ = psA[ci_ % 2]
                    proj_fm(p_[0][:, 0:n], p_[1], wwa, t0, n)
                    K.op(act, lambda e: e.activation(out=wadT[0:64, t0:t0 + n], in_=p_[0][0:64, 0:n], func=AF.Tanh),
                         r=[p_[1]], w=[wadTb])
                    K.op(act, lambda e: e.copy(out=wadT[64:128, t0:t0 + n], in_=p_[0][64:128, 0:n]), r=[p_[1]], w=[wadTb])

                v3 = lambda t_: t_[0][:].rearrange("p (c s) -> p c s", c=2)
                flat = lambda ap: ap.rearrange("p a b -> p (a b)")
                one_b = epsb[:, 1:2]

                def sigmoid_chain(C, src_ps, srcb, bias_ap, dst, extra_r=()):
                    if bias_ap is None:
                        K.op(act, lambda e: e.activation(out=dst[0][:], in_=src_ps, func=AF.Exp, scale=-1.0),
                             r=[srcb] + list(extra_r), w=[dst[1]])
                    else:
                        K.op(act, lambda e: e.activation(out=dst[0][:], in_=src_ps, func=AF.Exp, bias=bias_ap, scale=-1.0),
                             r=[srcb] + list(extra_r), w=[dst[1]])
                    K.op(act, lambda e: e.activation(out=dst[0][:], in_=dst[0][:], func=AF.Ln, bias=one_b, scale=1.0),
                         r=[dst[1], ssb], w=[dst[1]])
                    K.op(act, lambda e: e.activation(out=dst[0][:], in_=dst[0][:], func=AF.Exp, scale=-1.0),
                         r=[dst[1]], w=[dst[1]])

                def prep(C, p):
                    vcol = lambda vi: vecs[:, vi, p:p + 1]
                    wr, wk, wv, wg = C.wcp
                    bX = bY = C.bZ
                    for i in range(NT):
                        while C.back_done < i - 2:
                            yield False
                        S = C.PS[i % 3]
                        while C.zlock is not None and C.zlock != "prep":
                            yield False
                        C.zlock = "prep"
                        t0 = 128 * i
                        rf, kf, sgw, av, lw, cm, cmx, E1, E2, E3 = C.rf, C.kf, C.sgw, C.av, C.lw, C.cm, C.cmx, C.E1, C.E2, C.E3
                        kkr, sq, hsn, kk, t1, kp, bb, ke3, be3, gs = C.kkr, C.sq, C.hsn, C.kk, C.t1, C.kp, C.bb, C.ke3, C.be3, C.gs
                        AR, BK = S.AR, S.BK
                        proj_fm(bX[0][:, 0:128], bX[1], wr, t0, 128)
                        proj_fm(bX[0][:, 128:256], bX[1], wk, t0, 128)
                        proj_fm(bX[0][:, 256:384], bX[1], wg, t0, 128)
                        for c in range(16):
                            lhsT = uT[:, c, 1 + t0:1 + t0 + 128] if c < 8 else uT[:, c - 8, t0:t0 + 128]
                            K.op(pe, lambda e: e.matmul(bX[0][:, 384:512], lhsT=lhsT, rhs=wv[0][:, c, :],
                                                        start=(c == 0), stop=(c == 15)),
                                 r=[uTb, wv[1]], w=[bX[1]], inc=(c == 15))
                        yield True
                        K.op(act, lambda e: e.copy(out=rf[0][:], in_=bX[0][:, 0:128]), r=[bX[1]], w=[rf[1]])
                        K.op(act, lambda e: e.copy(out=kf[0][:], in_=bX[0][:, 128:256]), r=[bX[1]], w=[kf[1]])
                        sigmoid_chain(C, bX[0][:, 256:384], bX[1], None, gs)
                        K.op(dve, lambda e: e.tensor_tensor(out=S.sg[0][:], in0=bX[0][:, 256:384], in1=gs[0][:], op=ALU.mult),
                             r=[bX[1], gs[1]], w=[S.sg[1]])
                        K.op(dve, lambda e: e.tensor_copy(out=S.vtok[0][:], in_=bX[0][:, 384:512]), r=[bX[1]], w=[S.vtok[1]])
                        K.op(pe, lambda e: e.matmul(bY[0][:, 0:128], lhsT=wup[0:64, 128 * p:128 * p + 128],
                                                    rhs=wadT[0:64, t0:t0 + 128], start=True, stop=True),
                             r=[wupb, wadTb], w=[bY[1]])
                        K.op(pe, lambda e: e.matmul(bY[0][:, 128:256], lhsT=wup[64:128, 128 * p:128 * p + 128],
                                                    rhs=wadT[64:128, t0:t0 + 128], start=True, stop=True),
                             r=[wupb, wadTb], w=[bY[1]])
                        yield True
                        sigmoid_chain(C, bY[0][:, 0:128], bY[1], vcol(6), sgw, extra_r=[vecb])
                        sigmoid_chain(C, bY[0][:, 128:256], bY[1], vcol(7), av, extra_r=[vecb])
                        K.op(dve, lambda e: e.tensor_scalar(out=lw[0][:], in0=sgw[0][:], scalar1=-DECAY_SCALE, scalar2=None,
                                                            op0=ALU.mult), r=[sgw[1]], w=[lw[1]])
                        K.op(dve, lambda e: e.tensor_tensor_scan(out=cm[0][:], data0=ones_f[:], data1=lw[0][:], initial=0.0,
                                                                 op0=ALU.mult, op1=ALU.add), r=[lw[1], constb], w=[cm[1]])
                        K.op(dve, lambda e: e.tensor_tensor(out=cmx[0][:], in0=cm[0][:], in1=lw[0][:], op=ALU.subtract),
                             r=[cm[1], lw[1]], w=[cmx[1]])
                        yield True
                        K.op(act, lambda e: e.activation(out=E1[0][:], in_=cm[0][:], func=AF.Exp), r=[cm[1]], w=[E1[1]])
                        K.op(act, lambda e: e.activation(out=E2[0][:], in_=cmx[0][:], func=AF.Exp), r=[cmx[1]], w=[E2[1]])
                        K.op(act, lambda e: e.activation(out=E3[0][:], in_=cm[0][:], func=AF.Exp, scale=-1.0),
                             r=[cm[1]], w=[E3[1]])
                        K.op(act, lambda e: e.activation(out=S.GC[0][:, 0:1], in_=cm[0][:, 127:128], func=AF.Exp),
                             r=[cm[1]], w=[S.GC[1]])
                        K.op(dve, lambda e: e.tensor_scalar(out=kkr[0][:], in0=kf[0][:], scalar1=vcol(2), scalar2=None,
                                                            op0=ALU.mult), r=[kf[1], vecb], w=[kkr[1]])
                        K.op(dve, lambda e: e.tensor_tensor(out=sq[0][:], in0=kkr[0][:], in1=kkr[0][:], op=ALU.mult),
                             r=[kkr[1]], w=[sq[1]])
                        K.op(pe, lambda e: e.matmul(bY[0][:, 256:384], lhsT=bones[:], rhs=sq[0][:], start=True, stop=True),
                             r=[cb2, sq[1]], w=[bY[1]])
                        yield True
                        K.op(dve, lambda e: e.tensor_scalar(out=hsn[0][:], in0=bY[0][:, 256:384], scalar1=1e-24, scalar2=None,
                                                            op0=ALU.max), r=[bY[1]], w=[hsn[1]])
                        K.op(act, lambda e: e.activation(out=hsn[0][:], in_=hsn[0][:], func=AF.Ln), r=[hsn[1]], w=[hsn[1]])
                        K.op(act, lambda e: e.activation(out=hsn[0][:], in_=hsn[0][:], func=AF.Exp, scale=-0.5),
                             r=[hsn[1]], w=[hsn[1]])
                        K.op(dve, lambda e: e.tensor_scalar(out=t1[0][:], in0=av[0][:], scalar1=vcol(3), scalar2=vcol(5),
                                                            op0=ALU.mult, op1=ALU.add), r=[av[1], vecb], w=[t1[1]])
                        K.op(dve, lambda e: e.tensor_tensor(out=kp[0][:], in0=kf[0][:], in1=t1[0][:], op=ALU.mult),
                             r=[kf[1], t1[1]], w=[kp[1]])
                        K.op(dve, lambda e: e.tensor_tensor(out=AR[0][:, 1, :], in0=rf[0][:], in1=E1[0][:], op=ALU.mult),
                             r=[rf[1], E1[1]], w=[AR[1]])
                        K.op(dve, lambda e: e.tensor_tensor(out=ke3[0][:], in0=kp[0][:], in1=E3[0][:], op=ALU.mult),
                             r=[kp[1], E3[1]], w=[ke3[1]])
                        yield True
                        K.op(dve, lambda e: e.tensor_tensor(out=kk[0][:], in0=kkr[0][:], in1=hsn[0][:], op=ALU.mult),
                             r=[kkr[1], hsn[1]], w=[kk[1]])
                        K.op(dve, lambda e: e.tensor_tensor(out=bb[0][:], in0=kk[0][:], in1=av[0][:], op=ALU.mult),
                             r=[kk[1], av[1]], w=[bb[1]])
                        K.op(dve, lambda e: e.scalar_tensor_tensor(out=AR[0][:, 0, :], in0=kk[0][:], scalar=-1.0, in1=E2[0][:],
                                                                   op0=ALU.mult, op1=ALU.mult), r=[kk[1], E2[1]], w=[AR[1]])
                        K.op(dve, lambda e: e.tensor_tensor(out=be3[0][:], in0=bb[0][:], in1=E3[0][:], op=ALU.mult),
                             r=[bb[1], E3[1]], w=[be3[1]])
                        K.op(act, lambda e: e.copy(out=BK[0][:, 1, :], in_=ke3[0][:]), r=[ke3[1]], w=[BK[1]])
                        K.op(act, lambda e: e.copy(out=BK[0][:, 0, :], in_=be3[0][:]), r=[be3[1]], w=[BK[1]])
                        yield True
                        K.op(act, lambda e: e.copy(out=S.ARz[0][0:64, 0, :, :], in_=AR[0][0:64, :, :]), r=[AR[1]], w=[S.ARz[1]])
                        K.op(act, lambda e: e.copy(out=S.ARz[0][64:128, 1, :, :], in_=AR[0][64:128, :, :]), r=[AR[1]], w=[S.ARz[1]])
                        K.op(act, lambda e: e.copy(out=S.BKz[0][0:64, 0, :], in_=BK[0][0:64, 0, :]), r=[BK[1]], w=[S.BKz[1]])
                        K.op(act, lambda e: e.copy(out=S.BKz[0][64:128, 1, :], in_=BK[0][64:128, 0, :]), r=[BK[1]], w=[S.BKz[1]])
                        K.op(dve, lambda e: e.tensor_scalar(out=C.khT[0][:], in0=ke3[0][:], scalar1=S.GC[0][:, 0:1], scalar2=None,
                                                            op0=ALU.mult), r=[ke3[1], S.GC[1]], w=[C.khT[1]])
                        K.op(dve, lambda e: e.tensor_scalar(out=C.bhT[0][:], in0=be3[0][:], scalar1=S.GC[0][:, 0:1], scalar2=None,
                                                            op0=ALU.mult), r=[be3[1], S.GC[1]], w=[C.bhT[1]])
                        K.op(dve, lambda e: e.scalar_tensor_tensor(out=C.rkb[0][:], in0=rf[0][:], scalar=vcol(4), in1=kp[0][:],
                                                                   op0=ALU.mult, op1=ALU.mult), r=[rf[1], kp[1], vecb], w=[C.rkb[1]])
                        K.op(pe, lambda e: e.matmul(bY[0][:, 384:386], lhsT=C.rkb[0][:], rhs=hind[:], start=True, stop=True),
                             r=[C.rkb[1], cb2], w=[bY[1]])
                        K.op(pe, lambda e: e.transpose(out=psT[0][:, 4 + 2 * C.ci, :], in_=C.khT[0][:], identity=ident[:]),
                             r=[C.khT[1], identb], w=[psT[1]], inc=False)
                        K.op(pe, lambda e: e.transpose(out=psT[0][:, 5 + 2 * C.ci, :], in_=C.bhT[0][:], identity=ident[:]),
                             r=[C.bhT[1], identb], w=[psT[1]])
                        yield True
                        K.op(dve, lambda e: e.tensor_copy(out=S.bon[0][:], in_=bY[0][:, 384:386]), r=[bY[1]], w=[S.bon[1]])
                        K.op(act, lambda e: e.copy(out=S.kbhat[0][:], in_=psT[0][:, 4 + 2 * C.ci:6 + 2 * C.ci, :]), r=[psT[1]], w=[S.kbhat[1]])
                        C.zlock = None
                        C.prep_done = i + 1
                        yield True

                def inv(C, p):
                    bX, bY = C.bX, C.bY
                    for i in range(NT):
                        while C.prep_done < i + 1 or C.back_done < i - 1:
                            yield False
                        S = C.PS[i % 3]
                        I_ = C.IS[i % 2]
                        AR, BK, MB, MK = S.AR, S.BK, I_.MB, I_.MK
                        arz = S.ARz[0][:].rearrange("p h s t -> p (h s t)")
                        K.op(pe, lambda e: e.matmul(bX[0][:, 0:512], lhsT=BK[0][:, 0, :], rhs=arz, start=True, stop=True),
                             r=[BK[1], S.ARz[1]], w=[bX[1]])
                        K.op(pe, lambda e: e.matmul(bY[0][:, 0:512], lhsT=BK[0][:, 1, :], rhs=arz, start=True, stop=True),
                             r=[BK[1], S.ARz[1]], w=[bY[1]])
                        yield True
                        P0 = C.PQ0
                        K.op(dve, lambda e: e.tensor_tensor(out=MB[0][:].rearrange("p h c -> p (h c)"), in0=bX[0][:, 0:512],
                                                            in1=msc[:].rearrange("p h c -> p (h c)"), op=ALU.mult),
                             r=[bX[1], cb2], w=[MB[1]])
                        K.op(pe, lambda e: e.matmul(bX[0][:, 0:256], lhsT=AR[0][:, 0, :], rhs=S.BKz[0][:].rearrange("p h s -> p (h s)"),
                                                    start=True, stop=True), r=[AR[1], S.BKz[1]], w=[bX[1]])
                        K.op(dve, lambda e: e.tensor_tensor(out=MK[0][:].rearrange("p h c -> p (h c)"), in0=bY[0][:, 0:512],
                                                            in1=msc[:].rearrange("p h c -> p (h c)"), op=ALU.mult),
                             r=[bY[1], cb2], w=[MK[1]])
                        yield True
                        K.op(dve, lambda e: e.tensor_tensor(out=P0[0][:, :, 0:128], in0=bX[0][:, 0:256].rearrange("p (h s) -> p h s", h=2),
                                                            in1=mlow2[:], op=ALU.mult), r=[bX[1], cb2], w=[P0[1]])
                        K.op(act, lambda e: e.copy(out=P0[0][:, :, 128:256], in_=MB[0][:, :, 0:128]), r=[MB[1]], w=[P0[1]])
                        yield True
                        for stp_ in range(1, 8):
                            prev = C.PQ0 if stp_ == 1 else C.PQT[(stp_ - 1) % 2]
                            cur = C.PQT[stp_ % 2]
                            last = (stp_ == 7)
                            for hd in range(2):
                                bk = bX if hd == 0 else bY
                                Pm = prev[0][:, hd, 0:128]
                                Qm = prev[0][:, hd, 128:256]
                                if not last:
                                    K.op(pe, lambda e: e.matmul(bk[0][:, 0:128], lhsT=Qm, rhs=Pm, start=True, stop=True),
                                         r=[prev[1]], w=[bk[1]], inc=False)
                                    K.op(pe, lambda e: e.matmul(bk[0][:, 128:384], lhsT=Pm, rhs=prev[0][:, hd, 128:384], start=True, stop=False),
                                         r=[prev[1]], w=[bk[1]], inc=False)
                                else:
                                    K.op(pe, lambda e: e.matmul(bk[0][:, 256:384], lhsT=Pm, rhs=prev[0][:, hd, 256:384], start=True, stop=False),
                                         r=[prev[1]], w=[bk[1]], inc=False)
                                K.op(pe, lambda e: e.matmul(bk[0][:, 256:384], lhsT=ident[:], rhs=prev[0][:, hd, 256:384], start=False, stop=True),
                                     r=[prev[1], identb], w=[bk[1]])
                            yield True
                            for hd in range(2):
                                bk = bX if hd == 0 else bY
                                if not last:
                                    dst_, src_ = cur[0][:, hd, :], bk[0][:, 0:384]
                                    dstb_ = cur[1]
                                else:
                                    dst_, src_ = I_.TT[0][:, hd, :], bk[0][:, 256:384]
                                    dstb_ = I_.TT[1]
                                if hd == 0 or stp_ in (2, 4, 6, 7):
                                    K.op(act, lambda e: e.copy(out=dst_, in_=src_), r=[bk[1]], w=[dstb_])
                                else:
                                    K.op(dve, lambda e: e.tensor_copy(out=dst_, in_=src_), r=[bk[1]], w=[dstb_])
                            yield True
                        C.inv_done = i + 1
                        yield True

                def back(C, p):
                    bZ = C.bZ
                    Hs, Hb, Hsb, Hbb = C.Hs, C.Hb, C.Hsb, C.Hbb
                    Xs, Us, ytile, yn, ybf = C.Xs, C.Us, C.ytile, C.yn, C.ybf
                    for i in range(NT):
                        while C.inv_done < i + 1:
                            yield False
                        S = C.PS[i % 3]
                        I_ = C.IS[i % 2]
                        AR, MB, MK, vtok, kbhat, GC, Tf = S.AR, I_.MB, I_.MK, S.vtok, S.kbhat, S.GC, I_.TT
                        while C.zlock is not None and C.zlock != "back":
                            yield False
                        C.zlock = "back"
                        t0 = 128 * i
                        hc = lambda hd: slice(64 * hd, 64 * hd + 64)
                        K.op(pe, lambda e: e.matmul(bZ[0][:, 0:128], lhsT=AR[0][:, 0, :], rhs=Hb[:], start=True, stop=False, skip_group_check=True),
                             r=[AR[1], Hbb], w=[bZ[1]], inc=False)
                        for hd in range(2):
                            K.op(pe, lambda e: e.matmul(bZ[0][:, hc(hd)], lhsT=MK[0][:, hd, 0:128], rhs=vtok[0][:, hc(hd)],
                                                        start=False, stop=(hd == 1), skip_group_check=True),
                                 r=[MK[1], vtok[1]], w=[bZ[1]], inc=(hd == 1))
                        yield True
                        K.op(act, lambda e: e.copy(out=Xs[0][:], in_=bZ[0][:, 0:128]), r=[bZ[1]], w=[Xs[1]])
                        yield True
                        for hd in range(2):
                            K.op(pe, lambda e: e.matmul(bZ[0][:, 128 + 64 * hd:128 + 64 * hd + 64], lhsT=Tf[0][:, hd, :], rhs=Xs[0][:, hc(hd)],
                                                        start=True, stop=True), r=[Tf[1], Xs[1]], w=[bZ[1]], inc=(hd == 1))
                        yield True
                        K.op(dve, lambda e: e.tensor_copy(out=Us[0][:], in_=bZ[0][:, 128:256]), r=[bZ[1]], w=[Us[1]])
                        yield True
                        K.op(pe, lambda e: e.matmul(bZ[0][:, 256:384], lhsT=AR[0][:, 1, :], rhs=Hb[:], start=True, stop=False, skip_group_check=True),
                             r=[AR[1], Hbb], w=[bZ[1]], inc=False)
                        for hd in range(2):
                            o_ = bZ[0][:, 256 + 64 * hd:256 + 64 * hd + 64]
                            K.op(pe, lambda e: e.matmul(o_, lhsT=MB[0][:, hd, 128:256], rhs=Us[0][:, hc(hd)], start=False, stop=False,
                                                        skip_group_check=True), r=[MB[1], Us[1]], w=[bZ[1]], inc=False)
                            K.op(pe, lambda e: e.matmul(o_, lhsT=MK[0][:, hd, 128:256], rhs=vtok[0][:, hc(hd)], start=False, stop=(hd == 1),
                                                        skip_group_check=True), r=[MK[1], vtok[1]], w=[bZ[1]], inc=False)
                        for hd in range(2):
                            o2 = bZ[0][:, 384 + 64 * hd:384 + 64 * hd + 64]
                            K.op(pe, lambda e: e.matmul(o2, lhsT=kbhat[0][:, 1, :], rhs=Us[0][:, hc(hd)], start=True, stop=False),
                                 r=[kbhat[1], Us[1]], w=[bZ[1]], inc=False)
                            K.op(pe, lambda e: e.matmul(o2, lhsT=kbhat[0][:, 0, :], rhs=vtok[0][:, hc(hd)], start=False, stop=True),
                                 r=[kbhat[1], vtok[1]], w=[bZ[1]], inc=(hd == 1))
                        yield True
                        for hd in range(2):
                            lo = 64 * hd
                            K.op(dve, lambda e: e.scalar_tensor_tensor(out=Hs[lo:lo + 64, hc(hd)], in0=Hs[lo:lo + 64, hc(hd)], scalar=GC[0][lo:lo + 64, 0:1],
                                                                       in1=bZ[0][lo:lo + 64, 384 + 64 * hd:384 + 64 * hd + 64],
                                                                       op0=ALU.mult, op1=ALU.add), r=[Hsb, GC[1], bZ[1]], w=[Hsb])
                        K.op(dve, lambda e: e.tensor_copy(out=Hb[:], in_=Hs[:]), r=[Hsb], w=[Hbb])
                        K.op(dve, lambda e: e.tensor_copy(out=ytile[0][:], in_=bZ[0][:, 256:384]), r=[bZ[1]], w=[ytile[1]])
                        C.zlock = None
                        yield True
                        for hf in range(2):
                            K.op(dve, lambda e: e.bn_stats(out=C.st6[0][:, hf, :], in_=ytile[0][:, 64 * hf:64 * hf + 64]), r=[ytile[1]], w=[C.st6[1]])
                            K.op(dve, lambda e: e.bn_aggr(out=C.mv[0][:, hf, :], in_=C.st6[0][:, hf, :]), r=[C.st6[1]], w=[C.mv[1]])
                        yield True
                        K.op(act, lambda e: e.activation(out=C.rstd[0][:], in_=C.mv[0][:, :, 1], func=AF.Ln, bias=epsb[:, 2:3], scale=1.0),
                             r=[C.mv[1], ssb], w=[C.rstd[1]])
                        K.op(act, lambda e: e.activation(out=C.rstd[0][:], in_=C.rstd[0][:], func=AF.Exp, scale=-0.5), r=[C.rstd[1]], w=[C.rstd[1]])
                        yield True
                        for hf in range(2):
                            cs = slice(64 * hf, 64 * hf + 64)
                            K.op(dve, lambda e: e.tensor_scalar(out=yn[0][:, cs], in0=ytile[0][:, cs], scalar1=C.mv[0][:, hf, 0:1],
                                                                scalar2=C.rstd[0][:, hf:hf + 1], op0=ALU.subtract, op1=ALU.mult),
                                 r=[ytile[1], C.mv[1], C.rstd[1]], w=[yn[1]])
                        K.op(dve, lambda e: e.tensor_tensor(out=yn[0][:], in0=yn[0][:], in1=C.lnw[:], op=ALU.mult),
                             r=[yn[1], C.lnbuf], w=[yn[1]])
                        K.op(dve, lambda e: e.tensor_tensor(out=yn[0][:], in0=yn[0][:], in1=C.lnb[:], op=ALU.add),
                             r=[yn[1], C.lnbuf], w=[yn[1]])
                        for hf in range(2):
                            cs = slice(64 * hf, 64 * hf + 64)
                            K.op(dve, lambda e: e.scalar_tensor_tensor(out=ybf[0][:, cs], in0=vtok[0][:, cs], scalar=S.bon[0][:, hf:hf + 1],
                                                                       in1=yn[0][:, cs], op0=ALU.mult, op1=ALU.add),
                                 r=[vtok[1], S.bon[1], yn[1]], w=[ybf[1]])
                        yield True
                        K.op(pe, lambda e: e.transpose(out=psT[0][:, 2 + C.ci, :], in_=ybf[0][:], identity=ident[:]), r=[ybf[1], identb], w=[psT[1]])
                        yield True
                        K.op(dve, lambda e: e.tensor_tensor(out=ogT[:, p, t0:t0 + 128], in0=psT[0][:, 2 + C.ci, :], in1=S.sg[0][:], op=ALU.mult),
                             r=[psT[1], S.sg[1]], w=[ogTb])
                        C.back_done = i + 1
                        yield True

                def pair_stream(C):
                    for p in range(C.ci, 8, 2):
                        for t_ in range(4):
                            load_w(1024 * t_ + 128 * p, C.wcp[t_])
                            yield True
                        K.dma(sp, C.lnw[:], rwkv_ln_w[j, 128 * p:128 * p + 128].partition_broadcast(128), w=[C.lnbuf])
                        K.dma(sp, C.lnb[:], rwkv_ln_b[j, 128 * p:128 * p + 128].partition_broadcast(128), w=[C.lnbuf])
                        K.op(dve, lambda e: e.memset(C.Hs[:], 0.0), w=[C.Hsb])
                        K.op(dve, lambda e: e.memset(C.Hb[:], 0.0), w=[C.Hbb])
                        C.prep_done = 0
                        C.inv_done = 0
                        C.back_done = 0
                        C.zlock = None
                        gens = [prep(C, p), inv(C, p), back(C, p)]
                        while gens:
                            progressed = False
                            for g_ in list(gens):
                                try:
                                    if next(g_):
                                        progressed = True
                                except StopIteration:
                                    gens.remove(g_)
                                    progressed = True
                            yield progressed

                streams = [pair_stream(C) for C in ctxs]
                while streams:
                    for s_ in list(streams):
                        try:
                            next(s_)
                        except StopIteration:
                            streams.remove(s_)
                K.barrier()


        for layer in range(nlayers):
            phase_prenorm(layer)
            if layer % 2 == 0:
                fox_layer(layer)
                phase_post(layer, fox_w_out[layer // 2], layer == nlayers - 1)
            else:
                rwkv_layer(layer)
                phase_post(layer, rwkv_w_out[layer // 2], layer == nlayers - 1)
            K.barrier()
        K.barrier()
    return nc


_CACHE = {}


def _consts():
    idx = np.arange(128)
    tri = (idx[:, None] <= idx[None, :]).astype(np.float32)
    ident = np.eye(128, dtype=np.float32)
    ones = np.ones((128, 128), np.float32)
    m64 = np.zeros((128, 128), np.float32)
    s = idx[:, None] % 64
    t = idx[None, :] % 64
    m64[:, 0:64] = (s < t)[:, 0:64]
    m64[:, 64:128] = (s <= t)[:, 64:128]
    scan = np.ones((128, 512), np.float32)
    scan[:, ::64] = 0.0
    hind = np.zeros((128, 2), np.float32)
    hind[0:64, 0] = 1.0
    hind[64:128, 1] = 1.0
    m64x2 = np.concatenate([m64, m64], axis=1)
    r64 = idx[:, None] % 64
    c64 = np.arange(64)[None, :]
    mlow = (c64 < r64).astype(np.float32)
    mlow2 = np.concatenate([mlow, mlow], axis=1)
    i64 = (c64 == r64).astype(np.float32)
    idx2 = np.concatenate([i64, i64], axis=1)
    bones = ((idx[:, None] // 64) == (idx[None, :] // 64)).astype(np.float32)
    strict = (idx[:, None] < idx[None, :]).astype(np.float32)
    msc1 = np.concatenate([strict, tri], axis=1)
    msc = np.concatenate([msc1, msc1], axis=1)
    mlow128 = (idx[None, :] < idx[:, None]).astype(np.float32)
    return dict(c_ident=ident, c_tri=tri, c_ones=ones, c_m64=m64, c_scan=scan, c_hind=hind,
                c_m64x2=m64x2, c_mlow2=mlow2, c_idx2=idx2, c_bones=bones,
                c_msc=msc, c_mlow128=np.concatenate([mlow128, mlow128], axis=1),
                c_id2=np.concatenate([ident, ident], axis=1))


def kernel(x, meta_tokens, norm_pre, norm_post, fox_w_in, fox_b_f, fox_w_out,
           rwkv_w_in, rwkv_mu, rwkv_w0, rwkv_w_up, rwkv_a0, rwkv_a_up, rwkv_k_k,
           rwkv_k_a, rwkv_r_k, rwkv_ln_w, rwkv_ln_b, rwkv_w_out, _nlayers=DEPTH):
    f = lambda a: np.ascontiguousarray(np.asarray(a, dtype=np.float32))
    x = f(x)
    B = x.shape[0]
    meta = f(meta_tokens)
    h0 = np.zeros((B, T, D), np.float32)
    h0[:, :NMETA] = meta[None]
    h0[:, NMETA:NMETA + SEQ] = x
    shared = dict(
        norm_pre=f(norm_pre), norm_post=f(norm_post), fox_w_in=f(fox_w_in), fox_b_f=f(fox_b_f),
        fox_w_out=f(fox_w_out), rwkv_w_in=f(rwkv_w_in), rwkv_mu=f(rwkv_mu), rwkv_w0=f(rwkv_w0),
        rwkv_w_up=f(rwkv_w_up), rwkv_a0=f(rwkv_a0), rwkv_a_up=f(rwkv_a_up), rwkv_k_k=f(rwkv_k_k),
        rwkv_k_a=f(rwkv_k_a), rwkv_r_k=f(rwkv_r_k).reshape(2, D), rwkv_ln_w=f(rwkv_ln_w),
        rwkv_ln_b=f(rwkv_ln_b), rwkv_w_out=f(rwkv_w_out))
    shared.update(_consts())
    key = _nlayers
    if key not in _CACHE:
        _CACHE[key] = build_program(_nlayers)
    nc = _CACHE[key]
    in_maps = []
    for b in range(B):
        m = dict(shared)
        m["h0"] = h0[b]
        in_maps.append(m)
    res = run_bass_kernel_spmd(nc, in_maps, core_ids=list(range(B)))
    out = np.stack([np.asarray(r["y"])[NMETA:NMETA + SEQ] for r in res.results], axis=0)
    return out.astype(np.float32)
```
